# Optimizing a Trainium2 kernel written in Bass

```python
import math
import jax
import jax.numpy as jnp
from jax import lax
import numpy as np

D_MODEL = 2048
BATCH = 8
SEQ = 2048
DEPTH = 1

MEM_LEN = 256
RMS_EPS = 1e-6
DN_HEAD_DIM = 128
DN_WIDTH = D_MODEL // 2
DN_HEADS = DN_WIDTH // DN_HEAD_DIM
DN_CONV = 4
DN_CHUNK = 64
RW_HEAD_DIM = 64
RW_WIDTH = D_MODEL - DN_WIDTH
RW_HEADS = RW_WIDTH // RW_HEAD_DIM
RW_DECAY_LORA = 64
RW_AAA_LORA = 64
RW_GATE_LORA = 128
RW_GN_EPS = 64e-5
DN_COLS = 4 * DN_WIDTH + 2 * DN_HEADS
RW_COLS = 3 * RW_WIDTH + RW_DECAY_LORA + RW_AAA_LORA + RW_GATE_LORA
IN_COLS = DN_COLS + RW_COLS
XA_HEADS = 4
XA_HEAD_DIM = 128
XA_WIDTH = XA_HEADS * XA_HEAD_DIM
FFN_HIDDEN = 4 * D_MODEL

kernel_name = 'hybrid_gdn_rwkv7_memxattn_layer'


def rmsnorm(x, w):
    xf = x.astype(jnp.float32)
    y = xf * lax.rsqrt(jnp.mean(xf * xf, axis=-1, keepdims=True) + RMS_EPS)
    return (y * w.astype(jnp.float32)).astype(x.dtype)


def l2norm(x):
    return x * lax.rsqrt(jnp.sum(x * x, axis=-1, keepdims=True) + 1e-6)


def causal_depthwise_conv(x, w):
    k = w.shape[0]
    return lax.conv_general_dilated(
        x, w[:, None, :].astype(x.dtype), window_strides=(1,), padding=[(k - 1, 0)],
        dimension_numbers=('NWC', 'WIO', 'NWC'), feature_group_count=x.shape[-1])


def token_shift(p):
    return jnp.pad(p, ((0, 0), (1, 0), (0, 0)))[:, :-1]


def chunked_gated_delta_rule(q, k, v, g, beta):
    b, s, h, dk = q.shape
    dv = v.shape[-1]
    c = DN_CHUNK
    n = s // c

    def chunks(t):
        return t.reshape(b, n, c, h, -1).transpose(0, 3, 1, 2, 4)

    q = chunks(q) * (dk ** -0.5)
    k = chunks(k)
    v = chunks(v)
    g = g.reshape(b, n, c, h).transpose(0, 3, 1, 2)
    beta = beta.reshape(b, n, c, h).transpose(0, 3, 1, 2)
    gc = jnp.cumsum(g, axis=-1)
    idx = jnp.arange(c)
    causal = idx[:, None] >= idx[None, :]
    strict = idx[:, None] > idx[None, :]
    decay = jnp.exp(jnp.where(causal, gc[..., :, None] - gc[..., None, :], -jnp.inf))
    kb = k * beta[..., None]
    a_strict = jnp.where(strict, jnp.einsum('bhncd,bhnmd->bhncm', kb, k) * decay, 0.0)
    lower = a_strict + jnp.eye(c, dtype=q.dtype)
    rhs = jnp.concatenate([v * beta[..., None], kb * jnp.exp(gc)[..., None]], axis=-1)
    sol = lax.linalg.triangular_solve(lower, rhs, left_side=True, lower=True)
    u, w = sol[..., :dv], sol[..., dv:]
    attn = jnp.einsum('bhncd,bhnmd->bhncm', q, k) * decay
    qg = q * jnp.exp(gc)[..., None]
    g_last = gc[..., -1]
    kd = k * jnp.exp(g_last[..., None] - gc)[..., None]
    xs = tuple(jnp.moveaxis(t, 2, 0) for t in (u, w, qg, attn, kd, g_last))

    def step(state, xs_n):
        u_n, w_n, qg_n, attn_n, kd_n, gl_n = xs_n
        v_new = u_n - jnp.einsum('bhcd,bhde->bhce', w_n, state)
        o_n = jnp.einsum('bhcd,bhde->bhce', qg_n, state) + jnp.einsum('bhcm,bhme->bhce', attn_n, v_new)
        state = state * jnp.exp(gl_n)[..., None, None] + jnp.einsum('bhcd,bhce->bhde', kd_n, v_new)
        return state, o_n

    s0 = jnp.zeros((b, h, dk, dv), q.dtype)
    _, o = lax.scan(step, s0, xs)
    return o.transpose(1, 0, 3, 2, 4).reshape(b, s, h, dv)


def wkv7_scan(r, w, k, v, kk, a):
    b, s, h, nd = r.shape
    xs = tuple(jnp.swapaxes(t, 0, 1) for t in (r, w, k, v, kk, a))

    def step(state, xs_t):
        r_t, w_t, k_t, v_t, kk_t, a_t = xs_t
        sa = jnp.einsum('bhvk,bhk->bhv', state, -kk_t)
        state = (state * w_t[:, :, None, :] + sa[..., None] * (kk_t * a_t)[:, :, None, :]
                 + v_t[..., None] * k_t[:, :, None, :])
        return state, jnp.einsum('bhvk,bhk->bhv', state, r_t)

    s0 = jnp.zeros((b, h, nd, nd), r.dtype)
    _, y = lax.scan(step, s0, xs)
    return jnp.swapaxes(y, 0, 1)


def deltanet_group(p, conv_w, a_log, dt_bias, norm_w):
    p = p.astype(jnp.float32)
    b, s, _ = p.shape
    qkv, z, ga, gb = jnp.split(p, [3 * DN_WIDTH, 4 * DN_WIDTH, 4 * DN_WIDTH + DN_HEADS], axis=-1)
    qkv = jax.nn.silu(causal_depthwise_conv(qkv, conv_w))
    q, k, v = (t.reshape(b, s, DN_HEADS, DN_HEAD_DIM) for t in jnp.split(qkv, 3, axis=-1))
    q = l2norm(q)
    k = l2norm(k)
    beta = jax.nn.sigmoid(gb)
    g = -jnp.exp(a_log) * jax.nn.softplus(ga + dt_bias)
    o = chunked_gated_delta_rule(q, k, v, g, beta)
    o = rmsnorm(o, norm_w) * jax.nn.silu(z.reshape(b, s, DN_HEADS, DN_HEAD_DIM))
    return o.reshape(b, s, DN_WIDTH)


def rwkv7_group(p, mu, w0, w2, a0, a2, g2, k_k, k_a, r_k, ln_w, ln_b):
    p = p.astype(jnp.float32)
    b, s, _ = p.shape
    p = p + (token_shift(p) - p) * mu
    pr, pk, pv, pw, pa, pg = jnp.split(
        p, [RW_WIDTH, 2 * RW_WIDTH, 3 * RW_WIDTH, 3 * RW_WIDTH + RW_DECAY_LORA,
            3 * RW_WIDTH + RW_DECAY_LORA + RW_AAA_LORA], axis=-1)

    def heads(t):
        return t.reshape(b, s, RW_HEADS, RW_HEAD_DIM)

    log_w = -jax.nn.softplus(-(w0 + jnp.tanh(pw) @ w2)) - 0.5
    decay = jnp.exp(-jnp.exp(log_w))
    a = jax.nn.sigmoid(a0 + pa @ a2)
    gate = jax.nn.sigmoid(pg) @ g2
    kk = heads(pk * k_k)
    kk = kk / jnp.maximum(jnp.sqrt(jnp.sum(kk * kk, axis=-1, keepdims=True)), 1e-12)
    k = pk * (1.0 + (a - 1.0) * k_a)
    r_h, k_h, v_h = heads(pr), heads(k), heads(pv)
    y = wkv7_scan(r_h, heads(decay), k_h, v_h, kk, heads(a))
    mean = jnp.mean(y, axis=-1, keepdims=True)
    var = jnp.mean(jnp.square(y - mean), axis=-1, keepdims=True)
    y = ((y - mean) * lax.rsqrt(var + RW_GN_EPS)).reshape(b, s, RW_WIDTH) * ln_w + ln_b
    bonus = jnp.sum(r_h * k_h * r_k, axis=-1, keepdims=True) * v_h
    return (y + bonus.reshape(b, s, RW_WIDTH)) * gate


def memory_cross_attention(hn, mn, wq, wk, wv, wo):
    b, s, _ = hn.shape
    m = mn.shape[1]
    q = (hn @ wq).reshape(b, s, XA_HEADS, XA_HEAD_DIM)
    k = (mn @ wk).reshape(b, m, XA_HEADS, XA_HEAD_DIM)
    v = (mn @ wv).reshape(b, m, XA_HEADS, XA_HEAD_DIM)
    scores = jnp.einsum('bshd,bmhd->bhsm', q, k).astype(jnp.float32) * (XA_HEAD_DIM ** -0.5)
    probs = jax.nn.softmax(scores, axis=-1).astype(v.dtype)
    o = jnp.einsum('bhsm,bmhd->bshd', probs, v).reshape(b, s, XA_WIDTH)
    return o @ wo


def squared_relu_mlp(u, w1, w2):
    return jnp.square(jax.nn.relu(u @ w1)) @ w2


def setup_inputs(seed: int = 0) -> dict:
    key = jax.random.key(seed)
    ks = jax.random.split(key, 32)
    L = DEPTH
    D = D_MODEL

    def normal(i, shape, scale):
        return jax.random.normal(ks[i], shape, jnp.float32) * scale

    def uniform(i, shape, lo, hi):
        return jax.random.uniform(ks[i], shape, jnp.float32, lo, hi)

    dt = jnp.exp(uniform(6, (L, DN_HEADS), math.log(1e-3), math.log(1e-1)))
    return {
        'x': normal(0, (BATCH, SEQ, D), 1.0),
        'mem': normal(1, (BATCH, MEM_LEN, D), 1.0),
        'mix_norm_w': 1.0 + normal(2, (L, D), 0.02),
        'w_in': normal(3, (L, D, IN_COLS), D ** -0.5),
        'dn_conv_w': normal(4, (L, DN_CONV, 3 * DN_WIDTH), DN_CONV ** -0.5),
        'dn_a_log': jnp.log(uniform(5, (L, DN_HEADS), 1.0, 16.0)),
        'dn_dt_bias': dt + jnp.log(-jnp.expm1(-dt)),
        'dn_norm_w': 1.0 + normal(7, (L, DN_HEAD_DIM), 0.02),
        'rw_mu': uniform(8, (L, RW_COLS), 0.0, 1.0),
        'rw_w0': uniform(9, (L, RW_WIDTH), -6.0, 1.0),
        'rw_w2': normal(10, (L, RW_DECAY_LORA, RW_WIDTH), 0.1 * RW_DECAY_LORA ** -0.5),
        'rw_a0': normal(11, (L, RW_WIDTH), 0.1),
        'rw_a2': normal(12, (L, RW_AAA_LORA, RW_WIDTH), 0.1 * RW_AAA_LORA ** -0.5),
        'rw_g2': normal(13, (L, RW_GATE_LORA, RW_WIDTH), RW_GATE_LORA ** -0.5),
        'rw_k_k': 0.85 + normal(14, (L, RW_WIDTH), 0.02),
        'rw_k_a': 1.0 + normal(15, (L, RW_WIDTH), 0.02),
        'rw_r_k': normal(16, (L, RW_HEADS, RW_HEAD_DIM), 0.1),
        'rw_ln_w': 1.0 + normal(17, (L, RW_WIDTH), 0.02),
        'rw_ln_b': normal(18, (L, RW_WIDTH), 0.02),
        'w_out': normal(19, (L, D, D), D ** -0.5),
        'xa_norm_w': 1.0 + normal(20, (L, D), 0.02),
        'mem_norm_w': 1.0 + normal(21, (L, D), 0.02),
        'xa_wq': normal(22, (L, D, XA_WIDTH), D ** -0.5),
        'xa_wk': normal(23, (L, D, XA_WIDTH), D ** -0.5),
        'xa_wv': normal(24, (L, D, XA_WIDTH), D ** -0.5),
        'xa_wo': normal(25, (L, XA_WIDTH, D), XA_WIDTH ** -0.5),
        'ffn_norm_w': 1.0 + normal(26, (L, D), 0.02),
        'ffn_w1': normal(27, (L, D, FFN_HIDDEN), D ** -0.5),
        'ffn_w2': normal(28, (L, FFN_HIDDEN, D), FFN_HIDDEN ** -0.5),
        'final_norm_w': 1.0 + normal(29, (D,), 0.02),
    }


def reference(x, mem, mix_norm_w, w_in, dn_conv_w, dn_a_log, dn_dt_bias, dn_norm_w, rw_mu, rw_w0,
              rw_w2, rw_a0, rw_a2, rw_g2, rw_k_k, rw_k_a, rw_r_k, rw_ln_w, rw_ln_b, w_out,
              xa_norm_w, mem_norm_w, xa_wq, xa_wk, xa_wv, xa_wo, ffn_norm_w, ffn_w1, ffn_w2,
              final_norm_w):
    h = x
    for l in range(DEPTH):
        u = rmsnorm(h, mix_norm_w[l])
        p = u @ w_in[l]
        o_dn = deltanet_group(p[..., :DN_COLS], dn_conv_w[l], dn_a_log[l], dn_dt_bias[l], dn_norm_w[l])
        o_rw = rwkv7_group(p[..., DN_COLS:], rw_mu[l], rw_w0[l], rw_w2[l], rw_a0[l], rw_a2[l], rw_g2[l],
                           rw_k_k[l], rw_k_a[l], rw_r_k[l], rw_ln_w[l], rw_ln_b[l])
        h = h + jnp.concatenate([o_dn, o_rw], axis=-1).astype(h.dtype) @ w_out[l]
        h = h + memory_cross_attention(rmsnorm(h, xa_norm_w[l]), rmsnorm(mem, mem_norm_w[l]),
                                       xa_wq[l], xa_wk[l], xa_wv[l], xa_wo[l])
        h = h + squared_relu_mlp(rmsnorm(h, ffn_norm_w[l]), ffn_w1[l], ffn_w2[l])
    return rmsnorm(h, final_norm_w)
```

```python
import contextlib
import math
import numpy as np
import concourse.bass as bass
import concourse.mybir as mybir
from concourse.alu_op_type import AluOpType as ALU
from concourse.bass_utils import run_bass_kernel_spmd

F32 = mybir.dt.float32
BF16 = mybir.dt.bfloat16
F32R = mybir.dt.float32r
AF = mybir.ActivationFunctionType
AX = mybir.AxisListType

D = 2048
S = 2048
NT = 16
MEM = 256
DNC = 4112
INC = 7440
FF = 8192
EPS = 1e-6
CDEC = -math.exp(-0.5)
DBG_DN = 8
DBG_RW = 8
DBG_CH = NT
DBG_STEP = 99


class Sem:
    __slots__ = ("h", "name")

    def __init__(self, h, name):
        self.h = h
        self.name = name


class Buf:
    __slots__ = ("name", "w", "r", "dsem", "dcount", "excl")

    def __init__(self, name):
        self.name = name
        self.excl = False
        self.w = None
        self.r = {}
        self.dsem = None
        self.dcount = 0


class Eng:
    def __init__(self, name, h, sem):
        self.name = name
        self.h = h
        self.sem = sem
        self.count = 0
        self.waited = {}


class T:
    def __init__(self, t, name):
        self.t = t
        self.b = Buf(name)


class TV:
    def __init__(self, ap, b):
        self.t = ap
        self.b = b


class K:
    def __init__(self, nc):
        self.nc = nc
        self.es = contextlib.ExitStack()
        self.pe = self._eng("pe", nc.tensor)
        self.dve = self._eng("dve", nc.vector)
        self.act = self._eng("act", nc.scalar)
        self.pool = self._eng("pool", nc.gpsimd)
        self.sp = self._eng("sp", nc.sync)
        self.ninst = 0
        self._psi = 0
        self.psf = []
        self._ev = 0
        self.slots = []
        self.psfree = list(range(8))

    def new_sem(self, name):
        return Sem(self.es.enter_context(self.nc.semaphore(name)), name)

    def _eng(self, name, h):
        return Eng(name, h, self.new_sem("s_" + name))

    def sb(self, name, shape, dt, stack=None):
        t = (stack or self.es).enter_context(self.nc.sbuf_tensor(name, list(shape), dt))
        return T(t, name)

    def _deps(self, eng, reads, writes, extra=()):
        deps = {}
        for b in reads:
            if b.w is not None:
                s, v = b.w
                if v > deps.get(s, 0):
                    deps[s] = v
            if b.excl:
                for s, v in b.r.items():
                    if s is not eng.sem and v > deps.get(s, 0):
                        deps[s] = v
        for b in writes:
            if b.w is not None and not (eng is self.pe and b.w[0] is self.pe.sem):
                s, v = b.w
                if v > deps.get(s, 0):
                    deps[s] = v
            for s, v in b.r.items():
                if v > deps.get(s, 0):
                    deps[s] = v
        for s, v in extra:
            if v > deps.get(s, 0):
                deps[s] = v
        for s, v in deps.items():
            if eng.waited.get(s, 0) < v:
                eng.h.wait_ge(s.h, v)
                eng.waited[s] = v

    def op(self, eng, fn, reads=(), writes=()):
        self._deps(eng, reads, writes)
        inst = fn(eng.h)
        eng.count += 1
        inst.then_inc(eng.sem.h, 1)
        self.ninst += 1
        c = eng.count
        s = eng.sem
        for b in reads:
            b.r[s] = c
        for b in writes:
            b.w = (s, c)
            b.r = {}
        return inst

    def dma(self, q, out, in_, reads, writes, slot, **kw):
        if slot.dsem is None:
            slot.dsem = self.new_sem("d_" + slot.name)
            self.slots.append(slot)
        extra = [(slot.dsem, slot.dcount)] if slot.dcount else []
        self._deps(q, reads, writes, extra)
        inst = q.h.dma_start(out=out, in_=in_, **kw)
        slot.dcount += 16
        inst.then_inc(slot.dsem.h, 16)
        self.ninst += 1
        for b in reads:
            b.r[slot.dsem] = slot.dcount
        for b in writes:
            b.w = (slot.dsem, slot.dcount)
            b.r = {}
        return inst

    def ps(self):
        p = self.psf[self._psi % 8]
        self._psi += 1
        return p

    def psA(self):
        return self.psf[self.psfree.pop(0)]

    def psF(self, *ps_):
        for p in ps_:
            self.psfree.append(self.psf.index(p))

    def need(self, m):
        while len(self.psfree) < m:
            yield

    def barrier(self):
        engs = [self.pe, self.dve, self.act, self.pool, self.sp]
        for e in engs:
            for o in engs:
                if o is not e and o.count and e.waited.get(o.sem, 0) < o.count:
                    e.h.wait_ge(o.sem.h, o.count)
                    e.waited[o.sem] = o.count
            for sl in self.slots:
                if e.waited.get(sl.dsem, 0) < sl.dcount:
                    e.h.wait_ge(sl.dsem.h, sl.dcount)
                    e.waited[sl.dsem] = sl.dcount

    def mm(self, out, lhsT, rhs, start, stop, R, W):
        return self.op(self.pe, lambda e: e.matmul(out, lhsT, rhs, start=start, stop=stop), R, W)

    def tr(self, out, in_, ident, R, W):
        return self.op(self.pe, lambda e: e.transpose(out, in_, ident), R, W)

    def tt(self, eng, out, a, b, op, R, W):
        return self.op(eng, lambda e: e.tensor_tensor(out=out, in0=a, in1=b, op=op), R, W)

    def ts(self, eng, out, a, s1, s2, op0, op1, R, W):
        if op1 is None:
            return self.op(eng, lambda e: e.tensor_scalar(out=out, in0=a, scalar1=s1, scalar2=None, op0=op0), R, W)
        return self.op(eng, lambda e: e.tensor_scalar(out=out, in0=a, scalar1=s1, scalar2=s2, op0=op0, op1=op1), R, W)

    def stt(self, out, a, s, b, op0, op1, R, W):
        return self.op(self.dve, lambda e: e.scalar_tensor_tensor(out=out, in0=a, scalar=s, in1=b, op0=op0, op1=op1), R, W)

    def actf(self, out, in_, func, R, W, **kw):
        return self.op(self.act, lambda e: e.activation(out=out, in_=in_, func=func, **kw), R, W)

    def cp(self, out, in_, R, W, eng=None):
        if eng is None:
            self._ev += 1
            eng = self.act if (self._ev & 1) else self.dve
        if eng is self.act:
            return self.op(eng, lambda e: e.copy(out, in_), R, W)
        return self.op(eng, lambda e: e.tensor_copy(out, in_), R, W)


def build(debug=None):
    nc = bass.Bass("TRN2", target_bir_lowering=False)
    k = K(nc)

    def din(name, shape):
        return nc.dram_tensor(name, list(shape), F32, kind="ExternalInput").ap()

    x = din("x", [S, D]); mem = din("mem", [MEM, D])
    w_in = din("w_in", [D, INC]); w_out = din("w_out", [D, D])
    wq = din("xa_wq", [D, 512]); wk = din("xa_wk", [D, 512]); wv = din("xa_wv", [D, 512]); wo = din("xa_wo", [512, D])
    w1 = din("ffn_w1", [D, FF]); w2 = din("ffn_w2", [FF, D])
    nrm = din("norms", [5, D])
    convw = din("convw", [128, 24 * 4])
    dnsc = din("dnsc", [16, 2])
    dnw = din("dnw", [128, 1])
    mu = din("mu", [128, 26])
    rwv = din("rwv", [128, 7 * 8])
    lora = din("lora", [128, 1024])
    g2 = din("g2", [128, 1024])
    out = nc.dram_tensor("out", [S, D], F32, kind="ExternalOutput").ap()
    pT = T(nc.dram_tensor("pT", [INC, S], F32, kind="Internal").ap(), "pT")
    pT.bl = [Buf(f"pT{i}") for i in range(59)]
    if debug == "p4":
        oTd = T(nc.dram_tensor("oT_in", [D, S], F32, kind="ExternalInput").ap(), "oTd")
        oTd.bl = [Buf(f"oT{i}") for i in range(16)]
        oT_dt = F32
    else:
        oTd = T(nc.dram_tensor("oT", [D, S], BF16, kind="Internal").ap(), "oTd")
        oTd.bl = [Buf(f"oT{i}") for i in range(16)]
        oT_dt = BF16
    wsc = T(nc.dram_tensor("wsc", [38, 128, 8192], BF16, kind="Internal").ap(), "wsc")
    wsc.bl = [Buf(f"wsc{i}") for i in range(38)]
    dbg = None
    if debug == "p1":
        dbg = nc.dram_tensor("dbg", [INC, S], F32, kind="ExternalOutput").ap()
    if debug == "p2":
        dbg = nc.dram_tensor("dbg", [D, S], F32, kind="ExternalOutput").ap()

    for i in range(8):
        p = T(k.es.enter_context(nc.psum_tensor(f"ps{i}", [128, 512], F32)), f"ps{i}")
        p.b.excl = True
        k.psf.append(p)

    ones = k.sb("ones", [128, 128], F32)
    ident = k.sb("ident", [128, 128], F32)
    identb = k.sb("identb", [128, 128], BF16)
    epsT = k.sb("epsT", [128, 1], F32)
    k.op(k.pool, lambda e: e.memset(ones.t[:], 1.0), [], [ones.b])
    k.op(k.pool, lambda e: e.memset(epsT.t[:], EPS), [], [epsT.b])
    k.op(k.pool, lambda e: e.affine_select(out=ident.t[:], in_=ones.t[:], pattern=[[-1, 128]], compare_op=ALU.is_equal,
                                           fill=0.0, base=0, channel_multiplier=1), [ones.b], [ident.b])
    k.op(k.dve, lambda e: e.tensor_copy(identb.t[:], ident.t[:]), [ident.b], [identb.b])

    G = {}

    def alloc_norm(stack, tag):
        G["wbc"] = k.sb("wbc" + tag, [128, D], F32, stack)
        G["uns"] = [k.sb(f"un{i}" + tag, [128, D], BF16, stack) for i in range(2)]
        G["un"] = G["uns"][0]
        G["junk"] = k.sb("junk" + tag, [128, D], BF16, stack)
        G["ss"] = k.sb("ss" + tag, [128, 4], F32, stack)
        G["ssb"] = [Buf(f"ss{i}" + tag) for i in range(4)]
        G["ctr"] = 0

    def load_wbc(i):
        wbc = G["wbc"]
        k.dma(k.sp, wbc.t[:], nrm[i:i + 1, :].to_broadcast([128, D]), [], [wbc.b], wbc.b)
    wbufs = []
    wctr = [0]

    def precast_p4_weights():
        blocks = []
        for cb in range(4):
            blocks.append((cb, w_out[:, cb * 512:(cb + 1) * 512], 16))
        blocks.append((4, wq[:, :], 16))
        blocks.append((5, wo[:, :], 4))
        for fb in range(16):
            blocks.append((6 + fb, w1[:, fb * 512:(fb + 1) * 512], 16))
        for cb in range(4):
            for sub in range(4):
                blocks.append((22 + cb * 4 + sub, w2[sub * 2048:(sub + 1) * 2048, cb * 512:(cb + 1) * 512], 16))
        return blocks

    pc_blocks = precast_p4_weights()

    def bg_next(n):
        for _ in range(n):
            if not pc_blocks:
                return
            idx, src, kc = pc_blocks.pop(0)
            k.dma(k.pool, wsc.t[idx].rearrange("p (kc c) -> p kc c", kc=kc), src.rearrange("(kc p) c -> p kc c", p=128),
                  [], [wsc.bl[idx]], wsc.bl[idx])

    def alloc_wbufs(stack, tag):
        wbufs.clear()
        wbufs.extend(k.sb(f"wbuf{tag}{i}", [128, 8192], BF16, stack) for i in range(2))

    def load_w(src_ap, kc, cache=None):
        wb = wbufs[wctr[0] % len(wbufs)]
        wctr[0] += 1
        ncol = src_ap.shape[1]
        view = wb.t[:, 0:kc * ncol].rearrange("p (kc c) -> p kc c", kc=kc)
        if cache is not None:
            k.dma(k.sp, wb.t[:, :], wsc.t[cache[0]], [wsc.bl[cache[0]]], [wb.b], wb.b)
            return wb, view
        k.dma(k.pool, view, src_ap.rearrange("(kc p) c -> p kc c", p=128), [], [wb.b], wb.b)
        return wb, view

    def norm_transpose(src, srcB, dstT, dstB, dst_cols, slot):
        wbc, ss, junk = G["wbc"], G["ss"], G["junk"]
        un = G["uns"][G["ctr"] % 2]
        G["ctr"] += 1
        sb_ = G["ssb"][slot]
        k.actf(junk.t[:], src, AF.Square, [srcB], [junk.b, sb_], accum_out=ss.t[:, slot:slot + 1])
        k.actf(ss.t[:, slot:slot + 1], ss.t[:, slot:slot + 1], AF.Sqrt, [sb_, epsT.b], [sb_], scale=1.0 / D, bias=epsT.t[:, 0:1])
        k.op(k.dve, lambda e: e.reciprocal(ss.t[:, slot:slot + 1], ss.t[:, slot:slot + 1]), [sb_], [sb_])
        k.stt(un.t[:], src, ss.t[:, slot:slot + 1], wbc.t[:], ALU.mult, ALU.mult, [srcB, sb_, wbc.b], [un.b])
        for half in range(2):
            p = k.ps()
            pv = p.t[:].bitcast(BF16)
            for j in range(8):
                kc = half * 8 + j
                k.tr(pv[:, j * 128:(j + 1) * 128], un.t[:, kc * 128:(kc + 1) * 128], identb.t[:], [un.b, identb.b], [p.b])
            k.cp(dstT[:, half * 8:(half + 1) * 8, dst_cols], pv.rearrange("p (j c) -> p j c", c=128), [p.b], [dstB])

    if debug != "p4":
        with contextlib.ExitStack() as st1:
            alloc_wbufs(st1, "a")
            alloc_norm(st1, "a")
            uT = k.sb("uT", [128, 16, S], BF16, st1)
            uTb = [Buf(f"uT{i}") for i in range(4)]
            xts = [k.sb(f"xt{i}", [128, D], F32, st1) for i in range(2)]
            stg = [k.sb(f"stg{i}", [128, 512], F32, st1) for i in range(4)]
            load_wbc(0)
            si = 0

            def p1_block(c0, wb, wvw, tbs):
                nonlocal si
                ncol = min(512, INC - c0)
                for g0 in range(0, ncol, 128):
                    M = min(128, ncol - g0)
                    for tb in tbs:
                        p = k.ps()
                        for kc in range(16):
                            k.mm(p.t[0:M, :], wvw[:, kc, g0:g0 + M], uT.t[:, kc, tb * 512:(tb + 1) * 512], kc == 0, kc == 15,
                                 [wb.b, uTb[tb]], [p.b])
                        sg_ = stg[si % 4]; si += 1
                        k.cp(sg_.t[0:M, :], p.t[0:M, :], [p.b], [sg_.b])
                        k.dma(k.sp, pT.t[c0 + g0:c0 + g0 + M, tb * 512:(tb + 1) * 512], sg_.t[0:M, :], [sg_.b], [pT.bl[(c0 + g0) // 128]], sg_.b)
            wb0, wvw0 = load_w(w_in[:, 0:512], 16)
            for tb in range(4):
                for n in range(4 * tb, 4 * tb + 4):
                    xt = xts[n % 2]
                    k.dma(k.sp, xt.t[:], x[n * 128:(n + 1) * 128, :], [], [xt.b], xt.b)
                    norm_transpose(xt.t[:], xt.b, uT.t, uTb[n // 4], slice(n * 128, (n + 1) * 128), n % 4)
                p1_block(0, wb0, wvw0, [tb])
            for c0 in range(512, INC, 512):
                ncol = min(512, INC - c0)
                wb, wvw = load_w(w_in[:, c0:c0 + ncol], 16)
                p1_block(c0, wb, wvw, range(4))
            if debug == "p1":
                for r0 in range(0, INC, 128):
                    M = min(128, INC - r0)
                    xt = xts[(r0 // 128) % 2]
                    k.dma(k.sp, xt.t[0:M, :], pT.t[r0:r0 + M, :], [pT.bl[r0 // 128]], [xt.b], xt.b)
                    k.dma(k.sp, dbg[r0:r0 + M, :], xt.t[0:M, :], [xt.b], [], xt.b)
                k._deps(k.sp, [], [xts[0].b, xts[1].b])
        if debug == "p1":
            k.es.close()
            return nc

    if debug != "p4":
        k.barrier()
        build_mixers(nc, k, pT, oTd, convw, dnsc, dnw, mu, rwv, lora, g2, ones, ident, bg_next)
        bg_next(99)
        if debug == "p2":
            k.barrier()
            with contextlib.ExitStack() as st:
                a = k.sb("dba", [128, S], BF16, st); b = k.sb("dbb", [128, S], F32, st)
                for r0 in range(0, D, 128):
                    k.dma(k.sp, a.t[:], oTd.t[r0:r0 + 128, :], [oTd.bl[r0 // 128]], [a.b], a.b)
                    k.cp(b.t[:], a.t[:], [a.b], [b.b])
                    k.dma(k.sp, dbg[r0:r0 + 128, :], b.t[:], [b.b], [], b.b)
                k._deps(k.sp, [], [b.b])
            k.es.close()
            return nc

    k.barrier()
    if debug == "p4":
        bg_next(99)
    st4 = contextlib.ExitStack()
    alloc_wbufs(st4, "b")
    alloc_norm(st4, "b")
    wbc, un, ss = G["wbc"], G["junk"], G["ss"]
    KT = k.sb("KT", [128, 4, MEM], BF16, st4)
    Vm = k.sb("Vm", [128, 2, 512], BF16, st4)
    hnT = k.sb("hnT", [128, 16, 512], BF16, st4)
    h = k.sb("h", [128, 4, D], F32, st4)
    hB = [Buf(f"h{j}") for j in range(4)]
    hid = k.sb("hid", [128, 64, 512], BF16, st4)
    hidB = [Buf(f"hid{i}") for i in range(16)]
    qT = k.sb("qT", [128, 4, 512], BF16, st4)
    oxT = k.sb("oxT", [128, 4, 512], BF16, st4)
    atw = [dict(pr=k.sb(f"pr{i}", [128, MEM], F32, st4), prn=k.sb(f"prn{i}", [128, MEM], BF16, st4),
                prT=k.sb(f"prT{i}", [128, 2, 128], BF16, st4), sm=k.sb(f"sm{i}", [128, 4], F32, st4)) for i in range(4)]

    def run_rr(gens):
        gens = list(gens)
        while gens:
            for g_ in list(gens):
                try:
                    next(g_)
                except StopIteration:
                    gens.remove(g_)
    rl = [k.sb(f"rl{i}", [128, 512], F32, st4) for i in range(2)]
    memt = [k.sb(f"memt{i}", [128, D], F32, st4) for i in range(1)]

    load_wbc(2)
    for mt in range(2):
        m_ = memt[0]
        k.dma(k.sp, m_.t[:], mem[mt * 128:(mt + 1) * 128, :], [], [m_.b], m_.b)
        norm_transpose(m_.t[:], m_.b, hnT.t, hnT.b, slice(mt * 128, (mt + 1) * 128), mt)
    wb, wvw = load_w(wk[:, :], 16)
    for hd in range(4):
        p = k.ps()
        for kc in range(16):
            k.mm(p.t[:, 0:MEM], wvw[:, kc, hd * 128:(hd + 1) * 128], hnT.t[:, kc, 0:MEM], kc == 0, kc == 15, [wb.b, hnT.b], [p.b])
        k.cp(KT.t[:, hd, :], p.t[:, 0:MEM], [p.b], [KT.b])
    wb, wvw = load_w(wv[:, :], 16)
    for mc in range(2):
        p = k.ps()
        for kc in range(16):
            k.mm(p.t[:, :], hnT.t[:, kc, mc * 128:(mc + 1) * 128], wvw[:, kc, :], kc == 0, kc == 15, [wb.b, hnT.b], [p.b])
        k.cp(Vm.t[:, mc, :], p.t[:, :], [p.b], [Vm.b])

    oTb_view = hid.t[:, 0:16, :]
    oTb_bufs = hidB[0:4]
    for TB in range(4):
        t0 = TB * 512
        for j in range(4):
            k.dma(k.sp, h.t[:, j, :], x[t0 + j * 128:t0 + (j + 1) * 128, :], [], [hB[j]], hB[j])
        if oT_dt == BF16:
            k.dma(k.sp, oTb_view, oTd.t[:, t0:t0 + 512].rearrange("(kc p) t -> p kc t", p=128), oTd.bl, oTb_bufs, oTb_bufs[0])
        else:
            k.dma(k.pool, oTb_view, oTd.t[:, t0:t0 + 512].rearrange("(kc p) t -> p kc t", p=128), oTd.bl, oTb_bufs, oTb_bufs[0])
        for cb in range(4):
            wb, wvw = load_w(w_out[:, cb * 512:(cb + 1) * 512], 16, (cb, TB == 0))
            for j in range(4):
                p = k.ps()
                for kc in range(16):
                    k.mm(p.t[:, :], oTb_view[:, kc, j * 128:(j + 1) * 128], wvw[:, kc, :], kc == 0, kc == 15, [wb.b] + oTb_bufs, [p.b])
                hs = h.t[:, j, cb * 512:(cb + 1) * 512]
                k.tt(k.dve, hs, p.t[:, :], hs, ALU.add, [p.b, hB[j]], [hB[j]])
        load_wbc(1)
        for j in range(4):
            norm_transpose(h.t[:, j, :], hB[j], hnT.t, hnT.b, slice(j * 128, (j + 1) * 128), j)
        wb, wvw = load_w(wq[:, :], 16, (4, TB == 0))
        for hd in range(4):
            p = k.ps()
            for kc in range(16):
                k.mm(p.t[:, :], wvw[:, kc, hd * 128:(hd + 1) * 128], hnT.t[:, kc, :], kc == 0, kc == 15, [wb.b, hnT.b], [p.b])
            k.actf(qT.t[:, hd, :], p.t[:, :], AF.Copy, [p.b], [qT.b], scale=128 ** -0.5)
        def attn_worker(w_):
            a_ = atw[w_]
            pr, prn, prT, sm = a_["pr"], a_["prn"], a_["prT"], a_["sm"]
            for idx in range(w_, 16, 4):
                j, hd = divmod(idx, 4)
                yield from k.need(1)
                p = k.psA()
                k.mm(p.t[:, 0:MEM], qT.t[:, hd, j * 128:(j + 1) * 128], KT.t[:, hd, :], True, True, [qT.b, KT.b], [p.b])
                yield
                k.op(k.dve, lambda e: e.tensor_reduce(out=sm.t[:, 0:1], in_=p.t[:, 0:MEM], axis=AX.X, op=ALU.max, negate=True),
                     [p.b], [sm.b])
                yield
                k.actf(pr.t[:], p.t[:, 0:MEM], AF.Exp, [p.b, sm.b], [pr.b, sm.b], bias=sm.t[:, 0:1], accum_out=sm.t[:, 1:2])
                k.psF(p)
                yield
                k.op(k.dve, lambda e: e.reciprocal(sm.t[:, 2:3], sm.t[:, 1:2]), [sm.b], [sm.b])
                yield
                k.ts(k.dve, prn.t[:], pr.t[:], sm.t[:, 2:3], None, ALU.mult, None, [pr.b, sm.b], [prn.b])
                yield
                yield from k.need(1)
                p2 = k.psA()
                pv = p2.t[:].bitcast(BF16)
                for mc in range(2):
                    k.tr(pv[:, mc * 128:(mc + 1) * 128], prn.t[:, mc * 128:(mc + 1) * 128], identb.t[:], [prn.b, identb.b], [p2.b])
                yield
                k.cp(prT.t[:], pv[:, 0:256].rearrange("p (m c) -> p m c", c=128), [p2.b], [prT.b])
                k.psF(p2)
                yield
                yield from k.need(1)
                p3 = k.psA()
                for mc in range(2):
                    k.mm(p3.t[:, 0:128], Vm.t[:, mc, hd * 128:(hd + 1) * 128], prT.t[:, mc, :], mc == 0, mc == 1, [Vm.b, prT.b], [p3.b])
                yield
                k.cp(oxT.t[:, hd, j * 128:(j + 1) * 128], p3.t[:, 0:128], [p3.b], [oxT.b])
                k.psF(p3)
                yield
        run_rr([attn_worker(w_) for w_ in range(4)])
        wb, wvw = load_w(wo[:, :], 4, (5, TB == 0))
        for cb in range(4):
            for j in range(4):
                p = k.ps()
                for kc in range(4):
                    k.mm(p.t[:, :], oxT.t[:, kc, j * 128:(j + 1) * 128], wvw[:, kc, cb * 512:(cb + 1) * 512], kc == 0, kc == 3, [wb.b, oxT.b], [p.b])
                hs = h.t[:, j, cb * 512:(cb + 1) * 512]
                k.tt(k.dve, hs, p.t[:, :], hs, ALU.add, [p.b, hB[j]], [hB[j]])
        load_wbc(3)
        for j in range(4):
            norm_transpose(h.t[:, j, :], hB[j], hnT.t, hnT.b, slice(j * 128, (j + 1) * 128), j)
        for fb in range(16):
            wb, wvw = load_w(w1[:, fb * 512:(fb + 1) * 512], 16, (6 + fb, TB == 0))
            for fc in range(4):
                p = k.ps()
                for kc in range(16):
                    k.mm(p.t[:, :], wvw[:, kc, fc * 128:(fc + 1) * 128], hnT.t[:, kc, :], kc == 0, kc == 15, [wb.b, hnT.b], [p.b])
                r_ = rl[(fb * 4 + fc) % 2]
                k.actf(r_.t[:], p.t[:, :], AF.Relu, [p.b], [r_.b])
                k.tt(k.dve, hid.t[:, fb * 4 + fc, :], r_.t[:], r_.t[:], ALU.mult, [r_.b], [hidB[fb]])
        for cb in range(4):
            accs = [k.ps() for _ in range(4)]
            for sub in range(4):
                wb, wvw = load_w(w2[sub * 2048:(sub + 1) * 2048, cb * 512:(cb + 1) * 512], 16, (22 + cb * 4 + sub, TB == 0))
                for j in range(4):
                    for fc in range(16):
                        f = sub * 16 + fc
                        k.mm(accs[j].t[:, :], hid.t[:, f, j * 128:(j + 1) * 128], wvw[:, fc, :], f == 0, f == 63,
                             [wb.b, hidB[f // 4]], [accs[j].b])
            for j in range(4):
                hs = h.t[:, j, cb * 512:(cb + 1) * 512]
                k.tt(k.dve, hs, accs[j].t[:, :], hs, ALU.add, [accs[j].b, hB[j]], [hB[j]])
        load_wbc(4)
        for j in range(4):
            hj = h.t[:, j, :]
            sb_ = G["ssb"][j]
            k.actf(un.t[:], hj, AF.Square, [hB[j]], [un.b, sb_], accum_out=ss.t[:, j:j + 1])
            k.actf(ss.t[:, j:j + 1], ss.t[:, j:j + 1], AF.Sqrt, [sb_, epsT.b], [sb_], scale=1.0 / D, bias=epsT.t[:, 0:1])
            k.op(k.dve, lambda e: e.reciprocal(ss.t[:, j:j + 1], ss.t[:, j:j + 1]), [sb_], [sb_])
            k.stt(hj, hj, ss.t[:, j:j + 1], wbc.t[:], ALU.mult, ALU.mult, [hB[j], sb_, wbc.b], [hB[j]])
        for j in range(4):
            k.dma(k.sp, out[t0 + j * 128:t0 + (j + 1) * 128, :], h.t[:, j, :], [hB[j]], [], hB[j])
    k._deps(k.sp, [], hB)
    st4.close()
    k.es.close()
    return nc


def build_mixers(nc, k, pT, oTd, convw_d, dnsc_d, dnw_d, mu_d, rwv_d, lora_d, g2_d, ones, ident, bg_next):
    st = contextlib.ExitStack()
    r = lambda ap: ap.bitcast(F32R)
    NB = 10
    big = [k.sb(f"big{i}", [128, S], F32, st) for i in range(NB)]
    free = list(range(NB))

    def balloc():
        return big[free.pop(0)]

    def bfree(*ts):
        for t_ in ts:
            free.append(big.index(t_))

    def sm(name, shape, dt=F32):
        return k.sb("m_" + name, shape, dt, st)

    Ls = sm("Ls", [128, 128]); Li = sm("Li", [128, 128]); UU = sm("UU", [128, 256])
    blk = sm("blk", [128, 128]); rmask = sm("rmask", [128, S], BF16); selh = sm("selh", [16, 128])
    epsG = sm("epsG", [128, 2])

    def asel(out, pat, cm, op, R, W):
        k.op(k.pool, lambda e: e.affine_select(out=out, in_=ones.t[:], pattern=pat, compare_op=op, fill=0.0, base=0,
                                               channel_multiplier=cm), [ones.b] + R, W)
    asel(Ls.t[:], [[-1, 128]], 1, ALU.is_gt, [], [Ls.b])
    k.ts(k.dve, Ls.t[:], Ls.t[:], -1.0, None, ALU.mult, None, [Ls.b], [Ls.b])
    Li2 = sm("Li2", [128, 256]); II2 = sm("II2", [128, 256])
    for i_ in range(2):
        asel(Li2.t[:, i_ * 128:(i_ + 1) * 128], [[-1, 128]], 1, ALU.is_ge, [], [Li2.b])
        k.cp(II2.t[:, i_ * 128:(i_ + 1) * 128], ident.t[:], [ident.b], [II2.b], eng=k.pool)
    asel(Li.t[:], [[-1, 128]], 1, ALU.is_ge, [], [Li.b])
    asel(UU.t[:, 0:128], [[1, 128]], -1, ALU.is_gt, [], [UU.b])
    asel(UU.t[:, 128:256], [[1, 128]], -1, ALU.is_ge, [], [UU.b])
    k.op(k.pool, lambda e: e.memset(blk.t[:], 0.0), [], [blk.b])
    k.op(k.pool, lambda e: e.memset(blk.t[0:64, 0:64], 1.0), [], [blk.b])
    k.op(k.pool, lambda e: e.memset(blk.t[64:128, 64:128], 1.0), [], [blk.b])
    k.op(k.pool, lambda e: e.memset(rmask.t[:], 1.0), [], [rmask.b])
    k.op(k.pool, lambda e: e.memset(rmask.t[:].rearrange("p (c t) -> p c t", t=128)[:, :, 0:1], 0.0), [], [rmask.b])
    k.op(k.pool, lambda e: e.memset(epsG.t[:, 0:1], 64e-5), [], [epsG.b])
    k.op(k.pool, lambda e: e.memset(epsG.t[:, 1:2], 1e-6), [], [epsG.b])
    convw = sm("convw", [128, 96]); dnsc = sm("dnsc", [16, 2]); dnw = sm("dnw", [128, 1])
    mu = sm("mu", [128, 26]); omm = sm("omm", [128, 26]); rwv = sm("rwv", [128, 56])
    for t_, d_ in ((convw, convw_d), (dnsc, dnsc_d), (dnw, dnw_d), (mu, mu_d), (rwv, rwv_d)):
        k.dma(k.sp, t_.t[:], d_, [], [t_.b], t_.b)
    k.ts(k.dve, omm.t[:], mu.t[:], -1.0, 1.0, ALU.mult, ALU.add, [mu.b], [omm.b])

    def sq(name, w=128):
        return sm(name, [128, w])
    def run_rr(gens):
        gens = list(gens)
        while gens:
            for g_ in list(gens):
                try:
                    next(g_)
                except StopIteration:
                    gens.remove(g_)

    def neumann_multi(probs, nlev):
        for pr in probs:
            pr["Ao"] = pr["A1"]; pr["BPo"] = pr["BP"][0]
        for lv in range(nlev):
            last = lv == nlev - 1
            yield from k.need((1 if last else 2) * len(probs))
            for pr in probs:
                Ao, BPo = pr["Ao"], pr["BPo"]
                pr["pa"] = k.psA()
                if last:
                    k.mm(pr["pa"].t[:, 0:128], r(Ao.t[:]), r(BPo.t[:, 128:256]), True, True, [Ao.b, BPo.b], [pr["pa"].b])
                else:
                    k.mm(pr["pa"].t[:, 0:256], r(Ao.t[:]), r(BPo.t[:]), True, True, [Ao.b, BPo.b], [pr["pa"].b])
                    pr["pb"] = k.psA()
                    k.mm(pr["pb"].t[:, 0:128], r(BPo.t[:, 0:128]), r(Ao.t[:]), True, True, [Ao.b, BPo.b], [pr["pb"].b])
            yield
            for pr in probs:
                BPo = pr["BPo"]
                if last:
                    k.tt(k.dve, r(pr["Tout"].t[:]), pr["pa"].t[:, 0:128], BPo.t[:, 128:256], ALU.add, [pr["pa"].b, BPo.b], [pr["Tout"].b])
                    k.psF(pr["pa"])
                else:
                    An, BPn = pr["Ap"][lv % 2], pr["BP"][(lv + 1) % 2]
                    k.cp(r(An.t[:]), pr["pb"].t[:, 0:128], [pr["pb"].b], [An.b], eng=k.act)
                    k.cp(r(BPn.t[:, 0:128]), pr["pa"].t[:, 0:128], [pr["pa"].b], [BPn.b], eng=k.dve)
                    k.tt(k.dve, r(BPn.t[:, 128:256]), pr["pa"].t[:, 128:256], BPo.t[:, 128:256], ALU.add, [pr["pa"].b, BPo.b], [BPn.b])
                    k.psF(pr["pa"], pr["pb"])
                    pr["Ao"], pr["BPo"] = An, BPn
            yield

    def neumann_pairs(pairs, nlev):
        for pr in pairs:
            pr["Ao"] = pr["A1_2"]; pr["BPo"] = pr["BP2"][0]
        for lv in range(nlev):
            last = lv == nlev - 1
            yield from k.need((1 if last else 2) * len(pairs))
            for pr in pairs:
                Ao, BPo = pr["Ao"], pr["BPo"]
                pr["pa"] = k.psA()
                if not last:
                    pr["pb"] = k.psA()
                for i in range(2):
                    a_i = r(Ao.t[:, i * 128:(i + 1) * 128])
                    if last:
                        k.mm(pr["pa"].t[:, i * 128:(i + 1) * 128], a_i, r(BPo.t[:, i * 256 + 128:(i + 1) * 256]), True, True,
                             [Ao.b, BPo.b], [pr["pa"].b])
                    else:
                        k.mm(pr["pa"].t[:, i * 256:(i + 1) * 256], a_i, r(BPo.t[:, i * 256:(i + 1) * 256]), True, True,
                             [Ao.b, BPo.b], [pr["pa"].b])
                        k.mm(pr["pb"].t[:, i * 128:(i + 1) * 128], r(BPo.t[:, i * 256:i * 256 + 128]), a_i, True, True,
                             [Ao.b, BPo.b], [pr["pb"].b])
            yield
            for pr in pairs:
                BPo = pr["BPo"]
                bpo3 = BPo.t[:].rearrange("p (i c) -> p i c", c=256)
                if last:
                    k.tt(k.dve, r(pr["Tout2"].t[:].rearrange("p (i c) -> p i c", c=128)),
                         pr["pa"].t[:, 0:256].rearrange("p (i c) -> p i c", c=128), bpo3[:, :, 128:256], ALU.add,
                         [pr["pa"].b, BPo.b], [pr["Tout2"].b])
                    k.psF(pr["pa"])
                else:
                    An, BPn = pr["Ap2"][lv % 2], pr["BP2"][(lv + 1) % 2]
                    bpn3 = BPn.t[:].rearrange("p (i c) -> p i c", c=256)
                    pa3 = pr["pa"].t[:, :].rearrange("p (i c) -> p i c", c=256)
                    k.cp(r(An.t[:]), pr["pb"].t[:, 0:256], [pr["pb"].b], [An.b], eng=k.act)
                    k.cp(r(bpn3[:, :, 0:128]), pa3[:, :, 0:128], [pr["pa"].b], [BPn.b], eng=k.act)
                    k.tt(k.dve, r(bpn3[:, :, 128:256]), pa3[:, :, 128:256], bpo3[:, :, 128:256], ALU.add, [pr["pa"].b, BPo.b], [BPn.b])
                    k.psF(pr["pa"], pr["pb"])
                    pr["Ao"], pr["BPo"] = An, BPn
            yield

    def load_rows(dst, r0, nrows=128):
        k.dma(k.sp, dst.t[0:nrows, :], pT.t[r0:r0 + nrows, :], pT.bl[r0 // 128:(r0 + nrows - 1) // 128 + 1], [dst.b], dst.b)

    ones_bf = sm("ones_bf", [128, 128], BF16); blk_bf = sm("blk_bf", [128, 128], BF16)
    k.cp(ones_bf.t[:], ones.t[:], [ones.b], [ones_bf.b], eng=k.dve)
    k.cp(blk_bf.t[:], blk.t[:], [blk.b], [blk_bf.b], eng=k.dve)

    def bfv(t_):
        return t_.t[:].bitcast(BF16)[:, 0:S]

    def psum_bcast_sum(src, lhsT, lhsTb, fn):
        sv = bfv(src)
        for tb in range(4):
            p = k.ps()
            k.mm(p.t[:, :], lhsT, sv[:, tb * 512:(tb + 1) * 512], True, True, [lhsTb, src.b], [p.b])
            fn(tb, p)

    obf = [sm("obf0", [128, S], BF16)] * 2
    octr = [0]

    gc16 = balloc()
    st_dn = contextlib.ExitStack()

    def smd(name, shape, dt=F32):
        return k.sb("m_" + name, shape, dt, st_dn)
    gcT = smd("gcT", [128, 256]); betaT = smd("betaT", [128, 256]); kdT = smd("kdT", [128, 256]); egT = smd("egT", [128, 256])
    bgT = smd("bgT", [128, 16, 8]); negA = smd("negA", [16, 1])
    if True:
        ab = balloc(); t1 = balloc(); t2 = balloc(); beta16 = balloc(); kd16 = balloc()
        R16 = slice(0, 16)
        load_rows(ab, 4096, 16)
        dtb = dnsc.t[:, 1:2]
        k.actf(t1.t[R16, :], ab.t[R16, :], AF.Abs, [ab.b, dnsc.b], [t1.b], bias=dtb)
        k.actf(t1.t[R16, :], t1.t[R16, :], AF.Exp, [t1.b], [t1.b], scale=-1.0)
        k.actf(t1.t[R16, :], t1.t[R16, :], AF.Ln, [t1.b, ones.b], [t1.b], bias=ones.t[0:16, 0:1])
        k.ts(k.dve, t2.t[R16, :], ab.t[R16, :], dtb, 0.0, ALU.add, ALU.max, [ab.b, dnsc.b], [t2.b])
        k.tt(k.dve, t1.t[R16, :], t1.t[R16, :], t2.t[R16, :], ALU.add, [t1.b, t2.b], [t1.b])
        k.actf(negA.t[:], dnsc.t[:, 0:1], AF.Exp, [dnsc.b], [negA.b])
        k.ts(k.dve, negA.t[:], negA.t[:], -1.0, None, ALU.mult, None, [negA.b], [negA.b])
        k.ts(k.dve, t1.t[R16, :], t1.t[R16, :], negA.t[:, 0:1], None, ALU.mult, None, [t1.b, negA.b], [t1.b])
        k.actf(beta16.t[R16, :], ab.t[R16, :], AF.Sigmoid, [ab.b], [beta16.b])
        k.op(k.dve, lambda e: e.tensor_tensor_scan(gc16.t[R16, :], rmask.t[R16, :], t1.t[R16, :], 0.0, ALU.mult, ALU.add),
             [rmask.b, t1.b], [gc16.b])
        for n in range(NT):
            cs = slice(n * 128, (n + 1) * 128)
            k.ts(k.dve, kd16.t[R16, cs], gc16.t[R16, cs], gc16.t[R16, n * 128 + 127:n * 128 + 128], None, ALU.subtract, None,
                 [gc16.b], [kd16.b])
        k.actf(kd16.t[R16, :], kd16.t[R16, :], AF.Exp, [kd16.b], [kd16.b], scale=-1.0)
        for src, dst in ((gc16, gcT), (beta16, betaT), (kd16, kdT)):
            p = k.ps()
            for n in range(NT):
                k.mm(p.t[:, n * 16:(n + 1) * 16], src.t[R16, n * 128:(n + 1) * 128], ident.t[0:16, 0:16], True, True, [src.b, ident.b], [p.b])
            k.cp(dst.t[:], p.t[:, 0:256], [p.b], [dst.b])
        k.actf(egT.t[:], gcT.t[:], AF.Exp, [gcT.b], [egT.b])
        k.tt(k.dve, bgT.t[:], betaT.t[:].rearrange("p (n r) -> p n r", r=16)[:, :, 8:16],
             egT.t[:].rearrange("p (n r) -> p n r", r=16)[:, :, 0:8], ALU.mult, [betaT.b, egT.b], [bgT.b])
        bfree(ab, t1, t2, beta16, kd16)
    ngcT = smd("ngcT", [128, 256])
    k.ts(k.dve, ngcT.t[:], gcT.t[:], -1.0, None, ALU.mult, None, [gcT.b], [ngcT.b])
    ngcT3 = ngcT.t[:].rearrange("p (n r) -> p n r", r=16)
    gcT3 = gcT.t[:].rearrange("p (n r) -> p n r", r=16)
    betaT3 = betaT.t[:].rearrange("p (n r) -> p n r", r=16)
    kdT3 = kdT.t[:].rearrange("p (n r) -> p n r", r=16)

    WDN = 6

    def sqd(name, w=128):
        return k.sb("m_" + name, [128, w], F32, st_dn)
    St = [sqd("St0"), sqd("St1")]
    qTr = sqd("qTr", S); kTr = sqd("kTr", S); qgr = sqd("qgr", S)
    dnw_t = []
    WDP = 4
    for w_ in range(WDP):
        d_ = {nm: sqd(f"{nm}{w_}", 256) for nm in ("t1_2", "El_2", "MA_2", "MD_2", "at_2", "attnT_2", "nwT_2", "A1_2", "Tout2")}
        d_["Ap2"] = [sqd(f"Ap20_{w_}", 256), sqd(f"Ap21_{w_}", 256)]; d_["BP2"] = [sqd(f"BP20_{w_}", 512), sqd(f"BP21_{w_}", 512)]
        d_["c"] = [{nm: sqd(f"{nm}{w_}_{i_}") for nm in ("kbg", "kd", "vb", "vnew")} for i_ in range(2)]
        dnw_t.append(d_)

    def conv_silu(xr, gi):
        c = balloc()
        w = lambda j: convw.t[:, gi * 4 + j:gi * 4 + j + 1]
        k.ts(k.dve, c.t[:], xr.t[:], w(3), None, ALU.mult, None, [xr.b, convw.b], [c.b])
        for sh in (1, 2, 3):
            k.stt(c.t[:, sh:S], xr.t[:, 0:S - sh], w(3 - sh), c.t[:, sh:S], ALU.mult, ALU.add, [xr.b, convw.b, c.b], [c.b])
        k.actf(c.t[:], c.t[:], AF.Silu, [c.b], [c.b])
        bfree(xr)
        return c

    def l2n(xc, scale, dst):
        sq_ = balloc(); rn = balloc()
        k.actf(bfv(sq_), xc.t[:], AF.Square, [xc.b], [sq_.b])

        def fn(tb, p):
            ts_ = slice(tb * 512, (tb + 1) * 512)
            k.actf(rn.t[:, ts_], p.t[:, :], AF.Ln, [p.b, epsG.b], [rn.b], bias=epsG.t[:, 1:2])
        psum_bcast_sum(sq_, ones_bf.t[:], ones_bf.b, fn)
        k.actf(rn.t[:], rn.t[:], AF.Exp, [rn.b], [rn.b], scale=-0.5)
        k.stt(r(dst.t[:]), xc.t[:], scale, rn.t[:], ALU.mult, ALU.mult, [xc.b, rn.b], [dst.b])
        bfree(sq_, rn, xc)

    for h in range(DBG_DN):
        qr = balloc(); load_rows(qr, h * 128)
        kr = balloc(); load_rows(kr, 1024 + h * 128)
        vr = balloc(); load_rows(vr, 2048 + h * 128)
        if DBG_STEP == 0:
            st.close(); return
        qT = conv_silu(qr, h); kT = conv_silu(kr, 8 + h); vT = conv_silu(vr, 16 + h)
        if DBG_STEP == 1:
            st.close(); return
        l2n(qT, 128 ** -0.5, qTr); l2n(kT, 1.0, kTr)
        qT, kT = qTr, kTr
        if DBG_STEP == 2:
            st.close(); return
        gcb = balloc(); egcb = balloc()

        def fn(tb, p):
            ts_ = slice(tb * 512, (tb + 1) * 512)
            k.cp(gcb.t[:, ts_], p.t[:, :], [p.b], [gcb.b], eng=k.dve)
            k.actf(egcb.t[:, ts_], p.t[:, :], AF.Exp, [p.b], [egcb.b])
        k.ts(k.dve, selh.t[:], ones.t[0:16, :], ident.t[0:16, h:h + 1], None, ALU.mult, None, [ones.b, ident.b], [selh.b])
        for tb in range(4):
            p = k.ps()
            k.mm(p.t[:, :], selh.t[:], gc16.t[0:16, tb * 512:(tb + 1) * 512], True, True, [selh.b, gc16.b], [p.b])
            fn(tb, p)
        qg = qgr
        k.tt(k.dve, r(qg.t[:]), qT.t[:], egcb.t[:], ALU.mult, [qT.b, egcb.b], [qg.b])
        oT = balloc()
        if DBG_STEP == 3:
            st.close(); return
        k.ts(k.dve, r(St[0].t[:]), ident.t[:], 0.0, None, ALU.mult, None, [ident.b], [St[0].b])
        seq_done = [0]

        def dn_worker(w_, h=h, qT=qT, kT=kT, vT=vT, gcb=gcb, egcb=egcb, qg=qg, oT=oT, seq_done=seq_done):
            d_ = dnw_t[w_]
            C = d_["c"]
            H2 = [slice(0, 128), slice(128, 256)]
            for n0 in range(2 * w_, DBG_CH, 2 * WDP):
                ns = [n0, n0 + 1]
                css = [slice(n * 128, (n + 1) * 128) for n in ns]
                yield from k.need(2)
                pkt = k.psA(); pvt = k.psA()
                for i, n in enumerate(ns):
                    k.tr(pkt.t[:, H2[i]], kT.t[:, css[i]], ident.t[:], [kT.b, ident.b], [pkt.b])
                    k.tr(pvt.t[:, H2[i]], vT.t[:, css[i]], ident.t[:], [vT.b, ident.b], [pvt.b])
                    k.actf(d_["t1_2"].t[:, H2[i]], gcb.t[:, css[i]], AF.Relu, [gcb.b, ngcT.b], [d_["t1_2"].b], bias=ngcT3[:, n, h:h + 1])
                yield
                for i, n in enumerate(ns):
                    k.actf(r(C[i]["kbg"].t[:]), pkt.t[:, H2[i]], AF.Copy, [pkt.b, bgT.b], [C[i]["kbg"].b], scale=bgT.t[:, n, h:h + 1])
                    k.actf(r(C[i]["kd"].t[:]), pkt.t[:, H2[i]], AF.Copy, [pkt.b, kdT.b], [C[i]["kd"].b], scale=kdT3[:, n, h:h + 1])
                    k.ts(k.dve, r(C[i]["vb"].t[:]), pvt.t[:, H2[i]], betaT3[:, n, 8 + h:9 + h], None, ALU.mult, None, [pvt.b, betaT.b], [C[i]["vb"].b])
                k.psF(pkt, pvt)
                k.actf(d_["El_2"].t[:], d_["t1_2"].t[:], AF.Exp, [d_["t1_2"].b], [d_["El_2"].b], scale=-1.0)
                yield
                yield from k.need(2)
                pk = k.psA(); pq = k.psA()
                for i, n in enumerate(ns):
                    k.mm(pk.t[:, H2[i]], r(kT.t[:, css[i]]), r(kT.t[:, css[i]]), True, True, [kT.b], [pk.b])
                    k.mm(pq.t[:, H2[i]], r(qT.t[:, css[i]]), r(kT.t[:, css[i]]), True, True, [qT.b, kT.b], [pq.b])
                    k.stt(d_["MA_2"].t[:, H2[i]], d_["El_2"].t[:, H2[i]], betaT3[:, n, 8 + h:9 + h], Ls.t[:], ALU.mult, ALU.mult,
                          [d_["El_2"].b, betaT.b, Ls.b], [d_["MA_2"].b])
                k.tt(k.pool, d_["MD_2"].t[:], d_["El_2"].t[:], Li2.t[:], ALU.mult, [d_["El_2"].b, Li2.b], [d_["MD_2"].b])
                yield
                k.tt(k.dve, r(d_["A1_2"].t[:]), pk.t[:, 0:256], d_["MA_2"].t[:], ALU.mult, [pk.b, d_["MA_2"].b], [d_["A1_2"].b])
                k.tt(k.dve, d_["at_2"].t[:], pq.t[:, 0:256], d_["MD_2"].t[:], ALU.mult, [pq.b, d_["MD_2"].b], [d_["at_2"].b])
                k.psF(pk, pq)
                yield
                yield from k.need(2)
                pa = k.psA(); pb = k.psA()
                for i in range(2):
                    k.tr(pb.t[:, H2[i]], d_["A1_2"].t[:, H2[i]], ident.t[:], [d_["A1_2"].b, ident.b], [pb.b])
                    k.tr(pa.t[:, H2[i]], d_["at_2"].t[:, H2[i]], ident.t[:], [d_["at_2"].b, ident.b], [pa.b])
                yield
                bp3 = d_["BP2"][0].t[:].rearrange("p (i c) -> p i c", c=256)
                k.cp(r(bp3[:, :, 0:128]), pb.t[:, 0:256].rearrange("p (i c) -> p i c", c=128), [pb.b], [d_["BP2"][0].b], eng=k.act)
                k.cp(r(bp3[:, :, 128:256]), II2.t[:].rearrange("p (i c) -> p i c", c=128), [II2.b], [d_["BP2"][0].b], eng=k.pool)
                k.cp(r(d_["attnT_2"].t[:]), pa.t[:, 0:256], [pa.b], [d_["attnT_2"].b], eng=k.act)
                k.psF(pa, pb)
                yield
                yield from neumann_pairs([d_], 7)
                yield from k.need(1)
                pw = k.psA()
                for i in range(2):
                    k.mm(pw.t[:, H2[i]], r(C[i]["kbg"].t[:]), r(d_["Tout2"].t[:, H2[i]]), True, True, [C[i]["kbg"].b, d_["Tout2"].b], [pw.b])
                yield
                k.actf(r(d_["nwT_2"].t[:]), pw.t[:, 0:256], AF.Copy, [pw.b], [d_["nwT_2"].b], scale=-1.0)
                k.psF(pw)
                yield
                for i, n in enumerate(ns):
                    while seq_done[0] < n:
                        yield
                    cs = css[i]
                    So, Sn = St[n % 2], St[(n + 1) % 2]
                    yield from k.need(1)
                    pv = k.psA()
                    k.mm(pv.t[:, 0:128], r(d_["Tout2"].t[:, H2[i]]), r(C[i]["vb"].t[:]), True, False, [d_["Tout2"].b, C[i]["vb"].b], [pv.b])
                    k.mm(pv.t[:, 0:128], r(d_["nwT_2"].t[:, H2[i]]), r(So.t[:]), False, True, [d_["nwT_2"].b, So.b], [pv.b])
                    vn = C[i]["vnew"]
                    k.cp(r(vn.t[:]), pv.t[:, 0:128], [pv.b], [vn.b], eng=k.act)
                    k.psF(pv)
                    yield from k.need(2)
                    po = k.psA(); pS = k.psA()
                    k.mm(pS.t[:, 0:128], r(C[i]["kd"].t[:]), r(vn.t[:]), True, True, [C[i]["kd"].b, vn.b], [pS.b])
                    k.mm(po.t[:, 0:128], r(So.t[:]), r(qg.t[:, cs]), True, False, [So.b, qg.b], [po.b])
                    k.mm(po.t[:, 0:128], r(vn.t[:]), r(d_["attnT_2"].t[:, H2[i]]), False, True, [vn.b, d_["attnT_2"].b], [po.b])
                    k.stt(r(Sn.t[:]), So.t[:], egcb.t[:, n * 128 + 127:n * 128 + 128], pS.t[:, 0:128], ALU.mult, ALU.add,
                          [So.b, egcb.b, pS.b], [Sn.b])
                    k.cp(oT.t[:, cs], po.t[:, 0:128], [po.b], [oT.b], eng=k.act)
                    k.psF(po, pS)
                    seq_done[0] = n + 1
                    yield
        bg_next(3)
        run_rr([dn_worker(w_) for w_ in range(WDP)])
        if DBG_STEP == 14:
            st.close(); return
        bfree(vT, gcb, egcb)
        zr = balloc(); load_rows(zr, 3072 + h * 128)
        sq_ = balloc(); rn = balloc()
        k.actf(bfv(sq_), oT.t[:], AF.Square, [oT.b], [sq_.b])

        def fn2(tb, p):
            ts_ = slice(tb * 512, (tb + 1) * 512)
            k.actf(rn.t[:, ts_], p.t[:, :], AF.Ln, [p.b, epsG.b], [rn.b], bias=epsG.t[:, 1:2], scale=1.0 / 128)
        psum_bcast_sum(sq_, ones_bf.t[:], ones_bf.b, fn2)
        k.actf(rn.t[:], rn.t[:], AF.Exp, [rn.b], [rn.b], scale=-0.5)
        k.actf(zr.t[:], zr.t[:], AF.Silu, [zr.b], [zr.b])
        k.stt(oT.t[:], oT.t[:], dnw.t[:, 0:1], rn.t[:], ALU.mult, ALU.mult, [oT.b, dnw.b, rn.b], [oT.b])
        ob = obf[octr[0] % 2]; octr[0] += 1
        k.tt(k.dve, ob.t[:], oT.t[:], zr.t[:], ALU.mult, [oT.b, zr.b], [ob.b])
        k.dma(k.sp, oTd.t[h * 128:(h + 1) * 128, :], ob.t[:], [ob.b], [oTd.bl[h]], ob.b)
        bfree(zr, sq_, rn, oT)
    bfree(gc16)
    st_dn.close()
    k.barrier()

    wa = balloc(); sg = balloc()
    RW0 = DNC

    def lerp(xr, gi):
        t_ = balloc()
        k.op(k.pool, lambda e: e.memset(t_.t[:, 0:1], 0.0), [], [t_.b])
        k.ts(k.dve, t_.t[:, 1:S], xr.t[:, 0:S - 1], mu.t[:, gi:gi + 1], None, ALU.mult, None, [xr.b, mu.b], [t_.b])
        k.stt(xr.t[:], xr.t[:], omm.t[:, gi:gi + 1], t_.t[:], ALU.mult, ALU.add, [xr.b, omm.b, t_.b], [xr.b])
        bfree(t_)
    load_rows(wa, RW0 + 3072); lerp(wa, 24)
    load_rows(sg, RW0 + 3200); lerp(sg, 25)
    k.actf(wa.t[0:64, :], wa.t[0:64, :], AF.Tanh, [wa.b], [wa.b])
    k.actf(sg.t[:], sg.t[:], AF.Sigmoid, [sg.b], [sg.b])

    WRW = 4
    st_rw = contextlib.ExitStack()
    lora = k.sb("m_lora", [128, 1024], F32, st_rw); g2 = k.sb("m_g2", [128, 1024], F32, st_rw)
    for t_, d_ in ((lora, lora_d), (g2, g2_d)):
        k.dma(k.sp, t_.t[:], d_, [], [t_.b], t_.b)

    def sqr(name, w=128):
        return k.sb("m_" + name, [128, w], F32, st_rw)
    Ht = [sqr("Ht0", 128), sqr("Ht1", 128)]
    rww_t = []
    for w_ in range(WRW):
        d_ = {nm: sqr(f"r{nm}{w_}") for nm in ("e1", "e2", "e3", "e4", "Bt", "Kt", "bh", "kh", "rhs1", "AVc", "KVc", "YVc", "Gt")}
        for nm in ("BhP", "KhP", "VP", "UP"):
            t_ = sqr(f"r{nm}2_{w_}", 256)
            d_[nm + "2"] = t_
            k.ts(k.dve, r(t_.t[:]), II2.t[:], 0.0, None, ALU.mult, None, [II2.b], [t_.b])
            d_[nm] = [TV(t_.t[:, 0:128], t_.b), TV(t_.t[:, 64:192], t_.b)]
        d_["ar"] = sqr(f"rar{w_}", 256)
        d_["hd"] = []
        for hh in range(2):
            e_ = {}
            e_["mb"] = sqr(f"rmb{w_}_{hh}", 256); e_["mk"] = sqr(f"rmk{w_}_{hh}", 256)
            d_["hd"].append(e_)
        d_["A1_2"] = sqr(f"rA12_{w_}", 256); d_["Tout2"] = sqr(f"rTr2_{w_}", 256)
        d_["Ap2"] = [sqr(f"rAp20_{w_}", 256), sqr(f"rAp21_{w_}", 256)]
        d_["BP2"] = [sqr(f"rBP20_{w_}", 512), sqr(f"rBP21_{w_}", 512)]
        rww_t.append(d_)
    V = lambda j: rwv.t[:, j * 8:(j + 1) * 8]

    for g in range(DBG_RW):
        rT = balloc(); load_rows(rT, RW0 + g * 128); lerp(rT, g)
        kl = balloc(); load_rows(kl, RW0 + 1024 + g * 128); lerp(kl, 8 + g)
        vT = balloc(); load_rows(vT, RW0 + 2048 + g * 128); lerp(vT, 16 + g)
        sig = balloc(); a_ = balloc()
        gsl = slice(g * 128, (g + 1) * 128)
        for tb in range(4):
            ts_ = slice(tb * 512, (tb + 1) * 512)
            p = k.ps()
            k.mm(p.t[:, :], lora.t[0:64, gsl], wa.t[0:64, ts_], True, True, [lora.b, wa.b], [p.b])
            k.actf(sig.t[:, ts_], p.t[:, :], AF.Sigmoid, [p.b, rwv.b], [sig.b], bias=V(0)[:, g:g + 1])
            p = k.ps()
            k.mm(p.t[:, :], lora.t[64:128, gsl], wa.t[64:128, ts_], True, True, [lora.b, wa.b], [p.b])
            k.actf(a_.t[:, ts_], p.t[:, :], AF.Sigmoid, [p.b, rwv.b], [a_.b], bias=V(1)[:, g:g + 1])
        kk = balloc(); sq_ = balloc(); rn = balloc()
        k.ts(k.dve, kk.t[:], kl.t[:], V(2)[:, g:g + 1], None, ALU.mult, None, [kl.b, rwv.b], [kk.b])
        k.actf(bfv(sq_), kk.t[:], AF.Square, [kk.b], [sq_.b])

        def fnk(tb, p):
            ts_ = slice(tb * 512, (tb + 1) * 512)
            k.ts(k.dve, rn.t[:, ts_], p.t[:, :], 1e-24, None, ALU.max, None, [p.b], [rn.b])
        psum_bcast_sum(sq_, blk_bf.t[:], blk_bf.b, fnk)
        k.actf(rn.t[:], rn.t[:], AF.Ln, [rn.b], [rn.b])
        k.actf(rn.t[:], rn.t[:], AF.Exp, [rn.b], [rn.b], scale=-0.5)
        k.tt(k.dve, kk.t[:], kk.t[:], rn.t[:], ALU.mult, [kk.b, rn.b], [kk.b])
        bfree(sq_, rn)
        kf = balloc()
        k.ts(k.dve, kf.t[:], a_.t[:], -1.0, V(3)[:, g:g + 1], ALU.add, ALU.mult, [a_.b, rwv.b], [kf.b])
        k.stt(kf.t[:], kf.t[:], 1.0, kl.t[:], ALU.add, ALU.mult, [kf.b, kl.b], [kf.b])
        bT = balloc()
        k.tt(k.dve, bT.t[:], a_.t[:], kk.t[:], ALU.mult, [a_.b, kk.b], [bT.b])
        bfree(kl, a_)
        cum = balloc()
        k.op(k.dve, lambda e: e.tensor_tensor_scan(cum.t[:], rmask.t[:], sig.t[:], 0.0, ALU.mult, ALU.add), [rmask.b, sig.b], [cum.b])
        yT = balloc()
        k.ts(k.dve, r(Ht[0].t[:]), ident.t[:], 0.0, None, ALU.mult, None, [ident.b], [Ht[0].b])
        seq_done = [0]

        def rw_worker(w_, rT=rT, vT=vT, kk=kk, kf=kf, bT=bT, sig=sig, cum=cum, yT=yT, seq_done=seq_done):
            d_ = rww_t[w_]
            ar = d_["ar"]; e1 = d_["e1"]; e2 = d_["e2"]; e3 = d_["e3"]; e4 = d_["e4"]
            Bt_, Kt_, bh, kh = d_["Bt"], d_["Kt"], d_["bh"], d_["kh"]
            BhP, KhP, VP, UP, rhs1 = d_["BhP"], d_["KhP"], d_["VP"], d_["UP"], d_["rhs1"]
            HD = d_["hd"]
            RS = [slice(0, 64), slice(64, 128)]
            for n in range(w_, DBG_CH, WRW):
                cs = slice(n * 128, (n + 1) * 128)
                k.actf(e1.t[:], cum.t[:, cs], AF.Exp, [cum.b], [e1.b], scale=CDEC)
                k.actf(e2.t[:], cum.t[:, cs], AF.Exp, [cum.b], [e2.b], scale=-CDEC)
                k.tt(k.pool, e3.t[:], cum.t[:, cs], sig.t[:, cs], ALU.subtract, [cum.b, sig.b], [e3.b])
                k.ts(k.dve, e4.t[:], cum.t[:, cs], cum.t[:, n * 128 + 127:n * 128 + 128], None, ALU.subtract, None, [cum.b], [e4.b])
                yield
                k.actf(e3.t[:], e3.t[:], AF.Exp, [e3.b], [e3.b], scale=CDEC)
                k.actf(e4.t[:], e4.t[:], AF.Exp, [e4.b], [e4.b], scale=-CDEC)
                k.tt(k.pool, r(ar.t[:, 128:256]), rT.t[:, cs], e1.t[:], ALU.mult, [rT.b, e1.b], [ar.b])
                k.tt(k.pool, r(Bt_.t[:]), bT.t[:, cs], e2.t[:], ALU.mult, [bT.b, e2.b], [Bt_.b])
                k.tt(k.pool, r(Kt_.t[:]), kf.t[:, cs], e2.t[:], ALU.mult, [kf.b, e2.b], [Kt_.b])
                yield
                k.tt(k.dve, r(ar.t[:, 0:128]), kk.t[:, cs], e3.t[:], ALU.mult, [kk.b, e3.b], [ar.b])
                k.tt(k.pool, bh.t[:], bT.t[:, cs], e4.t[:], ALU.mult, [bT.b, e4.b], [bh.b])
                k.tt(k.pool, kh.t[:], kf.t[:, cs], e4.t[:], ALU.mult, [kf.b, e4.b], [kh.b])
                yield
                yield from k.need(3)
                trs = []
                for src, srcB, dst in ((bh.t[:], bh.b, d_["BhP2"]), (kh.t[:], kh.b, d_["KhP2"]), (vT.t[:, cs], vT.b, d_["VP2"])):
                    p = k.psA()
                    k.tr(p.t[:, 0:128], src, ident.t[:], [srcB, ident.b], [p.b])
                    trs.append((p, dst))
                yield
                for ti_, (p, dst2) in enumerate(trs):
                    k.cp(r(dst2.t[:].rearrange("p (i c) -> p i c", c=128)[:, :, 0:64]), p.t[:, 0:128].rearrange("p (i c) -> p i c", c=64),
                         [p.b], [dst2.b], eng=k.act)
                    k.psF(p)
                yield from k.need(4)
                pms = []
                for hh in range(2):
                    R = RS[hh]
                    pm = k.psA(); pm2 = k.psA()
                    k.mm(pm.t[:, 0:256], r(Bt_.t[R, :]), r(ar.t[R, :]), True, True, [Bt_.b, ar.b], [pm.b])
                    k.mm(pm2.t[:, 0:256], r(Kt_.t[R, :]), r(ar.t[R, :]), True, True, [Kt_.b, ar.b], [pm2.b])
                    pms.append((pm, pm2))
                yield
                for hh in range(2):
                    pm, pm2 = pms[hh]
                    k.tt(k.dve, r(HD[hh]["mb"].t[:]), pm.t[:, 0:256], UU.t[:], ALU.mult, [pm.b, UU.b], [HD[hh]["mb"].b])
                    k.tt(k.dve, r(HD[hh]["mk"].t[:]), pm2.t[:, 0:256], UU.t[:], ALU.mult, [pm2.b, UU.b], [HD[hh]["mk"].b])
                    k.psF(pm, pm2)
                yield from k.need(2)
                pas = []
                for hh in range(2):
                    R = RS[hh]
                    pa = k.psA()
                    k.mm(pa.t[:, 0:128], r(ar.t[R, 0:128]), r(Bt_.t[R, :]), True, True, [ar.b, Bt_.b], [pa.b])
                    pas.append(pa)
                yield
                for hh in range(2):
                    e_ = HD[hh]
                    k.tt(k.dve, r(d_["A1_2"].t[:, hh * 128:(hh + 1) * 128]), pas[hh].t[:, 0:128], Ls.t[:], ALU.mult,
                         [pas[hh].b, Ls.b], [d_["A1_2"].b])
                    k.actf(r(d_["BP2"][0].t[:, hh * 256:hh * 256 + 128]), e_["mb"].t[:, 0:128], AF.Copy, [e_["mb"].b], [d_["BP2"][0].b], scale=-1.0)
                    k.psF(pas[hh])
                yield
                k.cp(r(d_["BP2"][0].t[:].rearrange("p (i c) -> p i c", c=256)[:, :, 128:256]), II2.t[:].rearrange("p (i c) -> p i c", c=128),
                     [II2.b], [d_["BP2"][0].b], eng=k.pool)
                yield from k.need(3)
                pAV = k.psA(); pKV = k.psA(); pYV = k.psA()
                for hh in range(2):
                    e_ = HD[hh]
                    k.mm(pAV.t[:, 0:128], r(e_["mk"].t[:, 0:128]), r(VP[hh].t[:]), hh == 0, hh == 1, [e_["mk"].b, VP[hh].b], [pAV.b])
                    k.mm(pKV.t[:, RS[hh]], r(KhP[hh].t[:]), r(VP[hh].t[:, RS[hh]]), True, True, [KhP[hh].b, VP[hh].b], [pKV.b])
                    k.mm(pYV.t[:, 0:128], r(VP[hh].t[:]), r(e_["mk"].t[:, 128:256]), hh == 0, hh == 1, [VP[hh].b, e_["mk"].b], [pYV.b])
                yield
                k.cp(d_["AVc"].t[:], pAV.t[:, 0:128], [pAV.b], [d_["AVc"].b], eng=k.act)
                k.cp(d_["KVc"].t[:], pKV.t[:, 0:128], [pKV.b], [d_["KVc"].b], eng=k.act)
                k.cp(d_["YVc"].t[:], pYV.t[:, 0:128], [pYV.b], [d_["YVc"].b], eng=k.act)
                k.psF(pAV, pKV, pYV)
                yield
                yield from neumann_pairs([d_], 7)
                while seq_done[0] < n:
                    yield
                Ho, Hn = Ht[n % 2], Ht[(n + 1) % 2]
                yield from k.need(1)
                pr_ = k.psA()
                k.mm(pr_.t[:, 0:128], r(ar.t[:, 0:128]), r(Ho.t[:]), True, True, [ar.b, Ho.b], [pr_.b])
                k.stt(d_["Gt"].t[:], Ho.t[:], e1.t[:, 127:128], d_["KVc"].t[:], ALU.mult, ALU.add, [Ho.b, e1.b, d_["KVc"].b], [d_["Gt"].b])
                k.stt(r(rhs1.t[:]), pr_.t[:, 0:128], -1.0, d_["AVc"].t[:], ALU.mult, ALU.subtract, [pr_.b, d_["AVc"].b], [rhs1.b])
                k.psF(pr_)
                yield from k.need(1)
                pu = k.psA()
                for hh in range(2):
                    k.mm(pu.t[:, RS[hh]], r(d_["Tout2"].t[:, hh * 128:(hh + 1) * 128]), r(rhs1.t[:, RS[hh]]), True, True,
                         [d_["Tout2"].b, rhs1.b], [pu.b])
                k.cp(r(d_["UP2"].t[:].rearrange("p (i c) -> p i c", c=128)[:, :, 0:64]), pu.t[:, 0:128].rearrange("p (i c) -> p i c", c=64),
                     [pu.b], [d_["UP2"].b], eng=k.act)
                k.psF(pu)
                yield from k.need(2)
                pY = k.psA(); pS = k.psA()
                for hh in range(2):
                    k.mm(pS.t[:, RS[hh]], r(BhP[hh].t[:]), r(UP[hh].t[:, RS[hh]]), True, True, [BhP[hh].b, UP[hh].b], [pS.b])
                k.mm(pY.t[:, 0:128], r(Ho.t[:]), r(ar.t[:, 128:256]), True, False, [Ho.b, ar.b], [pY.b])
                for hh in range(2):
                    e_ = HD[hh]
                    k.mm(pY.t[:, 0:128], r(UP[hh].t[:]), r(e_["mb"].t[:, 128:256]), False, hh == 1, [UP[hh].b, e_["mb"].b], [pY.b])
                k.tt(k.dve, r(Hn.t[:]), pS.t[:, 0:128], d_["Gt"].t[:], ALU.add, [pS.b, d_["Gt"].b], [Hn.b])
                k.tt(k.dve, yT.t[:, cs], pY.t[:, 0:128], d_["YVc"].t[:], ALU.add, [pY.b, d_["YVc"].b], [yT.b])
                k.psF(pY, pS)
                seq_done[0] = n + 1
                yield
        bg_next(3)
        run_rr([rw_worker(w_) for w_ in range(WRW)])
        bfree(kk, bT, sig, cum)
        rk = balloc(); yc = balloc(); sq_ = balloc(); rs_ = balloc()
        k.stt(bfv(rk), rT.t[:], V(4)[:, g:g + 1], kf.t[:], ALU.mult, ALU.mult, [rT.b, rwv.b, kf.b], [rk.b])
        k.actf(bfv(sq_), yT.t[:], AF.Copy, [yT.b], [sq_.b])

        def fnm(tb, p):
            ts_ = slice(tb * 512, (tb + 1) * 512)
            k.stt(yc.t[:, ts_], p.t[:, :], -1.0 / 64, yT.t[:, ts_], ALU.mult, ALU.add, [p.b, yT.b], [yc.b])
        psum_bcast_sum(sq_, blk_bf.t[:], blk_bf.b, fnm)
        k.actf(bfv(sq_), yc.t[:], AF.Square, [yc.b], [sq_.b])

        def fnv(tb, p):
            ts_ = slice(tb * 512, (tb + 1) * 512)
            k.actf(rs_.t[:, ts_], p.t[:, :], AF.Ln, [p.b, epsG.b], [rs_.b], bias=epsG.t[:, 0:1], scale=1.0 / 64)
        psum_bcast_sum(sq_, blk_bf.t[:], blk_bf.b, fnv)
        k.actf(rs_.t[:], rs_.t[:], AF.Exp, [rs_.b], [rs_.b], scale=-0.5)
        k.tt(k.dve, yc.t[:], yc.t[:], rs_.t[:], ALU.mult, [yc.b, rs_.b], [yc.b])
        k.ts(k.dve, yc.t[:], yc.t[:], V(5)[:, g:g + 1], V(6)[:, g:g + 1], ALU.mult, ALU.add, [yc.b, rwv.b], [yc.b])

        def fnb(tb, p):
            ts_ = slice(tb * 512, (tb + 1) * 512)
            k.tt(k.dve, rs_.t[:, ts_], p.t[:, :], vT.t[:, ts_], ALU.mult, [p.b, vT.b], [rs_.b])
        psum_bcast_sum(rk, blk_bf.t[:], blk_bf.b, fnb)
        k.tt(k.dve, yc.t[:], yc.t[:], rs_.t[:], ALU.add, [yc.b, rs_.b], [yc.b])
        ob = obf[octr[0] % 2]; octr[0] += 1
        for tb in range(4):
            ts_ = slice(tb * 512, (tb + 1) * 512)
            p = k.ps()
            k.mm(p.t[:, :], g2.t[:, gsl], sg.t[:, ts_], True, True, [g2.b, sg.b], [p.b])
            k.tt(k.dve, ob.t[:, ts_], p.t[:, :], yc.t[:, ts_], ALU.mult, [p.b, yc.b], [ob.b])
        k.dma(k.sp, oTd.t[1024 + g * 128:1024 + (g + 1) * 128, :], ob.t[:], [ob.b], [oTd.bl[8 + g]], ob.b)
        bfree(rk, yc, sq_, rs_, rT, kf, vT, yT)
    bfree(wa, sg)
    st_rw.close()
    st.close()


def prep_shared(inp):
    f = lambda a: np.ascontiguousarray(np.asarray(a, dtype=np.float32))
    sh = {}
    for kk_ in ("w_in", "w_out", "xa_wq", "xa_wk", "xa_wv", "xa_wo", "ffn_w1", "ffn_w2"):
        sh[kk_] = f(inp[kk_][0])
    sh["norms"] = f(np.stack([inp["mix_norm_w"][0], inp["xa_norm_w"][0], inp["mem_norm_w"][0], inp["ffn_norm_w"][0],
                              inp["final_norm_w"]], axis=0))
    cw = np.asarray(inp["dn_conv_w"][0])
    sh["convw"] = f(cw.reshape(4, 24, 128).transpose(2, 1, 0).reshape(128, 96))
    dn = np.zeros((16, 2), np.float32)
    dn[0:8, 0] = np.asarray(inp["dn_a_log"][0]); dn[0:8, 1] = np.asarray(inp["dn_dt_bias"][0])
    sh["dnsc"] = dn
    sh["dnw"] = f(np.asarray(inp["dn_norm_w"][0]).reshape(128, 1))
    sh["mu"] = f(np.asarray(inp["rw_mu"][0]).reshape(26, 128).T)
    vs = [np.asarray(inp[n][0]).reshape(8, 128).T for n in ("rw_w0", "rw_a0", "rw_k_k", "rw_k_a", "rw_r_k", "rw_ln_w", "rw_ln_b")]
    sh["rwv"] = f(np.concatenate(vs, axis=1))
    sh["lora"] = f(np.concatenate([np.asarray(inp["rw_w2"][0]), np.asarray(inp["rw_a2"][0])], axis=0))
    sh["g2"] = f(inp["rw_g2"][0])
    return sh


def kernel(**inp):
    sh = prep_shared(inp)
    xs = np.asarray(inp["x"], dtype=np.float32)
    ms = np.asarray(inp["mem"], dtype=np.float32)
    nc = build()
    in_maps = []
    for b in range(8):
        m = dict(sh)
        m["x"] = np.ascontiguousarray(xs[b])
        m["mem"] = np.ascontiguousarray(ms[b])
        in_maps.append(m)
    res = run_bass_kernel_spmd(nc, in_maps, core_ids=list(range(8)))
    return np.stack([np.asarray(r["out"], dtype=np.float32) for r in res.results], axis=0)
```

```python
import contextlib
import math
import numpy as np
import concourse.bass as bass
import concourse.mybir as mybir
from concourse.alu_op_type import AluOpType as ALU
from concourse.bass_utils import run_bass_kernel_spmd

F32 = mybir.dt.float32
BF16 = mybir.dt.bfloat16
F32R = mybir.dt.float32r
AF = mybir.ActivationFunctionType
AX = mybir.AxisListType

D = 2048
S = 2048
NT = 16
MEM = 256
DNC = 4112
INC = 7440
FF = 8192
EPS = 1e-6
CDEC = -math.exp(-0.5)
DBG_DN = 8
DBG_RW = 8
DBG_CH = NT
DBG_STEP = 99


class Sem:
    __slots__ = ("h", "name")

    def __init__(self, h, name):
        self.h = h
        self.name = name


class Buf:
    __slots__ = ("name", "w", "r", "dsem", "dcount", "excl")

    def __init__(self, name):
        self.name = name
        self.excl = False
        self.w = None
        self.r = {}
        self.dsem = None
        self.dcount = 0


class Eng:
    def __init__(self, name, h, sem):
        self.name = name
        self.h = h
        self.sem = sem
        self.count = 0
        self.waited = {}


class T:
    def __init__(self, t, name):
        self.t = t
        self.b = Buf(name)


class TV:
    def __init__(self, ap, b):
        self.t = ap
        self.b = b


class K:
    def __init__(self, nc):
        self.nc = nc
        self.es = contextlib.ExitStack()
        self.pe = self._eng("pe", nc.tensor)
        self.dve = self._eng("dve", nc.vector)
        self.act = self._eng("act", nc.scalar)
        self.pool = self._eng("pool", nc.gpsimd)
        self.sp = self._eng("sp", nc.sync)
        self.ninst = 0
        self._psi = 0
        self.psf = []
        self._ev = 0
        self.slots = []
        self.psfree = list(range(8))

    def new_sem(self, name):
        return Sem(self.es.enter_context(self.nc.semaphore(name)), name)

    def _eng(self, name, h):
        return Eng(name, h, self.new_sem("s_" + name))

    def sb(self, name, shape, dt, stack=None):
        t = (stack or self.es).enter_context(self.nc.sbuf_tensor(name, list(shape), dt))
        return T(t, name)

    def _deps(self, eng, reads, writes, extra=()):
        deps = {}
        for b in reads:
            if b.w is not None:
                s, v = b.w
                if v > deps.get(s, 0):
                    deps[s] = v
            if b.excl:
                for s, v in b.r.items():
                    if s is not eng.sem and v > deps.get(s, 0):
                        deps[s] = v
        for b in writes:
            if b.w is not None and not (eng is self.pe and b.w[0] is self.pe.sem):
                s, v = b.w
                if v > deps.get(s, 0):
                    deps[s] = v
            for s, v in b.r.items():
                if v > deps.get(s, 0):
                    deps[s] = v
        for s, v in extra:
            if v > deps.get(s, 0):
                deps[s] = v
        for s, v in deps.items():
            if eng.waited.get(s, 0) < v:
                eng.h.wait_ge(s.h, v)
                eng.waited[s] = v

    def op(self, eng, fn, reads=(), writes=()):
        self._deps(eng, reads, writes)
        inst = fn(eng.h)
        eng.count += 1
        inst.then_inc(eng.sem.h, 1)
        self.ninst += 1
        c = eng.count
        s = eng.sem
        for b in reads:
            b.r[s] = c
        for b in writes:
            b.w = (s, c)
            b.r = {}
        return inst

    def dma(self, q, out, in_, reads, writes, slot, **kw):
        if slot.dsem is None:
            slot.dsem = self.new_sem("d_" + slot.name)
            self.slots.append(slot)
        extra = [(slot.dsem, slot.dcount)] if slot.dcount else []
        self._deps(q, reads, writes, extra)
        inst = q.h.dma_start(out=out, in_=in_, **kw)
        slot.dcount += 16
        inst.then_inc(slot.dsem.h, 16)
        self.ninst += 1
        for b in reads:
            b.r[slot.dsem] = slot.dcount
        for b in writes:
            b.w = (slot.dsem, slot.dcount)
            b.r = {}
        return inst

    def ps(self):
        p = self.psf[self._psi % 8]
        self._psi += 1
        return p

    def psA(self):
        return self.psf[self.psfree.pop(0)]

    def psF(self, *ps_):
        for p in ps_:
            self.psfree.append(self.psf.index(p))

    def need(self, m):
        while len(self.psfree) < m:
            yield

    def barrier(self):
        engs = [self.pe, self.dve, self.act, self.pool, self.sp]
        for e in engs:
            for o in engs:
                if o is not e and o.count and e.waited.get(o.sem, 0) < o.count:
                    e.h.wait_ge(o.sem.h, o.count)
                    e.waited[o.sem] = o.count
            for sl in self.slots:
                if e.waited.get(sl.dsem, 0) < sl.dcount:
                    e.h.wait_ge(sl.dsem.h, sl.dcount)
                    e.waited[sl.dsem] = sl.dcount

    def mm(self, out, lhsT, rhs, start, stop, R, W):
        return self.op(self.pe, lambda e: e.matmul(out, lhsT, rhs, start=start, stop=stop), R, W)

    def tr(self, out, in_, ident, R, W):
        return self.op(self.pe, lambda e: e.transpose(out, in_, ident), R, W)

    def tt(self, eng, out, a, b, op, R, W):
        return self.op(eng, lambda e: e.tensor_tensor(out=out, in0=a, in1=b, op=op), R, W)

    def ts(self, eng, out, a, s1, s2, op0, op1, R, W):
        if op1 is None:
            return self.op(eng, lambda e: e.tensor_scalar(out=out, in0=a, scalar1=s1, scalar2=None, op0=op0), R, W)
        return self.op(eng, lambda e: e.tensor_scalar(out=out, in0=a, scalar1=s1, scalar2=s2, op0=op0, op1=op1), R, W)

    def stt(self, out, a, s, b, op0, op1, R, W):
        return self.op(self.dve, lambda e: e.scalar_tensor_tensor(out=out, in0=a, scalar=s, in1=b, op0=op0, op1=op1), R, W)

    def actf(self, out, in_, func, R, W, **kw):
        return self.op(self.act, lambda e: e.activation(out=out, in_=in_, func=func, **kw), R, W)

    def cp(self, out, in_, R, W, eng=None):
        if eng is None:
            self._ev += 1
            eng = self.act if (self._ev & 1) else self.dve
        if eng is self.act:
            return self.op(eng, lambda e: e.copy(out, in_), R, W)
        return self.op(eng, lambda e: e.tensor_copy(out, in_), R, W)


def build(debug=None):
    nc = bass.Bass("TRN2", target_bir_lowering=False)
    k = K(nc)

    def din(name, shape):
        return nc.dram_tensor(name, list(shape), F32, kind="ExternalInput").ap()

    x = din("x", [S, D]); mem = din("mem", [MEM, D])
    w_in = din("w_in", [D, INC]); w_out = din("w_out", [D, D])
    wq = din("xa_wq", [D, 512]); wk = din("xa_wk", [D, 512]); wv = din("xa_wv", [D, 512]); wo = din("xa_wo", [512, D])
    w1 = din("ffn_w1", [D, FF]); w2 = din("ffn_w2", [FF, D])
    nrm = din("norms", [5, D])
    convw = din("convw", [128, 24 * 4])
    dnsc = din("dnsc", [16, 2])
    dnw = din("dnw", [128, 1])
    mu = din("mu", [128, 26])
    rwv = din("rwv", [128, 7 * 8])
    lora = din("lora", [128, 1024])
    g2 = din("g2", [128, 1024])
    out = nc.dram_tensor("out", [S, D], F32, kind="ExternalOutput").ap()
    pT = T(nc.dram_tensor("pT", [INC, S], F32, kind="Internal").ap(), "pT")
    pT.bl = [Buf(f"pT{i}") for i in range(59)]
    if debug == "p4":
        oTd = T(nc.dram_tensor("oT_in", [D, S], F32, kind="ExternalInput").ap(), "oTd")
        oTd.bl = [Buf(f"oT{i}") for i in range(16)]
        oT_dt = F32
    else:
        oTd = T(nc.dram_tensor("oT", [D, S], BF16, kind="Internal").ap(), "oTd")
        oTd.bl = [Buf(f"oT{i}") for i in range(16)]
        oT_dt = BF16
    wsc = T(nc.dram_tensor("wsc", [38, 128, 8192], BF16, kind="Internal").ap(), "wsc")
    wsc.bl = [Buf(f"wsc{i}") for i in range(38)]
    dbg = None
    if debug == "p1":
        dbg = nc.dram_tensor("dbg", [INC, S], F32, kind="ExternalOutput").ap()
    if debug == "p2":
        dbg = nc.dram_tensor("dbg", [D, S], F32, kind="ExternalOutput").ap()

    for i in range(8):
        p = T(k.es.enter_context(nc.psum_tensor(f"ps{i}", [128, 512], F32)), f"ps{i}")
        p.b.excl = True
        k.psf.append(p)

    ones = k.sb("ones", [128, 128], F32)
    ident = k.sb("ident", [128, 128], F32)
    identb = k.sb("identb", [128, 128], BF16)
    epsT = k.sb("epsT", [128, 1], F32)
    k.op(k.pool, lambda e: e.memset(ones.t[:], 1.0), [], [ones.b])
    k.op(k.pool, lambda e: e.memset(epsT.t[:], EPS), [], [epsT.b])
    k.op(k.pool, lambda e: e.affine_select(out=ident.t[:], in_=ones.t[:], pattern=[[-1, 128]], compare_op=ALU.is_equal,
                                           fill=0.0, base=0, channel_multiplier=1), [ones.b], [ident.b])
    k.op(k.dve, lambda e: e.tensor_copy(identb.t[:], ident.t[:]), [ident.b], [identb.b])

    G = {}

    def alloc_norm(stack, tag):
        G["wbc"] = k.sb("wbc" + tag, [128, D], F32, stack)
        G["uns"] = [k.sb(f"un{i}" + tag, [128, D], BF16, stack) for i in range(2)]
        G["un"] = G["uns"][0]
        G["junk"] = k.sb("junk" + tag, [128, D], BF16, stack)
        G["ss"] = k.sb("ss" + tag, [128, 4], F32, stack)
        G["ssb"] = [Buf(f"ss{i}" + tag) for i in range(4)]
        G["ctr"] = 0

    def load_wbc(i):
        wbc = G["wbc"]
        k.dma(k.sp, wbc.t[:], nrm[i:i + 1, :].to_broadcast([128, D]), [], [wbc.b], wbc.b)
    wbufs = []
    wctr = [0]

    def precast_p4_weights():
        blocks = []
        for cb in range(4):
            blocks.append((cb, w_out[:, cb * 512:(cb + 1) * 512], 16))
        blocks.append((4, wq[:, :], 16))
        blocks.append((5, wo[:, :], 4))
        for fb in range(16):
            blocks.append((6 + fb, w1[:, fb * 512:(fb + 1) * 512], 16))
        for cb in range(4):
            for sub in range(4):
                blocks.append((22 + cb * 4 + sub, w2[sub * 2048:(sub + 1) * 2048, cb * 512:(cb + 1) * 512], 16))
        return blocks

    pc_blocks = precast_p4_weights()

    def bg_next(n):
        for _ in range(n):
            if not pc_blocks:
                return
            idx, src, kc = pc_blocks.pop(0)
            k.dma(k.pool, wsc.t[idx].rearrange("p (kc c) -> p kc c", kc=kc), src.rearrange("(kc p) c -> p kc c", p=128),
                  [], [wsc.bl[idx]], wsc.bl[idx])

    def alloc_wbufs(stack, tag):
        wbufs.clear()
        wbufs.extend(k.sb(f"wbuf{tag}{i}", [128, 8192], BF16, stack) for i in range(2))

    def load_w(src_ap, kc, cache=None):
        wb = wbufs[wctr[0] % len(wbufs)]
        wctr[0] += 1
        ncol = src_ap.shape[1]
        view = wb.t[:, 0:kc * ncol].rearrange("p (kc c) -> p kc c", kc=kc)
        if cache is not None:
            k.dma(k.sp, wb.t[:, :], wsc.t[cache[0]], [wsc.bl[cache[0]]], [wb.b], wb.b)
            return wb, view
        k.dma(k.pool, view, src_ap.rearrange("(kc p) c -> p kc c", p=128), [], [wb.b], wb.b)
        return wb, view

    def norm_transpose(src, srcB, dstT, dstB, dst_cols, slot):
        wbc, ss, junk = G["wbc"], G["ss"], G["junk"]
        un = G["uns"][G["ctr"] % 2]
        G["ctr"] += 1
        sb_ = G["ssb"][slot]
        k.actf(junk.t[:], src, AF.Square, [srcB], [junk.b, sb_], accum_out=ss.t[:, slot:slot + 1])
        k.actf(ss.t[:, slot:slot + 1], ss.t[:, slot:slot + 1], AF.Sqrt, [sb_, epsT.b], [sb_], scale=1.0 / D, bias=epsT.t[:, 0:1])
        k.op(k.dve, lambda e: e.reciprocal(ss.t[:, slot:slot + 1], ss.t[:, slot:slot + 1]), [sb_], [sb_])
        k.stt(un.t[:], src, ss.t[:, slot:slot + 1], wbc.t[:], ALU.mult, ALU.mult, [srcB, sb_, wbc.b], [un.b])
        for half in range(2):
            p = k.ps()
            pv = p.t[:].bitcast(BF16)
            for j in range(8):
                kc = half * 8 + j
                k.tr(pv[:, j * 128:(j + 1) * 128], un.t[:, kc * 128:(kc + 1) * 128], identb.t[:], [un.b, identb.b], [p.b])
            k.cp(dstT[:, half * 8:(half + 1) * 8, dst_cols], pv.rearrange("p (j c) -> p j c", c=128), [p.b], [dstB])

    if debug != "p4":
        with contextlib.ExitStack() as st1:
            alloc_wbufs(st1, "a")
            alloc_norm(st1, "a")
            uT = k.sb("uT", [128, 16, S], BF16, st1)
            uTb = [Buf(f"uT{i}") for i in range(4)]
            xts = [k.sb(f"xt{i}", [128, D], F32, st1) for i in range(2)]
            stg = [k.sb(f"stg{i}", [128, 512], F32, st1) for i in range(4)]
            load_wbc(0)
            si = 0

            def p1_block(c0, wb, wvw, tbs):
                nonlocal si
                ncol = min(512, INC - c0)
                for g0 in range(0, ncol, 128):
                    M = min(128, ncol - g0)
                    for tb in tbs:
                        p = k.ps()
                        for kc in range(16):
                            k.mm(p.t[0:M, :], wvw[:, kc, g0:g0 + M], uT.t[:, kc, tb * 512:(tb + 1) * 512], kc == 0, kc == 15,
                                 [wb.b, uTb[tb]], [p.b])
                        sg_ = stg[si % 4]; si += 1
                        k.cp(sg_.t[0:M, :], p.t[0:M, :], [p.b], [sg_.b])
                        k.dma(k.sp, pT.t[c0 + g0:c0 + g0 + M, tb * 512:(tb + 1) * 512], sg_.t[0:M, :], [sg_.b], [pT.bl[(c0 + g0) // 128]], sg_.b)
            wb0, wvw0 = load_w(w_in[:, 0:512], 16)
            for tb in range(4):
                for n in range(4 * tb, 4 * tb + 4):
                    xt = xts[n % 2]
                    k.dma(k.sp, xt.t[:], x[n * 128:(n + 1) * 128, :], [], [xt.b], xt.b)
                    norm_transpose(xt.t[:], xt.b, uT.t, uTb[n // 4], slice(n * 128, (n + 1) * 128), n % 4)
                p1_block(0, wb0, wvw0, [tb])
            for c0 in range(512, INC, 512):
                ncol = min(512, INC - c0)
                wb, wvw = load_w(w_in[:, c0:c0 + ncol], 16)
                p1_block(c0, wb, wvw, range(4))
            if debug == "p1":
                for r0 in range(0, INC, 128):
                    M = min(128, INC - r0)
                    xt = xts[(r0 // 128) % 2]
                    k.dma(k.sp, xt.t[0:M, :], pT.t[r0:r0 + M, :], [pT.bl[r0 // 128]], [xt.b], xt.b)
                    k.dma(k.sp, dbg[r0:r0 + M, :], xt.t[0:M, :], [xt.b], [], xt.b)
                k._deps(k.sp, [], [xts[0].b, xts[1].b])
        if debug == "p1":
            k.es.close()
            return nc

    if debug != "p4":
        k.barrier()
        build_mixers(nc, k, pT, oTd, convw, dnsc, dnw, mu, rwv, lora, g2, ones, ident, bg_next)
        bg_next(99)
        if debug == "p2":
            k.barrier()
            with contextlib.ExitStack() as st:
                a = k.sb("dba", [128, S], BF16, st); b = k.sb("dbb", [128, S], F32, st)
                for r0 in range(0, D, 128):
                    k.dma(k.sp, a.t[:], oTd.t[r0:r0 + 128, :], [oTd.bl[r0 // 128]], [a.b], a.b)
                    k.cp(b.t[:], a.t[:], [a.b], [b.b])
                    k.dma(k.sp, dbg[r0:r0 + 128, :], b.t[:], [b.b], [], b.b)
                k._deps(k.sp, [], [b.b])
            k.es.close()
            return nc

    k.barrier()
    if debug == "p4":
        bg_next(99)
    st4 = contextlib.ExitStack()
    alloc_wbufs(st4, "b")
    alloc_norm(st4, "b")
    wbc, un, ss = G["wbc"], G["junk"], G["ss"]
    KT = k.sb("KT", [128, 4, MEM], BF16, st4)
    Vm = k.sb("Vm", [128, 2, 512], BF16, st4)
    hnT = k.sb("hnT", [128, 16, 512], BF16, st4)
    h = k.sb("h", [128, 4, D], F32, st4)
    hB = [Buf(f"h{j}") for j in range(4)]
    hid = k.sb("hid", [128, 64, 512], BF16, st4)
    hidB = [Buf(f"hid{i}") for i in range(16)]
    qT = k.sb("qT", [128, 4, 512], BF16, st4)
    oxT = k.sb("oxT", [128, 4, 512], BF16, st4)
    atw = [dict(pr=k.sb(f"pr{i}", [128, MEM], F32, st4), prn=k.sb(f"prn{i}", [128, MEM], BF16, st4),
                prT=k.sb(f"prT{i}", [128, 2, 128], BF16, st4), sm=k.sb(f"sm{i}", [128, 4], F32, st4)) for i in range(4)]

    def run_rr(gens):
        gens = list(gens)
        while gens:
            for g_ in list(gens):
                try:
                    next(g_)
                except StopIteration:
                    gens.remove(g_)
    rl = [k.sb(f"rl{i}", [128, 512], F32, st4) for i in range(2)]
    memt = [k.sb(f"memt{i}", [128, D], F32, st4) for i in range(1)]

    load_wbc(2)
    for mt in range(2):
        m_ = memt[0]
        k.dma(k.sp, m_.t[:], mem[mt * 128:(mt + 1) * 128, :], [], [m_.b], m_.b)
        norm_transpose(m_.t[:], m_.b, hnT.t, hnT.b, slice(mt * 128, (mt + 1) * 128), mt)
    wb, wvw = load_w(wk[:, :], 16)
    for hd in range(4):
        p = k.ps()
        for kc in range(16):
            k.mm(p.t[:, 0:MEM], wvw[:, kc, hd * 128:(hd + 1) * 128], hnT.t[:, kc, 0:MEM], kc == 0, kc == 15, [wb.b, hnT.b], [p.b])
        k.cp(KT.t[:, hd, :], p.t[:, 0:MEM], [p.b], [KT.b])
    wb, wvw = load_w(wv[:, :], 16)
    for mc in range(2):
        p = k.ps()
        for kc in range(16):
            k.mm(p.t[:, :], hnT.t[:, kc, mc * 128:(mc + 1) * 128], wvw[:, kc, :], kc == 0, kc == 15, [wb.b, hnT.b], [p.b])
        k.cp(Vm.t[:, mc, :], p.t[:, :], [p.b], [Vm.b])

    oTb_view = hid.t[:, 0:16, :]
    oTb_bufs = hidB[0:4]
    for TB in range(4):
        t0 = TB * 512
        if oT_dt == BF16:
            k.dma(k.sp, oTb_view, oTd.t[:, t0:t0 + 512].rearrange("(kc p) t -> p kc t", p=128), oTd.bl, oTb_bufs, oTb_bufs[0])
        else:
            k.dma(k.pool, oTb_view, oTd.t[:, t0:t0 + 512].rearrange("(kc p) t -> p kc t", p=128), oTd.bl, oTb_bufs, oTb_bufs[0])
        for cb in range(4):
            wb, wvw = load_w(w_out[:, cb * 512:(cb + 1) * 512], 16, (cb, TB == 0))
            if cb == 0:
                for j in range(4):
                    k.dma(k.sp, h.t[:, j, :], x[t0 + j * 128:t0 + (j + 1) * 128, :], [], [hB[j]], hB[j])
            for j in range(4):
                p = k.ps()
                for kc in range(16):
                    k.mm(p.t[:, :], oTb_view[:, kc, j * 128:(j + 1) * 128], wvw[:, kc, :], kc == 0, kc == 15, [wb.b] + oTb_bufs, [p.b])
                hs = h.t[:, j, cb * 512:(cb + 1) * 512]
                k.tt(k.dve, hs, p.t[:, :], hs, ALU.add, [p.b, hB[j]], [hB[j]])
        load_wbc(1)
        for j in range(4):
            norm_transpose(h.t[:, j, :], hB[j], hnT.t, hnT.b, slice(j * 128, (j + 1) * 128), j)
        wb, wvw = load_w(wq[:, :], 16, (4, TB == 0))
        for hd in range(4):
            p = k.ps()
            for kc in range(16):
                k.mm(p.t[:, :], wvw[:, kc, hd * 128:(hd + 1) * 128], hnT.t[:, kc, :], kc == 0, kc == 15, [wb.b, hnT.b], [p.b])
            k.actf(qT.t[:, hd, :], p.t[:, :], AF.Copy, [p.b], [qT.b], scale=128 ** -0.5)
        def attn_worker(w_):
            a_ = atw[w_]
            pr, prn, prT, sm = a_["pr"], a_["prn"], a_["prT"], a_["sm"]
            for idx in range(w_, 16, 4):
                j, hd = divmod(idx, 4)
                yield from k.need(1)
                p = k.psA()
                k.mm(p.t[:, 0:MEM], qT.t[:, hd, j * 128:(j + 1) * 128], KT.t[:, hd, :], True, True, [qT.b, KT.b], [p.b])
                yield
                k.op(k.dve, lambda e: e.tensor_reduce(out=sm.t[:, 0:1], in_=p.t[:, 0:MEM], axis=AX.X, op=ALU.max, negate=True),
                     [p.b], [sm.b])
                yield
                k.actf(pr.t[:], p.t[:, 0:MEM], AF.Exp, [p.b, sm.b], [pr.b, sm.b], bias=sm.t[:, 0:1], accum_out=sm.t[:, 1:2])
                k.psF(p)
                yield
                k.op(k.dve, lambda e: e.reciprocal(sm.t[:, 2:3], sm.t[:, 1:2]), [sm.b], [sm.b])
                yield
                k.ts(k.dve, prn.t[:], pr.t[:], sm.t[:, 2:3], None, ALU.mult, None, [pr.b, sm.b], [prn.b])
                yield
                yield from k.need(1)
                p2 = k.psA()
                pv = p2.t[:].bitcast(BF16)
                for mc in range(2):
                    k.tr(pv[:, mc * 128:(mc + 1) * 128], prn.t[:, mc * 128:(mc + 1) * 128], identb.t[:], [prn.b, identb.b], [p2.b])
                yield
                k.cp(prT.t[:], pv[:, 0:256].rearrange("p (m c) -> p m c", c=128), [p2.b], [prT.b])
                k.psF(p2)
                yield
                yield from k.need(1)
                p3 = k.psA()
                for mc in range(2):
                    k.mm(p3.t[:, 0:128], Vm.t[:, mc, hd * 128:(hd + 1) * 128], prT.t[:, mc, :], mc == 0, mc == 1, [Vm.b, prT.b], [p3.b])
                yield
                k.cp(oxT.t[:, hd, j * 128:(j + 1) * 128], p3.t[:, 0:128], [p3.b], [oxT.b])
                k.psF(p3)
                yield
        run_rr([attn_worker(w_) for w_ in range(4)])
        wb, wvw = load_w(wo[:, :], 4, (5, TB == 0))
        for cb in range(4):
            for j in range(4):
                p = k.ps()
                for kc in range(4):
                    k.mm(p.t[:, :], oxT.t[:, kc, j * 128:(j + 1) * 128], wvw[:, kc, cb * 512:(cb + 1) * 512], kc == 0, kc == 3, [wb.b, oxT.b], [p.b])
                hs = h.t[:, j, cb * 512:(cb + 1) * 512]
                k.tt(k.dve, hs, p.t[:, :], hs, ALU.add, [p.b, hB[j]], [hB[j]])
        load_wbc(3)
        for j in range(4):
            norm_transpose(h.t[:, j, :], hB[j], hnT.t, hnT.b, slice(j * 128, (j + 1) * 128), j)
        for fb in range(16):
            wb, wvw = load_w(w1[:, fb * 512:(fb + 1) * 512], 16, (6 + fb, TB == 0))
            for fc in range(4):
                p = k.ps()
                for kc in range(16):
                    k.mm(p.t[:, :], wvw[:, kc, fc * 128:(fc + 1) * 128], hnT.t[:, kc, :], kc == 0, kc == 15, [wb.b, hnT.b], [p.b])
                r_ = rl[(fb * 4 + fc) % 2]
                k.actf(r_.t[:], p.t[:, :], AF.Relu, [p.b], [r_.b])
                k.tt(k.dve, hid.t[:, fb * 4 + fc, :], r_.t[:], r_.t[:], ALU.mult, [r_.b], [hidB[fb]])
        for cb in range(4):
            accs = [k.ps() for _ in range(4)]
            for sub in range(4):
                wb, wvw = load_w(w2[sub * 2048:(sub + 1) * 2048, cb * 512:(cb + 1) * 512], 16, (22 + cb * 4 + sub, TB == 0))
                for j in range(4):
                    for fc in range(16):
                        f = sub * 16 + fc
                        k.mm(accs[j].t[:, :], hid.t[:, f, j * 128:(j + 1) * 128], wvw[:, fc, :], f == 0, f == 63,
                             [wb.b, hidB[f // 4]], [accs[j].b])
            for j in range(4):
                hs = h.t[:, j, cb * 512:(cb + 1) * 512]
                k.tt(k.dve, hs, accs[j].t[:, :], hs, ALU.add, [accs[j].b, hB[j]], [hB[j]])
        load_wbc(4)
        for j in range(4):
            hj = h.t[:, j, :]
            sb_ = G["ssb"][j]
            k.actf(un.t[:], hj, AF.Square, [hB[j]], [un.b, sb_], accum_out=ss.t[:, j:j + 1])
            k.actf(ss.t[:, j:j + 1], ss.t[:, j:j + 1], AF.Sqrt, [sb_, epsT.b], [sb_], scale=1.0 / D, bias=epsT.t[:, 0:1])
            k.op(k.dve, lambda e: e.reciprocal(ss.t[:, j:j + 1], ss.t[:, j:j + 1]), [sb_], [sb_])
            k.stt(hj, hj, ss.t[:, j:j + 1], wbc.t[:], ALU.mult, ALU.mult, [hB[j], sb_, wbc.b], [hB[j]])
        for j in range(4):
            k.dma(k.sp, out[t0 + j * 128:t0 + (j + 1) * 128, :], h.t[:, j, :], [hB[j]], [], hB[j])
    k._deps(k.sp, [], hB)
    st4.close()
    k.es.close()
    return nc


def build_mixers(nc, k, pT, oTd, convw_d, dnsc_d, dnw_d, mu_d, rwv_d, lora_d, g2_d, ones, ident, bg_next):
    st = contextlib.ExitStack()
    r = lambda ap: ap.bitcast(F32R)
    NB = 10
    big = [k.sb(f"big{i}", [128, S], F32, st) for i in range(NB)]
    free = list(range(NB))

    def balloc():
        return big[free.pop(0)]

    def bfree(*ts):
        for t_ in ts:
            free.append(big.index(t_))

    def sm(name, shape, dt=F32):
        return k.sb("m_" + name, shape, dt, st)

    Ls = sm("Ls", [128, 128]); Li = sm("Li", [128, 128]); UU = sm("UU", [128, 256])
    blk = sm("blk", [128, 128]); rmask = sm("rmask", [128, S], BF16); selh = sm("selh", [16, 128])
    epsG = sm("epsG", [128, 2])

    def asel(out, pat, cm, op, R, W):
        k.op(k.pool, lambda e: e.affine_select(out=out, in_=ones.t[:], pattern=pat, compare_op=op, fill=0.0, base=0,
                                               channel_multiplier=cm), [ones.b] + R, W)
    asel(Ls.t[:], [[-1, 128]], 1, ALU.is_gt, [], [Ls.b])
    k.ts(k.dve, Ls.t[:], Ls.t[:], -1.0, None, ALU.mult, None, [Ls.b], [Ls.b])
    Li2 = sm("Li2", [128, 256]); II2 = sm("II2", [128, 256])
    for i_ in range(2):
        asel(Li2.t[:, i_ * 128:(i_ + 1) * 128], [[-1, 128]], 1, ALU.is_ge, [], [Li2.b])
        k.cp(II2.t[:, i_ * 128:(i_ + 1) * 128], ident.t[:], [ident.b], [II2.b], eng=k.pool)
    asel(Li.t[:], [[-1, 128]], 1, ALU.is_ge, [], [Li.b])
    asel(UU.t[:, 0:128], [[1, 128]], -1, ALU.is_gt, [], [UU.b])
    asel(UU.t[:, 128:256], [[1, 128]], -1, ALU.is_ge, [], [UU.b])
    k.op(k.pool, lambda e: e.memset(blk.t[:], 0.0), [], [blk.b])
    k.op(k.pool, lambda e: e.memset(blk.t[0:64, 0:64], 1.0), [], [blk.b])
    k.op(k.pool, lambda e: e.memset(blk.t[64:128, 64:128], 1.0), [], [blk.b])
    k.op(k.pool, lambda e: e.memset(rmask.t[:], 1.0), [], [rmask.b])
    k.op(k.pool, lambda e: e.memset(rmask.t[:].rearrange("p (c t) -> p c t", t=128)[:, :, 0:1], 0.0), [], [rmask.b])
    k.op(k.pool, lambda e: e.memset(epsG.t[:, 0:1], 64e-5), [], [epsG.b])
    k.op(k.pool, lambda e: e.memset(epsG.t[:, 1:2], 1e-6), [], [epsG.b])
    convw = sm("convw", [128, 96]); dnsc = sm("dnsc", [16, 2]); dnw = sm("dnw", [128, 1])
    mu = sm("mu", [128, 26]); omm = sm("omm", [128, 26]); rwv = sm("rwv", [128, 56])
    for t_, d_ in ((convw, convw_d), (dnsc, dnsc_d), (dnw, dnw_d), (mu, mu_d), (rwv, rwv_d)):
        k.dma(k.sp, t_.t[:], d_, [], [t_.b], t_.b)
    k.ts(k.dve, omm.t[:], mu.t[:], -1.0, 1.0, ALU.mult, ALU.add, [mu.b], [omm.b])

    def sq(name, w=128):
        return sm(name, [128, w])
    def run_rr(gens):
        gens = list(gens)
        while gens:
            for g_ in list(gens):
                try:
                    next(g_)
                except StopIteration:
                    gens.remove(g_)

    def neumann_multi(probs, nlev):
        for pr in probs:
            pr["Ao"] = pr["A1"]; pr["BPo"] = pr["BP"][0]
        for lv in range(nlev):
            last = lv == nlev - 1
            yield from k.need((1 if last else 2) * len(probs))
            for pr in probs:
                Ao, BPo = pr["Ao"], pr["BPo"]
                pr["pa"] = k.psA()
                if last:
                    k.mm(pr["pa"].t[:, 0:128], r(Ao.t[:]), r(BPo.t[:, 128:256]), True, True, [Ao.b, BPo.b], [pr["pa"].b])
                else:
                    k.mm(pr["pa"].t[:, 0:256], r(Ao.t[:]), r(BPo.t[:]), True, True, [Ao.b, BPo.b], [pr["pa"].b])
                    pr["pb"] = k.psA()
                    k.mm(pr["pb"].t[:, 0:128], r(BPo.t[:, 0:128]), r(Ao.t[:]), True, True, [Ao.b, BPo.b], [pr["pb"].b])
            yield
            for pr in probs:
                BPo = pr["BPo"]
                if last:
                    k.tt(k.dve, r(pr["Tout"].t[:]), pr["pa"].t[:, 0:128], BPo.t[:, 128:256], ALU.add, [pr["pa"].b, BPo.b], [pr["Tout"].b])
                    k.psF(pr["pa"])
                else:
                    An, BPn = pr["Ap"][lv % 2], pr["BP"][(lv + 1) % 2]
                    k.cp(r(An.t[:]), pr["pb"].t[:, 0:128], [pr["pb"].b], [An.b], eng=k.act)
                    k.cp(r(BPn.t[:, 0:128]), pr["pa"].t[:, 0:128], [pr["pa"].b], [BPn.b], eng=k.dve)
                    k.tt(k.dve, r(BPn.t[:, 128:256]), pr["pa"].t[:, 128:256], BPo.t[:, 128:256], ALU.add, [pr["pa"].b, BPo.b], [BPn.b])
                    k.psF(pr["pa"], pr["pb"])
                    pr["Ao"], pr["BPo"] = An, BPn
            yield

    def neumann_pairs(pairs, nlev):
        for pr in pairs:
            pr["Ao"] = pr["A1_2"]; pr["BPo"] = pr["BP2"][0]
        for lv in range(nlev):
            last = lv == nlev - 1
            yield from k.need((1 if last else 2) * len(pairs))
            for pr in pairs:
                Ao, BPo = pr["Ao"], pr["BPo"]
                pr["pa"] = k.psA()
                if not last:
                    pr["pb"] = k.psA()
                for i in range(2):
                    a_i = r(Ao.t[:, i * 128:(i + 1) * 128])
                    if last:
                        k.mm(pr["pa"].t[:, i * 128:(i + 1) * 128], a_i, r(BPo.t[:, i * 256 + 128:(i + 1) * 256]), True, True,
                             [Ao.b, BPo.b], [pr["pa"].b])
                    else:
                        k.mm(pr["pa"].t[:, i * 256:(i + 1) * 256], a_i, r(BPo.t[:, i * 256:(i + 1) * 256]), True, True,
                             [Ao.b, BPo.b], [pr["pa"].b])
                        k.mm(pr["pb"].t[:, i * 128:(i + 1) * 128], r(BPo.t[:, i * 256:i * 256 + 128]), a_i, True, True,
                             [Ao.b, BPo.b], [pr["pb"].b])
            yield
            for pr in pairs:
                BPo = pr["BPo"]
                bpo3 = BPo.t[:].rearrange("p (i c) -> p i c", c=256)
                if last:
                    k.tt(k.dve, r(pr["Tout2"].t[:].rearrange("p (i c) -> p i c", c=128)),
                         pr["pa"].t[:, 0:256].rearrange("p (i c) -> p i c", c=128), bpo3[:, :, 128:256], ALU.add,
                         [pr["pa"].b, BPo.b], [pr["Tout2"].b])
                    k.psF(pr["pa"])
                else:
                    An, BPn = pr["Ap2"][lv % 2], pr["BP2"][(lv + 1) % 2]
                    bpn3 = BPn.t[:].rearrange("p (i c) -> p i c", c=256)
                    pa3 = pr["pa"].t[:, :].rearrange("p (i c) -> p i c", c=256)
                    k.cp(r(An.t[:]), pr["pb"].t[:, 0:256], [pr["pb"].b], [An.b], eng=k.act)
                    k.cp(r(bpn3[:, :, 0:128]), pa3[:, :, 0:128], [pr["pa"].b], [BPn.b], eng=k.act)
                    k.tt(k.dve, r(bpn3[:, :, 128:256]), pa3[:, :, 128:256], bpo3[:, :, 128:256], ALU.add, [pr["pa"].b, BPo.b], [BPn.b])
                    k.psF(pr["pa"], pr["pb"])
                    pr["Ao"], pr["BPo"] = An, BPn
            yield

    def load_rows(dst, r0, nrows=128):
        k.dma(k.sp, dst.t[0:nrows, :], pT.t[r0:r0 + nrows, :], pT.bl[r0 // 128:(r0 + nrows - 1) // 128 + 1], [dst.b], dst.b)

    ones_bf = sm("ones_bf", [128, 128], BF16); blk_bf = sm("blk_bf", [128, 128], BF16)
    k.cp(ones_bf.t[:], ones.t[:], [ones.b], [ones_bf.b], eng=k.dve)
    k.cp(blk_bf.t[:], blk.t[:], [blk.b], [blk_bf.b], eng=k.dve)

    def bfv(t_):
        return t_.t[:].bitcast(BF16)[:, 0:S]

    def psum_bcast_sum(src, lhsT, lhsTb, fn):
        sv = bfv(src)
        for tb in range(4):
            p = k.ps()
            k.mm(p.t[:, :], lhsT, sv[:, tb * 512:(tb + 1) * 512], True, True, [lhsTb, src.b], [p.b])
            fn(tb, p)

    obf = [sm("obf0", [128, S], BF16)] * 2
    octr = [0]

    gc16 = balloc()
    st_dn = contextlib.ExitStack()

    def smd(name, shape, dt=F32):
        return k.sb("m_" + name, shape, dt, st_dn)
    gcT = smd("gcT", [128, 256]); betaT = smd("betaT", [128, 256]); kdT = smd("kdT", [128, 256]); egT = smd("egT", [128, 256])
    bgT = smd("bgT", [128, 16, 8]); negA = smd("negA", [16, 1])
    if True:
        ab = balloc(); t1 = balloc(); t2 = balloc(); beta16 = balloc(); kd16 = balloc()
        R16 = slice(0, 16)
        load_rows(ab, 4096, 16)
        dtb = dnsc.t[:, 1:2]
        k.actf(t1.t[R16, :], ab.t[R16, :], AF.Abs, [ab.b, dnsc.b], [t1.b], bias=dtb)
        k.actf(t1.t[R16, :], t1.t[R16, :], AF.Exp, [t1.b], [t1.b], scale=-1.0)
        k.actf(t1.t[R16, :], t1.t[R16, :], AF.Ln, [t1.b, ones.b], [t1.b], bias=ones.t[0:16, 0:1])
        k.ts(k.dve, t2.t[R16, :], ab.t[R16, :], dtb, 0.0, ALU.add, ALU.max, [ab.b, dnsc.b], [t2.b])
        k.tt(k.dve, t1.t[R16, :], t1.t[R16, :], t2.t[R16, :], ALU.add, [t1.b, t2.b], [t1.b])
        k.actf(negA.t[:], dnsc.t[:, 0:1], AF.Exp, [dnsc.b], [negA.b])
        k.ts(k.dve, negA.t[:], negA.t[:], -1.0, None, ALU.mult, None, [negA.b], [negA.b])
        k.ts(k.dve, t1.t[R16, :], t1.t[R16, :], negA.t[:, 0:1], None, ALU.mult, None, [t1.b, negA.b], [t1.b])
        k.actf(beta16.t[R16, :], ab.t[R16, :], AF.Sigmoid, [ab.b], [beta16.b])
        k.op(k.dve, lambda e: e.tensor_tensor_scan(gc16.t[R16, :], rmask.t[R16, :], t1.t[R16, :], 0.0, ALU.mult, ALU.add),
             [rmask.b, t1.b], [gc16.b])
        for n in range(NT):
            cs = slice(n * 128, (n + 1) * 128)
            k.ts(k.dve, kd16.t[R16, cs], gc16.t[R16, cs], gc16.t[R16, n * 128 + 127:n * 128 + 128], None, ALU.subtract, None,
                 [gc16.b], [kd16.b])
        k.actf(kd16.t[R16, :], kd16.t[R16, :], AF.Exp, [kd16.b], [kd16.b], scale=-1.0)
        for src, dst in ((gc16, gcT), (beta16, betaT), (kd16, kdT)):
            p = k.ps()
            for n in range(NT):
                k.mm(p.t[:, n * 16:(n + 1) * 16], src.t[R16, n * 128:(n + 1) * 128], ident.t[0:16, 0:16], True, True, [src.b, ident.b], [p.b])
            k.cp(dst.t[:], p.t[:, 0:256], [p.b], [dst.b])
        k.actf(egT.t[:], gcT.t[:], AF.Exp, [gcT.b], [egT.b])
        k.tt(k.dve, bgT.t[:], betaT.t[:].rearrange("p (n r) -> p n r", r=16)[:, :, 8:16],
             egT.t[:].rearrange("p (n r) -> p n r", r=16)[:, :, 0:8], ALU.mult, [betaT.b, egT.b], [bgT.b])
        bfree(ab, t1, t2, beta16, kd16)
    ngcT = smd("ngcT", [128, 256])
    k.ts(k.dve, ngcT.t[:], gcT.t[:], -1.0, None, ALU.mult, None, [gcT.b], [ngcT.b])
    ngcT3 = ngcT.t[:].rearrange("p (n r) -> p n r", r=16)
    gcT3 = gcT.t[:].rearrange("p (n r) -> p n r", r=16)
    betaT3 = betaT.t[:].rearrange("p (n r) -> p n r", r=16)
    kdT3 = kdT.t[:].rearrange("p (n r) -> p n r", r=16)

    WDN = 6

    def sqd(name, w=128):
        return k.sb("m_" + name, [128, w], F32, st_dn)
    St = [sqd("St0"), sqd("St1")]
    qTr = sqd("qTr", S); kTr = sqd("kTr", S); qgr = sqd("qgr", S)
    dnw_t = []
    WDP = 4
    for w_ in range(WDP):
        d_ = {nm: sqd(f"{nm}{w_}", 256) for nm in ("t1_2", "El_2", "MA_2", "MD_2", "at_2", "attnT_2", "nwT_2", "A1_2", "Tout2")}
        d_["Ap2"] = [sqd(f"Ap20_{w_}", 256), sqd(f"Ap21_{w_}", 256)]; d_["BP2"] = [sqd(f"BP20_{w_}", 512), sqd(f"BP21_{w_}", 512)]
        d_["c"] = [{nm: sqd(f"{nm}{w_}_{i_}") for nm in ("kbg", "kd", "vb", "vnew")} for i_ in range(2)]
        dnw_t.append(d_)

    def conv_silu(xr, gi):
        c = balloc()
        w = lambda j: convw.t[:, gi * 4 + j:gi * 4 + j + 1]
        k.ts(k.dve, c.t[:], xr.t[:], w(3), None, ALU.mult, None, [xr.b, convw.b], [c.b])
        for sh in (1, 2, 3):
            k.stt(c.t[:, sh:S], xr.t[:, 0:S - sh], w(3 - sh), c.t[:, sh:S], ALU.mult, ALU.add, [xr.b, convw.b, c.b], [c.b])
        k.actf(c.t[:], c.t[:], AF.Silu, [c.b], [c.b])
        bfree(xr)
        return c

    def l2n(xc, scale, dst):
        sq_ = balloc(); rn = balloc()
        k.actf(bfv(sq_), xc.t[:], AF.Square, [xc.b], [sq_.b])

        def fn(tb, p):
            ts_ = slice(tb * 512, (tb + 1) * 512)
            k.actf(rn.t[:, ts_], p.t[:, :], AF.Ln, [p.b, epsG.b], [rn.b], bias=epsG.t[:, 1:2])
        psum_bcast_sum(sq_, ones_bf.t[:], ones_bf.b, fn)
        k.actf(rn.t[:], rn.t[:], AF.Exp, [rn.b], [rn.b], scale=-0.5)
        k.stt(r(dst.t[:]), xc.t[:], scale, rn.t[:], ALU.mult, ALU.mult, [xc.b, rn.b], [dst.b])
        bfree(sq_, rn, xc)

    for h in range(DBG_DN):
        qr = balloc(); load_rows(qr, h * 128)
        kr = balloc(); load_rows(kr, 1024 + h * 128)
        vr = balloc(); load_rows(vr, 2048 + h * 128)
        if DBG_STEP == 0:
            st.close(); return
        qT = conv_silu(qr, h); kT = conv_silu(kr, 8 + h); vT = conv_silu(vr, 16 + h)
        if DBG_STEP == 1:
            st.close(); return
        l2n(qT, 128 ** -0.5, qTr); l2n(kT, 1.0, kTr)
        qT, kT = qTr, kTr
        if DBG_STEP == 2:
            st.close(); return
        gcb = balloc(); egcb = balloc()

        def fn(tb, p):
            ts_ = slice(tb * 512, (tb + 1) * 512)
            k.cp(gcb.t[:, ts_], p.t[:, :], [p.b], [gcb.b], eng=k.dve)
            k.actf(egcb.t[:, ts_], p.t[:, :], AF.Exp, [p.b], [egcb.b])
        k.ts(k.dve, selh.t[:], ones.t[0:16, :], ident.t[0:16, h:h + 1], None, ALU.mult, None, [ones.b, ident.b], [selh.b])
        for tb in range(4):
            p = k.ps()
            k.mm(p.t[:, :], selh.t[:], gc16.t[0:16, tb * 512:(tb + 1) * 512], True, True, [selh.b, gc16.b], [p.b])
            fn(tb, p)
        qg = qgr
        k.tt(k.dve, r(qg.t[:]), qT.t[:], egcb.t[:], ALU.mult, [qT.b, egcb.b], [qg.b])
        oT = balloc()
        if DBG_STEP == 3:
            st.close(); return
        k.ts(k.dve, r(St[0].t[:]), ident.t[:], 0.0, None, ALU.mult, None, [ident.b], [St[0].b])
        seq_done = [0]

        def dn_worker(w_, h=h, qT=qT, kT=kT, vT=vT, gcb=gcb, egcb=egcb, qg=qg, oT=oT, seq_done=seq_done):
            d_ = dnw_t[w_]
            C = d_["c"]
            H2 = [slice(0, 128), slice(128, 256)]
            for n0 in range(2 * w_, DBG_CH, 2 * WDP):
                ns = [n0, n0 + 1]
                css = [slice(n * 128, (n + 1) * 128) for n in ns]
                yield from k.need(2)
                pkt = k.psA(); pvt = k.psA()
                for i, n in enumerate(ns):
                    k.tr(pkt.t[:, H2[i]], kT.t[:, css[i]], ident.t[:], [kT.b, ident.b], [pkt.b])
                    k.tr(pvt.t[:, H2[i]], vT.t[:, css[i]], ident.t[:], [vT.b, ident.b], [pvt.b])
                    k.actf(d_["t1_2"].t[:, H2[i]], gcb.t[:, css[i]], AF.Relu, [gcb.b, ngcT.b], [d_["t1_2"].b], bias=ngcT3[:, n, h:h + 1])
                yield
                for i, n in enumerate(ns):
                    k.actf(r(C[i]["kbg"].t[:]), pkt.t[:, H2[i]], AF.Copy, [pkt.b, bgT.b], [C[i]["kbg"].b], scale=bgT.t[:, n, h:h + 1])
                    k.actf(r(C[i]["kd"].t[:]), pkt.t[:, H2[i]], AF.Copy, [pkt.b, kdT.b], [C[i]["kd"].b], scale=kdT3[:, n, h:h + 1])
                    k.ts(k.dve, r(C[i]["vb"].t[:]), pvt.t[:, H2[i]], betaT3[:, n, 8 + h:9 + h], None, ALU.mult, None, [pvt.b, betaT.b], [C[i]["vb"].b])
                k.psF(pkt, pvt)
                k.actf(d_["El_2"].t[:], d_["t1_2"].t[:], AF.Exp, [d_["t1_2"].b], [d_["El_2"].b], scale=-1.0)
                yield
                yield from k.need(2)
                pk = k.psA(); pq = k.psA()
                for i, n in enumerate(ns):
                    k.mm(pk.t[:, H2[i]], r(kT.t[:, css[i]]), r(kT.t[:, css[i]]), True, True, [kT.b], [pk.b])
                    k.mm(pq.t[:, H2[i]], r(qT.t[:, css[i]]), r(kT.t[:, css[i]]), True, True, [qT.b, kT.b], [pq.b])
                    k.stt(d_["MA_2"].t[:, H2[i]], d_["El_2"].t[:, H2[i]], betaT3[:, n, 8 + h:9 + h], Ls.t[:], ALU.mult, ALU.mult,
                          [d_["El_2"].b, betaT.b, Ls.b], [d_["MA_2"].b])
                k.tt(k.pool, d_["MD_2"].t[:], d_["El_2"].t[:], Li2.t[:], ALU.mult, [d_["El_2"].b, Li2.b], [d_["MD_2"].b])
                yield
                k.tt(k.dve, r(d_["A1_2"].t[:]), pk.t[:, 0:256], d_["MA_2"].t[:], ALU.mult, [pk.b, d_["MA_2"].b], [d_["A1_2"].b])
                k.tt(k.dve, d_["at_2"].t[:], pq.t[:, 0:256], d_["MD_2"].t[:], ALU.mult, [pq.b, d_["MD_2"].b], [d_["at_2"].b])
                k.psF(pk, pq)
                yield
                yield from k.need(2)
                pa = k.psA(); pb = k.psA()
                for i in range(2):
                    k.tr(pb.t[:, H2[i]], d_["A1_2"].t[:, H2[i]], ident.t[:], [d_["A1_2"].b, ident.b], [pb.b])
                    k.tr(pa.t[:, H2[i]], d_["at_2"].t[:, H2[i]], ident.t[:], [d_["at_2"].b, ident.b], [pa.b])
                yield
                bp3 = d_["BP2"][0].t[:].rearrange("p (i c) -> p i c", c=256)
                k.cp(r(bp3[:, :, 0:128]), pb.t[:, 0:256].rearrange("p (i c) -> p i c", c=128), [pb.b], [d_["BP2"][0].b], eng=k.act)
                k.cp(r(bp3[:, :, 128:256]), II2.t[:].rearrange("p (i c) -> p i c", c=128), [II2.b], [d_["BP2"][0].b], eng=k.pool)
                k.cp(r(d_["attnT_2"].t[:]), pa.t[:, 0:256], [pa.b], [d_["attnT_2"].b], eng=k.act)
                k.psF(pa, pb)
                yield
                yield from neumann_pairs([d_], 7)
                yield from k.need(1)
                pw = k.psA()
                for i in range(2):
                    k.mm(pw.t[:, H2[i]], r(C[i]["kbg"].t[:]), r(d_["Tout2"].t[:, H2[i]]), True, True, [C[i]["kbg"].b, d_["Tout2"].b], [pw.b])
                yield
                k.actf(r(d_["nwT_2"].t[:]), pw.t[:, 0:256], AF.Copy, [pw.b], [d_["nwT_2"].b], scale=-1.0)
                k.psF(pw)
                yield
                for i, n in enumerate(ns):
                    while seq_done[0] < n:
                        yield
                    cs = css[i]
                    So, Sn = St[n % 2], St[(n + 1) % 2]
                    yield from k.need(1)
                    pv = k.psA()
                    k.mm(pv.t[:, 0:128], r(d_["Tout2"].t[:, H2[i]]), r(C[i]["vb"].t[:]), True, False, [d_["Tout2"].b, C[i]["vb"].b], [pv.b])
                    k.mm(pv.t[:, 0:128], r(d_["nwT_2"].t[:, H2[i]]), r(So.t[:]), False, True, [d_["nwT_2"].b, So.b], [pv.b])
                    vn = C[i]["vnew"]
                    k.cp(r(vn.t[:]), pv.t[:, 0:128], [pv.b], [vn.b], eng=k.act)
                    k.psF(pv)
                    yield from k.need(2)
                    po = k.psA(); pS = k.psA()
                    k.mm(pS.t[:, 0:128], r(C[i]["kd"].t[:]), r(vn.t[:]), True, True, [C[i]["kd"].b, vn.b], [pS.b])
                    k.mm(po.t[:, 0:128], r(So.t[:]), r(qg.t[:, cs]), True, False, [So.b, qg.b], [po.b])
                    k.mm(po.t[:, 0:128], r(vn.t[:]), r(d_["attnT_2"].t[:, H2[i]]), False, True, [vn.b, d_["attnT_2"].b], [po.b])
                    k.stt(r(Sn.t[:]), So.t[:], egcb.t[:, n * 128 + 127:n * 128 + 128], pS.t[:, 0:128], ALU.mult, ALU.add,
                          [So.b, egcb.b, pS.b], [Sn.b])
                    k.cp(oT.t[:, cs], po.t[:, 0:128], [po.b], [oT.b], eng=k.act)
                    k.psF(po, pS)
                    seq_done[0] = n + 1
                    yield
        bg_next(3)
        run_rr([dn_worker(w_) for w_ in range(WDP)])
        if DBG_STEP == 14:
            st.close(); return
        bfree(vT, gcb, egcb)
        zr = balloc(); load_rows(zr, 3072 + h * 128)
        sq_ = balloc(); rn = balloc()
        k.actf(bfv(sq_), oT.t[:], AF.Square, [oT.b], [sq_.b])

        def fn2(tb, p):
            ts_ = slice(tb * 512, (tb + 1) * 512)
            k.actf(rn.t[:, ts_], p.t[:, :], AF.Ln, [p.b, epsG.b], [rn.b], bias=epsG.t[:, 1:2], scale=1.0 / 128)
        psum_bcast_sum(sq_, ones_bf.t[:], ones_bf.b, fn2)
        k.actf(rn.t[:], rn.t[:], AF.Exp, [rn.b], [rn.b], scale=-0.5)
        k.actf(zr.t[:], zr.t[:], AF.Silu, [zr.b], [zr.b])
        k.stt(oT.t[:], oT.t[:], dnw.t[:, 0:1], rn.t[:], ALU.mult, ALU.mult, [oT.b, dnw.b, rn.b], [oT.b])
        ob = obf[octr[0] % 2]; octr[0] += 1
        k.tt(k.dve, ob.t[:], oT.t[:], zr.t[:], ALU.mult, [oT.b, zr.b], [ob.b])
        k.dma(k.sp, oTd.t[h * 128:(h + 1) * 128, :], ob.t[:], [ob.b], [oTd.bl[h]], ob.b)
        bfree(zr, sq_, rn, oT)
    bfree(gc16)
    st_dn.close()
    k.barrier()

    wa = balloc(); sg = balloc()
    RW0 = DNC

    def lerp(xr, gi):
        t_ = balloc()
        k.op(k.pool, lambda e: e.memset(t_.t[:, 0:1], 0.0), [], [t_.b])
        k.ts(k.dve, t_.t[:, 1:S], xr.t[:, 0:S - 1], mu.t[:, gi:gi + 1], None, ALU.mult, None, [xr.b, mu.b], [t_.b])
        k.stt(xr.t[:], xr.t[:], omm.t[:, gi:gi + 1], t_.t[:], ALU.mult, ALU.add, [xr.b, omm.b, t_.b], [xr.b])
        bfree(t_)
    load_rows(wa, RW0 + 3072); lerp(wa, 24)
    load_rows(sg, RW0 + 3200); lerp(sg, 25)
    k.actf(wa.t[0:64, :], wa.t[0:64, :], AF.Tanh, [wa.b], [wa.b])
    k.actf(sg.t[:], sg.t[:], AF.Sigmoid, [sg.b], [sg.b])

    WRW = 4
    st_rw = contextlib.ExitStack()
    lora = k.sb("m_lora", [128, 1024], F32, st_rw); g2 = k.sb("m_g2", [128, 1024], F32, st_rw)
    for t_, d_ in ((lora, lora_d), (g2, g2_d)):
        k.dma(k.sp, t_.t[:], d_, [], [t_.b], t_.b)

    def sqr(name, w=128):
        return k.sb("m_" + name, [128, w], F32, st_rw)
    Ht = [sqr("Ht0", 128), sqr("Ht1", 128)]
    rww_t = []
    for w_ in range(WRW):
        d_ = {nm: sqr(f"r{nm}{w_}") for nm in ("e1", "e2", "e3", "e4", "Bt", "Kt", "bh", "kh", "rhs1", "AVc", "KVc", "YVc", "Gt")}
        for nm in ("BhP", "KhP", "VP", "UP"):
            t_ = sqr(f"r{nm}2_{w_}", 256)
            d_[nm + "2"] = t_
            k.ts(k.dve, r(t_.t[:]), II2.t[:], 0.0, None, ALU.mult, None, [II2.b], [t_.b])
            d_[nm] = [TV(t_.t[:, 0:128], t_.b), TV(t_.t[:, 64:192], t_.b)]
        d_["ar"] = sqr(f"rar{w_}", 256)
        d_["hd"] = []
        for hh in range(2):
            e_ = {}
            e_["mb"] = sqr(f"rmb{w_}_{hh}", 256); e_["mk"] = sqr(f"rmk{w_}_{hh}", 256)
            d_["hd"].append(e_)
        d_["A1_2"] = sqr(f"rA12_{w_}", 256); d_["Tout2"] = sqr(f"rTr2_{w_}", 256)
        d_["Ap2"] = [sqr(f"rAp20_{w_}", 256), sqr(f"rAp21_{w_}", 256)]
        d_["BP2"] = [sqr(f"rBP20_{w_}", 512), sqr(f"rBP21_{w_}", 512)]
        rww_t.append(d_)
    V = lambda j: rwv.t[:, j * 8:(j + 1) * 8]

    for g in range(DBG_RW):
        rT = balloc(); load_rows(rT, RW0 + g * 128); lerp(rT, g)
        kl = balloc(); load_rows(kl, RW0 + 1024 + g * 128); lerp(kl, 8 + g)
        vT = balloc(); load_rows(vT, RW0 + 2048 + g * 128); lerp(vT, 16 + g)
        sig = balloc(); a_ = balloc()
        gsl = slice(g * 128, (g + 1) * 128)
        for tb in range(4):
            ts_ = slice(tb * 512, (tb + 1) * 512)
            p = k.ps()
            k.mm(p.t[:, :], lora.t[0:64, gsl], wa.t[0:64, ts_], True, True, [lora.b, wa.b], [p.b])
            k.actf(sig.t[:, ts_], p.t[:, :], AF.Sigmoid, [p.b, rwv.b], [sig.b], bias=V(0)[:, g:g + 1])
            p = k.ps()
            k.mm(p.t[:, :], lora.t[64:128, gsl], wa.t[64:128, ts_], True, True, [lora.b, wa.b], [p.b])
            k.actf(a_.t[:, ts_], p.t[:, :], AF.Sigmoid, [p.b, rwv.b], [a_.b], bias=V(1)[:, g:g + 1])
        kk = balloc(); sq_ = balloc(); rn = balloc()
        k.ts(k.dve, kk.t[:], kl.t[:], V(2)[:, g:g + 1], None, ALU.mult, None, [kl.b, rwv.b], [kk.b])
        k.actf(bfv(sq_), kk.t[:], AF.Square, [kk.b], [sq_.b])

        def fnk(tb, p):
            ts_ = slice(tb * 512, (tb + 1) * 512)
            k.ts(k.dve, rn.t[:, ts_], p.t[:, :], 1e-24, None, ALU.max, None, [p.b], [rn.b])
        psum_bcast_sum(sq_, blk_bf.t[:], blk_bf.b, fnk)
        k.actf(rn.t[:], rn.t[:], AF.Ln, [rn.b], [rn.b])
        k.actf(rn.t[:], rn.t[:], AF.Exp, [rn.b], [rn.b], scale=-0.5)
        k.tt(k.dve, kk.t[:], kk.t[:], rn.t[:], ALU.mult, [kk.b, rn.b], [kk.b])
        bfree(sq_, rn)
        kf = balloc()
        k.ts(k.dve, kf.t[:], a_.t[:], -1.0, V(3)[:, g:g + 1], ALU.add, ALU.mult, [a_.b, rwv.b], [kf.b])
        k.stt(kf.t[:], kf.t[:], 1.0, kl.t[:], ALU.add, ALU.mult, [kf.b, kl.b], [kf.b])
        bT = balloc()
        k.tt(k.dve, bT.t[:], a_.t[:], kk.t[:], ALU.mult, [a_.b, kk.b], [bT.b])
        bfree(kl, a_)
        cum = balloc()
        k.op(k.dve, lambda e: e.tensor_tensor_scan(cum.t[:], rmask.t[:], sig.t[:], 0.0, ALU.mult, ALU.add), [rmask.b, sig.b], [cum.b])
        yT = balloc()
        k.ts(k.dve, r(Ht[0].t[:]), ident.t[:], 0.0, None, ALU.mult, None, [ident.b], [Ht[0].b])
        seq_done = [0]

        def rw_worker(w_, rT=rT, vT=vT, kk=kk, kf=kf, bT=bT, sig=sig, cum=cum, yT=yT, seq_done=seq_done):
            d_ = rww_t[w_]
            ar = d_["ar"]; e1 = d_["e1"]; e2 = d_["e2"]; e3 = d_["e3"]; e4 = d_["e4"]
            Bt_, Kt_, bh, kh = d_["Bt"], d_["Kt"], d_["bh"], d_["kh"]
            BhP, KhP, VP, UP, rhs1 = d_["BhP"], d_["KhP"], d_["VP"], d_["UP"], d_["rhs1"]
            HD = d_["hd"]
            RS = [slice(0, 64), slice(64, 128)]
            for n in range(w_, DBG_CH, WRW):
                cs = slice(n * 128, (n + 1) * 128)
                k.actf(e1.t[:], cum.t[:, cs], AF.Exp, [cum.b], [e1.b], scale=CDEC)
                k.actf(e2.t[:], cum.t[:, cs], AF.Exp, [cum.b], [e2.b], scale=-CDEC)
                k.tt(k.pool, e3.t[:], cum.t[:, cs], sig.t[:, cs], ALU.subtract, [cum.b, sig.b], [e3.b])
                k.ts(k.dve, e4.t[:], cum.t[:, cs], cum.t[:, n * 128 + 127:n * 128 + 128], None, ALU.subtract, None, [cum.b], [e4.b])
                yield
                k.actf(e3.t[:], e3.t[:], AF.Exp, [e3.b], [e3.b], scale=CDEC)
                k.actf(e4.t[:], e4.t[:], AF.Exp, [e4.b], [e4.b], scale=-CDEC)
                k.tt(k.pool, r(ar.t[:, 128:256]), rT.t[:, cs], e1.t[:], ALU.mult, [rT.b, e1.b], [ar.b])
                k.tt(k.pool, r(Bt_.t[:]), bT.t[:, cs], e2.t[:], ALU.mult, [bT.b, e2.b], [Bt_.b])
                k.tt(k.pool, r(Kt_.t[:]), kf.t[:, cs], e2.t[:], ALU.mult, [kf.b, e2.b], [Kt_.b])
                yield
                k.tt(k.dve, r(ar.t[:, 0:128]), kk.t[:, cs], e3.t[:], ALU.mult, [kk.b, e3.b], [ar.b])
                k.tt(k.pool, bh.t[:], bT.t[:, cs], e4.t[:], ALU.mult, [bT.b, e4.b], [bh.b])
                k.tt(k.pool, kh.t[:], kf.t[:, cs], e4.t[:], ALU.mult, [kf.b, e4.b], [kh.b])
                yield
                yield from k.need(3)
                trs = []
                for src, srcB, dst in ((bh.t[:], bh.b, d_["BhP2"]), (kh.t[:], kh.b, d_["KhP2"]), (vT.t[:, cs], vT.b, d_["VP2"])):
                    p = k.psA()
                    k.tr(p.t[:, 0:128], src, ident.t[:], [srcB, ident.b], [p.b])
                    trs.append((p, dst))
                yield
                for ti_, (p, dst2) in enumerate(trs):
                    k.cp(r(dst2.t[:].rearrange("p (i c) -> p i c", c=128)[:, :, 0:64]), p.t[:, 0:128].rearrange("p (i c) -> p i c", c=64),
                         [p.b], [dst2.b], eng=k.act)
                    k.psF(p)
                yield from k.need(4)
                pms = []
                for hh in range(2):
                    R = RS[hh]
                    pm = k.psA(); pm2 = k.psA()
                    k.mm(pm.t[:, 0:256], r(Bt_.t[R, :]), r(ar.t[R, :]), True, True, [Bt_.b, ar.b], [pm.b])
                    k.mm(pm2.t[:, 0:256], r(Kt_.t[R, :]), r(ar.t[R, :]), True, True, [Kt_.b, ar.b], [pm2.b])
                    pms.append((pm, pm2))
                yield
                for hh in range(2):
                    pm, pm2 = pms[hh]
                    k.tt(k.dve, r(HD[hh]["mb"].t[:]), pm.t[:, 0:256], UU.t[:], ALU.mult, [pm.b, UU.b], [HD[hh]["mb"].b])
                    k.tt(k.dve, r(HD[hh]["mk"].t[:]), pm2.t[:, 0:256], UU.t[:], ALU.mult, [pm2.b, UU.b], [HD[hh]["mk"].b])
                    k.psF(pm, pm2)
                yield from k.need(2)
                pas = []
                for hh in range(2):
                    R = RS[hh]
                    pa = k.psA()
                    k.mm(pa.t[:, 0:128], r(ar.t[R, 0:128]), r(Bt_.t[R, :]), True, True, [ar.b, Bt_.b], [pa.b])
                    pas.append(pa)
                yield
                for hh in range(2):
                    e_ = HD[hh]
                    k.tt(k.dve, r(d_["A1_2"].t[:, hh * 128:(hh + 1) * 128]), pas[hh].t[:, 0:128], Ls.t[:], ALU.mult,
                         [pas[hh].b, Ls.b], [d_["A1_2"].b])
                    k.actf(r(d_["BP2"][0].t[:, hh * 256:hh * 256 + 128]), e_["mb"].t[:, 0:128], AF.Copy, [e_["mb"].b], [d_["BP2"][0].b], scale=-1.0)
                    k.psF(pas[hh])
                yield
                k.cp(r(d_["BP2"][0].t[:].rearrange("p (i c) -> p i c", c=256)[:, :, 128:256]), II2.t[:].rearrange("p (i c) -> p i c", c=128),
                     [II2.b], [d_["BP2"][0].b], eng=k.pool)
                yield from k.need(3)
                pAV = k.psA(); pKV = k.psA(); pYV = k.psA()
                for hh in range(2):
                    e_ = HD[hh]
                    k.mm(pAV.t[:, 0:128], r(e_["mk"].t[:, 0:128]), r(VP[hh].t[:]), hh == 0, hh == 1, [e_["mk"].b, VP[hh].b], [pAV.b])
                    k.mm(pKV.t[:, RS[hh]], r(KhP[hh].t[:]), r(VP[hh].t[:, RS[hh]]), True, True, [KhP[hh].b, VP[hh].b], [pKV.b])
                    k.mm(pYV.t[:, 0:128], r(VP[hh].t[:]), r(e_["mk"].t[:, 128:256]), hh == 0, hh == 1, [VP[hh].b, e_["mk"].b], [pYV.b])
                yield
                k.cp(d_["AVc"].t[:], pAV.t[:, 0:128], [pAV.b], [d_["AVc"].b], eng=k.act)
                k.cp(d_["KVc"].t[:], pKV.t[:, 0:128], [pKV.b], [d_["KVc"].b], eng=k.act)
                k.cp(d_["YVc"].t[:], pYV.t[:, 0:128], [pYV.b], [d_["YVc"].b], eng=k.act)
                k.psF(pAV, pKV, pYV)
                yield
                yield from neumann_pairs([d_], 7)
                while seq_done[0] < n:
                    yield
                Ho, Hn = Ht[n % 2], Ht[(n + 1) % 2]
                yield from k.need(1)
                pr_ = k.psA()
                k.mm(pr_.t[:, 0:128], r(ar.t[:, 0:128]), r(Ho.t[:]), True, True, [ar.b, Ho.b], [pr_.b])
                k.stt(d_["Gt"].t[:], Ho.t[:], e1.t[:, 127:128], d_["KVc"].t[:], ALU.mult, ALU.add, [Ho.b, e1.b, d_["KVc"].b], [d_["Gt"].b])
                k.stt(r(rhs1.t[:]), pr_.t[:, 0:128], -1.0, d_["AVc"].t[:], ALU.mult, ALU.subtract, [pr_.b, d_["AVc"].b], [rhs1.b])
                k.psF(pr_)
                yield from k.need(1)
                pu = k.psA()
                for hh in range(2):
                    k.mm(pu.t[:, RS[hh]], r(d_["Tout2"].t[:, hh * 128:(hh + 1) * 128]), r(rhs1.t[:, RS[hh]]), True, True,
                         [d_["Tout2"].b, rhs1.b], [pu.b])
                k.cp(r(d_["UP2"].t[:].rearrange("p (i c) -> p i c", c=128)[:, :, 0:64]), pu.t[:, 0:128].rearrange("p (i c) -> p i c", c=64),
                     [pu.b], [d_["UP2"].b], eng=k.act)
                k.psF(pu)
                yield from k.need(2)
                pY = k.psA(); pS = k.psA()
                for hh in range(2):
                    k.mm(pS.t[:, RS[hh]], r(BhP[hh].t[:]), r(UP[hh].t[:, RS[hh]]), True, True, [BhP[hh].b, UP[hh].b], [pS.b])
                k.mm(pY.t[:, 0:128], r(Ho.t[:]), r(ar.t[:, 128:256]), True, False, [Ho.b, ar.b], [pY.b])
                for hh in range(2):
                    e_ = HD[hh]
                    k.mm(pY.t[:, 0:128], r(UP[hh].t[:]), r(e_["mb"].t[:, 128:256]), False, hh == 1, [UP[hh].b, e_["mb"].b], [pY.b])
                k.tt(k.dve, r(Hn.t[:]), pS.t[:, 0:128], d_["Gt"].t[:], ALU.add, [pS.b, d_["Gt"].b], [Hn.b])
                k.tt(k.dve, yT.t[:, cs], pY.t[:, 0:128], d_["YVc"].t[:], ALU.add, [pY.b, d_["YVc"].b], [yT.b])
                k.psF(pY, pS)
                seq_done[0] = n + 1
                yield
        bg_next(3)
        run_rr([rw_worker(w_) for w_ in range(WRW)])
        bfree(kk, bT, sig, cum)
        rk = balloc(); yc = balloc(); sq_ = balloc(); rs_ = balloc()
        k.stt(bfv(rk), rT.t[:], V(4)[:, g:g + 1], kf.t[:], ALU.mult, ALU.mult, [rT.b, rwv.b, kf.b], [rk.b])
        k.actf(bfv(sq_), yT.t[:], AF.Copy, [yT.b], [sq_.b])

        def fnm(tb, p):
            ts_ = slice(tb * 512, (tb + 1) * 512)
            k.stt(yc.t[:, ts_], p.t[:, :], -1.0 / 64, yT.t[:, ts_], ALU.mult, ALU.add, [p.b, yT.b], [yc.b])
        psum_bcast_sum(sq_, blk_bf.t[:], blk_bf.b, fnm)
        k.actf(bfv(sq_), yc.t[:], AF.Square, [yc.b], [sq_.b])

        def fnv(tb, p):
            ts_ = slice(tb * 512, (tb + 1) * 512)
            k.actf(rs_.t[:, ts_], p.t[:, :], AF.Ln, [p.b, epsG.b], [rs_.b], bias=epsG.t[:, 0:1], scale=1.0 / 64)
        psum_bcast_sum(sq_, blk_bf.t[:], blk_bf.b, fnv)
        k.actf(rs_.t[:], rs_.t[:], AF.Exp, [rs_.b], [rs_.b], scale=-0.5)
        k.tt(k.dve, yc.t[:], yc.t[:], rs_.t[:], ALU.mult, [yc.b, rs_.b], [yc.b])
        k.ts(k.dve, yc.t[:], yc.t[:], V(5)[:, g:g + 1], V(6)[:, g:g + 1], ALU.mult, ALU.add, [yc.b, rwv.b], [yc.b])

        def fnb(tb, p):
            ts_ = slice(tb * 512, (tb + 1) * 512)
            k.tt(k.dve, rs_.t[:, ts_], p.t[:, :], vT.t[:, ts_], ALU.mult, [p.b, vT.b], [rs_.b])
        psum_bcast_sum(rk, blk_bf.t[:], blk_bf.b, fnb)
        k.tt(k.dve, yc.t[:], yc.t[:], rs_.t[:], ALU.add, [yc.b, rs_.b], [yc.b])
        ob = obf[octr[0] % 2]; octr[0] += 1
        for tb in range(4):
            ts_ = slice(tb * 512, (tb + 1) * 512)
            p = k.ps()
            k.mm(p.t[:, :], g2.t[:, gsl], sg.t[:, ts_], True, True, [g2.b, sg.b], [p.b])
            k.tt(k.dve, ob.t[:, ts_], p.t[:, :], yc.t[:, ts_], ALU.mult, [p.b, yc.b], [ob.b])
        k.dma(k.sp, oTd.t[1024 + g * 128:1024 + (g + 1) * 128, :], ob.t[:], [ob.b], [oTd.bl[8 + g]], ob.b)
        bfree(rk, yc, sq_, rs_, rT, kf, vT, yT)
    bfree(wa, sg)
    st_rw.close()
    st.close()


def prep_shared(inp):
    f = lambda a: np.ascontiguousarray(np.asarray(a, dtype=np.float32))
    sh = {}
    for kk_ in ("w_in", "w_out", "xa_wq", "xa_wk", "xa_wv", "xa_wo", "ffn_w1", "ffn_w2"):
        sh[kk_] = f(inp[kk_][0])
    sh["norms"] = f(np.stack([inp["mix_norm_w"][0], inp["xa_norm_w"][0], inp["mem_norm_w"][0], inp["ffn_norm_w"][0],
                              inp["final_norm_w"]], axis=0))
    cw = np.asarray(inp["dn_conv_w"][0])
    sh["convw"] = f(cw.reshape(4, 24, 128).transpose(2, 1, 0).reshape(128, 96))
    dn = np.zeros((16, 2), np.float32)
    dn[0:8, 0] = np.asarray(inp["dn_a_log"][0]); dn[0:8, 1] = np.asarray(inp["dn_dt_bias"][0])
    sh["dnsc"] = dn
    sh["dnw"] = f(np.asarray(inp["dn_norm_w"][0]).reshape(128, 1))
    sh["mu"] = f(np.asarray(inp["rw_mu"][0]).reshape(26, 128).T)
    vs = [np.asarray(inp[n][0]).reshape(8, 128).T for n in ("rw_w0", "rw_a0", "rw_k_k", "rw_k_a", "rw_r_k", "rw_ln_w", "rw_ln_b")]
    sh["rwv"] = f(np.concatenate(vs, axis=1))
    sh["lora"] = f(np.concatenate([np.asarray(inp["rw_w2"][0]), np.asarray(inp["rw_a2"][0])], axis=0))
    sh["g2"] = f(inp["rw_g2"][0])
    return sh


def kernel(**inp):
    sh = prep_shared(inp)
    xs = np.asarray(inp["x"], dtype=np.float32)
    ms = np.asarray(inp["mem"], dtype=np.float32)
    nc = build()
    in_maps = []
    for b in range(8):
        m = dict(sh)
        m["x"] = np.ascontiguousarray(xs[b])
        m["mem"] = np.ascontiguousarray(ms[b])
        in_maps.append(m)
    res = run_bass_kernel_spmd(nc, in_maps, core_ids=list(range(8)))
    return np.stack([np.asarray(r["out"], dtype=np.float32) for r in res.results], axis=0)
```

```python
import contextlib
import math
import numpy as np
import concourse.bass as bass
import concourse.mybir as mybir
from concourse.alu_op_type import AluOpType as ALU
from concourse.bass_utils import run_bass_kernel_spmd

F32 = mybir.dt.float32
BF16 = mybir.dt.bfloat16
F32R = mybir.dt.float32r
AF = mybir.ActivationFunctionType
AX = mybir.AxisListType

D = 2048
S = 2048
NT = 16
MEM = 256
DNC = 4112
INC = 7440
FF = 8192
EPS = 1e-6
CDEC = -math.exp(-0.5)
DBG_DN = 8
DBG_RW = 8
DBG_CH = NT
DBG_STEP = 99


class Sem:
    __slots__ = ("h", "name")

    def __init__(self, h, name):
        self.h = h
        self.name = name


class Buf:
    __slots__ = ("name", "w", "r", "dsem", "dcount", "excl")

    def __init__(self, name):
        self.name = name
        self.excl = False
        self.w = None
        self.r = {}
        self.dsem = None
        self.dcount = 0


class Eng:
    def __init__(self, name, h, sem):
        self.name = name
        self.h = h
        self.sem = sem
        self.count = 0
        self.waited = {}


class T:
    def __init__(self, t, name):
        self.t = t
        self.b = Buf(name)


class TV:
    def __init__(self, ap, b):
        self.t = ap
        self.b = b


class K:
    def __init__(self, nc):
        self.nc = nc
        self.es = contextlib.ExitStack()
        self.pe = self._eng("pe", nc.tensor)
        self.dve = self._eng("dve", nc.vector)
        self.act = self._eng("act", nc.scalar)
        self.pool = self._eng("pool", nc.gpsimd)
        self.sp = self._eng("sp", nc.sync)
        self.ninst = 0
        self._psi = 0
        self.psf = []
        self._ev = 0
        self.slots = []
        self.psfree = list(range(8))

    def new_sem(self, name):
        return Sem(self.es.enter_context(self.nc.semaphore(name)), name)

    def _eng(self, name, h):
        return Eng(name, h, self.new_sem("s_" + name))

    def sb(self, name, shape, dt, stack=None):
        t = (stack or self.es).enter_context(self.nc.sbuf_tensor(name, list(shape), dt))
        return T(t, name)

    def _deps(self, eng, reads, writes, extra=()):
        deps = {}
        for b in reads:
            if b.w is not None:
                s, v = b.w
                if v > deps.get(s, 0):
                    deps[s] = v
            if b.excl:
                for s, v in b.r.items():
                    if s is not eng.sem and v > deps.get(s, 0):
                        deps[s] = v
        for b in writes:
            if b.w is not None and not (eng is self.pe and b.w[0] is self.pe.sem):
                s, v = b.w
                if v > deps.get(s, 0):
                    deps[s] = v
            for s, v in b.r.items():
                if v > deps.get(s, 0):
                    deps[s] = v
        for s, v in extra:
            if v > deps.get(s, 0):
                deps[s] = v
        for s, v in deps.items():
            if eng.waited.get(s, 0) < v:
                eng.h.wait_ge(s.h, v)
                eng.waited[s] = v

    def op(self, eng, fn, reads=(), writes=()):
        self._deps(eng, reads, writes)
        inst = fn(eng.h)
        eng.count += 1
        inst.then_inc(eng.sem.h, 1)
        self.ninst += 1
        c = eng.count
        s = eng.sem
        for b in reads:
            b.r[s] = c
        for b in writes:
            b.w = (s, c)
            b.r = {}
        return inst

    def dma(self, q, out, in_, reads, writes, slot, **kw):
        if slot.dsem is None:
            slot.dsem = self.new_sem("d_" + slot.name)
            self.slots.append(slot)
        extra = [(slot.dsem, slot.dcount)] if slot.dcount else []
        self._deps(q, reads, writes, extra)
        inst = q.h.dma_start(out=out, in_=in_, **kw)
        slot.dcount += 16
        inst.then_inc(slot.dsem.h, 16)
        self.ninst += 1
        for b in reads:
            b.r[slot.dsem] = slot.dcount
        for b in writes:
            b.w = (slot.dsem, slot.dcount)
            b.r = {}
        return inst

    def ps(self):
        p = self.psf[self._psi % 8]
        self._psi += 1
        return p

    def psA(self):
        return self.psf[self.psfree.pop(0)]

    def psF(self, *ps_):
        for p in ps_:
            self.psfree.append(self.psf.index(p))

    def need(self, m):
        while len(self.psfree) < m:
            yield

    def barrier(self):
        engs = [self.pe, self.dve, self.act, self.pool, self.sp]
        for e in engs:
            for o in engs:
                if o is not e and o.count and e.waited.get(o.sem, 0) < o.count:
                    e.h.wait_ge(o.sem.h, o.count)
                    e.waited[o.sem] = o.count
            for sl in self.slots:
                if e.waited.get(sl.dsem, 0) < sl.dcount:
                    e.h.wait_ge(sl.dsem.h, sl.dcount)
                    e.waited[sl.dsem] = sl.dcount

    def mm(self, out, lhsT, rhs, start, stop, R, W):
        return self.op(self.pe, lambda e: e.matmul(out, lhsT, rhs, start=start, stop=stop), R, W)

    def tr(self, out, in_, ident, R, W):
        return self.op(self.pe, lambda e: e.transpose(out, in_, ident), R, W)

    def tt(self, eng, out, a, b, op, R, W):
        return self.op(eng, lambda e: e.tensor_tensor(out=out, in0=a, in1=b, op=op), R, W)

    def ts(self, eng, out, a, s1, s2, op0, op1, R, W):
        if op1 is None:
            return self.op(eng, lambda e: e.tensor_scalar(out=out, in0=a, scalar1=s1, scalar2=None, op0=op0), R, W)
        return self.op(eng, lambda e: e.tensor_scalar(out=out, in0=a, scalar1=s1, scalar2=s2, op0=op0, op1=op1), R, W)

    def stt(self, out, a, s, b, op0, op1, R, W):
        return self.op(self.dve, lambda e: e.scalar_tensor_tensor(out=out, in0=a, scalar=s, in1=b, op0=op0, op1=op1), R, W)

    def actf(self, out, in_, func, R, W, **kw):
        return self.op(self.act, lambda e: e.activation(out=out, in_=in_, func=func, **kw), R, W)

    def cp(self, out, in_, R, W, eng=None):
        if eng is None:
            self._ev += 1
            eng = self.act if (self._ev & 1) else self.dve
        if eng is self.act:
            return self.op(eng, lambda e: e.copy(out, in_), R, W)
        return self.op(eng, lambda e: e.tensor_copy(out, in_), R, W)


def build(debug=None):
    nc = bass.Bass("TRN2", target_bir_lowering=False)
    k = K(nc)

    def din(name, shape):
        return nc.dram_tensor(name, list(shape), F32, kind="ExternalInput").ap()

    x = din("x", [S, D]); mem = din("mem", [MEM, D])
    w_in = din("w_in", [D, INC]); w_out = din("w_out", [D, D])
    wq = din("xa_wq", [D, 512]); wk = din("xa_wk", [D, 512]); wv = din("xa_wv", [D, 512]); wo = din("xa_wo", [512, D])
    w1 = din("ffn_w1", [D, FF]); w2 = din("ffn_w2", [FF, D])
    nrm = din("norms", [5, D])
    convw = din("convw", [128, 24 * 4])
    dnsc = din("dnsc", [16, 2])
    dnw = din("dnw", [128, 1])
    mu = din("mu", [128, 26])
    rwv = din("rwv", [128, 7 * 8])
    lora = din("lora", [128, 1024])
    g2 = din("g2", [128, 1024])
    out = nc.dram_tensor("out", [S, D], F32, kind="ExternalOutput").ap()
    pT = T(nc.dram_tensor("pT", [INC, S], F32, kind="Internal").ap(), "pT")
    pT.bl = [Buf(f"pT{i}") for i in range(59)]
    if debug == "p4":
        oTd = T(nc.dram_tensor("oT_in", [D, S], F32, kind="ExternalInput").ap(), "oTd")
        oTd.bl = [Buf(f"oT{i}") for i in range(16)]
        oT_dt = F32
    else:
        oTd = T(nc.dram_tensor("oT", [D, S], BF16, kind="Internal").ap(), "oTd")
        oTd.bl = [Buf(f"oT{i}") for i in range(16)]
        oT_dt = BF16
    wsc = T(nc.dram_tensor("wsc", [38, 128, 8192], BF16, kind="Internal").ap(), "wsc")
    wsc.bl = [Buf(f"wsc{i}") for i in range(38)]
    dbg = None
    if debug == "p1":
        dbg = nc.dram_tensor("dbg", [INC, S], F32, kind="ExternalOutput").ap()
    if debug == "p2":
        dbg = nc.dram_tensor("dbg", [D, S], F32, kind="ExternalOutput").ap()

    for i in range(8):
        p = T(k.es.enter_context(nc.psum_tensor(f"ps{i}", [128, 512], F32)), f"ps{i}")
        p.b.excl = True
        k.psf.append(p)

    ones = k.sb("ones", [128, 128], F32)
    ident = k.sb("ident", [128, 128], F32)
    identb = k.sb("identb", [128, 128], BF16)
    epsT = k.sb("epsT", [128, 1], F32)
    k.op(k.pool, lambda e: e.memset(ones.t[:], 1.0), [], [ones.b])
    k.op(k.pool, lambda e: e.memset(epsT.t[:], EPS), [], [epsT.b])
    k.op(k.pool, lambda e: e.affine_select(out=ident.t[:], in_=ones.t[:], pattern=[[-1, 128]], compare_op=ALU.is_equal,
                                           fill=0.0, base=0, channel_multiplier=1), [ones.b], [ident.b])
    k.op(k.dve, lambda e: e.tensor_copy(identb.t[:], ident.t[:]), [ident.b], [identb.b])

    G = {}

    def alloc_norm(stack, tag):
        G["wbc"] = k.sb("wbc" + tag, [128, D], F32, stack)
        G["uns"] = [k.sb(f"un{i}" + tag, [128, D], BF16, stack) for i in range(2)]
        G["un"] = G["uns"][0]
        G["junk"] = k.sb("junk" + tag, [128, D], BF16, stack)
        G["ss"] = k.sb("ss" + tag, [128, 4], F32, stack)
        G["ssb"] = [Buf(f"ss{i}" + tag) for i in range(4)]
        G["ctr"] = 0

    def load_wbc(i):
        wbc = G["wbc"]
        k.dma(k.sp, wbc.t[:], nrm[i:i + 1, :].to_broadcast([128, D]), [], [wbc.b], wbc.b)
    wbufs = []
    wctr = [0]

    def precast_p4_weights():
        blocks = []
        for cb in range(4):
            blocks.append((cb, w_out[:, cb * 512:(cb + 1) * 512], 16))
        blocks.append((4, wq[:, :], 16))
        blocks.append((5, wo[:, :], 4))
        for fb in range(16):
            blocks.append((6 + fb, w1[:, fb * 512:(fb + 1) * 512], 16))
        for cb in range(4):
            for sub in range(4):
                blocks.append((22 + cb * 4 + sub, w2[sub * 2048:(sub + 1) * 2048, cb * 512:(cb + 1) * 512], 16))
        return blocks

    pc_blocks = precast_p4_weights()

    def bg_next(n):
        for _ in range(n):
            if not pc_blocks:
                return
            idx, src, kc = pc_blocks.pop(0)
            k.dma(k.pool, wsc.t[idx].rearrange("p (kc c) -> p kc c", kc=kc), src.rearrange("(kc p) c -> p kc c", p=128),
                  [], [wsc.bl[idx]], wsc.bl[idx])

    def alloc_wbufs(stack, tag):
        wbufs.clear()
        wbufs.extend(k.sb(f"wbuf{tag}{i}", [128, 8192], BF16, stack) for i in range(2))

    def load_w(src_ap, kc, cache=None):
        wb = wbufs[wctr[0] % len(wbufs)]
        wctr[0] += 1
        ncol = src_ap.shape[1]
        view = wb.t[:, 0:kc * ncol].rearrange("p (kc c) -> p kc c", kc=kc)
        if cache is not None:
            k.dma(k.sp, wb.t[:, :], wsc.t[cache[0]], [wsc.bl[cache[0]]], [wb.b], wb.b)
            return wb, view
        k.dma(k.pool, view, src_ap.rearrange("(kc p) c -> p kc c", p=128), [], [wb.b], wb.b)
        return wb, view

    def norm_transpose(src, srcB, dstT, dstB, dst_cols, slot):
        wbc, ss, junk = G["wbc"], G["ss"], G["junk"]
        un = G["uns"][G["ctr"] % 2]
        G["ctr"] += 1
        sb_ = G["ssb"][slot]
        k.actf(junk.t[:], src, AF.Square, [srcB], [junk.b, sb_], accum_out=ss.t[:, slot:slot + 1])
        k.actf(ss.t[:, slot:slot + 1], ss.t[:, slot:slot + 1], AF.Sqrt, [sb_, epsT.b], [sb_], scale=1.0 / D, bias=epsT.t[:, 0:1])
        k.op(k.dve, lambda e: e.reciprocal(ss.t[:, slot:slot + 1], ss.t[:, slot:slot + 1]), [sb_], [sb_])
        k.stt(un.t[:], src, ss.t[:, slot:slot + 1], wbc.t[:], ALU.mult, ALU.mult, [srcB, sb_, wbc.b], [un.b])
        for half in range(2):
            p = k.ps()
            pv = p.t[:].bitcast(BF16)
            for j in range(8):
                kc = half * 8 + j
                k.tr(pv[:, j * 128:(j + 1) * 128], un.t[:, kc * 128:(kc + 1) * 128], identb.t[:], [un.b, identb.b], [p.b])
            k.cp(dstT[:, half * 8:(half + 1) * 8, dst_cols], pv.rearrange("p (j c) -> p j c", c=128), [p.b], [dstB])

    if debug != "p4":
        with contextlib.ExitStack() as st1:
            alloc_wbufs(st1, "a")
            alloc_norm(st1, "a")
            uT = k.sb("uT", [128, 16, S], BF16, st1)
            uTb = [Buf(f"uT{i}") for i in range(4)]
            xts = [k.sb(f"xt{i}", [128, D], F32, st1) for i in range(2)]
            stg = [k.sb(f"stg{i}", [128, 512], F32, st1) for i in range(4)]
            load_wbc(0)
            si = 0

            def p1_block(c0, wb, wvw, tbs):
                nonlocal si
                ncol = min(512, INC - c0)
                for g0 in range(0, ncol, 128):
                    M = min(128, ncol - g0)
                    for tb in tbs:
                        p = k.ps()
                        for kc in range(16):
                            k.mm(p.t[0:M, :], wvw[:, kc, g0:g0 + M], uT.t[:, kc, tb * 512:(tb + 1) * 512], kc == 0, kc == 15,
                                 [wb.b, uTb[tb]], [p.b])
                        sg_ = stg[si % 4]; si += 1
                        k.cp(sg_.t[0:M, :], p.t[0:M, :], [p.b], [sg_.b])
                        k.dma(k.sp, pT.t[c0 + g0:c0 + g0 + M, tb * 512:(tb + 1) * 512], sg_.t[0:M, :], [sg_.b], [pT.bl[(c0 + g0) // 128]], sg_.b)
            wb0, wvw0 = load_w(w_in[:, 0:512], 16)
            for tb in range(4):
                for n in range(4 * tb, 4 * tb + 4):
                    xt = xts[n % 2]
                    k.dma(k.sp, xt.t[:], x[n * 128:(n + 1) * 128, :], [], [xt.b], xt.b)
                    norm_transpose(xt.t[:], xt.b, uT.t, uTb[n // 4], slice(n * 128, (n + 1) * 128), n % 4)
                p1_block(0, wb0, wvw0, [tb])
            for c0 in range(512, INC, 512):
                ncol = min(512, INC - c0)
                wb, wvw = load_w(w_in[:, c0:c0 + ncol], 16)
                p1_block(c0, wb, wvw, range(4))
            if debug == "p1":
                for r0 in range(0, INC, 128):
                    M = min(128, INC - r0)
                    xt = xts[(r0 // 128) % 2]
                    k.dma(k.sp, xt.t[0:M, :], pT.t[r0:r0 + M, :], [pT.bl[r0 // 128]], [xt.b], xt.b)
                    k.dma(k.sp, dbg[r0:r0 + M, :], xt.t[0:M, :], [xt.b], [], xt.b)
                k._deps(k.sp, [], [xts[0].b, xts[1].b])
        if debug == "p1":
            k.es.close()
            return nc

    if debug != "p4":
        k.barrier()
        build_mixers(nc, k, pT, oTd, convw, dnsc, dnw, mu, rwv, lora, g2, ones, ident, bg_next)
        bg_next(99)
        if debug == "p2":
            k.barrier()
            with contextlib.ExitStack() as st:
                a = k.sb("dba", [128, S], BF16, st); b = k.sb("dbb", [128, S], F32, st)
                for r0 in range(0, D, 128):
                    k.dma(k.sp, a.t[:], oTd.t[r0:r0 + 128, :], [oTd.bl[r0 // 128]], [a.b], a.b)
                    k.cp(b.t[:], a.t[:], [a.b], [b.b])
                    k.dma(k.sp, dbg[r0:r0 + 128, :], b.t[:], [b.b], [], b.b)
                k._deps(k.sp, [], [b.b])
            k.es.close()
            return nc

    k.barrier()
    if debug == "p4":
        bg_next(99)
    st4 = contextlib.ExitStack()
    alloc_wbufs(st4, "b")
    alloc_norm(st4, "b")
    wbc, un, ss = G["wbc"], G["junk"], G["ss"]
    KT = k.sb("KT", [128, 4, MEM], BF16, st4)
    Vm = k.sb("Vm", [128, 2, 512], BF16, st4)
    hnT = k.sb("hnT", [128, 16, 512], BF16, st4)
    h = k.sb("h", [128, 4, D], F32, st4)
    hB = [Buf(f"h{j}") for j in range(4)]
    hid = k.sb("hid", [128, 64, 512], BF16, st4)
    hidB = [Buf(f"hid{i}") for i in range(16)]
    qT = k.sb("qT", [128, 4, 512], BF16, st4)
    oxT = k.sb("oxT", [128, 4, 512], BF16, st4)
    atw = [dict(pr=k.sb(f"pr{i}", [128, MEM], F32, st4), prn=k.sb(f"prn{i}", [128, MEM], BF16, st4),
                prT=k.sb(f"prT{i}", [128, 2, 128], BF16, st4), sm=k.sb(f"sm{i}", [128, 4], F32, st4)) for i in range(4)]

    def run_rr(gens):
        gens = list(gens)
        while gens:
            for g_ in list(gens):
                try:
                    next(g_)
                except StopIteration:
                    gens.remove(g_)
    rl = [k.sb(f"rl{i}", [128, 512], F32, st4) for i in range(2)]
    memt = [k.sb(f"memt{i}", [128, D], F32, st4) for i in range(1)]

    load_wbc(2)
    for mt in range(2):
        m_ = memt[0]
        k.dma(k.sp, m_.t[:], mem[mt * 128:(mt + 1) * 128, :], [], [m_.b], m_.b)
        norm_transpose(m_.t[:], m_.b, hnT.t, hnT.b, slice(mt * 128, (mt + 1) * 128), mt)
    wb, wvw = load_w(wk[:, :], 16)
    for hd in range(4):
        p = k.ps()
        for kc in range(16):
            k.mm(p.t[:, 0:MEM], wvw[:, kc, hd * 128:(hd + 1) * 128], hnT.t[:, kc, 0:MEM], kc == 0, kc == 15, [wb.b, hnT.b], [p.b])
        k.cp(KT.t[:, hd, :], p.t[:, 0:MEM], [p.b], [KT.b])
    wb, wvw = load_w(wv[:, :], 16)
    for mc in range(2):
        p = k.ps()
        for kc in range(16):
            k.mm(p.t[:, :], hnT.t[:, kc, mc * 128:(mc + 1) * 128], wvw[:, kc, :], kc == 0, kc == 15, [wb.b, hnT.b], [p.b])
        k.cp(Vm.t[:, mc, :], p.t[:, :], [p.b], [Vm.b])

    oTb_view = hid.t[:, 0:16, :]
    oTb_bufs = hidB[0:4]
    for TB in range(4):
        t0 = TB * 512
        if oT_dt == BF16:
            k.dma(k.sp, oTb_view, oTd.t[:, t0:t0 + 512].rearrange("(kc p) t -> p kc t", p=128), oTd.bl, oTb_bufs, oTb_bufs[0])
        else:
            k.dma(k.pool, oTb_view, oTd.t[:, t0:t0 + 512].rearrange("(kc p) t -> p kc t", p=128), oTd.bl, oTb_bufs, oTb_bufs[0])
        for cb in range(4):
            wb, wvw = load_w(w_out[:, cb * 512:(cb + 1) * 512], 16, (cb, TB == 0))
            if cb == 0:
                for j in range(4):
                    k.dma(k.sp, h.t[:, j, :], x[t0 + j * 128:t0 + (j + 1) * 128, :], [], [hB[j]], hB[j])
            for j in range(4):
                p = k.ps()
                for kc in range(16):
                    k.mm(p.t[:, :], oTb_view[:, kc, j * 128:(j + 1) * 128], wvw[:, kc, :], kc == 0, kc == 15, [wb.b] + oTb_bufs, [p.b])
                hs = h.t[:, j, cb * 512:(cb + 1) * 512]
                k.tt(k.dve, hs, p.t[:, :], hs, ALU.add, [p.b, hB[j]], [hB[j]])
        load_wbc(1)
        for j in range(4):
            norm_transpose(h.t[:, j, :], hB[j], hnT.t, hnT.b, slice(j * 128, (j + 1) * 128), j)
        wb, wvw = load_w(wq[:, :], 16, (4, TB == 0))
        for hd in range(4):
            p = k.ps()
            for kc in range(16):
                k.mm(p.t[:, :], wvw[:, kc, hd * 128:(hd + 1) * 128], hnT.t[:, kc, :], kc == 0, kc == 15, [wb.b, hnT.b], [p.b])
            k.actf(qT.t[:, hd, :], p.t[:, :], AF.Copy, [p.b], [qT.b], scale=128 ** -0.5)
        def attn_worker(w_):
            a_ = atw[w_]
            pr, prn, prT, sm = a_["pr"], a_["prn"], a_["prT"], a_["sm"]
            for idx in range(w_, 16, 4):
                j, hd = divmod(idx, 4)
                yield from k.need(1)
                p = k.psA()
                k.mm(p.t[:, 0:MEM], qT.t[:, hd, j * 128:(j + 1) * 128], KT.t[:, hd, :], True, True, [qT.b, KT.b], [p.b])
                yield
                k.op(k.dve, lambda e: e.tensor_reduce(out=sm.t[:, 0:1], in_=p.t[:, 0:MEM], axis=AX.X, op=ALU.max, negate=True),
                     [p.b], [sm.b])
                yield
                k.actf(pr.t[:], p.t[:, 0:MEM], AF.Exp, [p.b, sm.b], [pr.b, sm.b], bias=sm.t[:, 0:1], accum_out=sm.t[:, 1:2])
                k.psF(p)
                yield
                k.op(k.dve, lambda e: e.reciprocal(sm.t[:, 2:3], sm.t[:, 1:2]), [sm.b], [sm.b])
                yield
                k.ts(k.dve, prn.t[:], pr.t[:], sm.t[:, 2:3], None, ALU.mult, None, [pr.b, sm.b], [prn.b])
                yield
                yield from k.need(1)
                p2 = k.psA()
                pv = p2.t[:].bitcast(BF16)
                for mc in range(2):
                    k.tr(pv[:, mc * 128:(mc + 1) * 128], prn.t[:, mc * 128:(mc + 1) * 128], identb.t[:], [prn.b, identb.b], [p2.b])
                yield
                k.cp(prT.t[:], pv[:, 0:256].rearrange("p (m c) -> p m c", c=128), [p2.b], [prT.b])
                k.psF(p2)
                yield
                yield from k.need(1)
                p3 = k.psA()
                for mc in range(2):
                    k.mm(p3.t[:, 0:128], Vm.t[:, mc, hd * 128:(hd + 1) * 128], prT.t[:, mc, :], mc == 0, mc == 1, [Vm.b, prT.b], [p3.b])
                yield
                k.cp(oxT.t[:, hd, j * 128:(j + 1) * 128], p3.t[:, 0:128], [p3.b], [oxT.b])
                k.psF(p3)
                yield
        run_rr([attn_worker(w_) for w_ in range(4)])
        wb, wvw = load_w(wo[:, :], 4, (5, TB == 0))
        for cb in range(4):
            for j in range(4):
                p = k.ps()
                for kc in range(4):
                    k.mm(p.t[:, :], oxT.t[:, kc, j * 128:(j + 1) * 128], wvw[:, kc, cb * 512:(cb + 1) * 512], kc == 0, kc == 3, [wb.b, oxT.b], [p.b])
                hs = h.t[:, j, cb * 512:(cb + 1) * 512]
                k.tt(k.dve, hs, p.t[:, :], hs, ALU.add, [p.b, hB[j]], [hB[j]])
        load_wbc(3)
        for j in range(4):
            norm_transpose(h.t[:, j, :], hB[j], hnT.t, hnT.b, slice(j * 128, (j + 1) * 128), j)
        for fb in range(16):
            wb, wvw = load_w(w1[:, fb * 512:(fb + 1) * 512], 16, (6 + fb, TB == 0))
            for fc in range(4):
                p = k.ps()
                for kc in range(16):
                    k.mm(p.t[:, :], wvw[:, kc, fc * 128:(fc + 1) * 128], hnT.t[:, kc, :], kc == 0, kc == 15, [wb.b, hnT.b], [p.b])
                r_ = rl[(fb * 4 + fc) % 2]
                k.actf(r_.t[:], p.t[:, :], AF.Relu, [p.b], [r_.b])
                k.tt(k.dve, hid.t[:, fb * 4 + fc, :], r_.t[:], r_.t[:], ALU.mult, [r_.b], [hidB[fb]])
        for cb in range(4):
            accs = [k.ps() for _ in range(4)]
            for sub in range(4):
                wb, wvw = load_w(w2[sub * 2048:(sub + 1) * 2048, cb * 512:(cb + 1) * 512], 16, (22 + cb * 4 + sub, TB == 0))
                for j in range(4):
                    for fc in range(16):
                        f = sub * 16 + fc
                        k.mm(accs[j].t[:, :], hid.t[:, f, j * 128:(j + 1) * 128], wvw[:, fc, :], f == 0, f == 63,
                             [wb.b, hidB[f // 4]], [accs[j].b])
            for j in range(4):
                hs = h.t[:, j, cb * 512:(cb + 1) * 512]
                k.tt(k.dve, hs, accs[j].t[:, :], hs, ALU.add, [accs[j].b, hB[j]], [hB[j]])
        load_wbc(4)
        for j in range(4):
            hj = h.t[:, j, :]
            sb_ = G["ssb"][j]
            k.actf(un.t[:], hj, AF.Square, [hB[j]], [un.b, sb_], accum_out=ss.t[:, j:j + 1])
            k.actf(ss.t[:, j:j + 1], ss.t[:, j:j + 1], AF.Sqrt, [sb_, epsT.b], [sb_], scale=1.0 / D, bias=epsT.t[:, 0:1])
            k.op(k.dve, lambda e: e.reciprocal(ss.t[:, j:j + 1], ss.t[:, j:j + 1]), [sb_], [sb_])
            k.stt(hj, hj, ss.t[:, j:j + 1], wbc.t[:], ALU.mult, ALU.mult, [hB[j], sb_, wbc.b], [hB[j]])
        for j in range(4):
            k.dma(k.sp, out[t0 + j * 128:t0 + (j + 1) * 128, :], h.t[:, j, :], [hB[j]], [], hB[j])
    k._deps(k.sp, [], hB)
    st4.close()
    k.es.close()
    return nc


def build_mixers(nc, k, pT, oTd, convw_d, dnsc_d, dnw_d, mu_d, rwv_d, lora_d, g2_d, ones, ident, bg_next):
    st = contextlib.ExitStack()
    r = lambda ap: ap.bitcast(F32R)
    NB = 10
    big = [k.sb(f"big{i}", [128, S], F32, st) for i in range(NB)]
    free = list(range(NB))

    def balloc():
        return big[free.pop(0)]

    def bfree(*ts):
        for t_ in ts:
            free.append(big.index(t_))

    def sm(name, shape, dt=F32):
        return k.sb("m_" + name, shape, dt, st)

    Ls = sm("Ls", [128, 128]); Li = sm("Li", [128, 128]); UU = sm("UU", [128, 256])
    blk = sm("blk", [128, 128]); rmask = sm("rmask", [128, S], BF16); selh = sm("selh", [16, 128])
    epsG = sm("epsG", [128, 2])

    def asel(out, pat, cm, op, R, W):
        k.op(k.pool, lambda e: e.affine_select(out=out, in_=ones.t[:], pattern=pat, compare_op=op, fill=0.0, base=0,
                                               channel_multiplier=cm), [ones.b] + R, W)
    asel(Ls.t[:], [[-1, 128]], 1, ALU.is_gt, [], [Ls.b])
    k.ts(k.dve, Ls.t[:], Ls.t[:], -1.0, None, ALU.mult, None, [Ls.b], [Ls.b])
    Li2 = sm("Li2", [128, 256]); II2 = sm("II2", [128, 256])
    for i_ in range(2):
        asel(Li2.t[:, i_ * 128:(i_ + 1) * 128], [[-1, 128]], 1, ALU.is_ge, [], [Li2.b])
        k.cp(II2.t[:, i_ * 128:(i_ + 1) * 128], ident.t[:], [ident.b], [II2.b], eng=k.pool)
    asel(Li.t[:], [[-1, 128]], 1, ALU.is_ge, [], [Li.b])
    asel(UU.t[:, 0:128], [[1, 128]], -1, ALU.is_gt, [], [UU.b])
    asel(UU.t[:, 128:256], [[1, 128]], -1, ALU.is_ge, [], [UU.b])
    k.op(k.pool, lambda e: e.memset(blk.t[:], 0.0), [], [blk.b])
    k.op(k.pool, lambda e: e.memset(blk.t[0:64, 0:64], 1.0), [], [blk.b])
    k.op(k.pool, lambda e: e.memset(blk.t[64:128, 64:128], 1.0), [], [blk.b])
    k.op(k.pool, lambda e: e.memset(rmask.t[:], 1.0), [], [rmask.b])
    k.op(k.pool, lambda e: e.memset(rmask.t[:].rearrange("p (c t) -> p c t", t=128)[:, :, 0:1], 0.0), [], [rmask.b])
    k.op(k.pool, lambda e: e.memset(epsG.t[:, 0:1], 64e-5), [], [epsG.b])
    k.op(k.pool, lambda e: e.memset(epsG.t[:, 1:2], 1e-6), [], [epsG.b])
    convw = sm("convw", [128, 96]); dnsc = sm("dnsc", [16, 2]); dnw = sm("dnw", [128, 1])
    mu = sm("mu", [128, 26]); omm = sm("omm", [128, 26]); rwv = sm("rwv", [128, 56])
    for t_, d_ in ((convw, convw_d), (dnsc, dnsc_d), (dnw, dnw_d), (mu, mu_d), (rwv, rwv_d)):
        k.dma(k.sp, t_.t[:], d_, [], [t_.b], t_.b)
    k.ts(k.dve, omm.t[:], mu.t[:], -1.0, 1.0, ALU.mult, ALU.add, [mu.b], [omm.b])

    def sq(name, w=128):
        return sm(name, [128, w])
    def run_rr(gens):
        gens = list(gens)
        while gens:
            for g_ in list(gens):
                try:
                    next(g_)
                except StopIteration:
                    gens.remove(g_)

    def neumann_multi(probs, nlev):
        for pr in probs:
            pr["Ao"] = pr["A1"]; pr["BPo"] = pr["BP"][0]
        for lv in range(nlev):
            last = lv == nlev - 1
            yield from k.need((1 if last else 2) * len(probs))
            for pr in probs:
                Ao, BPo = pr["Ao"], pr["BPo"]
                pr["pa"] = k.psA()
                if last:
                    k.mm(pr["pa"].t[:, 0:128], r(Ao.t[:]), r(BPo.t[:, 128:256]), True, True, [Ao.b, BPo.b], [pr["pa"].b])
                else:
                    k.mm(pr["pa"].t[:, 0:256], r(Ao.t[:]), r(BPo.t[:]), True, True, [Ao.b, BPo.b], [pr["pa"].b])
                    pr["pb"] = k.psA()
                    k.mm(pr["pb"].t[:, 0:128], r(BPo.t[:, 0:128]), r(Ao.t[:]), True, True, [Ao.b, BPo.b], [pr["pb"].b])
            yield
            for pr in probs:
                BPo = pr["BPo"]
                if last:
                    k.tt(k.dve, r(pr["Tout"].t[:]), pr["pa"].t[:, 0:128], BPo.t[:, 128:256], ALU.add, [pr["pa"].b, BPo.b], [pr["Tout"].b])
                    k.psF(pr["pa"])
                else:
                    An, BPn = pr["Ap"][lv % 2], pr["BP"][(lv + 1) % 2]
                    k.cp(r(An.t[:]), pr["pb"].t[:, 0:128], [pr["pb"].b], [An.b], eng=k.act)
                    k.cp(r(BPn.t[:, 0:128]), pr["pa"].t[:, 0:128], [pr["pa"].b], [BPn.b], eng=k.dve)
                    k.tt(k.dve, r(BPn.t[:, 128:256]), pr["pa"].t[:, 128:256], BPo.t[:, 128:256], ALU.add, [pr["pa"].b, BPo.b], [BPn.b])
                    k.psF(pr["pa"], pr["pb"])
                    pr["Ao"], pr["BPo"] = An, BPn
            yield

    def neumann_pairs(pairs, nlev):
        for pr in pairs:
            pr["Ao"] = pr["A1_2"]; pr["BPo"] = pr["BP2"][0]
        for lv in range(nlev):
            last = lv == nlev - 1
            yield from k.need((1 if last else 2) * len(pairs))
            for pr in pairs:
                Ao, BPo = pr["Ao"], pr["BPo"]
                pr["pa"] = k.psA()
                if not last:
                    pr["pb"] = k.psA()
                for i in range(2):
                    a_i = r(Ao.t[:, i * 128:(i + 1) * 128])
                    if last:
                        k.mm(pr["pa"].t[:, i * 128:(i + 1) * 128], a_i, r(BPo.t[:, i * 256 + 128:(i + 1) * 256]), True, True,
                             [Ao.b, BPo.b], [pr["pa"].b])
                    else:
                        k.mm(pr["pa"].t[:, i * 256:(i + 1) * 256], a_i, r(BPo.t[:, i * 256:(i + 1) * 256]), True, True,
                             [Ao.b, BPo.b], [pr["pa"].b])
                        k.mm(pr["pb"].t[:, i * 128:(i + 1) * 128], r(BPo.t[:, i * 256:i * 256 + 128]), a_i, True, True,
                             [Ao.b, BPo.b], [pr["pb"].b])
            yield
            for pr in pairs:
                BPo = pr["BPo"]
                bpo3 = BPo.t[:].rearrange("p (i c) -> p i c", c=256)
                if last:
                    k.tt(k.dve, r(pr["Tout2"].t[:].rearrange("p (i c) -> p i c", c=128)),
                         pr["pa"].t[:, 0:256].rearrange("p (i c) -> p i c", c=128), bpo3[:, :, 128:256], ALU.add,
                         [pr["pa"].b, BPo.b], [pr["Tout2"].b])
                    k.psF(pr["pa"])
                else:
                    An, BPn = pr["Ap2"][lv % 2], pr["BP2"][(lv + 1) % 2]
                    bpn3 = BPn.t[:].rearrange("p (i c) -> p i c", c=256)
                    pa3 = pr["pa"].t[:, :].rearrange("p (i c) -> p i c", c=256)
                    k.cp(r(An.t[:]), pr["pb"].t[:, 0:256], [pr["pb"].b], [An.b], eng=k.act)
                    k.cp(r(bpn3[:, :, 0:128]), pa3[:, :, 0:128], [pr["pa"].b], [BPn.b], eng=k.act)
                    k.tt(k.dve, r(bpn3[:, :, 128:256]), pa3[:, :, 128:256], bpo3[:, :, 128:256], ALU.add, [pr["pa"].b, BPo.b], [BPn.b])
                    k.psF(pr["pa"], pr["pb"])
                    pr["Ao"], pr["BPo"] = An, BPn
            yield

    def load_rows(dst, r0, nrows=128):
        k.dma(k.sp, dst.t[0:nrows, :], pT.t[r0:r0 + nrows, :], pT.bl[r0 // 128:(r0 + nrows - 1) // 128 + 1], [dst.b], dst.b)

    ones_bf = sm("ones_bf", [128, 128], BF16); blk_bf = sm("blk_bf", [128, 128], BF16)
    k.cp(ones_bf.t[:], ones.t[:], [ones.b], [ones_bf.b], eng=k.dve)
    k.cp(blk_bf.t[:], blk.t[:], [blk.b], [blk_bf.b], eng=k.dve)

    def bfv(t_):
        return t_.t[:].bitcast(BF16)[:, 0:S]

    def psum_bcast_sum(src, lhsT, lhsTb, fn):
        sv = bfv(src)
        for tb in range(4):
            p = k.ps()
            k.mm(p.t[:, :], lhsT, sv[:, tb * 512:(tb + 1) * 512], True, True, [lhsTb, src.b], [p.b])
            fn(tb, p)

    obf = [sm("obf0", [128, S], BF16)] * 2
    octr = [0]

    gc16 = balloc()
    st_dn = contextlib.ExitStack()

    def smd(name, shape, dt=F32):
        return k.sb("m_" + name, shape, dt, st_dn)
    gcT = smd("gcT", [128, 256]); betaT = smd("betaT", [128, 256]); kdT = smd("kdT", [128, 256]); egT = smd("egT", [128, 256])
    bgT = smd("bgT", [128, 16, 8]); negA = smd("negA", [16, 1])
    if True:
        ab = balloc(); t1 = balloc(); t2 = balloc(); beta16 = balloc(); kd16 = balloc()
        R16 = slice(0, 16)
        load_rows(ab, 4096, 16)
        dtb = dnsc.t[:, 1:2]
        k.actf(t1.t[R16, :], ab.t[R16, :], AF.Abs, [ab.b, dnsc.b], [t1.b], bias=dtb)
        k.actf(t1.t[R16, :], t1.t[R16, :], AF.Exp, [t1.b], [t1.b], scale=-1.0)
        k.actf(t1.t[R16, :], t1.t[R16, :], AF.Ln, [t1.b, ones.b], [t1.b], bias=ones.t[0:16, 0:1])
        k.ts(k.dve, t2.t[R16, :], ab.t[R16, :], dtb, 0.0, ALU.add, ALU.max, [ab.b, dnsc.b], [t2.b])
        k.tt(k.dve, t1.t[R16, :], t1.t[R16, :], t2.t[R16, :], ALU.add, [t1.b, t2.b], [t1.b])
        k.actf(negA.t[:], dnsc.t[:, 0:1], AF.Exp, [dnsc.b], [negA.b])
        k.ts(k.dve, negA.t[:], negA.t[:], -1.0, None, ALU.mult, None, [negA.b], [negA.b])
        k.ts(k.dve, t1.t[R16, :], t1.t[R16, :], negA.t[:, 0:1], None, ALU.mult, None, [t1.b, negA.b], [t1.b])
        k.actf(beta16.t[R16, :], ab.t[R16, :], AF.Sigmoid, [ab.b], [beta16.b])
        k.op(k.dve, lambda e: e.tensor_tensor_scan(gc16.t[R16, :], rmask.t[R16, :], t1.t[R16, :], 0.0, ALU.mult, ALU.add),
             [rmask.b, t1.b], [gc16.b])
        for n in range(NT):
            cs = slice(n * 128, (n + 1) * 128)
            k.ts(k.dve, kd16.t[R16, cs], gc16.t[R16, cs], gc16.t[R16, n * 128 + 127:n * 128 + 128], None, ALU.subtract, None,
                 [gc16.b], [kd16.b])
        k.actf(kd16.t[R16, :], kd16.t[R16, :], AF.Exp, [kd16.b], [kd16.b], scale=-1.0)
        for src, dst in ((gc16, gcT), (beta16, betaT), (kd16, kdT)):
            p = k.ps()
            for n in range(NT):
                k.mm(p.t[:, n * 16:(n + 1) * 16], src.t[R16, n * 128:(n + 1) * 128], ident.t[0:16, 0:16], True, True, [src.b, ident.b], [p.b])
            k.cp(dst.t[:], p.t[:, 0:256], [p.b], [dst.b])
        k.actf(egT.t[:], gcT.t[:], AF.Exp, [gcT.b], [egT.b])
        k.tt(k.dve, bgT.t[:], betaT.t[:].rearrange("p (n r) -> p n r", r=16)[:, :, 8:16],
             egT.t[:].rearrange("p (n r) -> p n r", r=16)[:, :, 0:8], ALU.mult, [betaT.b, egT.b], [bgT.b])
        bfree(ab, t1, t2, beta16, kd16)
    ngcT = smd("ngcT", [128, 256])
    k.ts(k.dve, ngcT.t[:], gcT.t[:], -1.0, None, ALU.mult, None, [gcT.b], [ngcT.b])
    ngcT3 = ngcT.t[:].rearrange("p (n r) -> p n r", r=16)
    gcT3 = gcT.t[:].rearrange("p (n r) -> p n r", r=16)
    betaT3 = betaT.t[:].rearrange("p (n r) -> p n r", r=16)
    kdT3 = kdT.t[:].rearrange("p (n r) -> p n r", r=16)

    WDN = 6

    def sqd(name, w=128):
        return k.sb("m_" + name, [128, w], F32, st_dn)
    St = [sqd("St0"), sqd("St1")]
    qTr = sqd("qTr", S); kTr = sqd("kTr", S); qgr = sqd("qgr", S)
    dnw_t = []
    WDP = 4
    for w_ in range(WDP):
        d_ = {nm: sqd(f"{nm}{w_}", 256) for nm in ("t1_2", "El_2", "MA_2", "MD_2", "at_2", "attnT_2", "nwT_2", "A1_2", "Tout2")}
        d_["Ap2"] = [sqd(f"Ap20_{w_}", 256), sqd(f"Ap21_{w_}", 256)]; d_["BP2"] = [sqd(f"BP20_{w_}", 512), sqd(f"BP21_{w_}", 512)]
        d_["c"] = [{nm: sqd(f"{nm}{w_}_{i_}") for nm in ("kbg", "kd", "vb", "vnew")} for i_ in range(2)]
        dnw_t.append(d_)

    def conv_silu(xr, gi):
        c = balloc()
        w = lambda j: convw.t[:, gi * 4 + j:gi * 4 + j + 1]
        k.ts(k.dve, c.t[:], xr.t[:], w(3), None, ALU.mult, None, [xr.b, convw.b], [c.b])
        for sh in (1, 2, 3):
            k.stt(c.t[:, sh:S], xr.t[:, 0:S - sh], w(3 - sh), c.t[:, sh:S], ALU.mult, ALU.add, [xr.b, convw.b, c.b], [c.b])
        k.actf(c.t[:], c.t[:], AF.Silu, [c.b], [c.b])
        bfree(xr)
        return c

    def l2n(xc, scale, dst):
        sq_ = balloc(); rn = balloc()
        k.actf(bfv(sq_), xc.t[:], AF.Square, [xc.b], [sq_.b])

        def fn(tb, p):
            ts_ = slice(tb * 512, (tb + 1) * 512)
            k.actf(rn.t[:, ts_], p.t[:, :], AF.Ln, [p.b, epsG.b], [rn.b], bias=epsG.t[:, 1:2])
        psum_bcast_sum(sq_, ones_bf.t[:], ones_bf.b, fn)
        k.actf(rn.t[:], rn.t[:], AF.Exp, [rn.b], [rn.b], scale=-0.5)
        k.stt(r(dst.t[:]), xc.t[:], scale, rn.t[:], ALU.mult, ALU.mult, [xc.b, rn.b], [dst.b])
        bfree(sq_, rn, xc)

    pre_ld = None
    for h in range(DBG_DN):
        if pre_ld is None:
            qr = balloc(); load_rows(qr, h * 128)
            kr = balloc(); load_rows(kr, 1024 + h * 128)
            vr = balloc(); load_rows(vr, 2048 + h * 128)
        else:
            qr, kr, vr = pre_ld
            pre_ld = None
        if DBG_STEP == 0:
            st.close(); return
        qT = conv_silu(qr, h); kT = conv_silu(kr, 8 + h); vT = conv_silu(vr, 16 + h)
        if DBG_STEP == 1:
            st.close(); return
        l2n(qT, 128 ** -0.5, qTr); l2n(kT, 1.0, kTr)
        qT, kT = qTr, kTr
        if DBG_STEP == 2:
            st.close(); return
        gcb = balloc(); egcb = balloc()

        def fn(tb, p):
            ts_ = slice(tb * 512, (tb + 1) * 512)
            k.cp(gcb.t[:, ts_], p.t[:, :], [p.b], [gcb.b], eng=k.dve)
            k.actf(egcb.t[:, ts_], p.t[:, :], AF.Exp, [p.b], [egcb.b])
        k.ts(k.dve, selh.t[:], ones.t[0:16, :], ident.t[0:16, h:h + 1], None, ALU.mult, None, [ones.b, ident.b], [selh.b])
        for tb in range(4):
            p = k.ps()
            k.mm(p.t[:, :], selh.t[:], gc16.t[0:16, tb * 512:(tb + 1) * 512], True, True, [selh.b, gc16.b], [p.b])
            fn(tb, p)
        qg = qgr
        k.tt(k.dve, r(qg.t[:]), qT.t[:], egcb.t[:], ALU.mult, [qT.b, egcb.b], [qg.b])
        oT = balloc()
        if DBG_STEP == 3:
            st.close(); return
        k.ts(k.dve, r(St[0].t[:]), ident.t[:], 0.0, None, ALU.mult, None, [ident.b], [St[0].b])
        seq_done = [0]

        def dn_worker(w_, h=h, qT=qT, kT=kT, vT=vT, gcb=gcb, egcb=egcb, qg=qg, oT=oT, seq_done=seq_done):
            d_ = dnw_t[w_]
            C = d_["c"]
            H2 = [slice(0, 128), slice(128, 256)]
            for n0 in range(2 * w_, DBG_CH, 2 * WDP):
                ns = [n0, n0 + 1]
                css = [slice(n * 128, (n + 1) * 128) for n in ns]
                yield from k.need(2)
                pkt = k.psA(); pvt = k.psA()
                for i, n in enumerate(ns):
                    k.tr(pkt.t[:, H2[i]], kT.t[:, css[i]], ident.t[:], [kT.b, ident.b], [pkt.b])
                    k.tr(pvt.t[:, H2[i]], vT.t[:, css[i]], ident.t[:], [vT.b, ident.b], [pvt.b])
                    k.actf(d_["t1_2"].t[:, H2[i]], gcb.t[:, css[i]], AF.Relu, [gcb.b, ngcT.b], [d_["t1_2"].b], bias=ngcT3[:, n, h:h + 1])
                yield
                for i, n in enumerate(ns):
                    k.actf(r(C[i]["kbg"].t[:]), pkt.t[:, H2[i]], AF.Copy, [pkt.b, bgT.b], [C[i]["kbg"].b], scale=bgT.t[:, n, h:h + 1])
                    k.actf(r(C[i]["kd"].t[:]), pkt.t[:, H2[i]], AF.Copy, [pkt.b, kdT.b], [C[i]["kd"].b], scale=kdT3[:, n, h:h + 1])
                    k.ts(k.dve, r(C[i]["vb"].t[:]), pvt.t[:, H2[i]], betaT3[:, n, 8 + h:9 + h], None, ALU.mult, None, [pvt.b, betaT.b], [C[i]["vb"].b])
                k.psF(pkt, pvt)
                k.actf(d_["El_2"].t[:], d_["t1_2"].t[:], AF.Exp, [d_["t1_2"].b], [d_["El_2"].b], scale=-1.0)
                yield
                yield from k.need(2)
                pk = k.psA(); pq = k.psA()
                for i, n in enumerate(ns):
                    k.mm(pk.t[:, H2[i]], r(kT.t[:, css[i]]), r(kT.t[:, css[i]]), True, True, [kT.b], [pk.b])
                    k.mm(pq.t[:, H2[i]], r(qT.t[:, css[i]]), r(kT.t[:, css[i]]), True, True, [qT.b, kT.b], [pq.b])
                    k.stt(d_["MA_2"].t[:, H2[i]], d_["El_2"].t[:, H2[i]], betaT3[:, n, 8 + h:9 + h], Ls.t[:], ALU.mult, ALU.mult,
                          [d_["El_2"].b, betaT.b, Ls.b], [d_["MA_2"].b])
                k.tt(k.pool, d_["MD_2"].t[:], d_["El_2"].t[:], Li2.t[:], ALU.mult, [d_["El_2"].b, Li2.b], [d_["MD_2"].b])
                yield
                k.tt(k.dve, r(d_["A1_2"].t[:]), pk.t[:, 0:256], d_["MA_2"].t[:], ALU.mult, [pk.b, d_["MA_2"].b], [d_["A1_2"].b])
                k.tt(k.dve, d_["at_2"].t[:], pq.t[:, 0:256], d_["MD_2"].t[:], ALU.mult, [pq.b, d_["MD_2"].b], [d_["at_2"].b])
                k.psF(pk, pq)
                yield
                yield from k.need(2)
                pa = k.psA(); pb = k.psA()
                for i in range(2):
                    k.tr(pb.t[:, H2[i]], d_["A1_2"].t[:, H2[i]], ident.t[:], [d_["A1_2"].b, ident.b], [pb.b])
                    k.tr(pa.t[:, H2[i]], d_["at_2"].t[:, H2[i]], ident.t[:], [d_["at_2"].b, ident.b], [pa.b])
                yield
                bp3 = d_["BP2"][0].t[:].rearrange("p (i c) -> p i c", c=256)
                k.cp(r(bp3[:, :, 0:128]), pb.t[:, 0:256].rearrange("p (i c) -> p i c", c=128), [pb.b], [d_["BP2"][0].b], eng=k.act)
                k.cp(r(bp3[:, :, 128:256]), II2.t[:].rearrange("p (i c) -> p i c", c=128), [II2.b], [d_["BP2"][0].b], eng=k.pool)
                k.cp(r(d_["attnT_2"].t[:]), pa.t[:, 0:256], [pa.b], [d_["attnT_2"].b], eng=k.act)
                k.psF(pa, pb)
                yield
                yield from neumann_pairs([d_], 7)
                yield from k.need(1)
                pw = k.psA()
                for i in range(2):
                    k.mm(pw.t[:, H2[i]], r(C[i]["kbg"].t[:]), r(d_["Tout2"].t[:, H2[i]]), True, True, [C[i]["kbg"].b, d_["Tout2"].b], [pw.b])
                yield
                k.actf(r(d_["nwT_2"].t[:]), pw.t[:, 0:256], AF.Copy, [pw.b], [d_["nwT_2"].b], scale=-1.0)
                k.psF(pw)
                yield
                for i, n in enumerate(ns):
                    while seq_done[0] < n:
                        yield
                    cs = css[i]
                    So, Sn = St[n % 2], St[(n + 1) % 2]
                    yield from k.need(1)
                    pv = k.psA()
                    k.mm(pv.t[:, 0:128], r(d_["Tout2"].t[:, H2[i]]), r(C[i]["vb"].t[:]), True, False, [d_["Tout2"].b, C[i]["vb"].b], [pv.b])
                    k.mm(pv.t[:, 0:128], r(d_["nwT_2"].t[:, H2[i]]), r(So.t[:]), False, True, [d_["nwT_2"].b, So.b], [pv.b])
                    vn = C[i]["vnew"]
                    k.cp(r(vn.t[:]), pv.t[:, 0:128], [pv.b], [vn.b], eng=k.act)
                    k.psF(pv)
                    yield from k.need(2)
                    po = k.psA(); pS = k.psA()
                    k.mm(pS.t[:, 0:128], r(C[i]["kd"].t[:]), r(vn.t[:]), True, True, [C[i]["kd"].b, vn.b], [pS.b])
                    k.mm(po.t[:, 0:128], r(So.t[:]), r(qg.t[:, cs]), True, False, [So.b, qg.b], [po.b])
                    k.mm(po.t[:, 0:128], r(vn.t[:]), r(d_["attnT_2"].t[:, H2[i]]), False, True, [vn.b, d_["attnT_2"].b], [po.b])
                    k.stt(r(Sn.t[:]), So.t[:], egcb.t[:, n * 128 + 127:n * 128 + 128], pS.t[:, 0:128], ALU.mult, ALU.add,
                          [So.b, egcb.b, pS.b], [Sn.b])
                    k.cp(oT.t[:, cs], po.t[:, 0:128], [po.b], [oT.b], eng=k.act)
                    k.psF(po, pS)
                    seq_done[0] = n + 1
                    yield
        zr = balloc(); load_rows(zr, 3072 + h * 128)
        bg_next(3)
        run_rr([dn_worker(w_) for w_ in range(WDP)])
        if DBG_STEP == 14:
            st.close(); return
        bfree(vT, gcb, egcb)
        if h + 1 < DBG_DN:
            nq = balloc(); load_rows(nq, (h + 1) * 128)
            nk = balloc(); load_rows(nk, 1024 + (h + 1) * 128)
            nv = balloc(); load_rows(nv, 2048 + (h + 1) * 128)
            pre_ld = (nq, nk, nv)
        sq_ = balloc(); rn = balloc()
        k.actf(bfv(sq_), oT.t[:], AF.Square, [oT.b], [sq_.b])

        def fn2(tb, p):
            ts_ = slice(tb * 512, (tb + 1) * 512)
            k.actf(rn.t[:, ts_], p.t[:, :], AF.Ln, [p.b, epsG.b], [rn.b], bias=epsG.t[:, 1:2], scale=1.0 / 128)
        psum_bcast_sum(sq_, ones_bf.t[:], ones_bf.b, fn2)
        k.actf(rn.t[:], rn.t[:], AF.Exp, [rn.b], [rn.b], scale=-0.5)
        k.actf(zr.t[:], zr.t[:], AF.Silu, [zr.b], [zr.b])
        k.stt(oT.t[:], oT.t[:], dnw.t[:, 0:1], rn.t[:], ALU.mult, ALU.mult, [oT.b, dnw.b, rn.b], [oT.b])
        ob = obf[octr[0] % 2]; octr[0] += 1
        k.tt(k.dve, ob.t[:], oT.t[:], zr.t[:], ALU.mult, [oT.b, zr.b], [ob.b])
        k.dma(k.sp, oTd.t[h * 128:(h + 1) * 128, :], ob.t[:], [ob.b], [oTd.bl[h]], ob.b)
        bfree(zr, sq_, rn, oT)
    bfree(gc16)
    st_dn.close()
    k.barrier()

    wa = balloc(); sg = balloc()
    RW0 = DNC

    def lerp(xr, gi):
        t_ = balloc()
        k.op(k.pool, lambda e: e.memset(t_.t[:, 0:1], 0.0), [], [t_.b])
        k.ts(k.dve, t_.t[:, 1:S], xr.t[:, 0:S - 1], mu.t[:, gi:gi + 1], None, ALU.mult, None, [xr.b, mu.b], [t_.b])
        k.stt(xr.t[:], xr.t[:], omm.t[:, gi:gi + 1], t_.t[:], ALU.mult, ALU.add, [xr.b, omm.b, t_.b], [xr.b])
        bfree(t_)
    load_rows(wa, RW0 + 3072); lerp(wa, 24)
    load_rows(sg, RW0 + 3200); lerp(sg, 25)
    k.actf(wa.t[0:64, :], wa.t[0:64, :], AF.Tanh, [wa.b], [wa.b])
    k.actf(sg.t[:], sg.t[:], AF.Sigmoid, [sg.b], [sg.b])

    WRW = 4
    st_rw = contextlib.ExitStack()
    lora = k.sb("m_lora", [128, 1024], F32, st_rw); g2 = k.sb("m_g2", [128, 1024], F32, st_rw)
    for t_, d_ in ((lora, lora_d), (g2, g2_d)):
        k.dma(k.sp, t_.t[:], d_, [], [t_.b], t_.b)

    def sqr(name, w=128):
        return k.sb("m_" + name, [128, w], F32, st_rw)
    Ht = [sqr("Ht0", 128), sqr("Ht1", 128)]
    rww_t = []
    for w_ in range(WRW):
        d_ = {nm: sqr(f"r{nm}{w_}") for nm in ("e1", "e2", "e3", "e4", "Bt", "Kt", "bh", "kh", "rhs1", "AVc", "KVc", "YVc", "Gt")}
        for nm in ("BhP", "KhP", "VP", "UP"):
            t_ = sqr(f"r{nm}2_{w_}", 256)
            d_[nm + "2"] = t_
            k.ts(k.dve, r(t_.t[:]), II2.t[:], 0.0, None, ALU.mult, None, [II2.b], [t_.b])
            d_[nm] = [TV(t_.t[:, 0:128], t_.b), TV(t_.t[:, 64:192], t_.b)]
        d_["ar"] = sqr(f"rar{w_}", 256)
        d_["hd"] = []
        for hh in range(2):
            e_ = {}
            e_["mb"] = sqr(f"rmb{w_}_{hh}", 256); e_["mk"] = sqr(f"rmk{w_}_{hh}", 256)
            d_["hd"].append(e_)
        d_["A1_2"] = sqr(f"rA12_{w_}", 256); d_["Tout2"] = sqr(f"rTr2_{w_}", 256)
        d_["Ap2"] = [sqr(f"rAp20_{w_}", 256), sqr(f"rAp21_{w_}", 256)]
        d_["BP2"] = [sqr(f"rBP20_{w_}", 512), sqr(f"rBP21_{w_}", 512)]
        rww_t.append(d_)
    V = lambda j: rwv.t[:, j * 8:(j + 1) * 8]

    for g in range(DBG_RW):
        rT = balloc(); load_rows(rT, RW0 + g * 128); lerp(rT, g)
        kl = balloc(); load_rows(kl, RW0 + 1024 + g * 128); lerp(kl, 8 + g)
        vT = balloc(); load_rows(vT, RW0 + 2048 + g * 128); lerp(vT, 16 + g)
        sig = balloc(); a_ = balloc()
        gsl = slice(g * 128, (g + 1) * 128)
        for tb in range(4):
            ts_ = slice(tb * 512, (tb + 1) * 512)
            p = k.ps()
            k.mm(p.t[:, :], lora.t[0:64, gsl], wa.t[0:64, ts_], True, True, [lora.b, wa.b], [p.b])
            k.actf(sig.t[:, ts_], p.t[:, :], AF.Sigmoid, [p.b, rwv.b], [sig.b], bias=V(0)[:, g:g + 1])
            p = k.ps()
            k.mm(p.t[:, :], lora.t[64:128, gsl], wa.t[64:128, ts_], True, True, [lora.b, wa.b], [p.b])
            k.actf(a_.t[:, ts_], p.t[:, :], AF.Sigmoid, [p.b, rwv.b], [a_.b], bias=V(1)[:, g:g + 1])
        kk = balloc(); sq_ = balloc(); rn = balloc()
        k.ts(k.dve, kk.t[:], kl.t[:], V(2)[:, g:g + 1], None, ALU.mult, None, [kl.b, rwv.b], [kk.b])
        k.actf(bfv(sq_), kk.t[:], AF.Square, [kk.b], [sq_.b])

        def fnk(tb, p):
            ts_ = slice(tb * 512, (tb + 1) * 512)
            k.ts(k.dve, rn.t[:, ts_], p.t[:, :], 1e-24, None, ALU.max, None, [p.b], [rn.b])
        psum_bcast_sum(sq_, blk_bf.t[:], blk_bf.b, fnk)
        k.actf(rn.t[:], rn.t[:], AF.Ln, [rn.b], [rn.b])
        k.actf(rn.t[:], rn.t[:], AF.Exp, [rn.b], [rn.b], scale=-0.5)
        k.tt(k.dve, kk.t[:], kk.t[:], rn.t[:], ALU.mult, [kk.b, rn.b], [kk.b])
        bfree(sq_, rn)
        kf = balloc()
        k.ts(k.dve, kf.t[:], a_.t[:], -1.0, V(3)[:, g:g + 1], ALU.add, ALU.mult, [a_.b, rwv.b], [kf.b])
        k.stt(kf.t[:], kf.t[:], 1.0, kl.t[:], ALU.add, ALU.mult, [kf.b, kl.b], [kf.b])
        bT = balloc()
        k.tt(k.dve, bT.t[:], a_.t[:], kk.t[:], ALU.mult, [a_.b, kk.b], [bT.b])
        bfree(kl, a_)
        cum = balloc()
        k.op(k.dve, lambda e: e.tensor_tensor_scan(cum.t[:], rmask.t[:], sig.t[:], 0.0, ALU.mult, ALU.add), [rmask.b, sig.b], [cum.b])
        yT = balloc()
        k.ts(k.dve, r(Ht[0].t[:]), ident.t[:], 0.0, None, ALU.mult, None, [ident.b], [Ht[0].b])
        seq_done = [0]

        def rw_worker(w_, rT=rT, vT=vT, kk=kk, kf=kf, bT=bT, sig=sig, cum=cum, yT=yT, seq_done=seq_done):
            d_ = rww_t[w_]
            ar = d_["ar"]; e1 = d_["e1"]; e2 = d_["e2"]; e3 = d_["e3"]; e4 = d_["e4"]
            Bt_, Kt_, bh, kh = d_["Bt"], d_["Kt"], d_["bh"], d_["kh"]
            BhP, KhP, VP, UP, rhs1 = d_["BhP"], d_["KhP"], d_["VP"], d_["UP"], d_["rhs1"]
            HD = d_["hd"]
            RS = [slice(0, 64), slice(64, 128)]
            for n in range(w_, DBG_CH, WRW):
                cs = slice(n * 128, (n + 1) * 128)
                k.actf(e1.t[:], cum.t[:, cs], AF.Exp, [cum.b], [e1.b], scale=CDEC)
                k.actf(e2.t[:], cum.t[:, cs], AF.Exp, [cum.b], [e2.b], scale=-CDEC)
                k.tt(k.pool, e3.t[:], cum.t[:, cs], sig.t[:, cs], ALU.subtract, [cum.b, sig.b], [e3.b])
                k.ts(k.dve, e4.t[:], cum.t[:, cs], cum.t[:, n * 128 + 127:n * 128 + 128], None, ALU.subtract, None, [cum.b], [e4.b])
                yield
                k.actf(e3.t[:], e3.t[:], AF.Exp, [e3.b], [e3.b], scale=CDEC)
                k.actf(e4.t[:], e4.t[:], AF.Exp, [e4.b], [e4.b], scale=-CDEC)
                k.tt(k.pool, r(ar.t[:, 128:256]), rT.t[:, cs], e1.t[:], ALU.mult, [rT.b, e1.b], [ar.b])
                k.tt(k.pool, r(Bt_.t[:]), bT.t[:, cs], e2.t[:], ALU.mult, [bT.b, e2.b], [Bt_.b])
                k.tt(k.pool, r(Kt_.t[:]), kf.t[:, cs], e2.t[:], ALU.mult, [kf.b, e2.b], [Kt_.b])
                yield
                k.tt(k.dve, r(ar.t[:, 0:128]), kk.t[:, cs], e3.t[:], ALU.mult, [kk.b, e3.b], [ar.b])
                k.tt(k.pool, bh.t[:], bT.t[:, cs], e4.t[:], ALU.mult, [bT.b, e4.b], [bh.b])
                k.tt(k.pool, kh.t[:], kf.t[:, cs], e4.t[:], ALU.mult, [kf.b, e4.b], [kh.b])
                yield
                yield from k.need(3)
                trs = []
                for src, srcB, dst in ((bh.t[:], bh.b, d_["BhP2"]), (kh.t[:], kh.b, d_["KhP2"]), (vT.t[:, cs], vT.b, d_["VP2"])):
                    p = k.psA()
                    k.tr(p.t[:, 0:128], src, ident.t[:], [srcB, ident.b], [p.b])
                    trs.append((p, dst))
                yield
                for ti_, (p, dst2) in enumerate(trs):
                    k.cp(r(dst2.t[:].rearrange("p (i c) -> p i c", c=128)[:, :, 0:64]), p.t[:, 0:128].rearrange("p (i c) -> p i c", c=64),
                         [p.b], [dst2.b], eng=k.act)
                    k.psF(p)
                yield from k.need(4)
                pms = []
                for hh in range(2):
                    R = RS[hh]
                    pm = k.psA(); pm2 = k.psA()
                    k.mm(pm.t[:, 0:256], r(Bt_.t[R, :]), r(ar.t[R, :]), True, True, [Bt_.b, ar.b], [pm.b])
                    k.mm(pm2.t[:, 0:256], r(Kt_.t[R, :]), r(ar.t[R, :]), True, True, [Kt_.b, ar.b], [pm2.b])
                    pms.append((pm, pm2))
                yield
                for hh in range(2):
                    pm, pm2 = pms[hh]
                    k.tt(k.dve, r(HD[hh]["mb"].t[:]), pm.t[:, 0:256], UU.t[:], ALU.mult, [pm.b, UU.b], [HD[hh]["mb"].b])
                    k.tt(k.dve, r(HD[hh]["mk"].t[:]), pm2.t[:, 0:256], UU.t[:], ALU.mult, [pm2.b, UU.b], [HD[hh]["mk"].b])
                    k.psF(pm, pm2)
                yield from k.need(2)
                pas = []
                for hh in range(2):
                    R = RS[hh]
                    pa = k.psA()
                    k.mm(pa.t[:, 0:128], r(ar.t[R, 0:128]), r(Bt_.t[R, :]), True, True, [ar.b, Bt_.b], [pa.b])
                    pas.append(pa)
                yield
                for hh in range(2):
                    e_ = HD[hh]
                    k.tt(k.dve, r(d_["A1_2"].t[:, hh * 128:(hh + 1) * 128]), pas[hh].t[:, 0:128], Ls.t[:], ALU.mult,
                         [pas[hh].b, Ls.b], [d_["A1_2"].b])
                    k.actf(r(d_["BP2"][0].t[:, hh * 256:hh * 256 + 128]), e_["mb"].t[:, 0:128], AF.Copy, [e_["mb"].b], [d_["BP2"][0].b], scale=-1.0)
                    k.psF(pas[hh])
                yield
                k.cp(r(d_["BP2"][0].t[:].rearrange("p (i c) -> p i c", c=256)[:, :, 128:256]), II2.t[:].rearrange("p (i c) -> p i c", c=128),
                     [II2.b], [d_["BP2"][0].b], eng=k.pool)
                yield from k.need(3)
                pAV = k.psA(); pKV = k.psA(); pYV = k.psA()
                for hh in range(2):
                    e_ = HD[hh]
                    k.mm(pAV.t[:, 0:128], r(e_["mk"].t[:, 0:128]), r(VP[hh].t[:]), hh == 0, hh == 1, [e_["mk"].b, VP[hh].b], [pAV.b])
                    k.mm(pKV.t[:, RS[hh]], r(KhP[hh].t[:]), r(VP[hh].t[:, RS[hh]]), True, True, [KhP[hh].b, VP[hh].b], [pKV.b])
                    k.mm(pYV.t[:, 0:128], r(VP[hh].t[:]), r(e_["mk"].t[:, 128:256]), hh == 0, hh == 1, [VP[hh].b, e_["mk"].b], [pYV.b])
                yield
                k.cp(d_["AVc"].t[:], pAV.t[:, 0:128], [pAV.b], [d_["AVc"].b], eng=k.act)
                k.cp(d_["KVc"].t[:], pKV.t[:, 0:128], [pKV.b], [d_["KVc"].b], eng=k.act)
                k.cp(d_["YVc"].t[:], pYV.t[:, 0:128], [pYV.b], [d_["YVc"].b], eng=k.act)
                k.psF(pAV, pKV, pYV)
                yield
                yield from neumann_pairs([d_], 7)
                while seq_done[0] < n:
                    yield
                Ho, Hn = Ht[n % 2], Ht[(n + 1) % 2]
                yield from k.need(1)
                pr_ = k.psA()
                k.mm(pr_.t[:, 0:128], r(ar.t[:, 0:128]), r(Ho.t[:]), True, True, [ar.b, Ho.b], [pr_.b])
                k.stt(d_["Gt"].t[:], Ho.t[:], e1.t[:, 127:128], d_["KVc"].t[:], ALU.mult, ALU.add, [Ho.b, e1.b, d_["KVc"].b], [d_["Gt"].b])
                k.stt(r(rhs1.t[:]), pr_.t[:, 0:128], -1.0, d_["AVc"].t[:], ALU.mult, ALU.subtract, [pr_.b, d_["AVc"].b], [rhs1.b])
                k.psF(pr_)
                yield from k.need(1)
                pu = k.psA()
                for hh in range(2):
                    k.mm(pu.t[:, RS[hh]], r(d_["Tout2"].t[:, hh * 128:(hh + 1) * 128]), r(rhs1.t[:, RS[hh]]), True, True,
                         [d_["Tout2"].b, rhs1.b], [pu.b])
                k.cp(r(d_["UP2"].t[:].rearrange("p (i c) -> p i c", c=128)[:, :, 0:64]), pu.t[:, 0:128].rearrange("p (i c) -> p i c", c=64),
                     [pu.b], [d_["UP2"].b], eng=k.act)
                k.psF(pu)
                yield from k.need(2)
                pY = k.psA(); pS = k.psA()
                for hh in range(2):
                    k.mm(pS.t[:, RS[hh]], r(BhP[hh].t[:]), r(UP[hh].t[:, RS[hh]]), True, True, [BhP[hh].b, UP[hh].b], [pS.b])
                k.mm(pY.t[:, 0:128], r(Ho.t[:]), r(ar.t[:, 128:256]), True, False, [Ho.b, ar.b], [pY.b])
                for hh in range(2):
                    e_ = HD[hh]
                    k.mm(pY.t[:, 0:128], r(UP[hh].t[:]), r(e_["mb"].t[:, 128:256]), False, hh == 1, [UP[hh].b, e_["mb"].b], [pY.b])
                k.tt(k.dve, r(Hn.t[:]), pS.t[:, 0:128], d_["Gt"].t[:], ALU.add, [pS.b, d_["Gt"].b], [Hn.b])
                k.tt(k.dve, yT.t[:, cs], pY.t[:, 0:128], d_["YVc"].t[:], ALU.add, [pY.b, d_["YVc"].b], [yT.b])
                k.psF(pY, pS)
                seq_done[0] = n + 1
                yield
        bg_next(3)
        run_rr([rw_worker(w_) for w_ in range(WRW)])
        bfree(kk, bT, sig, cum)
        rk = balloc(); yc = balloc(); sq_ = balloc(); rs_ = balloc()
        k.stt(bfv(rk), rT.t[:], V(4)[:, g:g + 1], kf.t[:], ALU.mult, ALU.mult, [rT.b, rwv.b, kf.b], [rk.b])
        k.actf(bfv(sq_), yT.t[:], AF.Copy, [yT.b], [sq_.b])

        def fnm(tb, p):
            ts_ = slice(tb * 512, (tb + 1) * 512)
            k.stt(yc.t[:, ts_], p.t[:, :], -1.0 / 64, yT.t[:, ts_], ALU.mult, ALU.add, [p.b, yT.b], [yc.b])
        psum_bcast_sum(sq_, blk_bf.t[:], blk_bf.b, fnm)
        k.actf(bfv(sq_), yc.t[:], AF.Square, [yc.b], [sq_.b])

        def fnv(tb, p):
            ts_ = slice(tb * 512, (tb + 1) * 512)
            k.actf(rs_.t[:, ts_], p.t[:, :], AF.Ln, [p.b, epsG.b], [rs_.b], bias=epsG.t[:, 0:1], scale=1.0 / 64)
        psum_bcast_sum(sq_, blk_bf.t[:], blk_bf.b, fnv)
        k.actf(rs_.t[:], rs_.t[:], AF.Exp, [rs_.b], [rs_.b], scale=-0.5)
        k.tt(k.dve, yc.t[:], yc.t[:], rs_.t[:], ALU.mult, [yc.b, rs_.b], [yc.b])
        k.ts(k.dve, yc.t[:], yc.t[:], V(5)[:, g:g + 1], V(6)[:, g:g + 1], ALU.mult, ALU.add, [yc.b, rwv.b], [yc.b])

        def fnb(tb, p):
            ts_ = slice(tb * 512, (tb + 1) * 512)
            k.tt(k.dve, rs_.t[:, ts_], p.t[:, :], vT.t[:, ts_], ALU.mult, [p.b, vT.b], [rs_.b])
        psum_bcast_sum(rk, blk_bf.t[:], blk_bf.b, fnb)
        k.tt(k.dve, yc.t[:], yc.t[:], rs_.t[:], ALU.add, [yc.b, rs_.b], [yc.b])
        ob = obf[octr[0] % 2]; octr[0] += 1
        for tb in range(4):
            ts_ = slice(tb * 512, (tb + 1) * 512)
            p = k.ps()
            k.mm(p.t[:, :], g2.t[:, gsl], sg.t[:, ts_], True, True, [g2.b, sg.b], [p.b])
            k.tt(k.dve, ob.t[:, ts_], p.t[:, :], yc.t[:, ts_], ALU.mult, [p.b, yc.b], [ob.b])
        k.dma(k.sp, oTd.t[1024 + g * 128:1024 + (g + 1) * 128, :], ob.t[:], [ob.b], [oTd.bl[8 + g]], ob.b)
        bfree(rk, yc, sq_, rs_, rT, kf, vT, yT)
    bfree(wa, sg)
    st_rw.close()
    st.close()


def prep_shared(inp):
    f = lambda a: np.ascontiguousarray(np.asarray(a, dtype=np.float32))
    sh = {}
    for kk_ in ("w_in", "w_out", "xa_wq", "xa_wk", "xa_wv", "xa_wo", "ffn_w1", "ffn_w2"):
        sh[kk_] = f(inp[kk_][0])
    sh["norms"] = f(np.stack([inp["mix_norm_w"][0], inp["xa_norm_w"][0], inp["mem_norm_w"][0], inp["ffn_norm_w"][0],
                              inp["final_norm_w"]], axis=0))
    cw = np.asarray(inp["dn_conv_w"][0])
    sh["convw"] = f(cw.reshape(4, 24, 128).transpose(2, 1, 0).reshape(128, 96))
    dn = np.zeros((16, 2), np.float32)
    dn[0:8, 0] = np.asarray(inp["dn_a_log"][0]); dn[0:8, 1] = np.asarray(inp["dn_dt_bias"][0])
    sh["dnsc"] = dn
    sh["dnw"] = f(np.asarray(inp["dn_norm_w"][0]).reshape(128, 1))
    sh["mu"] = f(np.asarray(inp["rw_mu"][0]).reshape(26, 128).T)
    vs = [np.asarray(inp[n][0]).reshape(8, 128).T for n in ("rw_w0", "rw_a0", "rw_k_k", "rw_k_a", "rw_r_k", "rw_ln_w", "rw_ln_b")]
    sh["rwv"] = f(np.concatenate(vs, axis=1))
    sh["lora"] = f(np.concatenate([np.asarray(inp["rw_w2"][0]), np.asarray(inp["rw_a2"][0])], axis=0))
    sh["g2"] = f(inp["rw_g2"][0])
    return sh


def kernel(**inp):
    sh = prep_shared(inp)
    xs = np.asarray(inp["x"], dtype=np.float32)
    ms = np.asarray(inp["mem"], dtype=np.float32)
    nc = build()
    in_maps = []
    for b in range(8):
        m = dict(sh)
        m["x"] = np.ascontiguousarray(xs[b])
        m["mem"] = np.ascontiguousarray(ms[b])
        in_maps.append(m)
    res = run_bass_kernel_spmd(nc, in_maps, core_ids=list(range(8)))
    return np.stack([np.asarray(r["out"], dtype=np.float32) for r in res.results], axis=0)
```

```python
import contextlib
import math
import numpy as np
import concourse.bass as bass
import concourse.mybir as mybir
from concourse.alu_op_type import AluOpType as ALU
from concourse.bass_utils import run_bass_kernel_spmd

F32 = mybir.dt.float32
BF16 = mybir.dt.bfloat16
F32R = mybir.dt.float32r
AF = mybir.ActivationFunctionType
AX = mybir.AxisListType

D = 2048
S = 2048
NT = 16
MEM = 256
DNC = 4112
INC = 7440
FF = 8192
EPS = 1e-6
CDEC = -math.exp(-0.5)
DBG_DN = 8
DBG_RW = 8
DBG_CH = NT
DBG_STEP = 99


class Sem:
    __slots__ = ("h", "name")

    def __init__(self, h, name):
        self.h = h
        self.name = name


class Buf:
    __slots__ = ("name", "w", "r", "dsem", "dcount", "excl")

    def __init__(self, name):
        self.name = name
        self.excl = False
        self.w = None
        self.r = {}
        self.dsem = None
        self.dcount = 0


class Eng:
    def __init__(self, name, h, sem):
        self.name = name
        self.h = h
        self.sem = sem
        self.count = 0
        self.waited = {}


class T:
    def __init__(self, t, name):
        self.t = t
        self.b = Buf(name)


class TV:
    def __init__(self, ap, b):
        self.t = ap
        self.b = b


class K:
    def __init__(self, nc):
        self.nc = nc
        self.es = contextlib.ExitStack()
        self.pe = self._eng("pe", nc.tensor)
        self.dve = self._eng("dve", nc.vector)
        self.act = self._eng("act", nc.scalar)
        self.pool = self._eng("pool", nc.gpsimd)
        self.sp = self._eng("sp", nc.sync)
        self.ninst = 0
        self._psi = 0
        self.psf = []
        self._ev = 0
        self.slots = []
        self.psfree = list(range(8))

    def new_sem(self, name):
        return Sem(self.es.enter_context(self.nc.semaphore(name)), name)

    def _eng(self, name, h):
        return Eng(name, h, self.new_sem("s_" + name))

    def sb(self, name, shape, dt, stack=None):
        t = (stack or self.es).enter_context(self.nc.sbuf_tensor(name, list(shape), dt))
        return T(t, name)

    def _deps(self, eng, reads, writes, extra=()):
        deps = {}
        for b in reads:
            if b.w is not None:
                s, v = b.w
                if v > deps.get(s, 0):
                    deps[s] = v
            if b.excl:
                for s, v in b.r.items():
                    if s is not eng.sem and v > deps.get(s, 0):
                        deps[s] = v
        for b in writes:
            if b.w is not None and not (eng is self.pe and b.w[0] is self.pe.sem):
                s, v = b.w
                if v > deps.get(s, 0):
                    deps[s] = v
            for s, v in b.r.items():
                if v > deps.get(s, 0):
                    deps[s] = v
        for s, v in extra:
            if v > deps.get(s, 0):
                deps[s] = v
        for s, v in deps.items():
            if eng.waited.get(s, 0) < v:
                eng.h.wait_ge(s.h, v)
                eng.waited[s] = v

    def op(self, eng, fn, reads=(), writes=()):
        self._deps(eng, reads, writes)
        inst = fn(eng.h)
        eng.count += 1
        inst.then_inc(eng.sem.h, 1)
        self.ninst += 1
        c = eng.count
        s = eng.sem
        for b in reads:
            b.r[s] = c
        for b in writes:
            b.w = (s, c)
            b.r = {}
        return inst

    def dma(self, q, out, in_, reads, writes, slot, **kw):
        if slot.dsem is None:
            slot.dsem = self.new_sem("d_" + slot.name)
            self.slots.append(slot)
        extra = [(slot.dsem, slot.dcount)] if slot.dcount else []
        self._deps(q, reads, writes, extra)
        inst = q.h.dma_start(out=out, in_=in_, **kw)
        slot.dcount += 16
        inst.then_inc(slot.dsem.h, 16)
        self.ninst += 1
        for b in reads:
            b.r[slot.dsem] = slot.dcount
        for b in writes:
            b.w = (slot.dsem, slot.dcount)
            b.r = {}
        return inst

    def ps(self):
        p = self.psf[self._psi % 8]
        self._psi += 1
        return p

    def psA(self):
        return self.psf[self.psfree.pop(0)]

    def psF(self, *ps_):
        for p in ps_:
            self.psfree.append(self.psf.index(p))

    def need(self, m):
        while len(self.psfree) < m:
            yield

    def barrier(self):
        engs = [self.pe, self.dve, self.act, self.pool, self.sp]
        for e in engs:
            for o in engs:
                if o is not e and o.count and e.waited.get(o.sem, 0) < o.count:
                    e.h.wait_ge(o.sem.h, o.count)
                    e.waited[o.sem] = o.count
            for sl in self.slots:
                if e.waited.get(sl.dsem, 0) < sl.dcount:
                    e.h.wait_ge(sl.dsem.h, sl.dcount)
                    e.waited[sl.dsem] = sl.dcount

    def mm(self, out, lhsT, rhs, start, stop, R, W):
        return self.op(self.pe, lambda e: e.matmul(out, lhsT, rhs, start=start, stop=stop), R, W)

    def tr(self, out, in_, ident, R, W):
        return self.op(self.pe, lambda e: e.transpose(out, in_, ident), R, W)

    def tt(self, eng, out, a, b, op, R, W):
        return self.op(eng, lambda e: e.tensor_tensor(out=out, in0=a, in1=b, op=op), R, W)

    def ts(self, eng, out, a, s1, s2, op0, op1, R, W):
        if op1 is None:
            return self.op(eng, lambda e: e.tensor_scalar(out=out, in0=a, scalar1=s1, scalar2=None, op0=op0), R, W)
        return self.op(eng, lambda e: e.tensor_scalar(out=out, in0=a, scalar1=s1, scalar2=s2, op0=op0, op1=op1), R, W)

    def stt(self, out, a, s, b, op0, op1, R, W):
        return self.op(self.dve, lambda e: e.scalar_tensor_tensor(out=out, in0=a, scalar=s, in1=b, op0=op0, op1=op1), R, W)

    def actf(self, out, in_, func, R, W, **kw):
        return self.op(self.act, lambda e: e.activation(out=out, in_=in_, func=func, **kw), R, W)

    def cp(self, out, in_, R, W, eng=None):
        if eng is None:
            self._ev += 1
            eng = self.act if (self._ev & 1) else self.dve
        if eng is self.act:
            return self.op(eng, lambda e: e.copy(out, in_), R, W)
        return self.op(eng, lambda e: e.tensor_copy(out, in_), R, W)


def build(debug=None):
    nc = bass.Bass("TRN2", target_bir_lowering=False)
    k = K(nc)

    def din(name, shape):
        return nc.dram_tensor(name, list(shape), F32, kind="ExternalInput").ap()

    x = din("x", [S, D]); mem = din("mem", [MEM, D])
    w_in = din("w_in", [D, INC]); w_out = din("w_out", [D, D])
    wq = din("xa_wq", [D, 512]); wk = din("xa_wk", [D, 512]); wv = din("xa_wv", [D, 512]); wo = din("xa_wo", [512, D])
    w1 = din("ffn_w1", [D, FF]); w2 = din("ffn_w2", [FF, D])
    nrm = din("norms", [5, D])
    convw = din("convw", [128, 24 * 4])
    dnsc = din("dnsc", [16, 2])
    dnw = din("dnw", [128, 1])
    mu = din("mu", [128, 26])
    rwv = din("rwv", [128, 7 * 8])
    lora = din("lora", [128, 1024])
    g2 = din("g2", [128, 1024])
    out = nc.dram_tensor("out", [S, D], F32, kind="ExternalOutput").ap()
    pT = T(nc.dram_tensor("pT", [INC, S], F32, kind="Internal").ap(), "pT")
    pT.bl = [Buf(f"pT{i}") for i in range(59)]
    if debug == "p4":
        oTd = T(nc.dram_tensor("oT_in", [D, S], F32, kind="ExternalInput").ap(), "oTd")
        oTd.bl = [Buf(f"oT{i}") for i in range(16)]
        oT_dt = F32
    else:
        oTd = T(nc.dram_tensor("oT", [D, S], BF16, kind="Internal").ap(), "oTd")
        oTd.bl = [Buf(f"oT{i}") for i in range(16)]
        oT_dt = BF16
    wsc = T(nc.dram_tensor("wsc", [38, 128, 8192], BF16, kind="Internal").ap(), "wsc")
    wsc.bl = [Buf(f"wsc{i}") for i in range(38)]
    dbg = None
    if debug == "p1":
        dbg = nc.dram_tensor("dbg", [INC, S], F32, kind="ExternalOutput").ap()
    if debug == "p2":
        dbg = nc.dram_tensor("dbg", [D, S], F32, kind="ExternalOutput").ap()

    for i in range(8):
        p = T(k.es.enter_context(nc.psum_tensor(f"ps{i}", [128, 512], F32)), f"ps{i}")
        p.b.excl = True
        k.psf.append(p)

    ones = k.sb("ones", [128, 128], F32)
    ident = k.sb("ident", [128, 128], F32)
    identb = k.sb("identb", [128, 128], BF16)
    epsT = k.sb("epsT", [128, 1], F32)
    k.op(k.pool, lambda e: e.memset(ones.t[:], 1.0), [], [ones.b])
    k.op(k.pool, lambda e: e.memset(epsT.t[:], EPS), [], [epsT.b])
    k.op(k.pool, lambda e: e.affine_select(out=ident.t[:], in_=ones.t[:], pattern=[[-1, 128]], compare_op=ALU.is_equal,
                                           fill=0.0, base=0, channel_multiplier=1), [ones.b], [ident.b])
    k.op(k.dve, lambda e: e.tensor_copy(identb.t[:], ident.t[:]), [ident.b], [identb.b])

    G = {}

    def alloc_norm(stack, tag):
        G["wbc"] = k.sb("wbc" + tag, [128, D], F32, stack)
        G["uns"] = [k.sb(f"un{i}" + tag, [128, D], BF16, stack) for i in range(2)]
        G["un"] = G["uns"][0]
        G["junk"] = k.sb("junk" + tag, [128, D], BF16, stack)
        G["ss"] = k.sb("ss" + tag, [128, 4], F32, stack)
        G["ssb"] = [Buf(f"ss{i}" + tag) for i in range(4)]
        G["ctr"] = 0

    def load_wbc(i):
        wbc = G["wbc"]
        k.dma(k.sp, wbc.t[:], nrm[i:i + 1, :].to_broadcast([128, D]), [], [wbc.b], wbc.b)
    wbufs = []
    wctr = [0]

    def precast_p4_weights():
        blocks = []
        for cb in range(4):
            blocks.append((cb, w_out[:, cb * 512:(cb + 1) * 512], 16))
        blocks.append((4, wq[:, :], 16))
        blocks.append((5, wo[:, :], 4))
        for fb in range(16):
            blocks.append((6 + fb, w1[:, fb * 512:(fb + 1) * 512], 16))
        for cb in range(4):
            for sub in range(4):
                blocks.append((22 + cb * 4 + sub, w2[sub * 2048:(sub + 1) * 2048, cb * 512:(cb + 1) * 512], 16))
        return blocks

    pc_blocks = precast_p4_weights()

    def bg_next(n):
        for _ in range(n):
            if not pc_blocks:
                return
            idx, src, kc = pc_blocks.pop(0)
            k.dma(k.pool, wsc.t[idx].rearrange("p (kc c) -> p kc c", kc=kc), src.rearrange("(kc p) c -> p kc c", p=128),
                  [], [wsc.bl[idx]], wsc.bl[idx])

    def alloc_wbufs(stack, tag):
        wbufs.clear()
        wbufs.extend(k.sb(f"wbuf{tag}{i}", [128, 8192], BF16, stack) for i in range(2))

    def load_w(src_ap, kc, cache=None):
        wb = wbufs[wctr[0] % len(wbufs)]
        wctr[0] += 1
        ncol = src_ap.shape[1]
        view = wb.t[:, 0:kc * ncol].rearrange("p (kc c) -> p kc c", kc=kc)
        if cache is not None:
            k.dma(k.sp, wb.t[:, :], wsc.t[cache[0]], [wsc.bl[cache[0]]], [wb.b], wb.b)
            return wb, view
        k.dma(k.pool, view, src_ap.rearrange("(kc p) c -> p kc c", p=128), [], [wb.b], wb.b)
        return wb, view

    def norm_transpose(src, srcB, dstT, dstB, dst_cols, slot):
        wbc, ss, junk = G["wbc"], G["ss"], G["junk"]
        un = G["uns"][G["ctr"] % 2]
        G["ctr"] += 1
        sb_ = G["ssb"][slot]
        k.actf(junk.t[:], src, AF.Square, [srcB], [junk.b, sb_], accum_out=ss.t[:, slot:slot + 1])
        k.actf(ss.t[:, slot:slot + 1], ss.t[:, slot:slot + 1], AF.Sqrt, [sb_, epsT.b], [sb_], scale=1.0 / D, bias=epsT.t[:, 0:1])
        k.op(k.dve, lambda e: e.reciprocal(ss.t[:, slot:slot + 1], ss.t[:, slot:slot + 1]), [sb_], [sb_])
        k.stt(un.t[:], src, ss.t[:, slot:slot + 1], wbc.t[:], ALU.mult, ALU.mult, [srcB, sb_, wbc.b], [un.b])
        for half in range(2):
            p = k.ps()
            pv = p.t[:].bitcast(BF16)
            for j in range(8):
                kc = half * 8 + j
                k.tr(pv[:, j * 128:(j + 1) * 128], un.t[:, kc * 128:(kc + 1) * 128], identb.t[:], [un.b, identb.b], [p.b])
            k.cp(dstT[:, half * 8:(half + 1) * 8, dst_cols], pv.rearrange("p (j c) -> p j c", c=128), [p.b], [dstB])

    if debug != "p4":
        with contextlib.ExitStack() as st1:
            alloc_wbufs(st1, "a")
            alloc_norm(st1, "a")
            uT = k.sb("uT", [128, 16, S], BF16, st1)
            uTb = [Buf(f"uT{i}") for i in range(4)]
            xts = [k.sb(f"xt{i}", [128, D], F32, st1) for i in range(2)]
            stg = [k.sb(f"stg{i}", [128, 512], F32, st1) for i in range(4)]
            load_wbc(0)
            si = 0

            def p1_block(c0, wb, wvw, tbs):
                nonlocal si
                ncol = min(512, INC - c0)
                for g0 in range(0, ncol, 128):
                    M = min(128, ncol - g0)
                    for tb in tbs:
                        p = k.ps()
                        for kc in range(16):
                            k.mm(p.t[0:M, :], wvw[:, kc, g0:g0 + M], uT.t[:, kc, tb * 512:(tb + 1) * 512], kc == 0, kc == 15,
                                 [wb.b, uTb[tb]], [p.b])
                        sg_ = stg[si % 4]; si += 1
                        k.cp(sg_.t[0:M, :], p.t[0:M, :], [p.b], [sg_.b])
                        k.dma(k.sp, pT.t[c0 + g0:c0 + g0 + M, tb * 512:(tb + 1) * 512], sg_.t[0:M, :], [sg_.b], [pT.bl[(c0 + g0) // 128]], sg_.b)
            wb0, wvw0 = load_w(w_in[:, 0:512], 16)
            for tb in range(4):
                for n in range(4 * tb, 4 * tb + 4):
                    xt = xts[n % 2]
                    k.dma(k.sp, xt.t[:], x[n * 128:(n + 1) * 128, :], [], [xt.b], xt.b)
                    norm_transpose(xt.t[:], xt.b, uT.t, uTb[n // 4], slice(n * 128, (n + 1) * 128), n % 4)
                p1_block(0, wb0, wvw0, [tb])
            for c0 in range(512, INC, 512):
                ncol = min(512, INC - c0)
                wb, wvw = load_w(w_in[:, c0:c0 + ncol], 16)
                p1_block(c0, wb, wvw, range(4))
            if debug == "p1":
                for r0 in range(0, INC, 128):
                    M = min(128, INC - r0)
                    xt = xts[(r0 // 128) % 2]
                    k.dma(k.sp, xt.t[0:M, :], pT.t[r0:r0 + M, :], [pT.bl[r0 // 128]], [xt.b], xt.b)
                    k.dma(k.sp, dbg[r0:r0 + M, :], xt.t[0:M, :], [xt.b], [], xt.b)
                k._deps(k.sp, [], [xts[0].b, xts[1].b])
        if debug == "p1":
            k.es.close()
            return nc

    if debug != "p4":
        k.barrier()
        build_mixers(nc, k, pT, oTd, convw, dnsc, dnw, mu, rwv, lora, g2, ones, ident, bg_next)
        bg_next(99)
        if debug == "p2":
            k.barrier()
            with contextlib.ExitStack() as st:
                a = k.sb("dba", [128, S], BF16, st); b = k.sb("dbb", [128, S], F32, st)
                for r0 in range(0, D, 128):
                    k.dma(k.sp, a.t[:], oTd.t[r0:r0 + 128, :], [oTd.bl[r0 // 128]], [a.b], a.b)
                    k.cp(b.t[:], a.t[:], [a.b], [b.b])
                    k.dma(k.sp, dbg[r0:r0 + 128, :], b.t[:], [b.b], [], b.b)
                k._deps(k.sp, [], [b.b])
            k.es.close()
            return nc

    k.barrier()
    if debug == "p4":
        bg_next(99)
    st4 = contextlib.ExitStack()
    alloc_wbufs(st4, "b")
    alloc_norm(st4, "b")
    wbc, un, ss = G["wbc"], G["junk"], G["ss"]
    KT = k.sb("KT", [128, 4, MEM], BF16, st4)
    Vm = k.sb("Vm", [128, 2, 512], BF16, st4)
    hnT = k.sb("hnT", [128, 16, 512], BF16, st4)
    h = k.sb("h", [128, 4, D], F32, st4)
    hB = [Buf(f"h{j}") for j in range(4)]
    hid = k.sb("hid", [128, 64, 512], BF16, st4)
    hidB = [Buf(f"hid{i}") for i in range(16)]
    qT = k.sb("qT", [128, 4, 512], BF16, st4)
    oxT = k.sb("oxT", [128, 4, 512], BF16, st4)
    atw = [dict(pr=k.sb(f"pr{i}", [128, MEM], F32, st4), prn=k.sb(f"prn{i}", [128, MEM], BF16, st4),
                prT=k.sb(f"prT{i}", [128, 2, 128], BF16, st4), sm=k.sb(f"sm{i}", [128, 4], F32, st4)) for i in range(4)]

    def run_rr(gens):
        gens = list(gens)
        while gens:
            for g_ in list(gens):
                try:
                    next(g_)
                except StopIteration:
                    gens.remove(g_)
    rl = [k.sb(f"rl{i}", [128, 512], F32, st4) for i in range(2)]
    memt = [k.sb(f"memt{i}", [128, D], F32, st4) for i in range(1)]

    load_wbc(2)
    for mt in range(2):
        m_ = memt[0]
        k.dma(k.sp, m_.t[:], mem[mt * 128:(mt + 1) * 128, :], [], [m_.b], m_.b)
        norm_transpose(m_.t[:], m_.b, hnT.t, hnT.b, slice(mt * 128, (mt + 1) * 128), mt)
    wb, wvw = load_w(wk[:, :], 16)
    for hd in range(4):
        p = k.ps()
        for kc in range(16):
            k.mm(p.t[:, 0:MEM], wvw[:, kc, hd * 128:(hd + 1) * 128], hnT.t[:, kc, 0:MEM], kc == 0, kc == 15, [wb.b, hnT.b], [p.b])
        k.cp(KT.t[:, hd, :], p.t[:, 0:MEM], [p.b], [KT.b])
    wb, wvw = load_w(wv[:, :], 16)
    for mc in range(2):
        p = k.ps()
        for kc in range(16):
            k.mm(p.t[:, :], hnT.t[:, kc, mc * 128:(mc + 1) * 128], wvw[:, kc, :], kc == 0, kc == 15, [wb.b, hnT.b], [p.b])
        k.cp(Vm.t[:, mc, :], p.t[:, :], [p.b], [Vm.b])

    oTb_view = hid.t[:, 0:16, :]
    oTb_bufs = hidB[0:4]
    def p4_prefetch(TB_):
        t0_ = TB_ * 512
        if oT_dt == BF16:
            k.dma(k.sp, oTb_view, oTd.t[:, t0_:t0_ + 512].rearrange("(kc p) t -> p kc t", p=128), oTd.bl, oTb_bufs, oTb_bufs[0])
        else:
            k.dma(k.pool, oTb_view, oTd.t[:, t0_:t0_ + 512].rearrange("(kc p) t -> p kc t", p=128), oTd.bl, oTb_bufs, oTb_bufs[0])
        return load_w(w_out[:, 0:512], 16, (0, TB_ == 0))

    pf = p4_prefetch(0)
    for TB in range(4):
        t0 = TB * 512
        for cb in range(4):
            if cb == 0:
                wb, wvw = pf
            else:
                wb, wvw = load_w(w_out[:, cb * 512:(cb + 1) * 512], 16, (cb, TB == 0))
            if cb == 0:
                for j in range(4):
                    k.dma(k.sp, h.t[:, j, :], x[t0 + j * 128:t0 + (j + 1) * 128, :], [], [hB[j]], hB[j])
            for j in range(4):
                p = k.ps()
                for kc in range(16):
                    k.mm(p.t[:, :], oTb_view[:, kc, j * 128:(j + 1) * 128], wvw[:, kc, :], kc == 0, kc == 15, [wb.b] + oTb_bufs, [p.b])
                hs = h.t[:, j, cb * 512:(cb + 1) * 512]
                k.tt(k.dve, hs, p.t[:, :], hs, ALU.add, [p.b, hB[j]], [hB[j]])
        load_wbc(1)
        for j in range(4):
            norm_transpose(h.t[:, j, :], hB[j], hnT.t, hnT.b, slice(j * 128, (j + 1) * 128), j)
        wb, wvw = load_w(wq[:, :], 16, (4, TB == 0))
        for hd in range(4):
            p = k.ps()
            for kc in range(16):
                k.mm(p.t[:, :], wvw[:, kc, hd * 128:(hd + 1) * 128], hnT.t[:, kc, :], kc == 0, kc == 15, [wb.b, hnT.b], [p.b])
            k.actf(qT.t[:, hd, :], p.t[:, :], AF.Copy, [p.b], [qT.b], scale=128 ** -0.5)
        def attn_worker(w_):
            a_ = atw[w_]
            pr, prn, prT, sm = a_["pr"], a_["prn"], a_["prT"], a_["sm"]
            for idx in range(w_, 16, 4):
                j, hd = divmod(idx, 4)
                yield from k.need(1)
                p = k.psA()
                k.mm(p.t[:, 0:MEM], qT.t[:, hd, j * 128:(j + 1) * 128], KT.t[:, hd, :], True, True, [qT.b, KT.b], [p.b])
                yield
                k.op(k.dve, lambda e: e.tensor_reduce(out=sm.t[:, 0:1], in_=p.t[:, 0:MEM], axis=AX.X, op=ALU.max, negate=True),
                     [p.b], [sm.b])
                yield
                k.actf(pr.t[:], p.t[:, 0:MEM], AF.Exp, [p.b, sm.b], [pr.b, sm.b], bias=sm.t[:, 0:1], accum_out=sm.t[:, 1:2])
                k.psF(p)
                yield
                k.op(k.dve, lambda e: e.reciprocal(sm.t[:, 2:3], sm.t[:, 1:2]), [sm.b], [sm.b])
                yield
                k.ts(k.dve, prn.t[:], pr.t[:], sm.t[:, 2:3], None, ALU.mult, None, [pr.b, sm.b], [prn.b])
                yield
                yield from k.need(1)
                p2 = k.psA()
                pv = p2.t[:].bitcast(BF16)
                for mc in range(2):
                    k.tr(pv[:, mc * 128:(mc + 1) * 128], prn.t[:, mc * 128:(mc + 1) * 128], identb.t[:], [prn.b, identb.b], [p2.b])
                yield
                k.cp(prT.t[:], pv[:, 0:256].rearrange("p (m c) -> p m c", c=128), [p2.b], [prT.b])
                k.psF(p2)
                yield
                yield from k.need(1)
                p3 = k.psA()
                for mc in range(2):
                    k.mm(p3.t[:, 0:128], Vm.t[:, mc, hd * 128:(hd + 1) * 128], prT.t[:, mc, :], mc == 0, mc == 1, [Vm.b, prT.b], [p3.b])
                yield
                k.cp(oxT.t[:, hd, j * 128:(j + 1) * 128], p3.t[:, 0:128], [p3.b], [oxT.b])
                k.psF(p3)
                yield
        run_rr([attn_worker(w_) for w_ in range(4)])
        wb, wvw = load_w(wo[:, :], 4, (5, TB == 0))
        for cb in range(4):
            for j in range(4):
                p = k.ps()
                for kc in range(4):
                    k.mm(p.t[:, :], oxT.t[:, kc, j * 128:(j + 1) * 128], wvw[:, kc, cb * 512:(cb + 1) * 512], kc == 0, kc == 3, [wb.b, oxT.b], [p.b])
                hs = h.t[:, j, cb * 512:(cb + 1) * 512]
                k.tt(k.dve, hs, p.t[:, :], hs, ALU.add, [p.b, hB[j]], [hB[j]])
        load_wbc(3)
        for j in range(4):
            norm_transpose(h.t[:, j, :], hB[j], hnT.t, hnT.b, slice(j * 128, (j + 1) * 128), j)
        for fb in range(16):
            wb, wvw = load_w(w1[:, fb * 512:(fb + 1) * 512], 16, (6 + fb, TB == 0))
            for fc in range(4):
                p = k.ps()
                for kc in range(16):
                    k.mm(p.t[:, :], wvw[:, kc, fc * 128:(fc + 1) * 128], hnT.t[:, kc, :], kc == 0, kc == 15, [wb.b, hnT.b], [p.b])
                r_ = rl[(fb * 4 + fc) % 2]
                k.actf(r_.t[:], p.t[:, :], AF.Relu, [p.b], [r_.b])
                k.tt(k.dve, hid.t[:, fb * 4 + fc, :], r_.t[:], r_.t[:], ALU.mult, [r_.b], [hidB[fb]])
        for cb in range(4):
            accs = [k.ps() for _ in range(4)]
            for sub in range(4):
                wb, wvw = load_w(w2[sub * 2048:(sub + 1) * 2048, cb * 512:(cb + 1) * 512], 16, (22 + cb * 4 + sub, TB == 0))
                for j in range(4):
                    for fc in range(16):
                        f = sub * 16 + fc
                        k.mm(accs[j].t[:, :], hid.t[:, f, j * 128:(j + 1) * 128], wvw[:, fc, :], f == 0, f == 63,
                             [wb.b, hidB[f // 4]], [accs[j].b])
            for j in range(4):
                hs = h.t[:, j, cb * 512:(cb + 1) * 512]
                k.tt(k.dve, hs, accs[j].t[:, :], hs, ALU.add, [accs[j].b, hB[j]], [hB[j]])
        if TB + 1 < 4:
            pf = p4_prefetch(TB + 1)
        load_wbc(4)
        for j in range(4):
            hj = h.t[:, j, :]
            sb_ = G["ssb"][j]
            k.actf(un.t[:], hj, AF.Square, [hB[j]], [un.b, sb_], accum_out=ss.t[:, j:j + 1])
            k.actf(ss.t[:, j:j + 1], ss.t[:, j:j + 1], AF.Sqrt, [sb_, epsT.b], [sb_], scale=1.0 / D, bias=epsT.t[:, 0:1])
            k.op(k.dve, lambda e: e.reciprocal(ss.t[:, j:j + 1], ss.t[:, j:j + 1]), [sb_], [sb_])
            k.stt(hj, hj, ss.t[:, j:j + 1], wbc.t[:], ALU.mult, ALU.mult, [hB[j], sb_, wbc.b], [hB[j]])
        for j in range(4):
            k.dma(k.sp, out[t0 + j * 128:t0 + (j + 1) * 128, :], h.t[:, j, :], [hB[j]], [], hB[j])
    k._deps(k.sp, [], hB)
    st4.close()
    k.es.close()
    return nc


def build_mixers(nc, k, pT, oTd, convw_d, dnsc_d, dnw_d, mu_d, rwv_d, lora_d, g2_d, ones, ident, bg_next):
    st = contextlib.ExitStack()
    r = lambda ap: ap.bitcast(F32R)
    NB = 10
    big = [k.sb(f"big{i}", [128, S], F32, st) for i in range(NB)]
    free = list(range(NB))

    def balloc():
        return big[free.pop(0)]

    def bfree(*ts):
        for t_ in ts:
            free.append(big.index(t_))

    def sm(name, shape, dt=F32):
        return k.sb("m_" + name, shape, dt, st)

    Ls = sm("Ls", [128, 128]); Li = sm("Li", [128, 128]); UU = sm("UU", [128, 256])
    blk = sm("blk", [128, 128]); rmask = sm("rmask", [128, S], BF16); selh = sm("selh", [16, 128])
    epsG = sm("epsG", [128, 2])

    def asel(out, pat, cm, op, R, W):
        k.op(k.pool, lambda e: e.affine_select(out=out, in_=ones.t[:], pattern=pat, compare_op=op, fill=0.0, base=0,
                                               channel_multiplier=cm), [ones.b] + R, W)
    asel(Ls.t[:], [[-1, 128]], 1, ALU.is_gt, [], [Ls.b])
    k.ts(k.dve, Ls.t[:], Ls.t[:], -1.0, None, ALU.mult, None, [Ls.b], [Ls.b])
    Li2 = sm("Li2", [128, 256]); II2 = sm("II2", [128, 256])
    for i_ in range(2):
        asel(Li2.t[:, i_ * 128:(i_ + 1) * 128], [[-1, 128]], 1, ALU.is_ge, [], [Li2.b])
        k.cp(II2.t[:, i_ * 128:(i_ + 1) * 128], ident.t[:], [ident.b], [II2.b], eng=k.pool)
    asel(Li.t[:], [[-1, 128]], 1, ALU.is_ge, [], [Li.b])
    asel(UU.t[:, 0:128], [[1, 128]], -1, ALU.is_gt, [], [UU.b])
    asel(UU.t[:, 128:256], [[1, 128]], -1, ALU.is_ge, [], [UU.b])
    k.op(k.pool, lambda e: e.memset(blk.t[:], 0.0), [], [blk.b])
    k.op(k.pool, lambda e: e.memset(blk.t[0:64, 0:64], 1.0), [], [blk.b])
    k.op(k.pool, lambda e: e.memset(blk.t[64:128, 64:128], 1.0), [], [blk.b])
    k.op(k.pool, lambda e: e.memset(rmask.t[:], 1.0), [], [rmask.b])
    k.op(k.pool, lambda e: e.memset(rmask.t[:].rearrange("p (c t) -> p c t", t=128)[:, :, 0:1], 0.0), [], [rmask.b])
    k.op(k.pool, lambda e: e.memset(epsG.t[:, 0:1], 64e-5), [], [epsG.b])
    k.op(k.pool, lambda e: e.memset(epsG.t[:, 1:2], 1e-6), [], [epsG.b])
    convw = sm("convw", [128, 96]); dnsc = sm("dnsc", [16, 2]); dnw = sm("dnw", [128, 1])
    mu = sm("mu", [128, 26]); omm = sm("omm", [128, 26]); rwv = sm("rwv", [128, 56])
    for t_, d_ in ((convw, convw_d), (dnsc, dnsc_d), (dnw, dnw_d), (mu, mu_d), (rwv, rwv_d)):
        k.dma(k.sp, t_.t[:], d_, [], [t_.b], t_.b)
    k.ts(k.dve, omm.t[:], mu.t[:], -1.0, 1.0, ALU.mult, ALU.add, [mu.b], [omm.b])

    def sq(name, w=128):
        return sm(name, [128, w])
    def run_rr(gens):
        gens = list(gens)
        while gens:
            for g_ in list(gens):
                try:
                    next(g_)
                except StopIteration:
                    gens.remove(g_)

    def neumann_multi(probs, nlev):
        for pr in probs:
            pr["Ao"] = pr["A1"]; pr["BPo"] = pr["BP"][0]
        for lv in range(nlev):
            last = lv == nlev - 1
            yield from k.need((1 if last else 2) * len(probs))
            for pr in probs:
                Ao, BPo = pr["Ao"], pr["BPo"]
                pr["pa"] = k.psA()
                if last:
                    k.mm(pr["pa"].t[:, 0:128], r(Ao.t[:]), r(BPo.t[:, 128:256]), True, True, [Ao.b, BPo.b], [pr["pa"].b])
                else:
                    k.mm(pr["pa"].t[:, 0:256], r(Ao.t[:]), r(BPo.t[:]), True, True, [Ao.b, BPo.b], [pr["pa"].b])
                    pr["pb"] = k.psA()
                    k.mm(pr["pb"].t[:, 0:128], r(BPo.t[:, 0:128]), r(Ao.t[:]), True, True, [Ao.b, BPo.b], [pr["pb"].b])
            yield
            for pr in probs:
                BPo = pr["BPo"]
                if last:
                    k.tt(k.dve, r(pr["Tout"].t[:]), pr["pa"].t[:, 0:128], BPo.t[:, 128:256], ALU.add, [pr["pa"].b, BPo.b], [pr["Tout"].b])
                    k.psF(pr["pa"])
                else:
                    An, BPn = pr["Ap"][lv % 2], pr["BP"][(lv + 1) % 2]
                    k.cp(r(An.t[:]), pr["pb"].t[:, 0:128], [pr["pb"].b], [An.b], eng=k.act)
                    k.cp(r(BPn.t[:, 0:128]), pr["pa"].t[:, 0:128], [pr["pa"].b], [BPn.b], eng=k.dve)
                    k.tt(k.dve, r(BPn.t[:, 128:256]), pr["pa"].t[:, 128:256], BPo.t[:, 128:256], ALU.add, [pr["pa"].b, BPo.b], [BPn.b])
                    k.psF(pr["pa"], pr["pb"])
                    pr["Ao"], pr["BPo"] = An, BPn
            yield

    def neumann_pairs(pairs, nlev):
        for pr in pairs:
            pr["Ao"] = pr["A1_2"]; pr["BPo"] = pr["BP2"][0]
        for lv in range(nlev):
            last = lv == nlev - 1
            yield from k.need((1 if last else 2) * len(pairs))
            for pr in pairs:
                Ao, BPo = pr["Ao"], pr["BPo"]
                pr["pa"] = k.psA()
                if not last:
                    pr["pb"] = k.psA()
                for i in range(2):
                    a_i = r(Ao.t[:, i * 128:(i + 1) * 128])
                    if last:
                        k.mm(pr["pa"].t[:, i * 128:(i + 1) * 128], a_i, r(BPo.t[:, i * 256 + 128:(i + 1) * 256]), True, True,
                             [Ao.b, BPo.b], [pr["pa"].b])
                    else:
                        k.mm(pr["pa"].t[:, i * 256:(i + 1) * 256], a_i, r(BPo.t[:, i * 256:(i + 1) * 256]), True, True,
                             [Ao.b, BPo.b], [pr["pa"].b])
                        k.mm(pr["pb"].t[:, i * 128:(i + 1) * 128], r(BPo.t[:, i * 256:i * 256 + 128]), a_i, True, True,
                             [Ao.b, BPo.b], [pr["pb"].b])
            yield
            for pr in pairs:
                BPo = pr["BPo"]
                bpo3 = BPo.t[:].rearrange("p (i c) -> p i c", c=256)
                if last:
                    k.tt(k.dve, r(pr["Tout2"].t[:].rearrange("p (i c) -> p i c", c=128)),
                         pr["pa"].t[:, 0:256].rearrange("p (i c) -> p i c", c=128), bpo3[:, :, 128:256], ALU.add,
                         [pr["pa"].b, BPo.b], [pr["Tout2"].b])
                    k.psF(pr["pa"])
                else:
                    An, BPn = pr["Ap2"][lv % 2], pr["BP2"][(lv + 1) % 2]
                    bpn3 = BPn.t[:].rearrange("p (i c) -> p i c", c=256)
                    pa3 = pr["pa"].t[:, :].rearrange("p (i c) -> p i c", c=256)
                    k.cp(r(An.t[:]), pr["pb"].t[:, 0:256], [pr["pb"].b], [An.b], eng=k.act)
                    k.cp(r(bpn3[:, :, 0:128]), pa3[:, :, 0:128], [pr["pa"].b], [BPn.b], eng=k.act)
                    k.tt(k.dve, r(bpn3[:, :, 128:256]), pa3[:, :, 128:256], bpo3[:, :, 128:256], ALU.add, [pr["pa"].b, BPo.b], [BPn.b])
                    k.psF(pr["pa"], pr["pb"])
                    pr["Ao"], pr["BPo"] = An, BPn
            yield

    def load_rows(dst, r0, nrows=128):
        k.dma(k.sp, dst.t[0:nrows, :], pT.t[r0:r0 + nrows, :], pT.bl[r0 // 128:(r0 + nrows - 1) // 128 + 1], [dst.b], dst.b)

    ones_bf = sm("ones_bf", [128, 128], BF16); blk_bf = sm("blk_bf", [128, 128], BF16)
    k.cp(ones_bf.t[:], ones.t[:], [ones.b], [ones_bf.b], eng=k.dve)
    k.cp(blk_bf.t[:], blk.t[:], [blk.b], [blk_bf.b], eng=k.dve)

    def bfv(t_):
        return t_.t[:].bitcast(BF16)[:, 0:S]

    def psum_bcast_sum(src, lhsT, lhsTb, fn):
        sv = bfv(src)
        for tb in range(4):
            p = k.ps()
            k.mm(p.t[:, :], lhsT, sv[:, tb * 512:(tb + 1) * 512], True, True, [lhsTb, src.b], [p.b])
            fn(tb, p)

    obf = [sm("obf0", [128, S], BF16)] * 2
    octr = [0]

    gc16 = balloc()
    st_dn = contextlib.ExitStack()

    def smd(name, shape, dt=F32):
        return k.sb("m_" + name, shape, dt, st_dn)
    gcT = smd("gcT", [128, 256]); betaT = smd("betaT", [128, 256]); kdT = smd("kdT", [128, 256]); egT = smd("egT", [128, 256])
    bgT = smd("bgT", [128, 16, 8]); negA = smd("negA", [16, 1])
    if True:
        ab = balloc(); t1 = balloc(); t2 = balloc(); beta16 = balloc(); kd16 = balloc()
        R16 = slice(0, 16)
        load_rows(ab, 4096, 16)
        dtb = dnsc.t[:, 1:2]
        k.actf(t1.t[R16, :], ab.t[R16, :], AF.Abs, [ab.b, dnsc.b], [t1.b], bias=dtb)
        k.actf(t1.t[R16, :], t1.t[R16, :], AF.Exp, [t1.b], [t1.b], scale=-1.0)
        k.actf(t1.t[R16, :], t1.t[R16, :], AF.Ln, [t1.b, ones.b], [t1.b], bias=ones.t[0:16, 0:1])
        k.ts(k.dve, t2.t[R16, :], ab.t[R16, :], dtb, 0.0, ALU.add, ALU.max, [ab.b, dnsc.b], [t2.b])
        k.tt(k.dve, t1.t[R16, :], t1.t[R16, :], t2.t[R16, :], ALU.add, [t1.b, t2.b], [t1.b])
        k.actf(negA.t[:], dnsc.t[:, 0:1], AF.Exp, [dnsc.b], [negA.b])
        k.ts(k.dve, negA.t[:], negA.t[:], -1.0, None, ALU.mult, None, [negA.b], [negA.b])
        k.ts(k.dve, t1.t[R16, :], t1.t[R16, :], negA.t[:, 0:1], None, ALU.mult, None, [t1.b, negA.b], [t1.b])
        k.actf(beta16.t[R16, :], ab.t[R16, :], AF.Sigmoid, [ab.b], [beta16.b])
        k.op(k.dve, lambda e: e.tensor_tensor_scan(gc16.t[R16, :], rmask.t[R16, :], t1.t[R16, :], 0.0, ALU.mult, ALU.add),
             [rmask.b, t1.b], [gc16.b])
        for n in range(NT):
            cs = slice(n * 128, (n + 1) * 128)
            k.ts(k.dve, kd16.t[R16, cs], gc16.t[R16, cs], gc16.t[R16, n * 128 + 127:n * 128 + 128], None, ALU.subtract, None,
                 [gc16.b], [kd16.b])
        k.actf(kd16.t[R16, :], kd16.t[R16, :], AF.Exp, [kd16.b], [kd16.b], scale=-1.0)
        for src, dst in ((gc16, gcT), (beta16, betaT), (kd16, kdT)):
            p = k.ps()
            for n in range(NT):
                k.mm(p.t[:, n * 16:(n + 1) * 16], src.t[R16, n * 128:(n + 1) * 128], ident.t[0:16, 0:16], True, True, [src.b, ident.b], [p.b])
            k.cp(dst.t[:], p.t[:, 0:256], [p.b], [dst.b])
        k.actf(egT.t[:], gcT.t[:], AF.Exp, [gcT.b], [egT.b])
        k.tt(k.dve, bgT.t[:], betaT.t[:].rearrange("p (n r) -> p n r", r=16)[:, :, 8:16],
             egT.t[:].rearrange("p (n r) -> p n r", r=16)[:, :, 0:8], ALU.mult, [betaT.b, egT.b], [bgT.b])
        bfree(ab, t1, t2, beta16, kd16)
    ngcT = smd("ngcT", [128, 256])
    k.ts(k.dve, ngcT.t[:], gcT.t[:], -1.0, None, ALU.mult, None, [gcT.b], [ngcT.b])
    ngcT3 = ngcT.t[:].rearrange("p (n r) -> p n r", r=16)
    gcT3 = gcT.t[:].rearrange("p (n r) -> p n r", r=16)
    betaT3 = betaT.t[:].rearrange("p (n r) -> p n r", r=16)
    kdT3 = kdT.t[:].rearrange("p (n r) -> p n r", r=16)

    WDN = 6

    def sqd(name, w=128):
        return k.sb("m_" + name, [128, w], F32, st_dn)
    St = [sqd("St0"), sqd("St1")]
    qTr = sqd("qTr", S); kTr = sqd("kTr", S); qgr = sqd("qgr", S)
    dnw_t = []
    WDP = 4
    for w_ in range(WDP):
        d_ = {nm: sqd(f"{nm}{w_}", 256) for nm in ("t1_2", "El_2", "MA_2", "MD_2", "at_2", "attnT_2", "nwT_2", "A1_2", "Tout2")}
        d_["Ap2"] = [sqd(f"Ap20_{w_}", 256), sqd(f"Ap21_{w_}", 256)]; d_["BP2"] = [sqd(f"BP20_{w_}", 512), sqd(f"BP21_{w_}", 512)]
        d_["c"] = [{nm: sqd(f"{nm}{w_}_{i_}") for nm in ("kbg", "kd", "vb", "vnew")} for i_ in range(2)]
        dnw_t.append(d_)

    def conv_silu(xr, gi):
        c = balloc()
        w = lambda j: convw.t[:, gi * 4 + j:gi * 4 + j + 1]
        k.ts(k.dve, c.t[:], xr.t[:], w(3), None, ALU.mult, None, [xr.b, convw.b], [c.b])
        for sh in (1, 2, 3):
            k.stt(c.t[:, sh:S], xr.t[:, 0:S - sh], w(3 - sh), c.t[:, sh:S], ALU.mult, ALU.add, [xr.b, convw.b, c.b], [c.b])
        k.actf(c.t[:], c.t[:], AF.Silu, [c.b], [c.b])
        bfree(xr)
        return c

    def l2n(xc, scale, dst):
        sq_ = balloc(); rn = balloc()
        k.actf(bfv(sq_), xc.t[:], AF.Square, [xc.b], [sq_.b])

        def fn(tb, p):
            ts_ = slice(tb * 512, (tb + 1) * 512)
            k.actf(rn.t[:, ts_], p.t[:, :], AF.Ln, [p.b, epsG.b], [rn.b], bias=epsG.t[:, 1:2])
        psum_bcast_sum(sq_, ones_bf.t[:], ones_bf.b, fn)
        k.actf(rn.t[:], rn.t[:], AF.Exp, [rn.b], [rn.b], scale=-0.5)
        k.stt(r(dst.t[:]), xc.t[:], scale, rn.t[:], ALU.mult, ALU.mult, [xc.b, rn.b], [dst.b])
        bfree(sq_, rn, xc)

    pre_ld = None
    for h in range(DBG_DN):
        if pre_ld is None:
            qr = balloc(); load_rows(qr, h * 128)
            kr = balloc(); load_rows(kr, 1024 + h * 128)
            vr = balloc(); load_rows(vr, 2048 + h * 128)
        else:
            qr, kr, vr = pre_ld
            pre_ld = None
        if DBG_STEP == 0:
            st.close(); return
        qT = conv_silu(qr, h); kT = conv_silu(kr, 8 + h); vT = conv_silu(vr, 16 + h)
        if DBG_STEP == 1:
            st.close(); return
        l2n(qT, 128 ** -0.5, qTr); l2n(kT, 1.0, kTr)
        qT, kT = qTr, kTr
        if DBG_STEP == 2:
            st.close(); return
        gcb = balloc(); egcb = balloc()

        def fn(tb, p):
            ts_ = slice(tb * 512, (tb + 1) * 512)
            k.cp(gcb.t[:, ts_], p.t[:, :], [p.b], [gcb.b], eng=k.dve)
            k.actf(egcb.t[:, ts_], p.t[:, :], AF.Exp, [p.b], [egcb.b])
        k.ts(k.dve, selh.t[:], ones.t[0:16, :], ident.t[0:16, h:h + 1], None, ALU.mult, None, [ones.b, ident.b], [selh.b])
        for tb in range(4):
            p = k.ps()
            k.mm(p.t[:, :], selh.t[:], gc16.t[0:16, tb * 512:(tb + 1) * 512], True, True, [selh.b, gc16.b], [p.b])
            fn(tb, p)
        qg = qgr
        k.tt(k.dve, r(qg.t[:]), qT.t[:], egcb.t[:], ALU.mult, [qT.b, egcb.b], [qg.b])
        oT = balloc()
        if DBG_STEP == 3:
            st.close(); return
        k.ts(k.dve, r(St[0].t[:]), ident.t[:], 0.0, None, ALU.mult, None, [ident.b], [St[0].b])
        seq_done = [0]

        def dn_worker(w_, h=h, qT=qT, kT=kT, vT=vT, gcb=gcb, egcb=egcb, qg=qg, oT=oT, seq_done=seq_done):
            d_ = dnw_t[w_]
            C = d_["c"]
            H2 = [slice(0, 128), slice(128, 256)]
            for n0 in range(2 * w_, DBG_CH, 2 * WDP):
                ns = [n0, n0 + 1]
                css = [slice(n * 128, (n + 1) * 128) for n in ns]
                yield from k.need(2)
                pkt = k.psA(); pvt = k.psA()
                for i, n in enumerate(ns):
                    k.tr(pkt.t[:, H2[i]], kT.t[:, css[i]], ident.t[:], [kT.b, ident.b], [pkt.b])
                    k.tr(pvt.t[:, H2[i]], vT.t[:, css[i]], ident.t[:], [vT.b, ident.b], [pvt.b])
                    k.actf(d_["t1_2"].t[:, H2[i]], gcb.t[:, css[i]], AF.Relu, [gcb.b, ngcT.b], [d_["t1_2"].b], bias=ngcT3[:, n, h:h + 1])
                yield
                for i, n in enumerate(ns):
                    k.actf(r(C[i]["kbg"].t[:]), pkt.t[:, H2[i]], AF.Copy, [pkt.b, bgT.b], [C[i]["kbg"].b], scale=bgT.t[:, n, h:h + 1])
                    k.actf(r(C[i]["kd"].t[:]), pkt.t[:, H2[i]], AF.Copy, [pkt.b, kdT.b], [C[i]["kd"].b], scale=kdT3[:, n, h:h + 1])
                    k.ts(k.dve, r(C[i]["vb"].t[:]), pvt.t[:, H2[i]], betaT3[:, n, 8 + h:9 + h], None, ALU.mult, None, [pvt.b, betaT.b], [C[i]["vb"].b])
                k.psF(pkt, pvt)
                k.actf(d_["El_2"].t[:], d_["t1_2"].t[:], AF.Exp, [d_["t1_2"].b], [d_["El_2"].b], scale=-1.0)
                yield
                yield from k.need(2)
                pk = k.psA(); pq = k.psA()
                for i, n in enumerate(ns):
                    k.mm(pk.t[:, H2[i]], r(kT.t[:, css[i]]), r(kT.t[:, css[i]]), True, True, [kT.b], [pk.b])
                    k.mm(pq.t[:, H2[i]], r(qT.t[:, css[i]]), r(kT.t[:, css[i]]), True, True, [qT.b, kT.b], [pq.b])
                    k.stt(d_["MA_2"].t[:, H2[i]], d_["El_2"].t[:, H2[i]], betaT3[:, n, 8 + h:9 + h], Ls.t[:], ALU.mult, ALU.mult,
                          [d_["El_2"].b, betaT.b, Ls.b], [d_["MA_2"].b])
                k.tt(k.pool, d_["MD_2"].t[:], d_["El_2"].t[:], Li2.t[:], ALU.mult, [d_["El_2"].b, Li2.b], [d_["MD_2"].b])
                yield
                k.tt(k.dve, r(d_["A1_2"].t[:]), pk.t[:, 0:256], d_["MA_2"].t[:], ALU.mult, [pk.b, d_["MA_2"].b], [d_["A1_2"].b])
                k.tt(k.dve, d_["at_2"].t[:], pq.t[:, 0:256], d_["MD_2"].t[:], ALU.mult, [pq.b, d_["MD_2"].b], [d_["at_2"].b])
                k.psF(pk, pq)
                yield
                yield from k.need(2)
                pa = k.psA(); pb = k.psA()
                for i in range(2):
                    k.tr(pb.t[:, H2[i]], d_["A1_2"].t[:, H2[i]], ident.t[:], [d_["A1_2"].b, ident.b], [pb.b])
                    k.tr(pa.t[:, H2[i]], d_["at_2"].t[:, H2[i]], ident.t[:], [d_["at_2"].b, ident.b], [pa.b])
                yield
                bp3 = d_["BP2"][0].t[:].rearrange("p (i c) -> p i c", c=256)
                k.cp(r(bp3[:, :, 0:128]), pb.t[:, 0:256].rearrange("p (i c) -> p i c", c=128), [pb.b], [d_["BP2"][0].b], eng=k.act)
                k.cp(r(bp3[:, :, 128:256]), II2.t[:].rearrange("p (i c) -> p i c", c=128), [II2.b], [d_["BP2"][0].b], eng=k.pool)
                k.cp(r(d_["attnT_2"].t[:]), pa.t[:, 0:256], [pa.b], [d_["attnT_2"].b], eng=k.act)
                k.psF(pa, pb)
                yield
                yield from neumann_pairs([d_], 7)
                yield from k.need(1)
                pw = k.psA()
                for i in range(2):
                    k.mm(pw.t[:, H2[i]], r(C[i]["kbg"].t[:]), r(d_["Tout2"].t[:, H2[i]]), True, True, [C[i]["kbg"].b, d_["Tout2"].b], [pw.b])
                yield
                k.actf(r(d_["nwT_2"].t[:]), pw.t[:, 0:256], AF.Copy, [pw.b], [d_["nwT_2"].b], scale=-1.0)
                k.psF(pw)
                yield
                for i, n in enumerate(ns):
                    while seq_done[0] < n:
                        yield
                    cs = css[i]
                    So, Sn = St[n % 2], St[(n + 1) % 2]
                    yield from k.need(1)
                    pv = k.psA()
                    k.mm(pv.t[:, 0:128], r(d_["Tout2"].t[:, H2[i]]), r(C[i]["vb"].t[:]), True, False, [d_["Tout2"].b, C[i]["vb"].b], [pv.b])
                    k.mm(pv.t[:, 0:128], r(d_["nwT_2"].t[:, H2[i]]), r(So.t[:]), False, True, [d_["nwT_2"].b, So.b], [pv.b])
                    vn = C[i]["vnew"]
                    k.cp(r(vn.t[:]), pv.t[:, 0:128], [pv.b], [vn.b], eng=k.act)
                    k.psF(pv)
                    yield from k.need(2)
                    po = k.psA(); pS = k.psA()
                    k.mm(pS.t[:, 0:128], r(C[i]["kd"].t[:]), r(vn.t[:]), True, True, [C[i]["kd"].b, vn.b], [pS.b])
                    k.mm(po.t[:, 0:128], r(So.t[:]), r(qg.t[:, cs]), True, False, [So.b, qg.b], [po.b])
                    k.mm(po.t[:, 0:128], r(vn.t[:]), r(d_["attnT_2"].t[:, H2[i]]), False, True, [vn.b, d_["attnT_2"].b], [po.b])
                    k.stt(r(Sn.t[:]), So.t[:], egcb.t[:, n * 128 + 127:n * 128 + 128], pS.t[:, 0:128], ALU.mult, ALU.add,
                          [So.b, egcb.b, pS.b], [Sn.b])
                    k.cp(oT.t[:, cs], po.t[:, 0:128], [po.b], [oT.b], eng=k.act)
                    k.psF(po, pS)
                    seq_done[0] = n + 1
                    yield
        zr = balloc(); load_rows(zr, 3072 + h * 128)
        bg_next(3)
        run_rr([dn_worker(w_) for w_ in range(WDP)])
        if DBG_STEP == 14:
            st.close(); return
        bfree(vT, gcb, egcb)
        if h + 1 < DBG_DN:
            nq = balloc(); load_rows(nq, (h + 1) * 128)
            nk = balloc(); load_rows(nk, 1024 + (h + 1) * 128)
            nv = balloc(); load_rows(nv, 2048 + (h + 1) * 128)
            pre_ld = (nq, nk, nv)
        sq_ = balloc(); rn = balloc()
        k.actf(bfv(sq_), oT.t[:], AF.Square, [oT.b], [sq_.b])

        def fn2(tb, p):
            ts_ = slice(tb * 512, (tb + 1) * 512)
            k.actf(rn.t[:, ts_], p.t[:, :], AF.Ln, [p.b, epsG.b], [rn.b], bias=epsG.t[:, 1:2], scale=1.0 / 128)
        psum_bcast_sum(sq_, ones_bf.t[:], ones_bf.b, fn2)
        k.actf(rn.t[:], rn.t[:], AF.Exp, [rn.b], [rn.b], scale=-0.5)
        k.actf(zr.t[:], zr.t[:], AF.Silu, [zr.b], [zr.b])
        k.stt(oT.t[:], oT.t[:], dnw.t[:, 0:1], rn.t[:], ALU.mult, ALU.mult, [oT.b, dnw.b, rn.b], [oT.b])
        ob = obf[octr[0] % 2]; octr[0] += 1
        k.tt(k.dve, ob.t[:], oT.t[:], zr.t[:], ALU.mult, [oT.b, zr.b], [ob.b])
        k.dma(k.sp, oTd.t[h * 128:(h + 1) * 128, :], ob.t[:], [ob.b], [oTd.bl[h]], ob.b)
        bfree(zr, sq_, rn, oT)
    bfree(gc16)
    st_dn.close()
    k.barrier()

    wa = balloc(); sg = balloc()
    RW0 = DNC

    def lerp(xr, gi):
        t_ = balloc()
        k.op(k.pool, lambda e: e.memset(t_.t[:, 0:1], 0.0), [], [t_.b])
        k.ts(k.dve, t_.t[:, 1:S], xr.t[:, 0:S - 1], mu.t[:, gi:gi + 1], None, ALU.mult, None, [xr.b, mu.b], [t_.b])
        k.stt(xr.t[:], xr.t[:], omm.t[:, gi:gi + 1], t_.t[:], ALU.mult, ALU.add, [xr.b, omm.b, t_.b], [xr.b])
        bfree(t_)
    load_rows(wa, RW0 + 3072); lerp(wa, 24)
    load_rows(sg, RW0 + 3200); lerp(sg, 25)
    k.actf(wa.t[0:64, :], wa.t[0:64, :], AF.Tanh, [wa.b], [wa.b])
    k.actf(sg.t[:], sg.t[:], AF.Sigmoid, [sg.b], [sg.b])

    WRW = 4
    st_rw = contextlib.ExitStack()
    lora = k.sb("m_lora", [128, 1024], F32, st_rw); g2 = k.sb("m_g2", [128, 1024], F32, st_rw)
    for t_, d_ in ((lora, lora_d), (g2, g2_d)):
        k.dma(k.sp, t_.t[:], d_, [], [t_.b], t_.b)

    def sqr(name, w=128):
        return k.sb("m_" + name, [128, w], F32, st_rw)
    Ht = [sqr("Ht0", 128), sqr("Ht1", 128)]
    rww_t = []
    for w_ in range(WRW):
        d_ = {nm: sqr(f"r{nm}{w_}") for nm in ("e1", "e2", "e3", "e4", "Bt", "Kt", "bh", "kh", "rhs1", "AVc", "KVc", "YVc", "Gt")}
        for nm in ("BhP", "KhP", "VP", "UP"):
            t_ = sqr(f"r{nm}2_{w_}", 256)
            d_[nm + "2"] = t_
            k.ts(k.dve, r(t_.t[:]), II2.t[:], 0.0, None, ALU.mult, None, [II2.b], [t_.b])
            d_[nm] = [TV(t_.t[:, 0:128], t_.b), TV(t_.t[:, 64:192], t_.b)]
        d_["ar"] = sqr(f"rar{w_}", 256)
        d_["hd"] = []
        for hh in range(2):
            e_ = {}
            e_["mb"] = sqr(f"rmb{w_}_{hh}", 256); e_["mk"] = sqr(f"rmk{w_}_{hh}", 256)
            d_["hd"].append(e_)
        d_["A1_2"] = sqr(f"rA12_{w_}", 256); d_["Tout2"] = sqr(f"rTr2_{w_}", 256)
        d_["Ap2"] = [sqr(f"rAp20_{w_}", 256), sqr(f"rAp21_{w_}", 256)]
        d_["BP2"] = [sqr(f"rBP20_{w_}", 512), sqr(f"rBP21_{w_}", 512)]
        rww_t.append(d_)
    V = lambda j: rwv.t[:, j * 8:(j + 1) * 8]

    for g in range(DBG_RW):
        rT = balloc(); load_rows(rT, RW0 + g * 128); lerp(rT, g)
        kl = balloc(); load_rows(kl, RW0 + 1024 + g * 128); lerp(kl, 8 + g)
        vT = balloc(); load_rows(vT, RW0 + 2048 + g * 128); lerp(vT, 16 + g)
        sig = balloc(); a_ = balloc()
        gsl = slice(g * 128, (g + 1) * 128)
        for tb in range(4):
            ts_ = slice(tb * 512, (tb + 1) * 512)
            p = k.ps()
            k.mm(p.t[:, :], lora.t[0:64, gsl], wa.t[0:64, ts_], True, True, [lora.b, wa.b], [p.b])
            k.actf(sig.t[:, ts_], p.t[:, :], AF.Sigmoid, [p.b, rwv.b], [sig.b], bias=V(0)[:, g:g + 1])
            p = k.ps()
            k.mm(p.t[:, :], lora.t[64:128, gsl], wa.t[64:128, ts_], True, True, [lora.b, wa.b], [p.b])
            k.actf(a_.t[:, ts_], p.t[:, :], AF.Sigmoid, [p.b, rwv.b], [a_.b], bias=V(1)[:, g:g + 1])
        kk = balloc(); sq_ = balloc(); rn = balloc()
        k.ts(k.dve, kk.t[:], kl.t[:], V(2)[:, g:g + 1], None, ALU.mult, None, [kl.b, rwv.b], [kk.b])
        k.actf(bfv(sq_), kk.t[:], AF.Square, [kk.b], [sq_.b])

        def fnk(tb, p):
            ts_ = slice(tb * 512, (tb + 1) * 512)
            k.ts(k.dve, rn.t[:, ts_], p.t[:, :], 1e-24, None, ALU.max, None, [p.b], [rn.b])
        psum_bcast_sum(sq_, blk_bf.t[:], blk_bf.b, fnk)
        k.actf(rn.t[:], rn.t[:], AF.Ln, [rn.b], [rn.b])
        k.actf(rn.t[:], rn.t[:], AF.Exp, [rn.b], [rn.b], scale=-0.5)
        k.tt(k.dve, kk.t[:], kk.t[:], rn.t[:], ALU.mult, [kk.b, rn.b], [kk.b])
        bfree(sq_, rn)
        kf = balloc()
        k.ts(k.dve, kf.t[:], a_.t[:], -1.0, V(3)[:, g:g + 1], ALU.add, ALU.mult, [a_.b, rwv.b], [kf.b])
        k.stt(kf.t[:], kf.t[:], 1.0, kl.t[:], ALU.add, ALU.mult, [kf.b, kl.b], [kf.b])
        bT = balloc()
        k.tt(k.dve, bT.t[:], a_.t[:], kk.t[:], ALU.mult, [a_.b, kk.b], [bT.b])
        bfree(kl, a_)
        cum = balloc()
        k.op(k.dve, lambda e: e.tensor_tensor_scan(cum.t[:], rmask.t[:], sig.t[:], 0.0, ALU.mult, ALU.add), [rmask.b, sig.b], [cum.b])
        yT = balloc()
        k.ts(k.dve, r(Ht[0].t[:]), ident.t[:], 0.0, None, ALU.mult, None, [ident.b], [Ht[0].b])
        seq_done = [0]

        def rw_worker(w_, rT=rT, vT=vT, kk=kk, kf=kf, bT=bT, sig=sig, cum=cum, yT=yT, seq_done=seq_done):
            d_ = rww_t[w_]
            ar = d_["ar"]; e1 = d_["e1"]; e2 = d_["e2"]; e3 = d_["e3"]; e4 = d_["e4"]
            Bt_, Kt_, bh, kh = d_["Bt"], d_["Kt"], d_["bh"], d_["kh"]
            BhP, KhP, VP, UP, rhs1 = d_["BhP"], d_["KhP"], d_["VP"], d_["UP"], d_["rhs1"]
            HD = d_["hd"]
            RS = [slice(0, 64), slice(64, 128)]
            for n in range(w_, DBG_CH, WRW):
                cs = slice(n * 128, (n + 1) * 128)
                k.actf(e1.t[:], cum.t[:, cs], AF.Exp, [cum.b], [e1.b], scale=CDEC)
                k.actf(e2.t[:], cum.t[:, cs], AF.Exp, [cum.b], [e2.b], scale=-CDEC)
                k.tt(k.pool, e3.t[:], cum.t[:, cs], sig.t[:, cs], ALU.subtract, [cum.b, sig.b], [e3.b])
                k.ts(k.dve, e4.t[:], cum.t[:, cs], cum.t[:, n * 128 + 127:n * 128 + 128], None, ALU.subtract, None, [cum.b], [e4.b])
                yield
                k.actf(e3.t[:], e3.t[:], AF.Exp, [e3.b], [e3.b], scale=CDEC)
                k.actf(e4.t[:], e4.t[:], AF.Exp, [e4.b], [e4.b], scale=-CDEC)
                k.tt(k.pool, r(ar.t[:, 128:256]), rT.t[:, cs], e1.t[:], ALU.mult, [rT.b, e1.b], [ar.b])
                k.tt(k.pool, r(Bt_.t[:]), bT.t[:, cs], e2.t[:], ALU.mult, [bT.b, e2.b], [Bt_.b])
                k.tt(k.pool, r(Kt_.t[:]), kf.t[:, cs], e2.t[:], ALU.mult, [kf.b, e2.b], [Kt_.b])
                yield
                k.tt(k.dve, r(ar.t[:, 0:128]), kk.t[:, cs], e3.t[:], ALU.mult, [kk.b, e3.b], [ar.b])
                k.tt(k.pool, bh.t[:], bT.t[:, cs], e4.t[:], ALU.mult, [bT.b, e4.b], [bh.b])
                k.tt(k.pool, kh.t[:], kf.t[:, cs], e4.t[:], ALU.mult, [kf.b, e4.b], [kh.b])
                yield
                yield from k.need(3)
                trs = []
                for src, srcB, dst in ((bh.t[:], bh.b, d_["BhP2"]), (kh.t[:], kh.b, d_["KhP2"]), (vT.t[:, cs], vT.b, d_["VP2"])):
                    p = k.psA()
                    k.tr(p.t[:, 0:128], src, ident.t[:], [srcB, ident.b], [p.b])
                    trs.append((p, dst))
                yield
                for ti_, (p, dst2) in enumerate(trs):
                    k.cp(r(dst2.t[:].rearrange("p (i c) -> p i c", c=128)[:, :, 0:64]), p.t[:, 0:128].rearrange("p (i c) -> p i c", c=64),
                         [p.b], [dst2.b], eng=k.act)
                    k.psF(p)
                yield from k.need(4)
                pms = []
                for hh in range(2):
                    R = RS[hh]
                    pm = k.psA(); pm2 = k.psA()
                    k.mm(pm.t[:, 0:256], r(Bt_.t[R, :]), r(ar.t[R, :]), True, True, [Bt_.b, ar.b], [pm.b])
                    k.mm(pm2.t[:, 0:256], r(Kt_.t[R, :]), r(ar.t[R, :]), True, True, [Kt_.b, ar.b], [pm2.b])
                    pms.append((pm, pm2))
                yield
                for hh in range(2):
                    pm, pm2 = pms[hh]
                    k.tt(k.dve, r(HD[hh]["mb"].t[:]), pm.t[:, 0:256], UU.t[:], ALU.mult, [pm.b, UU.b], [HD[hh]["mb"].b])
                    k.tt(k.dve, r(HD[hh]["mk"].t[:]), pm2.t[:, 0:256], UU.t[:], ALU.mult, [pm2.b, UU.b], [HD[hh]["mk"].b])
                    k.psF(pm, pm2)
                yield from k.need(2)
                pas = []
                for hh in range(2):
                    R = RS[hh]
                    pa = k.psA()
                    k.mm(pa.t[:, 0:128], r(ar.t[R, 0:128]), r(Bt_.t[R, :]), True, True, [ar.b, Bt_.b], [pa.b])
                    pas.append(pa)
                yield
                for hh in range(2):
                    e_ = HD[hh]
                    k.tt(k.dve, r(d_["A1_2"].t[:, hh * 128:(hh + 1) * 128]), pas[hh].t[:, 0:128], Ls.t[:], ALU.mult,
                         [pas[hh].b, Ls.b], [d_["A1_2"].b])
                    k.actf(r(d_["BP2"][0].t[:, hh * 256:hh * 256 + 128]), e_["mb"].t[:, 0:128], AF.Copy, [e_["mb"].b], [d_["BP2"][0].b], scale=-1.0)
                    k.psF(pas[hh])
                yield
                k.cp(r(d_["BP2"][0].t[:].rearrange("p (i c) -> p i c", c=256)[:, :, 128:256]), II2.t[:].rearrange("p (i c) -> p i c", c=128),
                     [II2.b], [d_["BP2"][0].b], eng=k.pool)
                yield from k.need(3)
                pAV = k.psA(); pKV = k.psA(); pYV = k.psA()
                for hh in range(2):
                    e_ = HD[hh]
                    k.mm(pAV.t[:, 0:128], r(e_["mk"].t[:, 0:128]), r(VP[hh].t[:]), hh == 0, hh == 1, [e_["mk"].b, VP[hh].b], [pAV.b])
                    k.mm(pKV.t[:, RS[hh]], r(KhP[hh].t[:]), r(VP[hh].t[:, RS[hh]]), True, True, [KhP[hh].b, VP[hh].b], [pKV.b])
                    k.mm(pYV.t[:, 0:128], r(VP[hh].t[:]), r(e_["mk"].t[:, 128:256]), hh == 0, hh == 1, [VP[hh].b, e_["mk"].b], [pYV.b])
                yield
                k.cp(d_["AVc"].t[:], pAV.t[:, 0:128], [pAV.b], [d_["AVc"].b], eng=k.act)
                k.cp(d_["KVc"].t[:], pKV.t[:, 0:128], [pKV.b], [d_["KVc"].b], eng=k.act)
                k.cp(d_["YVc"].t[:], pYV.t[:, 0:128], [pYV.b], [d_["YVc"].b], eng=k.act)
                k.psF(pAV, pKV, pYV)
                yield
                yield from neumann_pairs([d_], 7)
                while seq_done[0] < n:
                    yield
                Ho, Hn = Ht[n % 2], Ht[(n + 1) % 2]
                yield from k.need(1)
                pr_ = k.psA()
                k.mm(pr_.t[:, 0:128], r(ar.t[:, 0:128]), r(Ho.t[:]), True, True, [ar.b, Ho.b], [pr_.b])
                k.stt(d_["Gt"].t[:], Ho.t[:], e1.t[:, 127:128], d_["KVc"].t[:], ALU.mult, ALU.add, [Ho.b, e1.b, d_["KVc"].b], [d_["Gt"].b])
                k.stt(r(rhs1.t[:]), pr_.t[:, 0:128], -1.0, d_["AVc"].t[:], ALU.mult, ALU.subtract, [pr_.b, d_["AVc"].b], [rhs1.b])
                k.psF(pr_)
                yield from k.need(1)
                pu = k.psA()
                for hh in range(2):
                    k.mm(pu.t[:, RS[hh]], r(d_["Tout2"].t[:, hh * 128:(hh + 1) * 128]), r(rhs1.t[:, RS[hh]]), True, True,
                         [d_["Tout2"].b, rhs1.b], [pu.b])
                k.cp(r(d_["UP2"].t[:].rearrange("p (i c) -> p i c", c=128)[:, :, 0:64]), pu.t[:, 0:128].rearrange("p (i c) -> p i c", c=64),
                     [pu.b], [d_["UP2"].b], eng=k.act)
                k.psF(pu)
                yield from k.need(2)
                pY = k.psA(); pS = k.psA()
                for hh in range(2):
                    k.mm(pS.t[:, RS[hh]], r(BhP[hh].t[:]), r(UP[hh].t[:, RS[hh]]), True, True, [BhP[hh].b, UP[hh].b], [pS.b])
                k.mm(pY.t[:, 0:128], r(Ho.t[:]), r(ar.t[:, 128:256]), True, False, [Ho.b, ar.b], [pY.b])
                for hh in range(2):
                    e_ = HD[hh]
                    k.mm(pY.t[:, 0:128], r(UP[hh].t[:]), r(e_["mb"].t[:, 128:256]), False, hh == 1, [UP[hh].b, e_["mb"].b], [pY.b])
                k.tt(k.dve, r(Hn.t[:]), pS.t[:, 0:128], d_["Gt"].t[:], ALU.add, [pS.b, d_["Gt"].b], [Hn.b])
                k.tt(k.dve, yT.t[:, cs], pY.t[:, 0:128], d_["YVc"].t[:], ALU.add, [pY.b, d_["YVc"].b], [yT.b])
                k.psF(pY, pS)
                seq_done[0] = n + 1
                yield
        bg_next(3)
        run_rr([rw_worker(w_) for w_ in range(WRW)])
        bfree(kk, bT, sig, cum)
        rk = balloc(); yc = balloc(); sq_ = balloc(); rs_ = balloc()
        k.stt(bfv(rk), rT.t[:], V(4)[:, g:g + 1], kf.t[:], ALU.mult, ALU.mult, [rT.b, rwv.b, kf.b], [rk.b])
        k.actf(bfv(sq_), yT.t[:], AF.Copy, [yT.b], [sq_.b])

        def fnm(tb, p):
            ts_ = slice(tb * 512, (tb + 1) * 512)
            k.stt(yc.t[:, ts_], p.t[:, :], -1.0 / 64, yT.t[:, ts_], ALU.mult, ALU.add, [p.b, yT.b], [yc.b])
        psum_bcast_sum(sq_, blk_bf.t[:], blk_bf.b, fnm)
        k.actf(bfv(sq_), yc.t[:], AF.Square, [yc.b], [sq_.b])

        def fnv(tb, p):
            ts_ = slice(tb * 512, (tb + 1) * 512)
            k.actf(rs_.t[:, ts_], p.t[:, :], AF.Ln, [p.b, epsG.b], [rs_.b], bias=epsG.t[:, 0:1], scale=1.0 / 64)
        psum_bcast_sum(sq_, blk_bf.t[:], blk_bf.b, fnv)
        k.actf(rs_.t[:], rs_.t[:], AF.Exp, [rs_.b], [rs_.b], scale=-0.5)
        k.tt(k.dve, yc.t[:], yc.t[:], rs_.t[:], ALU.mult, [yc.b, rs_.b], [yc.b])
        k.ts(k.dve, yc.t[:], yc.t[:], V(5)[:, g:g + 1], V(6)[:, g:g + 1], ALU.mult, ALU.add, [yc.b, rwv.b], [yc.b])

        def fnb(tb, p):
            ts_ = slice(tb * 512, (tb + 1) * 512)
            k.tt(k.dve, rs_.t[:, ts_], p.t[:, :], vT.t[:, ts_], ALU.mult, [p.b, vT.b], [rs_.b])
        psum_bcast_sum(rk, blk_bf.t[:], blk_bf.b, fnb)
        k.tt(k.dve, yc.t[:], yc.t[:], rs_.t[:], ALU.add, [yc.b, rs_.b], [yc.b])
        ob = obf[octr[0] % 2]; octr[0] += 1
        for tb in range(4):
            ts_ = slice(tb * 512, (tb + 1) * 512)
            p = k.ps()
            k.mm(p.t[:, :], g2.t[:, gsl], sg.t[:, ts_], True, True, [g2.b, sg.b], [p.b])
            k.tt(k.dve, ob.t[:, ts_], p.t[:, :], yc.t[:, ts_], ALU.mult, [p.b, yc.b], [ob.b])
        k.dma(k.sp, oTd.t[1024 + g * 128:1024 + (g + 1) * 128, :], ob.t[:], [ob.b], [oTd.bl[8 + g]], ob.b)
        bfree(rk, yc, sq_, rs_, rT, kf, vT, yT)
    bfree(wa, sg)
    st_rw.close()
    st.close()


def prep_shared(inp):
    f = lambda a: np.ascontiguousarray(np.asarray(a, dtype=np.float32))
    sh = {}
    for kk_ in ("w_in", "w_out", "xa_wq", "xa_wk", "xa_wv", "xa_wo", "ffn_w1", "ffn_w2"):
        sh[kk_] = f(inp[kk_][0])
    sh["norms"] = f(np.stack([inp["mix_norm_w"][0], inp["xa_norm_w"][0], inp["mem_norm_w"][0], inp["ffn_norm_w"][0],
                              inp["final_norm_w"]], axis=0))
    cw = np.asarray(inp["dn_conv_w"][0])
    sh["convw"] = f(cw.reshape(4, 24, 128).transpose(2, 1, 0).reshape(128, 96))
    dn = np.zeros((16, 2), np.float32)
    dn[0:8, 0] = np.asarray(inp["dn_a_log"][0]); dn[0:8, 1] = np.asarray(inp["dn_dt_bias"][0])
    sh["dnsc"] = dn
    sh["dnw"] = f(np.asarray(inp["dn_norm_w"][0]).reshape(128, 1))
    sh["mu"] = f(np.asarray(inp["rw_mu"][0]).reshape(26, 128).T)
    vs = [np.asarray(inp[n][0]).reshape(8, 128).T for n in ("rw_w0", "rw_a0", "rw_k_k", "rw_k_a", "rw_r_k", "rw_ln_w", "rw_ln_b")]
    sh["rwv"] = f(np.concatenate(vs, axis=1))
    sh["lora"] = f(np.concatenate([np.asarray(inp["rw_w2"][0]), np.asarray(inp["rw_a2"][0])], axis=0))
    sh["g2"] = f(inp["rw_g2"][0])
    return sh


def kernel(**inp):
    sh = prep_shared(inp)
    xs = np.asarray(inp["x"], dtype=np.float32)
    ms = np.asarray(inp["mem"], dtype=np.float32)
    nc = build()
    in_maps = []
    for b in range(8):
        m = dict(sh)
        m["x"] = np.ascontiguousarray(xs[b])
        m["mem"] = np.ascontiguousarray(ms[b])
        in_maps.append(m)
    res = run_bass_kernel_spmd(nc, in_maps, core_ids=list(range(8)))
    return np.stack([np.asarray(r["out"], dtype=np.float32) for r in res.results], axis=0)
```

```python
import contextlib
import math
import numpy as np
import concourse.bass as bass
import concourse.mybir as mybir
from concourse.alu_op_type import AluOpType as ALU
from concourse.bass_utils import run_bass_kernel_spmd

F32 = mybir.dt.float32
BF16 = mybir.dt.bfloat16
F32R = mybir.dt.float32r
AF = mybir.ActivationFunctionType
AX = mybir.AxisListType

D = 2048
S = 2048
NT = 16
MEM = 256
DNC = 4112
INC = 7440
FF = 8192
EPS = 1e-6
CDEC = -math.exp(-0.5)
DBG_DN = 8
DBG_RW = 8
DBG_CH = NT
DBG_STEP = 99


class Sem:
    __slots__ = ("h", "name")

    def __init__(self, h, name):
        self.h = h
        self.name = name


class Buf:
    __slots__ = ("name", "w", "r", "dsem", "dcount", "excl")

    def __init__(self, name):
        self.name = name
        self.excl = False
        self.w = None
        self.r = {}
        self.dsem = None
        self.dcount = 0


class Eng:
    def __init__(self, name, h, sem):
        self.name = name
        self.h = h
        self.sem = sem
        self.count = 0
        self.waited = {}


class T:
    def __init__(self, t, name):
        self.t = t
        self.b = Buf(name)


class TV:
    def __init__(self, ap, b):
        self.t = ap
        self.b = b


class K:
    def __init__(self, nc):
        self.nc = nc
        self.es = contextlib.ExitStack()
        self.pe = self._eng("pe", nc.tensor)
        self.dve = self._eng("dve", nc.vector)
        self.act = self._eng("act", nc.scalar)
        self.pool = self._eng("pool", nc.gpsimd)
        self.sp = self._eng("sp", nc.sync)
        self.ninst = 0
        self._psi = 0
        self.psf = []
        self._ev = 0
        self.slots = []
        self.psfree = list(range(8))

    def new_sem(self, name):
        return Sem(self.es.enter_context(self.nc.semaphore(name)), name)

    def _eng(self, name, h):
        return Eng(name, h, self.new_sem("s_" + name))

    def sb(self, name, shape, dt, stack=None):
        t = (stack or self.es).enter_context(self.nc.sbuf_tensor(name, list(shape), dt))
        return T(t, name)

    def _deps(self, eng, reads, writes, extra=()):
        deps = {}
        for b in reads:
            if b.w is not None:
                s, v = b.w
                if v > deps.get(s, 0):
                    deps[s] = v
            if b.excl:
                for s, v in b.r.items():
                    if s is not eng.sem and v > deps.get(s, 0):
                        deps[s] = v
        for b in writes:
            if b.w is not None and not (eng is self.pe and b.w[0] is self.pe.sem):
                s, v = b.w
                if v > deps.get(s, 0):
                    deps[s] = v
            for s, v in b.r.items():
                if v > deps.get(s, 0):
                    deps[s] = v
        for s, v in extra:
            if v > deps.get(s, 0):
                deps[s] = v
        for s, v in deps.items():
            if eng.waited.get(s, 0) < v:
                eng.h.wait_ge(s.h, v)
                eng.waited[s] = v

    def op(self, eng, fn, reads=(), writes=()):
        self._deps(eng, reads, writes)
        inst = fn(eng.h)
        eng.count += 1
        inst.then_inc(eng.sem.h, 1)
        self.ninst += 1
        c = eng.count
        s = eng.sem
        for b in reads:
            b.r[s] = c
        for b in writes:
            b.w = (s, c)
            b.r = {}
        return inst

    def dma(self, q, out, in_, reads, writes, slot, **kw):
        if slot.dsem is None:
            slot.dsem = self.new_sem("d_" + slot.name)
            self.slots.append(slot)
        extra = [(slot.dsem, slot.dcount)] if slot.dcount else []
        self._deps(q, reads, writes, extra)
        inst = q.h.dma_start(out=out, in_=in_, **kw)
        slot.dcount += 16
        inst.then_inc(slot.dsem.h, 16)
        self.ninst += 1
        for b in reads:
            b.r[slot.dsem] = slot.dcount
        for b in writes:
            b.w = (slot.dsem, slot.dcount)
            b.r = {}
        return inst

    def ps(self):
        p = self.psf[self._psi % 8]
        self._psi += 1
        return p

    def psA(self):
        return self.psf[self.psfree.pop(0)]

    def psF(self, *ps_):
        for p in ps_:
            self.psfree.append(self.psf.index(p))

    def need(self, m):
        while len(self.psfree) < m:
            yield

    def barrier(self):
        engs = [self.pe, self.dve, self.act, self.pool, self.sp]
        for e in engs:
            for o in engs:
                if o is not e and o.count and e.waited.get(o.sem, 0) < o.count:
                    e.h.wait_ge(o.sem.h, o.count)
                    e.waited[o.sem] = o.count
            for sl in self.slots:
                if e.waited.get(sl.dsem, 0) < sl.dcount:
                    e.h.wait_ge(sl.dsem.h, sl.dcount)
                    e.waited[sl.dsem] = sl.dcount

    def mm(self, out, lhsT, rhs, start, stop, R, W):
        return self.op(self.pe, lambda e: e.matmul(out, lhsT, rhs, start=start, stop=stop), R, W)

    def tr(self, out, in_, ident, R, W):
        return self.op(self.pe, lambda e: e.transpose(out, in_, ident), R, W)

    def tt(self, eng, out, a, b, op, R, W):
        return self.op(eng, lambda e: e.tensor_tensor(out=out, in0=a, in1=b, op=op), R, W)

    def ts(self, eng, out, a, s1, s2, op0, op1, R, W):
        if op1 is None:
            return self.op(eng, lambda e: e.tensor_scalar(out=out, in0=a, scalar1=s1, scalar2=None, op0=op0), R, W)
        return self.op(eng, lambda e: e.tensor_scalar(out=out, in0=a, scalar1=s1, scalar2=s2, op0=op0, op1=op1), R, W)

    def stt(self, out, a, s, b, op0, op1, R, W):
        return self.op(self.dve, lambda e: e.scalar_tensor_tensor(out=out, in0=a, scalar=s, in1=b, op0=op0, op1=op1), R, W)

    def actf(self, out, in_, func, R, W, **kw):
        return self.op(self.act, lambda e: e.activation(out=out, in_=in_, func=func, **kw), R, W)

    def cp(self, out, in_, R, W, eng=None):
        if eng is None:
            self._ev += 1
            eng = self.act if (self._ev & 1) else self.dve
        if eng is self.act:
            return self.op(eng, lambda e: e.copy(out, in_), R, W)
        return self.op(eng, lambda e: e.tensor_copy(out, in_), R, W)


def build(debug=None):
    nc = bass.Bass("TRN2", target_bir_lowering=False)
    k = K(nc)

    def din(name, shape):
        return nc.dram_tensor(name, list(shape), F32, kind="ExternalInput").ap()

    x = din("x", [S, D]); mem = din("mem", [MEM, D])
    w_in = din("w_in", [D, INC]); w_out = din("w_out", [D, D])
    wq = din("xa_wq", [D, 512]); wk = din("xa_wk", [D, 512]); wv = din("xa_wv", [D, 512]); wo = din("xa_wo", [512, D])
    w1 = din("ffn_w1", [D, FF]); w2 = din("ffn_w2", [FF, D])
    nrm = din("norms", [5, D])
    convw = din("convw", [128, 24 * 4])
    dnsc = din("dnsc", [16, 2])
    dnw = din("dnw", [128, 1])
    mu = din("mu", [128, 26])
    rwv = din("rwv", [128, 7 * 8])
    lora = din("lora", [128, 1024])
    g2 = din("g2", [128, 1024])
    out = nc.dram_tensor("out", [S, D], F32, kind="ExternalOutput").ap()
    pT = T(nc.dram_tensor("pT", [INC, S], F32, kind="Internal").ap(), "pT")
    pT.bl = [Buf(f"pT{i}") for i in range(59)]
    if debug == "p4":
        oTd = T(nc.dram_tensor("oT_in", [D, S], F32, kind="ExternalInput").ap(), "oTd")
        oTd.bl = [Buf(f"oT{i}") for i in range(16)]
        oT_dt = F32
    else:
        oTd = T(nc.dram_tensor("oT", [D, S], BF16, kind="Internal").ap(), "oTd")
        oTd.bl = [Buf(f"oT{i}") for i in range(16)]
        oT_dt = BF16
    wsc = T(nc.dram_tensor("wsc", [38, 128, 8192], BF16, kind="Internal").ap(), "wsc")
    wsc.bl = [Buf(f"wsc{i}") for i in range(38)]
    dbg = None
    if debug == "p1":
        dbg = nc.dram_tensor("dbg", [INC, S], F32, kind="ExternalOutput").ap()
    if debug == "p2":
        dbg = nc.dram_tensor("dbg", [D, S], F32, kind="ExternalOutput").ap()

    for i in range(8):
        p = T(k.es.enter_context(nc.psum_tensor(f"ps{i}", [128, 512], F32)), f"ps{i}")
        p.b.excl = True
        k.psf.append(p)

    ones = k.sb("ones", [128, 128], F32)
    ident = k.sb("ident", [128, 128], F32)
    identb = k.sb("identb", [128, 128], BF16)
    epsT = k.sb("epsT", [128, 1], F32)
    k.op(k.pool, lambda e: e.memset(ones.t[:], 1.0), [], [ones.b])
    k.op(k.pool, lambda e: e.memset(epsT.t[:], EPS), [], [epsT.b])
    k.op(k.pool, lambda e: e.affine_select(out=ident.t[:], in_=ones.t[:], pattern=[[-1, 128]], compare_op=ALU.is_equal,
                                           fill=0.0, base=0, channel_multiplier=1), [ones.b], [ident.b])
    k.op(k.dve, lambda e: e.tensor_copy(identb.t[:], ident.t[:]), [ident.b], [identb.b])

    G = {}

    def alloc_norm(stack, tag):
        G["wbc"] = k.sb("wbc" + tag, [128, D], F32, stack)
        G["uns"] = [k.sb(f"un{i}" + tag, [128, D], BF16, stack) for i in range(2)]
        G["un"] = G["uns"][0]
        G["junk"] = k.sb("junk" + tag, [128, D], BF16, stack)
        G["ss"] = k.sb("ss" + tag, [128, 4], F32, stack)
        G["ssb"] = [Buf(f"ss{i}" + tag) for i in range(4)]
        G["ctr"] = 0

    def load_wbc(i):
        wbc = G["wbc"]
        k.dma(k.sp, wbc.t[:], nrm[i:i + 1, :].to_broadcast([128, D]), [], [wbc.b], wbc.b)
    wbufs = []
    wctr = [0]

    def precast_p4_weights():
        blocks = []
        for cb in range(4):
            blocks.append((cb, w_out[:, cb * 512:(cb + 1) * 512], 16))
        blocks.append((4, wq[:, :], 16))
        blocks.append((5, wo[:, :], 4))
        for fb in range(16):
            blocks.append((6 + fb, w1[:, fb * 512:(fb + 1) * 512], 16))
        for cb in range(4):
            for sub in range(4):
                blocks.append((22 + cb * 4 + sub, w2[sub * 2048:(sub + 1) * 2048, cb * 512:(cb + 1) * 512], 16))
        return blocks

    pc_blocks = precast_p4_weights()

    def bg_next(n):
        for _ in range(n):
            if not pc_blocks:
                return
            idx, src, kc = pc_blocks.pop(0)
            k.dma(k.pool, wsc.t[idx].rearrange("p (kc c) -> p kc c", kc=kc), src.rearrange("(kc p) c -> p kc c", p=128),
                  [], [wsc.bl[idx]], wsc.bl[idx])

    def alloc_wbufs(stack, tag):
        wbufs.clear()
        wbufs.extend(k.sb(f"wbuf{tag}{i}", [128, 8192], BF16, stack) for i in range(2))

    def load_w(src_ap, kc, cache=None):
        wb = wbufs[wctr[0] % len(wbufs)]
        wctr[0] += 1
        ncol = src_ap.shape[1]
        view = wb.t[:, 0:kc * ncol].rearrange("p (kc c) -> p kc c", kc=kc)
        if cache is not None:
            k.dma(k.pool, wb.t[:, :], wsc.t[cache[0]], [wsc.bl[cache[0]]], [wb.b], wb.b)
            return wb, view
        k.dma(k.pool, view, src_ap.rearrange("(kc p) c -> p kc c", p=128), [], [wb.b], wb.b)
        return wb, view

    def norm_transpose(src, srcB, dstT, dstB, dst_cols, slot):
        wbc, ss, junk = G["wbc"], G["ss"], G["junk"]
        un = G["uns"][G["ctr"] % 2]
        G["ctr"] += 1
        sb_ = G["ssb"][slot]
        k.actf(junk.t[:], src, AF.Square, [srcB], [junk.b, sb_], accum_out=ss.t[:, slot:slot + 1])
        k.actf(ss.t[:, slot:slot + 1], ss.t[:, slot:slot + 1], AF.Sqrt, [sb_, epsT.b], [sb_], scale=1.0 / D, bias=epsT.t[:, 0:1])
        k.op(k.dve, lambda e: e.reciprocal(ss.t[:, slot:slot + 1], ss.t[:, slot:slot + 1]), [sb_], [sb_])
        k.stt(un.t[:], src, ss.t[:, slot:slot + 1], wbc.t[:], ALU.mult, ALU.mult, [srcB, sb_, wbc.b], [un.b])
        for half in range(2):
            p = k.ps()
            pv = p.t[:].bitcast(BF16)
            for j in range(8):
                kc = half * 8 + j
                k.tr(pv[:, j * 128:(j + 1) * 128], un.t[:, kc * 128:(kc + 1) * 128], identb.t[:], [un.b, identb.b], [p.b])
            k.cp(dstT[:, half * 8:(half + 1) * 8, dst_cols], pv.rearrange("p (j c) -> p j c", c=128), [p.b], [dstB])

    if debug != "p4":
        with contextlib.ExitStack() as st1:
            alloc_wbufs(st1, "a")
            alloc_norm(st1, "a")
            uT = k.sb("uT", [128, 16, S], BF16, st1)
            uTb = [Buf(f"uT{i}") for i in range(4)]
            xts = [k.sb(f"xt{i}", [128, D], F32, st1) for i in range(2)]
            stg = [k.sb(f"stg{i}", [128, 512], F32, st1) for i in range(4)]
            load_wbc(0)
            si = 0

            def p1_block(c0, wb, wvw, tbs):
                nonlocal si
                ncol = min(512, INC - c0)
                for g0 in range(0, ncol, 128):
                    M = min(128, ncol - g0)
                    for tb in tbs:
                        p = k.ps()
                        for kc in range(16):
                            k.mm(p.t[0:M, :], wvw[:, kc, g0:g0 + M], uT.t[:, kc, tb * 512:(tb + 1) * 512], kc == 0, kc == 15,
                                 [wb.b, uTb[tb]], [p.b])
                        sg_ = stg[si % 4]; si += 1
                        k.cp(sg_.t[0:M, :], p.t[0:M, :], [p.b], [sg_.b])
                        k.dma(k.sp, pT.t[c0 + g0:c0 + g0 + M, tb * 512:(tb + 1) * 512], sg_.t[0:M, :], [sg_.b], [pT.bl[(c0 + g0) // 128]], sg_.b)
            wb0, wvw0 = load_w(w_in[:, 0:512], 16)
            for tb in range(4):
                for n in range(4 * tb, 4 * tb + 4):
                    xt = xts[n % 2]
                    k.dma(k.sp, xt.t[:], x[n * 128:(n + 1) * 128, :], [], [xt.b], xt.b)
                    norm_transpose(xt.t[:], xt.b, uT.t, uTb[n // 4], slice(n * 128, (n + 1) * 128), n % 4)
                p1_block(0, wb0, wvw0, [tb])
            for c0 in range(512, INC, 512):
                ncol = min(512, INC - c0)
                wb, wvw = load_w(w_in[:, c0:c0 + ncol], 16)
                p1_block(c0, wb, wvw, range(4))
            if debug == "p1":
                for r0 in range(0, INC, 128):
                    M = min(128, INC - r0)
                    xt = xts[(r0 // 128) % 2]
                    k.dma(k.sp, xt.t[0:M, :], pT.t[r0:r0 + M, :], [pT.bl[r0 // 128]], [xt.b], xt.b)
                    k.dma(k.sp, dbg[r0:r0 + M, :], xt.t[0:M, :], [xt.b], [], xt.b)
                k._deps(k.sp, [], [xts[0].b, xts[1].b])
        if debug == "p1":
            k.es.close()
            return nc

    if debug != "p4":
        k.barrier()
        build_mixers(nc, k, pT, oTd, convw, dnsc, dnw, mu, rwv, lora, g2, ones, ident, bg_next)
        bg_next(99)
        if debug == "p2":
            k.barrier()
            with contextlib.ExitStack() as st:
                a = k.sb("dba", [128, S], BF16, st); b = k.sb("dbb", [128, S], F32, st)
                for r0 in range(0, D, 128):
                    k.dma(k.sp, a.t[:], oTd.t[r0:r0 + 128, :], [oTd.bl[r0 // 128]], [a.b], a.b)
                    k.cp(b.t[:], a.t[:], [a.b], [b.b])
                    k.dma(k.sp, dbg[r0:r0 + 128, :], b.t[:], [b.b], [], b.b)
                k._deps(k.sp, [], [b.b])
            k.es.close()
            return nc

    k.barrier()
    if debug == "p4":
        bg_next(99)
    st4 = contextlib.ExitStack()
    alloc_wbufs(st4, "b")
    alloc_norm(st4, "b")
    wbc, un, ss = G["wbc"], G["junk"], G["ss"]
    KT = k.sb("KT", [128, 4, MEM], BF16, st4)
    Vm = k.sb("Vm", [128, 2, 512], BF16, st4)
    hnT = k.sb("hnT", [128, 16, 512], BF16, st4)
    h = k.sb("h", [128, 4, D], F32, st4)
    hB = [Buf(f"h{j}") for j in range(4)]
    hid = k.sb("hid", [128, 64, 512], BF16, st4)
    hidB = [Buf(f"hid{i}") for i in range(16)]
    qT = k.sb("qT", [128, 4, 512], BF16, st4)
    oxT = k.sb("oxT", [128, 4, 512], BF16, st4)
    atw = [dict(pr=k.sb(f"pr{i}", [128, MEM], F32, st4), prn=k.sb(f"prn{i}", [128, MEM], BF16, st4),
                prT=k.sb(f"prT{i}", [128, 2, 128], BF16, st4), sm=k.sb(f"sm{i}", [128, 4], F32, st4)) for i in range(4)]

    def run_rr(gens):
        gens = list(gens)
        while gens:
            for g_ in list(gens):
                try:
                    next(g_)
                except StopIteration:
                    gens.remove(g_)
    rl = [k.sb(f"rl{i}", [128, 512], F32, st4) for i in range(2)]
    memt = [k.sb(f"memt{i}", [128, D], F32, st4) for i in range(1)]

    load_wbc(2)
    for mt in range(2):
        m_ = memt[0]
        k.dma(k.sp, m_.t[:], mem[mt * 128:(mt + 1) * 128, :], [], [m_.b], m_.b)
        norm_transpose(m_.t[:], m_.b, hnT.t, hnT.b, slice(mt * 128, (mt + 1) * 128), mt)
    wb, wvw = load_w(wk[:, :], 16)
    for hd in range(4):
        p = k.ps()
        for kc in range(16):
            k.mm(p.t[:, 0:MEM], wvw[:, kc, hd * 128:(hd + 1) * 128], hnT.t[:, kc, 0:MEM], kc == 0, kc == 15, [wb.b, hnT.b], [p.b])
        k.cp(KT.t[:, hd, :], p.t[:, 0:MEM], [p.b], [KT.b])
    wb, wvw = load_w(wv[:, :], 16)
    for mc in range(2):
        p = k.ps()
        for kc in range(16):
            k.mm(p.t[:, :], hnT.t[:, kc, mc * 128:(mc + 1) * 128], wvw[:, kc, :], kc == 0, kc == 15, [wb.b, hnT.b], [p.b])
        k.cp(Vm.t[:, mc, :], p.t[:, :], [p.b], [Vm.b])

    oTb_view = hid.t[:, 0:16, :]
    oTb_bufs = hidB[0:4]
    for TB in range(4):
        t0 = TB * 512
        if oT_dt == BF16:
            k.dma(k.pool, oTb_view, oTd.t[:, t0:t0 + 512].rearrange("(kc p) t -> p kc t", p=128), oTd.bl, oTb_bufs, oTb_bufs[0])
        else:
            k.dma(k.pool, oTb_view, oTd.t[:, t0:t0 + 512].rearrange("(kc p) t -> p kc t", p=128), oTd.bl, oTb_bufs, oTb_bufs[0])
        for cb in range(4):
            wb, wvw = load_w(w_out[:, cb * 512:(cb + 1) * 512], 16, (cb, TB == 0))
            if cb == 0:
                for j in range(4):
                    k.dma(k.sp, h.t[:, j, :], x[t0 + j * 128:t0 + (j + 1) * 128, :], [], [hB[j]], hB[j])
            for j in range(4):
                p = k.ps()
                for kc in range(16):
                    k.mm(p.t[:, :], oTb_view[:, kc, j * 128:(j + 1) * 128], wvw[:, kc, :], kc == 0, kc == 15, [wb.b] + oTb_bufs, [p.b])
                hs = h.t[:, j, cb * 512:(cb + 1) * 512]
                k.tt(k.dve, hs, p.t[:, :], hs, ALU.add, [p.b, hB[j]], [hB[j]])
        load_wbc(1)
        for j in range(4):
            norm_transpose(h.t[:, j, :], hB[j], hnT.t, hnT.b, slice(j * 128, (j + 1) * 128), j)
        wb, wvw = load_w(wq[:, :], 16, (4, TB == 0))
        for hd in range(4):
            p = k.ps()
            for kc in range(16):
                k.mm(p.t[:, :], wvw[:, kc, hd * 128:(hd + 1) * 128], hnT.t[:, kc, :], kc == 0, kc == 15, [wb.b, hnT.b], [p.b])
            k.actf(qT.t[:, hd, :], p.t[:, :], AF.Copy, [p.b], [qT.b], scale=128 ** -0.5)
        def attn_worker(w_):
            a_ = atw[w_]
            pr, prn, prT, sm = a_["pr"], a_["prn"], a_["prT"], a_["sm"]
            for idx in range(w_, 16, 4):
                j, hd = divmod(idx, 4)
                yield from k.need(1)
                p = k.psA()
                k.mm(p.t[:, 0:MEM], qT.t[:, hd, j * 128:(j + 1) * 128], KT.t[:, hd, :], True, True, [qT.b, KT.b], [p.b])
                yield
                k.op(k.dve, lambda e: e.tensor_reduce(out=sm.t[:, 0:1], in_=p.t[:, 0:MEM], axis=AX.X, op=ALU.max, negate=True),
                     [p.b], [sm.b])
                yield
                k.actf(pr.t[:], p.t[:, 0:MEM], AF.Exp, [p.b, sm.b], [pr.b, sm.b], bias=sm.t[:, 0:1], accum_out=sm.t[:, 1:2])
                k.psF(p)
                yield
                k.op(k.dve, lambda e: e.reciprocal(sm.t[:, 2:3], sm.t[:, 1:2]), [sm.b], [sm.b])
                yield
                k.ts(k.dve, prn.t[:], pr.t[:], sm.t[:, 2:3], None, ALU.mult, None, [pr.b, sm.b], [prn.b])
                yield
                yield from k.need(1)
                p2 = k.psA()
                pv = p2.t[:].bitcast(BF16)
                for mc in range(2):
                    k.tr(pv[:, mc * 128:(mc + 1) * 128], prn.t[:, mc * 128:(mc + 1) * 128], identb.t[:], [prn.b, identb.b], [p2.b])
                yield
                k.cp(prT.t[:], pv[:, 0:256].rearrange("p (m c) -> p m c", c=128), [p2.b], [prT.b])
                k.psF(p2)
                yield
                yield from k.need(1)
                p3 = k.psA()
                for mc in range(2):
                    k.mm(p3.t[:, 0:128], Vm.t[:, mc, hd * 128:(hd + 1) * 128], prT.t[:, mc, :], mc == 0, mc == 1, [Vm.b, prT.b], [p3.b])
                yield
                k.cp(oxT.t[:, hd, j * 128:(j + 1) * 128], p3.t[:, 0:128], [p3.b], [oxT.b])
                k.psF(p3)
                yield
        run_rr([attn_worker(w_) for w_ in range(4)])
        wb, wvw = load_w(wo[:, :], 4, (5, TB == 0))
        for cb in range(4):
            for j in range(4):
                p = k.ps()
                for kc in range(4):
                    k.mm(p.t[:, :], oxT.t[:, kc, j * 128:(j + 1) * 128], wvw[:, kc, cb * 512:(cb + 1) * 512], kc == 0, kc == 3, [wb.b, oxT.b], [p.b])
                hs = h.t[:, j, cb * 512:(cb + 1) * 512]
                k.tt(k.dve, hs, p.t[:, :], hs, ALU.add, [p.b, hB[j]], [hB[j]])
        load_wbc(3)
        for j in range(4):
            norm_transpose(h.t[:, j, :], hB[j], hnT.t, hnT.b, slice(j * 128, (j + 1) * 128), j)
        for fb in range(16):
            wb, wvw = load_w(w1[:, fb * 512:(fb + 1) * 512], 16, (6 + fb, TB == 0))
            for fc in range(4):
                p = k.ps()
                for kc in range(16):
                    k.mm(p.t[:, :], wvw[:, kc, fc * 128:(fc + 1) * 128], hnT.t[:, kc, :], kc == 0, kc == 15, [wb.b, hnT.b], [p.b])
                r_ = rl[(fb * 4 + fc) % 2]
                k.actf(r_.t[:], p.t[:, :], AF.Relu, [p.b], [r_.b])
                k.tt(k.dve, hid.t[:, fb * 4 + fc, :], r_.t[:], r_.t[:], ALU.mult, [r_.b], [hidB[fb]])
        for cb in range(4):
            accs = [k.ps() for _ in range(4)]
            for sub in range(4):
                wb, wvw = load_w(w2[sub * 2048:(sub + 1) * 2048, cb * 512:(cb + 1) * 512], 16, (22 + cb * 4 + sub, TB == 0))
                for j in range(4):
                    for fc in range(16):
                        f = sub * 16 + fc
                        k.mm(accs[j].t[:, :], hid.t[:, f, j * 128:(j + 1) * 128], wvw[:, fc, :], f == 0, f == 63,
                             [wb.b, hidB[f // 4]], [accs[j].b])
            for j in range(4):
                hs = h.t[:, j, cb * 512:(cb + 1) * 512]
                k.tt(k.dve, hs, accs[j].t[:, :], hs, ALU.add, [accs[j].b, hB[j]], [hB[j]])
        load_wbc(4)
        for j in range(4):
            hj = h.t[:, j, :]
            sb_ = G["ssb"][j]
            k.actf(un.t[:], hj, AF.Square, [hB[j]], [un.b, sb_], accum_out=ss.t[:, j:j + 1])
            k.actf(ss.t[:, j:j + 1], ss.t[:, j:j + 1], AF.Sqrt, [sb_, epsT.b], [sb_], scale=1.0 / D, bias=epsT.t[:, 0:1])
            k.op(k.dve, lambda e: e.reciprocal(ss.t[:, j:j + 1], ss.t[:, j:j + 1]), [sb_], [sb_])
            k.stt(hj, hj, ss.t[:, j:j + 1], wbc.t[:], ALU.mult, ALU.mult, [hB[j], sb_, wbc.b], [hB[j]])
        for j in range(4):
            k.dma(k.sp, out[t0 + j * 128:t0 + (j + 1) * 128, :], h.t[:, j, :], [hB[j]], [], hB[j])
    k._deps(k.sp, [], hB)
    st4.close()
    k.es.close()
    return nc


def build_mixers(nc, k, pT, oTd, convw_d, dnsc_d, dnw_d, mu_d, rwv_d, lora_d, g2_d, ones, ident, bg_next):
    st = contextlib.ExitStack()
    r = lambda ap: ap.bitcast(F32R)
    NB = 10
    big = [k.sb(f"big{i}", [128, S], F32, st) for i in range(NB)]
    free = list(range(NB))

    def balloc():
        return big[free.pop(0)]

    def bfree(*ts):
        for t_ in ts:
            free.append(big.index(t_))

    def sm(name, shape, dt=F32):
        return k.sb("m_" + name, shape, dt, st)

    Ls = sm("Ls", [128, 128]); Li = sm("Li", [128, 128]); UU = sm("UU", [128, 256])
    blk = sm("blk", [128, 128]); rmask = sm("rmask", [128, S], BF16); selh = sm("selh", [16, 128])
    epsG = sm("epsG", [128, 2])

    def asel(out, pat, cm, op, R, W):
        k.op(k.pool, lambda e: e.affine_select(out=out, in_=ones.t[:], pattern=pat, compare_op=op, fill=0.0, base=0,
                                               channel_multiplier=cm), [ones.b] + R, W)
    asel(Ls.t[:], [[-1, 128]], 1, ALU.is_gt, [], [Ls.b])
    k.ts(k.dve, Ls.t[:], Ls.t[:], -1.0, None, ALU.mult, None, [Ls.b], [Ls.b])
    Li2 = sm("Li2", [128, 256]); II2 = sm("II2", [128, 256])
    for i_ in range(2):
        asel(Li2.t[:, i_ * 128:(i_ + 1) * 128], [[-1, 128]], 1, ALU.is_ge, [], [Li2.b])
        k.cp(II2.t[:, i_ * 128:(i_ + 1) * 128], ident.t[:], [ident.b], [II2.b], eng=k.pool)
    asel(Li.t[:], [[-1, 128]], 1, ALU.is_ge, [], [Li.b])
    asel(UU.t[:, 0:128], [[1, 128]], -1, ALU.is_gt, [], [UU.b])
    asel(UU.t[:, 128:256], [[1, 128]], -1, ALU.is_ge, [], [UU.b])
    k.op(k.pool, lambda e: e.memset(blk.t[:], 0.0), [], [blk.b])
    k.op(k.pool, lambda e: e.memset(blk.t[0:64, 0:64], 1.0), [], [blk.b])
    k.op(k.pool, lambda e: e.memset(blk.t[64:128, 64:128], 1.0), [], [blk.b])
    k.op(k.pool, lambda e: e.memset(rmask.t[:], 1.0), [], [rmask.b])
    k.op(k.pool, lambda e: e.memset(rmask.t[:].rearrange("p (c t) -> p c t", t=128)[:, :, 0:1], 0.0), [], [rmask.b])
    k.op(k.pool, lambda e: e.memset(epsG.t[:, 0:1], 64e-5), [], [epsG.b])
    k.op(k.pool, lambda e: e.memset(epsG.t[:, 1:2], 1e-6), [], [epsG.b])
    convw = sm("convw", [128, 96]); dnsc = sm("dnsc", [16, 2]); dnw = sm("dnw", [128, 1])
    mu = sm("mu", [128, 26]); omm = sm("omm", [128, 26]); rwv = sm("rwv", [128, 56])
    for t_, d_ in ((convw, convw_d), (dnsc, dnsc_d), (dnw, dnw_d), (mu, mu_d), (rwv, rwv_d)):
        k.dma(k.sp, t_.t[:], d_, [], [t_.b], t_.b)
    k.ts(k.dve, omm.t[:], mu.t[:], -1.0, 1.0, ALU.mult, ALU.add, [mu.b], [omm.b])

    def sq(name, w=128):
        return sm(name, [128, w])
    def run_rr(gens):
        gens = list(gens)
        while gens:
            for g_ in list(gens):
                try:
                    next(g_)
                except StopIteration:
                    gens.remove(g_)

    def neumann_multi(probs, nlev):
        for pr in probs:
            pr["Ao"] = pr["A1"]; pr["BPo"] = pr["BP"][0]
        for lv in range(nlev):
            last = lv == nlev - 1
            yield from k.need((1 if last else 2) * len(probs))
            for pr in probs:
                Ao, BPo = pr["Ao"], pr["BPo"]
                pr["pa"] = k.psA()
                if last:
                    k.mm(pr["pa"].t[:, 0:128], r(Ao.t[:]), r(BPo.t[:, 128:256]), True, True, [Ao.b, BPo.b], [pr["pa"].b])
                else:
                    k.mm(pr["pa"].t[:, 0:256], r(Ao.t[:]), r(BPo.t[:]), True, True, [Ao.b, BPo.b], [pr["pa"].b])
                    pr["pb"] = k.psA()
                    k.mm(pr["pb"].t[:, 0:128], r(BPo.t[:, 0:128]), r(Ao.t[:]), True, True, [Ao.b, BPo.b], [pr["pb"].b])
            yield
            for pr in probs:
                BPo = pr["BPo"]
                if last:
                    k.tt(k.dve, r(pr["Tout"].t[:]), pr["pa"].t[:, 0:128], BPo.t[:, 128:256], ALU.add, [pr["pa"].b, BPo.b], [pr["Tout"].b])
                    k.psF(pr["pa"])
                else:
                    An, BPn = pr["Ap"][lv % 2], pr["BP"][(lv + 1) % 2]
                    k.cp(r(An.t[:]), pr["pb"].t[:, 0:128], [pr["pb"].b], [An.b], eng=k.act)
                    k.cp(r(BPn.t[:, 0:128]), pr["pa"].t[:, 0:128], [pr["pa"].b], [BPn.b], eng=k.dve)
                    k.tt(k.dve, r(BPn.t[:, 128:256]), pr["pa"].t[:, 128:256], BPo.t[:, 128:256], ALU.add, [pr["pa"].b, BPo.b], [BPn.b])
                    k.psF(pr["pa"], pr["pb"])
                    pr["Ao"], pr["BPo"] = An, BPn
            yield

    def neumann_pairs(pairs, nlev):
        for pr in pairs:
            pr["Ao"] = pr["A1_2"]; pr["BPo"] = pr["BP2"][0]
        for lv in range(nlev):
            last = lv == nlev - 1
            yield from k.need((1 if last else 2) * len(pairs))
            for pr in pairs:
                Ao, BPo = pr["Ao"], pr["BPo"]
                pr["pa"] = k.psA()
                if not last:
                    pr["pb"] = k.psA()
                for i in range(2):
                    a_i = r(Ao.t[:, i * 128:(i + 1) * 128])
                    if last:
                        k.mm(pr["pa"].t[:, i * 128:(i + 1) * 128], a_i, r(BPo.t[:, i * 256 + 128:(i + 1) * 256]), True, True,
                             [Ao.b, BPo.b], [pr["pa"].b])
                    else:
                        k.mm(pr["pa"].t[:, i * 256:(i + 1) * 256], a_i, r(BPo.t[:, i * 256:(i + 1) * 256]), True, True,
                             [Ao.b, BPo.b], [pr["pa"].b])
                        k.mm(pr["pb"].t[:, i * 128:(i + 1) * 128], r(BPo.t[:, i * 256:i * 256 + 128]), a_i, True, True,
                             [Ao.b, BPo.b], [pr["pb"].b])
            yield
            for pr in pairs:
                BPo = pr["BPo"]
                bpo3 = BPo.t[:].rearrange("p (i c) -> p i c", c=256)
                if last:
                    k.tt(k.dve, r(pr["Tout2"].t[:].rearrange("p (i c) -> p i c", c=128)),
                         pr["pa"].t[:, 0:256].rearrange("p (i c) -> p i c", c=128), bpo3[:, :, 128:256], ALU.add,
                         [pr["pa"].b, BPo.b], [pr["Tout2"].b])
                    k.psF(pr["pa"])
                else:
                    An, BPn = pr["Ap2"][lv % 2], pr["BP2"][(lv + 1) % 2]
                    bpn3 = BPn.t[:].rearrange("p (i c) -> p i c", c=256)
                    pa3 = pr["pa"].t[:, :].rearrange("p (i c) -> p i c", c=256)
                    k.cp(r(An.t[:]), pr["pb"].t[:, 0:256], [pr["pb"].b], [An.b], eng=k.act)
                    k.cp(r(bpn3[:, :, 0:128]), pa3[:, :, 0:128], [pr["pa"].b], [BPn.b], eng=k.act)
                    k.tt(k.dve, r(bpn3[:, :, 128:256]), pa3[:, :, 128:256], bpo3[:, :, 128:256], ALU.add, [pr["pa"].b, BPo.b], [BPn.b])
                    k.psF(pr["pa"], pr["pb"])
                    pr["Ao"], pr["BPo"] = An, BPn
            yield

    def load_rows(dst, r0, nrows=128):
        k.dma(k.sp, dst.t[0:nrows, :], pT.t[r0:r0 + nrows, :], pT.bl[r0 // 128:(r0 + nrows - 1) // 128 + 1], [dst.b], dst.b)

    ones_bf = sm("ones_bf", [128, 128], BF16); blk_bf = sm("blk_bf", [128, 128], BF16)
    k.cp(ones_bf.t[:], ones.t[:], [ones.b], [ones_bf.b], eng=k.dve)
    k.cp(blk_bf.t[:], blk.t[:], [blk.b], [blk_bf.b], eng=k.dve)

    def bfv(t_):
        return t_.t[:].bitcast(BF16)[:, 0:S]

    def psum_bcast_sum(src, lhsT, lhsTb, fn):
        sv = bfv(src)
        for tb in range(4):
            p = k.ps()
            k.mm(p.t[:, :], lhsT, sv[:, tb * 512:(tb + 1) * 512], True, True, [lhsTb, src.b], [p.b])
            fn(tb, p)

    obf = [sm("obf0", [128, S], BF16)] * 2
    octr = [0]

    gc16 = balloc()
    st_dn = contextlib.ExitStack()

    def smd(name, shape, dt=F32):
        return k.sb("m_" + name, shape, dt, st_dn)
    gcT = smd("gcT", [128, 256]); betaT = smd("betaT", [128, 256]); kdT = smd("kdT", [128, 256]); egT = smd("egT", [128, 256])
    bgT = smd("bgT", [128, 16, 8]); negA = smd("negA", [16, 1])
    if True:
        ab = balloc(); t1 = balloc(); t2 = balloc(); beta16 = balloc(); kd16 = balloc()
        R16 = slice(0, 16)
        load_rows(ab, 4096, 16)
        dtb = dnsc.t[:, 1:2]
        k.actf(t1.t[R16, :], ab.t[R16, :], AF.Abs, [ab.b, dnsc.b], [t1.b], bias=dtb)
        k.actf(t1.t[R16, :], t1.t[R16, :], AF.Exp, [t1.b], [t1.b], scale=-1.0)
        k.actf(t1.t[R16, :], t1.t[R16, :], AF.Ln, [t1.b, ones.b], [t1.b], bias=ones.t[0:16, 0:1])
        k.ts(k.dve, t2.t[R16, :], ab.t[R16, :], dtb, 0.0, ALU.add, ALU.max, [ab.b, dnsc.b], [t2.b])
        k.tt(k.dve, t1.t[R16, :], t1.t[R16, :], t2.t[R16, :], ALU.add, [t1.b, t2.b], [t1.b])
        k.actf(negA.t[:], dnsc.t[:, 0:1], AF.Exp, [dnsc.b], [negA.b])
        k.ts(k.dve, negA.t[:], negA.t[:], -1.0, None, ALU.mult, None, [negA.b], [negA.b])
        k.ts(k.dve, t1.t[R16, :], t1.t[R16, :], negA.t[:, 0:1], None, ALU.mult, None, [t1.b, negA.b], [t1.b])
        k.actf(beta16.t[R16, :], ab.t[R16, :], AF.Sigmoid, [ab.b], [beta16.b])
        k.op(k.dve, lambda e: e.tensor_tensor_scan(gc16.t[R16, :], rmask.t[R16, :], t1.t[R16, :], 0.0, ALU.mult, ALU.add),
             [rmask.b, t1.b], [gc16.b])
        for n in range(NT):
            cs = slice(n * 128, (n + 1) * 128)
            k.ts(k.dve, kd16.t[R16, cs], gc16.t[R16, cs], gc16.t[R16, n * 128 + 127:n * 128 + 128], None, ALU.subtract, None,
                 [gc16.b], [kd16.b])
        k.actf(kd16.t[R16, :], kd16.t[R16, :], AF.Exp, [kd16.b], [kd16.b], scale=-1.0)
        for src, dst in ((gc16, gcT), (beta16, betaT), (kd16, kdT)):
            p = k.ps()
            for n in range(NT):
                k.mm(p.t[:, n * 16:(n + 1) * 16], src.t[R16, n * 128:(n + 1) * 128], ident.t[0:16, 0:16], True, True, [src.b, ident.b], [p.b])
            k.cp(dst.t[:], p.t[:, 0:256], [p.b], [dst.b])
        k.actf(egT.t[:], gcT.t[:], AF.Exp, [gcT.b], [egT.b])
        k.tt(k.dve, bgT.t[:], betaT.t[:].rearrange("p (n r) -> p n r", r=16)[:, :, 8:16],
             egT.t[:].rearrange("p (n r) -> p n r", r=16)[:, :, 0:8], ALU.mult, [betaT.b, egT.b], [bgT.b])
        bfree(ab, t1, t2, beta16, kd16)
    ngcT = smd("ngcT", [128, 256])
    k.ts(k.dve, ngcT.t[:], gcT.t[:], -1.0, None, ALU.mult, None, [gcT.b], [ngcT.b])
    ngcT3 = ngcT.t[:].rearrange("p (n r) -> p n r", r=16)
    gcT3 = gcT.t[:].rearrange("p (n r) -> p n r", r=16)
    betaT3 = betaT.t[:].rearrange("p (n r) -> p n r", r=16)
    kdT3 = kdT.t[:].rearrange("p (n r) -> p n r", r=16)

    WDN = 6

    def sqd(name, w=128):
        return k.sb("m_" + name, [128, w], F32, st_dn)
    St = [sqd("St0"), sqd("St1")]
    qTr = sqd("qTr", S); kTr = sqd("kTr", S); qgr = sqd("qgr", S)
    dnw_t = []
    WDP = 4
    for w_ in range(WDP):
        d_ = {nm: sqd(f"{nm}{w_}", 256) for nm in ("t1_2", "El_2", "MA_2", "MD_2", "at_2", "attnT_2", "nwT_2", "A1_2", "Tout2")}
        d_["Ap2"] = [sqd(f"Ap20_{w_}", 256), sqd(f"Ap21_{w_}", 256)]; d_["BP2"] = [sqd(f"BP20_{w_}", 512), sqd(f"BP21_{w_}", 512)]
        d_["c"] = [{nm: sqd(f"{nm}{w_}_{i_}") for nm in ("kbg", "kd", "vb", "vnew")} for i_ in range(2)]
        dnw_t.append(d_)

    def conv_silu(xr, gi):
        c = balloc()
        w = lambda j: convw.t[:, gi * 4 + j:gi * 4 + j + 1]
        k.ts(k.dve, c.t[:], xr.t[:], w(3), None, ALU.mult, None, [xr.b, convw.b], [c.b])
        for sh in (1, 2, 3):
            k.stt(c.t[:, sh:S], xr.t[:, 0:S - sh], w(3 - sh), c.t[:, sh:S], ALU.mult, ALU.add, [xr.b, convw.b, c.b], [c.b])
        k.actf(c.t[:], c.t[:], AF.Silu, [c.b], [c.b])
        bfree(xr)
        return c

    def l2n(xc, scale, dst):
        sq_ = balloc(); rn = balloc()
        k.actf(bfv(sq_), xc.t[:], AF.Square, [xc.b], [sq_.b])

        def fn(tb, p):
            ts_ = slice(tb * 512, (tb + 1) * 512)
            k.actf(rn.t[:, ts_], p.t[:, :], AF.Ln, [p.b, epsG.b], [rn.b], bias=epsG.t[:, 1:2])
        psum_bcast_sum(sq_, ones_bf.t[:], ones_bf.b, fn)
        k.actf(rn.t[:], rn.t[:], AF.Exp, [rn.b], [rn.b], scale=-0.5)
        k.stt(r(dst.t[:]), xc.t[:], scale, rn.t[:], ALU.mult, ALU.mult, [xc.b, rn.b], [dst.b])
        bfree(sq_, rn, xc)

    pre_ld = None
    for h in range(DBG_DN):
        if pre_ld is None:
            qr = balloc(); load_rows(qr, h * 128)
            kr = balloc(); load_rows(kr, 1024 + h * 128)
            vr = balloc(); load_rows(vr, 2048 + h * 128)
        else:
            qr, kr, vr = pre_ld
            pre_ld = None
        if DBG_STEP == 0:
            st.close(); return
        qT = conv_silu(qr, h); kT = conv_silu(kr, 8 + h); vT = conv_silu(vr, 16 + h)
        if DBG_STEP == 1:
            st.close(); return
        l2n(qT, 128 ** -0.5, qTr); l2n(kT, 1.0, kTr)
        qT, kT = qTr, kTr
        if DBG_STEP == 2:
            st.close(); return
        gcb = balloc(); egcb = balloc()

        def fn(tb, p):
            ts_ = slice(tb * 512, (tb + 1) * 512)
            k.cp(gcb.t[:, ts_], p.t[:, :], [p.b], [gcb.b], eng=k.dve)
            k.actf(egcb.t[:, ts_], p.t[:, :], AF.Exp, [p.b], [egcb.b])
        k.ts(k.dve, selh.t[:], ones.t[0:16, :], ident.t[0:16, h:h + 1], None, ALU.mult, None, [ones.b, ident.b], [selh.b])
        for tb in range(4):
            p = k.ps()
            k.mm(p.t[:, :], selh.t[:], gc16.t[0:16, tb * 512:(tb + 1) * 512], True, True, [selh.b, gc16.b], [p.b])
            fn(tb, p)
        qg = qgr
        k.tt(k.dve, r(qg.t[:]), qT.t[:], egcb.t[:], ALU.mult, [qT.b, egcb.b], [qg.b])
        oT = balloc()
        if DBG_STEP == 3:
            st.close(); return
        k.ts(k.dve, r(St[0].t[:]), ident.t[:], 0.0, None, ALU.mult, None, [ident.b], [St[0].b])
        seq_done = [0]

        def dn_worker(w_, h=h, qT=qT, kT=kT, vT=vT, gcb=gcb, egcb=egcb, qg=qg, oT=oT, seq_done=seq_done):
            d_ = dnw_t[w_]
            C = d_["c"]
            H2 = [slice(0, 128), slice(128, 256)]
            for n0 in range(2 * w_, DBG_CH, 2 * WDP):
                ns = [n0, n0 + 1]
                css = [slice(n * 128, (n + 1) * 128) for n in ns]
                yield from k.need(2)
                pkt = k.psA(); pvt = k.psA()
                for i, n in enumerate(ns):
                    k.tr(pkt.t[:, H2[i]], kT.t[:, css[i]], ident.t[:], [kT.b, ident.b], [pkt.b])
                    k.tr(pvt.t[:, H2[i]], vT.t[:, css[i]], ident.t[:], [vT.b, ident.b], [pvt.b])
                    k.actf(d_["t1_2"].t[:, H2[i]], gcb.t[:, css[i]], AF.Relu, [gcb.b, ngcT.b], [d_["t1_2"].b], bias=ngcT3[:, n, h:h + 1])
                yield
                for i, n in enumerate(ns):
                    k.actf(r(C[i]["kbg"].t[:]), pkt.t[:, H2[i]], AF.Copy, [pkt.b, bgT.b], [C[i]["kbg"].b], scale=bgT.t[:, n, h:h + 1])
                    k.actf(r(C[i]["kd"].t[:]), pkt.t[:, H2[i]], AF.Copy, [pkt.b, kdT.b], [C[i]["kd"].b], scale=kdT3[:, n, h:h + 1])
                    k.ts(k.dve, r(C[i]["vb"].t[:]), pvt.t[:, H2[i]], betaT3[:, n, 8 + h:9 + h], None, ALU.mult, None, [pvt.b, betaT.b], [C[i]["vb"].b])
                k.psF(pkt, pvt)
                k.actf(d_["El_2"].t[:], d_["t1_2"].t[:], AF.Exp, [d_["t1_2"].b], [d_["El_2"].b], scale=-1.0)
                yield
                yield from k.need(2)
                pk = k.psA(); pq = k.psA()
                for i, n in enumerate(ns):
                    k.mm(pk.t[:, H2[i]], r(kT.t[:, css[i]]), r(kT.t[:, css[i]]), True, True, [kT.b], [pk.b])
                    k.mm(pq.t[:, H2[i]], r(qT.t[:, css[i]]), r(kT.t[:, css[i]]), True, True, [qT.b, kT.b], [pq.b])
                    k.stt(d_["MA_2"].t[:, H2[i]], d_["El_2"].t[:, H2[i]], betaT3[:, n, 8 + h:9 + h], Ls.t[:], ALU.mult, ALU.mult,
                          [d_["El_2"].b, betaT.b, Ls.b], [d_["MA_2"].b])
                k.tt(k.pool, d_["MD_2"].t[:], d_["El_2"].t[:], Li2.t[:], ALU.mult, [d_["El_2"].b, Li2.b], [d_["MD_2"].b])
                yield
                k.tt(k.dve, r(d_["A1_2"].t[:]), pk.t[:, 0:256], d_["MA_2"].t[:], ALU.mult, [pk.b, d_["MA_2"].b], [d_["A1_2"].b])
                k.tt(k.dve, d_["at_2"].t[:], pq.t[:, 0:256], d_["MD_2"].t[:], ALU.mult, [pq.b, d_["MD_2"].b], [d_["at_2"].b])
                k.psF(pk, pq)
                yield
                yield from k.need(2)
                pa = k.psA(); pb = k.psA()
                for i in range(2):
                    k.tr(pb.t[:, H2[i]], d_["A1_2"].t[:, H2[i]], ident.t[:], [d_["A1_2"].b, ident.b], [pb.b])
                    k.tr(pa.t[:, H2[i]], d_["at_2"].t[:, H2[i]], ident.t[:], [d_["at_2"].b, ident.b], [pa.b])
                yield
                bp3 = d_["BP2"][0].t[:].rearrange("p (i c) -> p i c", c=256)
                k.cp(r(bp3[:, :, 0:128]), pb.t[:, 0:256].rearrange("p (i c) -> p i c", c=128), [pb.b], [d_["BP2"][0].b], eng=k.act)
                k.cp(r(bp3[:, :, 128:256]), II2.t[:].rearrange("p (i c) -> p i c", c=128), [II2.b], [d_["BP2"][0].b], eng=k.pool)
                k.cp(r(d_["attnT_2"].t[:]), pa.t[:, 0:256], [pa.b], [d_["attnT_2"].b], eng=k.act)
                k.psF(pa, pb)
                yield
                yield from neumann_pairs([d_], 7)
                yield from k.need(1)
                pw = k.psA()
                for i in range(2):
                    k.mm(pw.t[:, H2[i]], r(C[i]["kbg"].t[:]), r(d_["Tout2"].t[:, H2[i]]), True, True, [C[i]["kbg"].b, d_["Tout2"].b], [pw.b])
                yield
                k.actf(r(d_["nwT_2"].t[:]), pw.t[:, 0:256], AF.Copy, [pw.b], [d_["nwT_2"].b], scale=-1.0)
                k.psF(pw)
                yield
                for i, n in enumerate(ns):
                    while seq_done[0] < n:
                        yield
                    cs = css[i]
                    So, Sn = St[n % 2], St[(n + 1) % 2]
                    yield from k.need(1)
                    pv = k.psA()
                    k.mm(pv.t[:, 0:128], r(d_["Tout2"].t[:, H2[i]]), r(C[i]["vb"].t[:]), True, False, [d_["Tout2"].b, C[i]["vb"].b], [pv.b])
                    k.mm(pv.t[:, 0:128], r(d_["nwT_2"].t[:, H2[i]]), r(So.t[:]), False, True, [d_["nwT_2"].b, So.b], [pv.b])
                    vn = C[i]["vnew"]
                    k.cp(r(vn.t[:]), pv.t[:, 0:128], [pv.b], [vn.b], eng=k.act)
                    k.psF(pv)
                    yield from k.need(2)
                    po = k.psA(); pS = k.psA()
                    k.mm(pS.t[:, 0:128], r(C[i]["kd"].t[:]), r(vn.t[:]), True, True, [C[i]["kd"].b, vn.b], [pS.b])
                    k.mm(po.t[:, 0:128], r(So.t[:]), r(qg.t[:, cs]), True, False, [So.b, qg.b], [po.b])
                    k.mm(po.t[:, 0:128], r(vn.t[:]), r(d_["attnT_2"].t[:, H2[i]]), False, True, [vn.b, d_["attnT_2"].b], [po.b])
                    k.stt(r(Sn.t[:]), So.t[:], egcb.t[:, n * 128 + 127:n * 128 + 128], pS.t[:, 0:128], ALU.mult, ALU.add,
                          [So.b, egcb.b, pS.b], [Sn.b])
                    k.cp(oT.t[:, cs], po.t[:, 0:128], [po.b], [oT.b], eng=k.act)
                    k.psF(po, pS)
                    seq_done[0] = n + 1
                    yield
        zr = balloc(); load_rows(zr, 3072 + h * 128)
        bg_next(3)
        run_rr([dn_worker(w_) for w_ in range(WDP)])
        if DBG_STEP == 14:
            st.close(); return
        bfree(vT, gcb, egcb)
        if h + 1 < DBG_DN:
            nq = balloc(); load_rows(nq, (h + 1) * 128)
            nk = balloc(); load_rows(nk, 1024 + (h + 1) * 128)
            nv = balloc(); load_rows(nv, 2048 + (h + 1) * 128)
            pre_ld = (nq, nk, nv)
        sq_ = balloc(); rn = balloc()
        k.actf(bfv(sq_), oT.t[:], AF.Square, [oT.b], [sq_.b])

        def fn2(tb, p):
            ts_ = slice(tb * 512, (tb + 1) * 512)
            k.actf(rn.t[:, ts_], p.t[:, :], AF.Ln, [p.b, epsG.b], [rn.b], bias=epsG.t[:, 1:2], scale=1.0 / 128)
        psum_bcast_sum(sq_, ones_bf.t[:], ones_bf.b, fn2)
        k.actf(rn.t[:], rn.t[:], AF.Exp, [rn.b], [rn.b], scale=-0.5)
        k.actf(zr.t[:], zr.t[:], AF.Silu, [zr.b], [zr.b])
        k.stt(oT.t[:], oT.t[:], dnw.t[:, 0:1], rn.t[:], ALU.mult, ALU.mult, [oT.b, dnw.b, rn.b], [oT.b])
        ob = obf[octr[0] % 2]; octr[0] += 1
        k.tt(k.dve, ob.t[:], oT.t[:], zr.t[:], ALU.mult, [oT.b, zr.b], [ob.b])
        k.dma(k.sp, oTd.t[h * 128:(h + 1) * 128, :], ob.t[:], [ob.b], [oTd.bl[h]], ob.b)
        bfree(zr, sq_, rn, oT)
    bfree(gc16)
    st_dn.close()
    k.barrier()

    wa = balloc(); sg = balloc()
    RW0 = DNC

    def lerp(xr, gi):
        t_ = balloc()
        k.op(k.pool, lambda e: e.memset(t_.t[:, 0:1], 0.0), [], [t_.b])
        k.ts(k.dve, t_.t[:, 1:S], xr.t[:, 0:S - 1], mu.t[:, gi:gi + 1], None, ALU.mult, None, [xr.b, mu.b], [t_.b])
        k.stt(xr.t[:], xr.t[:], omm.t[:, gi:gi + 1], t_.t[:], ALU.mult, ALU.add, [xr.b, omm.b, t_.b], [xr.b])
        bfree(t_)
    load_rows(wa, RW0 + 3072); lerp(wa, 24)
    load_rows(sg, RW0 + 3200); lerp(sg, 25)
    k.actf(wa.t[0:64, :], wa.t[0:64, :], AF.Tanh, [wa.b], [wa.b])
    k.actf(sg.t[:], sg.t[:], AF.Sigmoid, [sg.b], [sg.b])

    WRW = 4
    st_rw = contextlib.ExitStack()
    lora = k.sb("m_lora", [128, 1024], F32, st_rw); g2 = k.sb("m_g2", [128, 1024], F32, st_rw)
    for t_, d_ in ((lora, lora_d), (g2, g2_d)):
        k.dma(k.sp, t_.t[:], d_, [], [t_.b], t_.b)

    def sqr(name, w=128):
        return k.sb("m_" + name, [128, w], F32, st_rw)
    Ht = [sqr("Ht0", 128), sqr("Ht1", 128)]
    rww_t = []
    for w_ in range(WRW):
        d_ = {nm: sqr(f"r{nm}{w_}") for nm in ("e1", "e2", "e3", "e4", "Bt", "Kt", "bh", "kh", "rhs1", "AVc", "KVc", "YVc", "Gt")}
        for nm in ("BhP", "KhP", "VP", "UP"):
            t_ = sqr(f"r{nm}2_{w_}", 256)
            d_[nm + "2"] = t_
            k.ts(k.dve, r(t_.t[:]), II2.t[:], 0.0, None, ALU.mult, None, [II2.b], [t_.b])
            d_[nm] = [TV(t_.t[:, 0:128], t_.b), TV(t_.t[:, 64:192], t_.b)]
        d_["ar"] = sqr(f"rar{w_}", 256)
        d_["hd"] = []
        for hh in range(2):
            e_ = {}
            e_["mb"] = sqr(f"rmb{w_}_{hh}", 256); e_["mk"] = sqr(f"rmk{w_}_{hh}", 256)
            d_["hd"].append(e_)
        d_["A1_2"] = sqr(f"rA12_{w_}", 256); d_["Tout2"] = sqr(f"rTr2_{w_}", 256)
        d_["Ap2"] = [sqr(f"rAp20_{w_}", 256), sqr(f"rAp21_{w_}", 256)]
        d_["BP2"] = [sqr(f"rBP20_{w_}", 512), sqr(f"rBP21_{w_}", 512)]
        rww_t.append(d_)
    V = lambda j: rwv.t[:, j * 8:(j + 1) * 8]

    for g in range(DBG_RW):
        rT = balloc(); load_rows(rT, RW0 + g * 128); lerp(rT, g)
        kl = balloc(); load_rows(kl, RW0 + 1024 + g * 128); lerp(kl, 8 + g)
        vT = balloc(); load_rows(vT, RW0 + 2048 + g * 128); lerp(vT, 16 + g)
        sig = balloc(); a_ = balloc()
        gsl = slice(g * 128, (g + 1) * 128)
        for tb in range(4):
            ts_ = slice(tb * 512, (tb + 1) * 512)
            p = k.ps()
            k.mm(p.t[:, :], lora.t[0:64, gsl], wa.t[0:64, ts_], True, True, [lora.b, wa.b], [p.b])
            k.actf(sig.t[:, ts_], p.t[:, :], AF.Sigmoid, [p.b, rwv.b], [sig.b], bias=V(0)[:, g:g + 1])
            p = k.ps()
            k.mm(p.t[:, :], lora.t[64:128, gsl], wa.t[64:128, ts_], True, True, [lora.b, wa.b], [p.b])
            k.actf(a_.t[:, ts_], p.t[:, :], AF.Sigmoid, [p.b, rwv.b], [a_.b], bias=V(1)[:, g:g + 1])
        kk = balloc(); sq_ = balloc(); rn = balloc()
        k.ts(k.dve, kk.t[:], kl.t[:], V(2)[:, g:g + 1], None, ALU.mult, None, [kl.b, rwv.b], [kk.b])
        k.actf(bfv(sq_), kk.t[:], AF.Square, [kk.b], [sq_.b])

        def fnk(tb, p):
            ts_ = slice(tb * 512, (tb + 1) * 512)
            k.ts(k.dve, rn.t[:, ts_], p.t[:, :], 1e-24, None, ALU.max, None, [p.b], [rn.b])
        psum_bcast_sum(sq_, blk_bf.t[:], blk_bf.b, fnk)
        k.actf(rn.t[:], rn.t[:], AF.Ln, [rn.b], [rn.b])
        k.actf(rn.t[:], rn.t[:], AF.Exp, [rn.b], [rn.b], scale=-0.5)
        k.tt(k.dve, kk.t[:], kk.t[:], rn.t[:], ALU.mult, [kk.b, rn.b], [kk.b])
        bfree(sq_, rn)
        kf = balloc()
        k.ts(k.dve, kf.t[:], a_.t[:], -1.0, V(3)[:, g:g + 1], ALU.add, ALU.mult, [a_.b, rwv.b], [kf.b])
        k.stt(kf.t[:], kf.t[:], 1.0, kl.t[:], ALU.add, ALU.mult, [kf.b, kl.b], [kf.b])
        bT = balloc()
        k.tt(k.dve, bT.t[:], a_.t[:], kk.t[:], ALU.mult, [a_.b, kk.b], [bT.b])
        bfree(kl, a_)
        cum = balloc()
        k.op(k.dve, lambda e: e.tensor_tensor_scan(cum.t[:], rmask.t[:], sig.t[:], 0.0, ALU.mult, ALU.add), [rmask.b, sig.b], [cum.b])
        yT = balloc()
        k.ts(k.dve, r(Ht[0].t[:]), ident.t[:], 0.0, None, ALU.mult, None, [ident.b], [Ht[0].b])
        seq_done = [0]

        def rw_worker(w_, rT=rT, vT=vT, kk=kk, kf=kf, bT=bT, sig=sig, cum=cum, yT=yT, seq_done=seq_done):
            d_ = rww_t[w_]
            ar = d_["ar"]; e1 = d_["e1"]; e2 = d_["e2"]; e3 = d_["e3"]; e4 = d_["e4"]
            Bt_, Kt_, bh, kh = d_["Bt"], d_["Kt"], d_["bh"], d_["kh"]
            BhP, KhP, VP, UP, rhs1 = d_["BhP"], d_["KhP"], d_["VP"], d_["UP"], d_["rhs1"]
            HD = d_["hd"]
            RS = [slice(0, 64), slice(64, 128)]
            for n in range(w_, DBG_CH, WRW):
                cs = slice(n * 128, (n + 1) * 128)
                k.actf(e1.t[:], cum.t[:, cs], AF.Exp, [cum.b], [e1.b], scale=CDEC)
                k.actf(e2.t[:], cum.t[:, cs], AF.Exp, [cum.b], [e2.b], scale=-CDEC)
                k.tt(k.pool, e3.t[:], cum.t[:, cs], sig.t[:, cs], ALU.subtract, [cum.b, sig.b], [e3.b])
                k.ts(k.dve, e4.t[:], cum.t[:, cs], cum.t[:, n * 128 + 127:n * 128 + 128], None, ALU.subtract, None, [cum.b], [e4.b])
                yield
                k.actf(e3.t[:], e3.t[:], AF.Exp, [e3.b], [e3.b], scale=CDEC)
                k.actf(e4.t[:], e4.t[:], AF.Exp, [e4.b], [e4.b], scale=-CDEC)
                k.tt(k.pool, r(ar.t[:, 128:256]), rT.t[:, cs], e1.t[:], ALU.mult, [rT.b, e1.b], [ar.b])
                k.tt(k.pool, r(Bt_.t[:]), bT.t[:, cs], e2.t[:], ALU.mult, [bT.b, e2.b], [Bt_.b])
                k.tt(k.pool, r(Kt_.t[:]), kf.t[:, cs], e2.t[:], ALU.mult, [kf.b, e2.b], [Kt_.b])
                yield
                k.tt(k.dve, r(ar.t[:, 0:128]), kk.t[:, cs], e3.t[:], ALU.mult, [kk.b, e3.b], [ar.b])
                k.tt(k.pool, bh.t[:], bT.t[:, cs], e4.t[:], ALU.mult, [bT.b, e4.b], [bh.b])
                k.tt(k.pool, kh.t[:], kf.t[:, cs], e4.t[:], ALU.mult, [kf.b, e4.b], [kh.b])
                yield
                yield from k.need(3)
                trs = []
                for src, srcB, dst in ((bh.t[:], bh.b, d_["BhP2"]), (kh.t[:], kh.b, d_["KhP2"]), (vT.t[:, cs], vT.b, d_["VP2"])):
                    p = k.psA()
                    k.tr(p.t[:, 0:128], src, ident.t[:], [srcB, ident.b], [p.b])
                    trs.append((p, dst))
                yield
                for ti_, (p, dst2) in enumerate(trs):
                    k.cp(r(dst2.t[:].rearrange("p (i c) -> p i c", c=128)[:, :, 0:64]), p.t[:, 0:128].rearrange("p (i c) -> p i c", c=64),
                         [p.b], [dst2.b], eng=k.act)
                    k.psF(p)
                yield from k.need(4)
                pms = []
                for hh in range(2):
                    R = RS[hh]
                    pm = k.psA(); pm2 = k.psA()
                    k.mm(pm.t[:, 0:256], r(Bt_.t[R, :]), r(ar.t[R, :]), True, True, [Bt_.b, ar.b], [pm.b])
                    k.mm(pm2.t[:, 0:256], r(Kt_.t[R, :]), r(ar.t[R, :]), True, True, [Kt_.b, ar.b], [pm2.b])
                    pms.append((pm, pm2))
                yield
                for hh in range(2):
                    pm, pm2 = pms[hh]
                    k.tt(k.dve, r(HD[hh]["mb"].t[:]), pm.t[:, 0:256], UU.t[:], ALU.mult, [pm.b, UU.b], [HD[hh]["mb"].b])
                    k.tt(k.dve, r(HD[hh]["mk"].t[:]), pm2.t[:, 0:256], UU.t[:], ALU.mult, [pm2.b, UU.b], [HD[hh]["mk"].b])
                    k.psF(pm, pm2)
                yield from k.need(2)
                pas = []
                for hh in range(2):
                    R = RS[hh]
                    pa = k.psA()
                    k.mm(pa.t[:, 0:128], r(ar.t[R, 0:128]), r(Bt_.t[R, :]), True, True, [ar.b, Bt_.b], [pa.b])
                    pas.append(pa)
                yield
                for hh in range(2):
                    e_ = HD[hh]
                    k.tt(k.dve, r(d_["A1_2"].t[:, hh * 128:(hh + 1) * 128]), pas[hh].t[:, 0:128], Ls.t[:], ALU.mult,
                         [pas[hh].b, Ls.b], [d_["A1_2"].b])
                    k.actf(r(d_["BP2"][0].t[:, hh * 256:hh * 256 + 128]), e_["mb"].t[:, 0:128], AF.Copy, [e_["mb"].b], [d_["BP2"][0].b], scale=-1.0)
                    k.psF(pas[hh])
                yield
                k.cp(r(d_["BP2"][0].t[:].rearrange("p (i c) -> p i c", c=256)[:, :, 128:256]), II2.t[:].rearrange("p (i c) -> p i c", c=128),
                     [II2.b], [d_["BP2"][0].b], eng=k.pool)
                yield from k.need(3)
                pAV = k.psA(); pKV = k.psA(); pYV = k.psA()
                for hh in range(2):
                    e_ = HD[hh]
                    k.mm(pAV.t[:, 0:128], r(e_["mk"].t[:, 0:128]), r(VP[hh].t[:]), hh == 0, hh == 1, [e_["mk"].b, VP[hh].b], [pAV.b])
                    k.mm(pKV.t[:, RS[hh]], r(KhP[hh].t[:]), r(VP[hh].t[:, RS[hh]]), True, True, [KhP[hh].b, VP[hh].b], [pKV.b])
                    k.mm(pYV.t[:, 0:128], r(VP[hh].t[:]), r(e_["mk"].t[:, 128:256]), hh == 0, hh == 1, [VP[hh].b, e_["mk"].b], [pYV.b])
                yield
                k.cp(d_["AVc"].t[:], pAV.t[:, 0:128], [pAV.b], [d_["AVc"].b], eng=k.act)
                k.cp(d_["KVc"].t[:], pKV.t[:, 0:128], [pKV.b], [d_["KVc"].b], eng=k.act)
                k.cp(d_["YVc"].t[:], pYV.t[:, 0:128], [pYV.b], [d_["YVc"].b], eng=k.act)
                k.psF(pAV, pKV, pYV)
                yield
                yield from neumann_pairs([d_], 7)
                while seq_done[0] < n:
                    yield
                Ho, Hn = Ht[n % 2], Ht[(n + 1) % 2]
                yield from k.need(1)
                pr_ = k.psA()
                k.mm(pr_.t[:, 0:128], r(ar.t[:, 0:128]), r(Ho.t[:]), True, True, [ar.b, Ho.b], [pr_.b])
                k.stt(d_["Gt"].t[:], Ho.t[:], e1.t[:, 127:128], d_["KVc"].t[:], ALU.mult, ALU.add, [Ho.b, e1.b, d_["KVc"].b], [d_["Gt"].b])
                k.stt(r(rhs1.t[:]), pr_.t[:, 0:128], -1.0, d_["AVc"].t[:], ALU.mult, ALU.subtract, [pr_.b, d_["AVc"].b], [rhs1.b])
                k.psF(pr_)
                yield from k.need(1)
                pu = k.psA()
                for hh in range(2):
                    k.mm(pu.t[:, RS[hh]], r(d_["Tout2"].t[:, hh * 128:(hh + 1) * 128]), r(rhs1.t[:, RS[hh]]), True, True,
                         [d_["Tout2"].b, rhs1.b], [pu.b])
                k.cp(r(d_["UP2"].t[:].rearrange("p (i c) -> p i c", c=128)[:, :, 0:64]), pu.t[:, 0:128].rearrange("p (i c) -> p i c", c=64),
                     [pu.b], [d_["UP2"].b], eng=k.act)
                k.psF(pu)
                yield from k.need(2)
                pY = k.psA(); pS = k.psA()
                for hh in range(2):
                    k.mm(pS.t[:, RS[hh]], r(BhP[hh].t[:]), r(UP[hh].t[:, RS[hh]]), True, True, [BhP[hh].b, UP[hh].b], [pS.b])
                k.mm(pY.t[:, 0:128], r(Ho.t[:]), r(ar.t[:, 128:256]), True, False, [Ho.b, ar.b], [pY.b])
                for hh in range(2):
                    e_ = HD[hh]
                    k.mm(pY.t[:, 0:128], r(UP[hh].t[:]), r(e_["mb"].t[:, 128:256]), False, hh == 1, [UP[hh].b, e_["mb"].b], [pY.b])
                k.tt(k.dve, r(Hn.t[:]), pS.t[:, 0:128], d_["Gt"].t[:], ALU.add, [pS.b, d_["Gt"].b], [Hn.b])
                k.tt(k.dve, yT.t[:, cs], pY.t[:, 0:128], d_["YVc"].t[:], ALU.add, [pY.b, d_["YVc"].b], [yT.b])
                k.psF(pY, pS)
                seq_done[0] = n + 1
                yield
        bg_next(3)
        run_rr([rw_worker(w_) for w_ in range(WRW)])
        bfree(kk, bT, sig, cum)
        rk = balloc(); yc = balloc(); sq_ = balloc(); rs_ = balloc()
        k.stt(bfv(rk), rT.t[:], V(4)[:, g:g + 1], kf.t[:], ALU.mult, ALU.mult, [rT.b, rwv.b, kf.b], [rk.b])
        k.actf(bfv(sq_), yT.t[:], AF.Copy, [yT.b], [sq_.b])

        def fnm(tb, p):
            ts_ = slice(tb * 512, (tb + 1) * 512)
            k.stt(yc.t[:, ts_], p.t[:, :], -1.0 / 64, yT.t[:, ts_], ALU.mult, ALU.add, [p.b, yT.b], [yc.b])
        psum_bcast_sum(sq_, blk_bf.t[:], blk_bf.b, fnm)
        k.actf(bfv(sq_), yc.t[:], AF.Square, [yc.b], [sq_.b])

        def fnv(tb, p):
            ts_ = slice(tb * 512, (tb + 1) * 512)
            k.actf(rs_.t[:, ts_], p.t[:, :], AF.Ln, [p.b, epsG.b], [rs_.b], bias=epsG.t[:, 0:1], scale=1.0 / 64)
        psum_bcast_sum(sq_, blk_bf.t[:], blk_bf.b, fnv)
        k.actf(rs_.t[:], rs_.t[:], AF.Exp, [rs_.b], [rs_.b], scale=-0.5)
        k.tt(k.dve, yc.t[:], yc.t[:], rs_.t[:], ALU.mult, [yc.b, rs_.b], [yc.b])
        k.ts(k.dve, yc.t[:], yc.t[:], V(5)[:, g:g + 1], V(6)[:, g:g + 1], ALU.mult, ALU.add, [yc.b, rwv.b], [yc.b])

        def fnb(tb, p):
            ts_ = slice(tb * 512, (tb + 1) * 512)
            k.tt(k.dve, rs_.t[:, ts_], p.t[:, :], vT.t[:, ts_], ALU.mult, [p.b, vT.b], [rs_.b])
        psum_bcast_sum(rk, blk_bf.t[:], blk_bf.b, fnb)
        k.tt(k.dve, yc.t[:], yc.t[:], rs_.t[:], ALU.add, [yc.b, rs_.b], [yc.b])
        ob = obf[octr[0] % 2]; octr[0] += 1
        for tb in range(4):
            ts_ = slice(tb * 512, (tb + 1) * 512)
            p = k.ps()
            k.mm(p.t[:, :], g2.t[:, gsl], sg.t[:, ts_], True, True, [g2.b, sg.b], [p.b])
            k.tt(k.dve, ob.t[:, ts_], p.t[:, :], yc.t[:, ts_], ALU.mult, [p.b, yc.b], [ob.b])
        k.dma(k.sp, oTd.t[1024 + g * 128:1024 + (g + 1) * 128, :], ob.t[:], [ob.b], [oTd.bl[8 + g]], ob.b)
        bfree(rk, yc, sq_, rs_, rT, kf, vT, yT)
    bfree(wa, sg)
    st_rw.close()
    st.close()


def prep_shared(inp):
    f = lambda a: np.ascontiguousarray(np.asarray(a, dtype=np.float32))
    sh = {}
    for kk_ in ("w_in", "w_out", "xa_wq", "xa_wk", "xa_wv", "xa_wo", "ffn_w1", "ffn_w2"):
        sh[kk_] = f(inp[kk_][0])
    sh["norms"] = f(np.stack([inp["mix_norm_w"][0], inp["xa_norm_w"][0], inp["mem_norm_w"][0], inp["ffn_norm_w"][0],
                              inp["final_norm_w"]], axis=0))
    cw = np.asarray(inp["dn_conv_w"][0])
    sh["convw"] = f(cw.reshape(4, 24, 128).transpose(2, 1, 0).reshape(128, 96))
    dn = np.zeros((16, 2), np.float32)
    dn[0:8, 0] = np.asarray(inp["dn_a_log"][0]); dn[0:8, 1] = np.asarray(inp["dn_dt_bias"][0])
    sh["dnsc"] = dn
    sh["dnw"] = f(np.asarray(inp["dn_norm_w"][0]).reshape(128, 1))
    sh["mu"] = f(np.asarray(inp["rw_mu"][0]).reshape(26, 128).T)
    vs = [np.asarray(inp[n][0]).reshape(8, 128).T for n in ("rw_w0", "rw_a0", "rw_k_k", "rw_k_a", "rw_r_k", "rw_ln_w", "rw_ln_b")]
    sh["rwv"] = f(np.concatenate(vs, axis=1))
    sh["lora"] = f(np.concatenate([np.asarray(inp["rw_w2"][0]), np.asarray(inp["rw_a2"][0])], axis=0))
    sh["g2"] = f(inp["rw_g2"][0])
    return sh


def kernel(**inp):
    sh = prep_shared(inp)
    xs = np.asarray(inp["x"], dtype=np.float32)
    ms = np.asarray(inp["mem"], dtype=np.float32)
    nc = build()
    in_maps = []
    for b in range(8):
        m = dict(sh)
        m["x"] = np.ascontiguousarray(xs[b])
        m["mem"] = np.ascontiguousarray(ms[b])
        in_maps.append(m)
    res = run_bass_kernel_spmd(nc, in_maps, core_ids=list(range(8)))
    return np.stack([np.asarray(r["out"], dtype=np.float32) for r in res.results], axis=0)
```

```python
import contextlib
import math
import numpy as np
import concourse.bass as bass
import concourse.mybir as mybir
from concourse.alu_op_type import AluOpType as ALU
from concourse.bass_utils import run_bass_kernel_spmd

F32 = mybir.dt.float32
BF16 = mybir.dt.bfloat16
F32R = mybir.dt.float32r
AF = mybir.ActivationFunctionType
AX = mybir.AxisListType

D = 2048
S = 2048
NT = 16
MEM = 256
DNC = 4112
INC = 7440
FF = 8192
EPS = 1e-6
CDEC = -math.exp(-0.5)
DBG_DN = 8
DBG_RW = 8
DBG_CH = NT
DBG_STEP = 99


class Sem:
    __slots__ = ("h", "name")

    def __init__(self, h, name):
        self.h = h
        self.name = name


class Buf:
    __slots__ = ("name", "w", "r", "dsem", "dcount", "excl")

    def __init__(self, name):
        self.name = name
        self.excl = False
        self.w = None
        self.r = {}
        self.dsem = None
        self.dcount = 0


class Eng:
    def __init__(self, name, h, sem):
        self.name = name
        self.h = h
        self.sem = sem
        self.count = 0
        self.waited = {}


class T:
    def __init__(self, t, name):
        self.t = t
        self.b = Buf(name)


class TV:
    def __init__(self, ap, b):
        self.t = ap
        self.b = b


class K:
    def __init__(self, nc):
        self.nc = nc
        self.es = contextlib.ExitStack()
        self.pe = self._eng("pe", nc.tensor)
        self.dve = self._eng("dve", nc.vector)
        self.act = self._eng("act", nc.scalar)
        self.pool = self._eng("pool", nc.gpsimd)
        self.sp = self._eng("sp", nc.sync)
        self.ninst = 0
        self._psi = 0
        self.psf = []
        self._ev = 0
        self.slots = []
        self.psfree = list(range(8))

    def new_sem(self, name):
        return Sem(self.es.enter_context(self.nc.semaphore(name)), name)

    def _eng(self, name, h):
        return Eng(name, h, self.new_sem("s_" + name))

    def sb(self, name, shape, dt, stack=None):
        t = (stack or self.es).enter_context(self.nc.sbuf_tensor(name, list(shape), dt))
        return T(t, name)

    def _deps(self, eng, reads, writes, extra=()):
        deps = {}
        for b in reads:
            if b.w is not None:
                s, v = b.w
                if v > deps.get(s, 0):
                    deps[s] = v
            if b.excl:
                for s, v in b.r.items():
                    if s is not eng.sem and v > deps.get(s, 0):
                        deps[s] = v
        for b in writes:
            if b.w is not None and not (eng is self.pe and b.w[0] is self.pe.sem):
                s, v = b.w
                if v > deps.get(s, 0):
                    deps[s] = v
            for s, v in b.r.items():
                if v > deps.get(s, 0):
                    deps[s] = v
        for s, v in extra:
            if v > deps.get(s, 0):
                deps[s] = v
        for s, v in deps.items():
            if eng.waited.get(s, 0) < v:
                eng.h.wait_ge(s.h, v)
                eng.waited[s] = v

    def op(self, eng, fn, reads=(), writes=()):
        self._deps(eng, reads, writes)
        inst = fn(eng.h)
        eng.count += 1
        inst.then_inc(eng.sem.h, 1)
        self.ninst += 1
        c = eng.count
        s = eng.sem
        for b in reads:
            b.r[s] = c
        for b in writes:
            b.w = (s, c)
            b.r = {}
        return inst

    def dma(self, q, out, in_, reads, writes, slot, **kw):
        if slot.dsem is None:
            slot.dsem = self.new_sem("d_" + slot.name)
            self.slots.append(slot)
        extra = [(slot.dsem, slot.dcount)] if slot.dcount else []
        self._deps(q, reads, writes, extra)
        inst = q.h.dma_start(out=out, in_=in_, **kw)
        slot.dcount += 16
        inst.then_inc(slot.dsem.h, 16)
        self.ninst += 1
        for b in reads:
            b.r[slot.dsem] = slot.dcount
        for b in writes:
            b.w = (slot.dsem, slot.dcount)
            b.r = {}
        return inst

    def ps(self):
        p = self.psf[self._psi % 8]
        self._psi += 1
        return p

    def psA(self):
        return self.psf[self.psfree.pop(0)]

    def psF(self, *ps_):
        for p in ps_:
            self.psfree.append(self.psf.index(p))

    def need(self, m):
        while len(self.psfree) < m:
            yield

    def barrier(self):
        engs = [self.pe, self.dve, self.act, self.pool, self.sp]
        for e in engs:
            for o in engs:
                if o is not e and o.count and e.waited.get(o.sem, 0) < o.count:
                    e.h.wait_ge(o.sem.h, o.count)
                    e.waited[o.sem] = o.count
            for sl in self.slots:
                if e.waited.get(sl.dsem, 0) < sl.dcount:
                    e.h.wait_ge(sl.dsem.h, sl.dcount)
                    e.waited[sl.dsem] = sl.dcount

    def mm(self, out, lhsT, rhs, start, stop, R, W):
        return self.op(self.pe, lambda e: e.matmul(out, lhsT, rhs, start=start, stop=stop), R, W)

    def tr(self, out, in_, ident, R, W):
        return self.op(self.pe, lambda e: e.transpose(out, in_, ident), R, W)

    def tt(self, eng, out, a, b, op, R, W):
        return self.op(eng, lambda e: e.tensor_tensor(out=out, in0=a, in1=b, op=op), R, W)

    def ts(self, eng, out, a, s1, s2, op0, op1, R, W):
        if op1 is None:
            return self.op(eng, lambda e: e.tensor_scalar(out=out, in0=a, scalar1=s1, scalar2=None, op0=op0), R, W)
        return self.op(eng, lambda e: e.tensor_scalar(out=out, in0=a, scalar1=s1, scalar2=s2, op0=op0, op1=op1), R, W)

    def stt(self, out, a, s, b, op0, op1, R, W):
        return self.op(self.dve, lambda e: e.scalar_tensor_tensor(out=out, in0=a, scalar=s, in1=b, op0=op0, op1=op1), R, W)

    def actf(self, out, in_, func, R, W, **kw):
        return self.op(self.act, lambda e: e.activation(out=out, in_=in_, func=func, **kw), R, W)

    def cp(self, out, in_, R, W, eng=None):
        if eng is None:
            self._ev += 1
            eng = self.act if (self._ev & 1) else self.dve
        if eng is self.act:
            return self.op(eng, lambda e: e.copy(out, in_), R, W)
        return self.op(eng, lambda e: e.tensor_copy(out, in_), R, W)


def build(debug=None):
    nc = bass.Bass("TRN2", target_bir_lowering=False)
    k = K(nc)

    def din(name, shape):
        return nc.dram_tensor(name, list(shape), F32, kind="ExternalInput").ap()

    x = din("x", [S, D]); mem = din("mem", [MEM, D])
    w_in = din("w_in", [D, INC]); w_out = din("w_out", [D, D])
    wq = din("xa_wq", [D, 512]); wk = din("xa_wk", [D, 512]); wv = din("xa_wv", [D, 512]); wo = din("xa_wo", [512, D])
    w1 = din("ffn_w1", [D, FF]); w2 = din("ffn_w2", [FF, D])
    nrm = din("norms", [5, D])
    convw = din("convw", [128, 24 * 4])
    dnsc = din("dnsc", [16, 2])
    dnw = din("dnw", [128, 1])
    mu = din("mu", [128, 26])
    rwv = din("rwv", [128, 7 * 8])
    lora = din("lora", [128, 1024])
    g2 = din("g2", [128, 1024])
    out = nc.dram_tensor("out", [S, D], F32, kind="ExternalOutput").ap()
    pT = T(nc.dram_tensor("pT", [INC, S], F32, kind="Internal").ap(), "pT")
    pT.bl = [Buf(f"pT{i}") for i in range(59)]
    if debug == "p4":
        oTd = T(nc.dram_tensor("oT_in", [D, S], F32, kind="ExternalInput").ap(), "oTd")
        oTd.bl = [Buf(f"oT{i}") for i in range(16)]
        oT_dt = F32
    else:
        oTd = T(nc.dram_tensor("oT", [D, S], BF16, kind="Internal").ap(), "oTd")
        oTd.bl = [Buf(f"oT{i}") for i in range(16)]
        oT_dt = BF16
    wsc = T(nc.dram_tensor("wsc", [38, 128, 8192], BF16, kind="Internal").ap(), "wsc")
    wsc.bl = [Buf(f"wsc{i}") for i in range(38)]
    dbg = None
    if debug == "p1":
        dbg = nc.dram_tensor("dbg", [INC, S], F32, kind="ExternalOutput").ap()
    if debug == "p2":
        dbg = nc.dram_tensor("dbg", [D, S], F32, kind="ExternalOutput").ap()

    for i in range(8):
        p = T(k.es.enter_context(nc.psum_tensor(f"ps{i}", [128, 512], F32)), f"ps{i}")
        p.b.excl = True
        k.psf.append(p)

    ones = k.sb("ones", [128, 128], F32)
    ident = k.sb("ident", [128, 128], F32)
    identb = k.sb("identb", [128, 128], BF16)
    epsT = k.sb("epsT", [128, 1], F32)
    k.op(k.pool, lambda e: e.memset(ones.t[:], 1.0), [], [ones.b])
    k.op(k.pool, lambda e: e.memset(epsT.t[:], EPS), [], [epsT.b])
    k.op(k.pool, lambda e: e.affine_select(out=ident.t[:], in_=ones.t[:], pattern=[[-1, 128]], compare_op=ALU.is_equal,
                                           fill=0.0, base=0, channel_multiplier=1), [ones.b], [ident.b])
    k.op(k.dve, lambda e: e.tensor_copy(identb.t[:], ident.t[:]), [ident.b], [identb.b])

    G = {}

    def alloc_norm(stack, tag):
        G["wbc"] = k.sb("wbc" + tag, [128, D], F32, stack)
        G["uns"] = [k.sb(f"un{i}" + tag, [128, D], BF16, stack) for i in range(2)]
        G["un"] = G["uns"][0]
        G["junk"] = k.sb("junk" + tag, [128, D], BF16, stack)
        G["ss"] = k.sb("ss" + tag, [128, 4], F32, stack)
        G["ssb"] = [Buf(f"ss{i}" + tag) for i in range(4)]
        G["ctr"] = 0

    def load_wbc(i):
        wbc = G["wbc"]
        k.dma(k.sp, wbc.t[:], nrm[i:i + 1, :].to_broadcast([128, D]), [], [wbc.b], wbc.b)
    wbufs = []
    wctr = [0]

    def precast_p4_weights():
        blocks = []
        for cb in range(4):
            blocks.append((cb, w_out[:, cb * 512:(cb + 1) * 512], 16))
        blocks.append((4, wq[:, :], 16))
        blocks.append((5, wo[:, :], 4))
        for fb in range(16):
            blocks.append((6 + fb, w1[:, fb * 512:(fb + 1) * 512], 16))
        for cb in range(4):
            for sub in range(4):
                blocks.append((22 + cb * 4 + sub, w2[sub * 2048:(sub + 1) * 2048, cb * 512:(cb + 1) * 512], 16))
        return blocks

    pc_blocks = precast_p4_weights()

    def bg_next(n):
        for _ in range(n):
            if not pc_blocks:
                return
            idx, src, kc = pc_blocks.pop(0)
            k.dma(k.pool, wsc.t[idx].rearrange("p (kc c) -> p kc c", kc=kc), src.rearrange("(kc p) c -> p kc c", p=128),
                  [], [wsc.bl[idx]], wsc.bl[idx])

    def alloc_wbufs(stack, tag):
        wbufs.clear()
        wbufs.extend(k.sb(f"wbuf{tag}{i}", [128, 8192], BF16, stack) for i in range(2))

    def load_w(src_ap, kc, cache=None):
        wb = wbufs[wctr[0] % len(wbufs)]
        wctr[0] += 1
        ncol = src_ap.shape[1]
        view = wb.t[:, 0:kc * ncol].rearrange("p (kc c) -> p kc c", kc=kc)
        if cache is not None:
            k.dma(k.pool, wb.t[:, :], wsc.t[cache[0]], [wsc.bl[cache[0]]], [wb.b], wb.b)
            return wb, view
        k.dma(k.pool, view, src_ap.rearrange("(kc p) c -> p kc c", p=128), [], [wb.b], wb.b)
        return wb, view

    def norm_transpose(src, srcB, dstT, dstB, dst_cols, slot):
        wbc, ss, junk = G["wbc"], G["ss"], G["junk"]
        un = G["uns"][G["ctr"] % 2]
        G["ctr"] += 1
        sb_ = G["ssb"][slot]
        k.actf(junk.t[:], src, AF.Square, [srcB], [junk.b, sb_], accum_out=ss.t[:, slot:slot + 1])
        k.actf(ss.t[:, slot:slot + 1], ss.t[:, slot:slot + 1], AF.Sqrt, [sb_, epsT.b], [sb_], scale=1.0 / D, bias=epsT.t[:, 0:1])
        k.op(k.dve, lambda e: e.reciprocal(ss.t[:, slot:slot + 1], ss.t[:, slot:slot + 1]), [sb_], [sb_])
        k.stt(un.t[:], src, ss.t[:, slot:slot + 1], wbc.t[:], ALU.mult, ALU.mult, [srcB, sb_, wbc.b], [un.b])
        for half in range(2):
            p = k.ps()
            pv = p.t[:].bitcast(BF16)
            for j in range(8):
                kc = half * 8 + j
                k.tr(pv[:, j * 128:(j + 1) * 128], un.t[:, kc * 128:(kc + 1) * 128], identb.t[:], [un.b, identb.b], [p.b])
            k.cp(dstT[:, half * 8:(half + 1) * 8, dst_cols], pv.rearrange("p (j c) -> p j c", c=128), [p.b], [dstB])

    if debug != "p4":
        with contextlib.ExitStack() as st1:
            alloc_wbufs(st1, "a")
            alloc_norm(st1, "a")
            uT = k.sb("uT", [128, 16, S], BF16, st1)
            uTb = [Buf(f"uT{i}") for i in range(4)]
            xts = [k.sb(f"xt{i}", [128, D], F32, st1) for i in range(2)]
            stg = [k.sb(f"stg{i}", [128, 512], F32, st1) for i in range(4)]
            load_wbc(0)
            si = 0

            def p1_block(c0, wb, wvw, tbs):
                nonlocal si
                ncol = min(512, INC - c0)
                for g0 in range(0, ncol, 128):
                    M = min(128, ncol - g0)
                    for tb in tbs:
                        p = k.ps()
                        for kc in range(16):
                            k.mm(p.t[0:M, :], wvw[:, kc, g0:g0 + M], uT.t[:, kc, tb * 512:(tb + 1) * 512], kc == 0, kc == 15,
                                 [wb.b, uTb[tb]], [p.b])
                        sg_ = stg[si % 4]; si += 1
                        k.cp(sg_.t[0:M, :], p.t[0:M, :], [p.b], [sg_.b])
                        k.dma(k.sp, pT.t[c0 + g0:c0 + g0 + M, tb * 512:(tb + 1) * 512], sg_.t[0:M, :], [sg_.b], [pT.bl[(c0 + g0) // 128]], sg_.b)
            wb0, wvw0 = load_w(w_in[:, 0:512], 16)
            for tb in range(4):
                for n in range(4 * tb, 4 * tb + 4):
                    xt = xts[n % 2]
                    k.dma(k.sp, xt.t[:], x[n * 128:(n + 1) * 128, :], [], [xt.b], xt.b)
                    norm_transpose(xt.t[:], xt.b, uT.t, uTb[n // 4], slice(n * 128, (n + 1) * 128), n % 4)
                p1_block(0, wb0, wvw0, [tb])
            for c0 in range(512, INC, 512):
                ncol = min(512, INC - c0)
                wb, wvw = load_w(w_in[:, c0:c0 + ncol], 16)
                p1_block(c0, wb, wvw, range(4))
            if debug == "p1":
                for r0 in range(0, INC, 128):
                    M = min(128, INC - r0)
                    xt = xts[(r0 // 128) % 2]
                    k.dma(k.sp, xt.t[0:M, :], pT.t[r0:r0 + M, :], [pT.bl[r0 // 128]], [xt.b], xt.b)
                    k.dma(k.sp, dbg[r0:r0 + M, :], xt.t[0:M, :], [xt.b], [], xt.b)
                k._deps(k.sp, [], [xts[0].b, xts[1].b])
        if debug == "p1":
            k.es.close()
            return nc

    if debug != "p4":
        k.barrier()
        build_mixers(nc, k, pT, oTd, convw, dnsc, dnw, mu, rwv, lora, g2, ones, ident, bg_next)
        bg_next(99)
        if debug == "p2":
            k.barrier()
            with contextlib.ExitStack() as st:
                a = k.sb("dba", [128, S], BF16, st); b = k.sb("dbb", [128, S], F32, st)
                for r0 in range(0, D, 128):
                    k.dma(k.sp, a.t[:], oTd.t[r0:r0 + 128, :], [oTd.bl[r0 // 128]], [a.b], a.b)
                    k.cp(b.t[:], a.t[:], [a.b], [b.b])
                    k.dma(k.sp, dbg[r0:r0 + 128, :], b.t[:], [b.b], [], b.b)
                k._deps(k.sp, [], [b.b])
            k.es.close()
            return nc

    k.barrier()
    if debug == "p4":
        bg_next(99)
    st4 = contextlib.ExitStack()
    alloc_wbufs(st4, "b")
    alloc_norm(st4, "b")
    wbc, un, ss = G["wbc"], G["junk"], G["ss"]
    KT = k.sb("KT", [128, 4, MEM], BF16, st4)
    Vm = k.sb("Vm", [128, 2, 512], BF16, st4)
    hnT = k.sb("hnT", [128, 16, 512], BF16, st4)
    h = k.sb("h", [128, 4, D], F32, st4)
    hB = [Buf(f"h{j}") for j in range(4)]
    hid = k.sb("hid", [128, 64, 512], BF16, st4)
    hidB = [Buf(f"hid{i}") for i in range(16)]
    qT = k.sb("qT", [128, 4, 512], BF16, st4)
    oxT = k.sb("oxT", [128, 4, 512], BF16, st4)
    atw = [dict(pr=k.sb(f"pr{i}", [128, MEM], F32, st4), prn=k.sb(f"prn{i}", [128, MEM], BF16, st4),
                prT=k.sb(f"prT{i}", [128, 2, 128], BF16, st4), sm=k.sb(f"sm{i}", [128, 4], F32, st4)) for i in range(4)]

    def run_rr(gens):
        gens = list(gens)
        while gens:
            for g_ in list(gens):
                try:
                    next(g_)
                except StopIteration:
                    gens.remove(g_)
    rl = [k.sb(f"rl{i}", [128, 512], F32, st4) for i in range(2)]
    memt = [k.sb(f"memt{i}", [128, D], F32, st4) for i in range(1)]

    load_wbc(2)
    for mt in range(2):
        m_ = memt[0]
        k.dma(k.sp, m_.t[:], mem[mt * 128:(mt + 1) * 128, :], [], [m_.b], m_.b)
        norm_transpose(m_.t[:], m_.b, hnT.t, hnT.b, slice(mt * 128, (mt + 1) * 128), mt)
    wb, wvw = load_w(wk[:, :], 16)
    for hd in range(4):
        p = k.ps()
        for kc in range(16):
            k.mm(p.t[:, 0:MEM], wvw[:, kc, hd * 128:(hd + 1) * 128], hnT.t[:, kc, 0:MEM], kc == 0, kc == 15, [wb.b, hnT.b], [p.b])
        k.cp(KT.t[:, hd, :], p.t[:, 0:MEM], [p.b], [KT.b])
    wb, wvw = load_w(wv[:, :], 16)
    for mc in range(2):
        p = k.ps()
        for kc in range(16):
            k.mm(p.t[:, :], hnT.t[:, kc, mc * 128:(mc + 1) * 128], wvw[:, kc, :], kc == 0, kc == 15, [wb.b, hnT.b], [p.b])
        k.cp(Vm.t[:, mc, :], p.t[:, :], [p.b], [Vm.b])

    oTb_view = hid.t[:, 0:16, :]
    oTb_bufs = hidB[0:4]
    for TB in range(4):
        t0 = TB * 512
        if oT_dt == BF16:
            k.dma(k.pool, oTb_view, oTd.t[:, t0:t0 + 512].rearrange("(kc p) t -> p kc t", p=128), oTd.bl, oTb_bufs, oTb_bufs[0])
        else:
            k.dma(k.pool, oTb_view, oTd.t[:, t0:t0 + 512].rearrange("(kc p) t -> p kc t", p=128), oTd.bl, oTb_bufs, oTb_bufs[0])
        for cb in range(4):
            wb, wvw = load_w(w_out[:, cb * 512:(cb + 1) * 512], 16, (cb, TB == 0))
            if cb == 0:
                for j in range(4):
                    k.dma(k.sp, h.t[:, j, :], x[t0 + j * 128:t0 + (j + 1) * 128, :], [], [hB[j]], hB[j])
            for j in range(4):
                p = k.ps()
                for kc in range(16):
                    k.mm(p.t[:, :], oTb_view[:, kc, j * 128:(j + 1) * 128], wvw[:, kc, :], kc == 0, kc == 15, [wb.b] + oTb_bufs, [p.b])
                hs = h.t[:, j, cb * 512:(cb + 1) * 512]
                k.tt(k.dve, hs, p.t[:, :], hs, ALU.add, [p.b, hB[j]], [hB[j]])
        load_wbc(1)
        for j in range(4):
            norm_transpose(h.t[:, j, :], hB[j], hnT.t, hnT.b, slice(j * 128, (j + 1) * 128), j)
        wb, wvw = load_w(wq[:, :], 16, (4, TB == 0))
        for hd in range(4):
            p = k.ps()
            for kc in range(16):
                k.mm(p.t[:, :], wvw[:, kc, hd * 128:(hd + 1) * 128], hnT.t[:, kc, :], kc == 0, kc == 15, [wb.b, hnT.b], [p.b])
            k.actf(qT.t[:, hd, :], p.t[:, :], AF.Copy, [p.b], [qT.b], scale=128 ** -0.5)
        def attn_worker(w_):
            a_ = atw[w_]
            pr, prn, prT, sm = a_["pr"], a_["prn"], a_["prT"], a_["sm"]
            for idx in range(w_, 16, 4):
                j, hd = divmod(idx, 4)
                yield from k.need(1)
                p = k.psA()
                k.mm(p.t[:, 0:MEM], qT.t[:, hd, j * 128:(j + 1) * 128], KT.t[:, hd, :], True, True, [qT.b, KT.b], [p.b])
                yield
                k.op(k.dve, lambda e: e.tensor_reduce(out=sm.t[:, 0:1], in_=p.t[:, 0:MEM], axis=AX.X, op=ALU.max, negate=True),
                     [p.b], [sm.b])
                yield
                k.actf(pr.t[:], p.t[:, 0:MEM], AF.Exp, [p.b, sm.b], [pr.b, sm.b], bias=sm.t[:, 0:1], accum_out=sm.t[:, 1:2])
                k.psF(p)
                yield
                k.op(k.dve, lambda e: e.reciprocal(sm.t[:, 2:3], sm.t[:, 1:2]), [sm.b], [sm.b])
                yield
                k.ts(k.dve, prn.t[:], pr.t[:], sm.t[:, 2:3], None, ALU.mult, None, [pr.b, sm.b], [prn.b])
                yield
                yield from k.need(1)
                p2 = k.psA()
                pv = p2.t[:].bitcast(BF16)
                for mc in range(2):
                    k.tr(pv[:, mc * 128:(mc + 1) * 128], prn.t[:, mc * 128:(mc + 1) * 128], identb.t[:], [prn.b, identb.b], [p2.b])
                yield
                k.cp(prT.t[:], pv[:, 0:256].rearrange("p (m c) -> p m c", c=128), [p2.b], [prT.b])
                k.psF(p2)
                yield
                yield from k.need(1)
                p3 = k.psA()
                for mc in range(2):
                    k.mm(p3.t[:, 0:128], Vm.t[:, mc, hd * 128:(hd + 1) * 128], prT.t[:, mc, :], mc == 0, mc == 1, [Vm.b, prT.b], [p3.b])
                yield
                k.cp(oxT.t[:, hd, j * 128:(j + 1) * 128], p3.t[:, 0:128], [p3.b], [oxT.b])
                k.psF(p3)
                yield
        run_rr([attn_worker(w_) for w_ in range(4)])
        wb, wvw = load_w(wo[:, :], 4, (5, TB == 0))
        for cb in range(4):
            for j in range(4):
                p = k.ps()
                for kc in range(4):
                    k.mm(p.t[:, :], oxT.t[:, kc, j * 128:(j + 1) * 128], wvw[:, kc, cb * 512:(cb + 1) * 512], kc == 0, kc == 3, [wb.b, oxT.b], [p.b])
                hs = h.t[:, j, cb * 512:(cb + 1) * 512]
                k.tt(k.dve, hs, p.t[:, :], hs, ALU.add, [p.b, hB[j]], [hB[j]])
        load_wbc(3)
        for j in range(4):
            norm_transpose(h.t[:, j, :], hB[j], hnT.t, hnT.b, slice(j * 128, (j + 1) * 128), j)
        for fb in range(16):
            wb, wvw = load_w(w1[:, fb * 512:(fb + 1) * 512], 16, (6 + fb, TB == 0))
            for fc in range(4):
                p = k.ps()
                for kc in range(16):
                    k.mm(p.t[:, :], wvw[:, kc, fc * 128:(fc + 1) * 128], hnT.t[:, kc, :], kc == 0, kc == 15, [wb.b, hnT.b], [p.b])
                r_ = rl[(fb * 4 + fc) % 2]
                k.actf(r_.t[:], p.t[:, :], AF.Relu, [p.b], [r_.b])
                k.tt(k.dve, hid.t[:, fb * 4 + fc, :], r_.t[:], r_.t[:], ALU.mult, [r_.b], [hidB[fb]])
        for cb in range(4):
            accs = [k.ps() for _ in range(4)]
            for sub in range(4):
                wb, wvw = load_w(w2[sub * 2048:(sub + 1) * 2048, cb * 512:(cb + 1) * 512], 16, (22 + cb * 4 + sub, TB == 0))
                for j in range(4):
                    for fc in range(16):
                        f = sub * 16 + fc
                        k.mm(accs[j].t[:, :], hid.t[:, f, j * 128:(j + 1) * 128], wvw[:, fc, :], f == 0, f == 63,
                             [wb.b, hidB[f // 4]], [accs[j].b])
            for j in range(4):
                hs = h.t[:, j, cb * 512:(cb + 1) * 512]
                k.tt(k.dve, hs, accs[j].t[:, :], hs, ALU.add, [accs[j].b, hB[j]], [hB[j]])
        load_wbc(4)
        for j in range(4):
            hj = h.t[:, j, :]
            sb_ = G["ssb"][j]
            k.actf(un.t[:], hj, AF.Square, [hB[j]], [un.b, sb_], accum_out=ss.t[:, j:j + 1])
            k.actf(ss.t[:, j:j + 1], ss.t[:, j:j + 1], AF.Sqrt, [sb_, epsT.b], [sb_], scale=1.0 / D, bias=epsT.t[:, 0:1])
            k.op(k.dve, lambda e: e.reciprocal(ss.t[:, j:j + 1], ss.t[:, j:j + 1]), [sb_], [sb_])
            k.stt(hj, hj, ss.t[:, j:j + 1], wbc.t[:], ALU.mult, ALU.mult, [hB[j], sb_, wbc.b], [hB[j]])
        for j in range(4):
            k.dma(k.sp, out[t0 + j * 128:t0 + (j + 1) * 128, :], h.t[:, j, :], [hB[j]], [], hB[j])
    k._deps(k.sp, [], hB)
    st4.close()
    k.es.close()
    return nc


def build_mixers(nc, k, pT, oTd, convw_d, dnsc_d, dnw_d, mu_d, rwv_d, lora_d, g2_d, ones, ident, bg_next):
    st = contextlib.ExitStack()
    r = lambda ap: ap.bitcast(F32R)
    NB = 10
    big = [k.sb(f"big{i}", [128, S], F32, st) for i in range(NB)]
    free = list(range(NB))

    def balloc():
        return big[free.pop(0)]

    def bfree(*ts):
        for t_ in ts:
            free.append(big.index(t_))

    def sm(name, shape, dt=F32):
        return k.sb("m_" + name, shape, dt, st)

    Ls = sm("Ls", [128, 128]); Li = sm("Li", [128, 128]); UU = sm("UU", [128, 256])
    blk = sm("blk", [128, 128]); rmask = sm("rmask", [128, S], BF16); selh = sm("selh", [16, 128])
    epsG = sm("epsG", [128, 2])

    def asel(out, pat, cm, op, R, W):
        k.op(k.pool, lambda e: e.affine_select(out=out, in_=ones.t[:], pattern=pat, compare_op=op, fill=0.0, base=0,
                                               channel_multiplier=cm), [ones.b] + R, W)
    asel(Ls.t[:], [[-1, 128]], 1, ALU.is_gt, [], [Ls.b])
    k.ts(k.dve, Ls.t[:], Ls.t[:], -1.0, None, ALU.mult, None, [Ls.b], [Ls.b])
    Li2 = sm("Li2", [128, 256]); II2 = sm("II2", [128, 256])
    for i_ in range(2):
        asel(Li2.t[:, i_ * 128:(i_ + 1) * 128], [[-1, 128]], 1, ALU.is_ge, [], [Li2.b])
        k.cp(II2.t[:, i_ * 128:(i_ + 1) * 128], ident.t[:], [ident.b], [II2.b], eng=k.pool)
    asel(Li.t[:], [[-1, 128]], 1, ALU.is_ge, [], [Li.b])
    asel(UU.t[:, 0:128], [[1, 128]], -1, ALU.is_gt, [], [UU.b])
    asel(UU.t[:, 128:256], [[1, 128]], -1, ALU.is_ge, [], [UU.b])
    k.op(k.pool, lambda e: e.memset(blk.t[:], 0.0), [], [blk.b])
    k.op(k.pool, lambda e: e.memset(blk.t[0:64, 0:64], 1.0), [], [blk.b])
    k.op(k.pool, lambda e: e.memset(blk.t[64:128, 64:128], 1.0), [], [blk.b])
    k.op(k.pool, lambda e: e.memset(rmask.t[:], 1.0), [], [rmask.b])
    k.op(k.pool, lambda e: e.memset(rmask.t[:].rearrange("p (c t) -> p c t", t=128)[:, :, 0:1], 0.0), [], [rmask.b])
    k.op(k.pool, lambda e: e.memset(epsG.t[:, 0:1], 64e-5), [], [epsG.b])
    k.op(k.pool, lambda e: e.memset(epsG.t[:, 1:2], 1e-6), [], [epsG.b])
    convw = sm("convw", [128, 96]); dnsc = sm("dnsc", [16, 2]); dnw = sm("dnw", [128, 1])
    mu = sm("mu", [128, 26]); omm = sm("omm", [128, 26]); rwv = sm("rwv", [128, 56])
    for t_, d_ in ((convw, convw_d), (dnsc, dnsc_d), (dnw, dnw_d), (mu, mu_d), (rwv, rwv_d)):
        k.dma(k.sp, t_.t[:], d_, [], [t_.b], t_.b)
    k.ts(k.dve, omm.t[:], mu.t[:], -1.0, 1.0, ALU.mult, ALU.add, [mu.b], [omm.b])

    def sq(name, w=128):
        return sm(name, [128, w])
    def run_rr(gens):
        gens = list(gens)
        while gens:
            for g_ in list(gens):
                try:
                    next(g_)
                except StopIteration:
                    gens.remove(g_)

    def neumann_multi(probs, nlev):
        for pr in probs:
            pr["Ao"] = pr["A1"]; pr["BPo"] = pr["BP"][0]
        for lv in range(nlev):
            last = lv == nlev - 1
            yield from k.need((1 if last else 2) * len(probs))
            for pr in probs:
                Ao, BPo = pr["Ao"], pr["BPo"]
                pr["pa"] = k.psA()
                if last:
                    k.mm(pr["pa"].t[:, 0:128], r(Ao.t[:]), r(BPo.t[:, 128:256]), True, True, [Ao.b, BPo.b], [pr["pa"].b])
                else:
                    k.mm(pr["pa"].t[:, 0:256], r(Ao.t[:]), r(BPo.t[:]), True, True, [Ao.b, BPo.b], [pr["pa"].b])
                    pr["pb"] = k.psA()
                    k.mm(pr["pb"].t[:, 0:128], r(BPo.t[:, 0:128]), r(Ao.t[:]), True, True, [Ao.b, BPo.b], [pr["pb"].b])
            yield
            for pr in probs:
                BPo = pr["BPo"]
                if last:
                    k.tt(k.dve, r(pr["Tout"].t[:]), pr["pa"].t[:, 0:128], BPo.t[:, 128:256], ALU.add, [pr["pa"].b, BPo.b], [pr["Tout"].b])
                    k.psF(pr["pa"])
                else:
                    An, BPn = pr["Ap"][lv % 2], pr["BP"][(lv + 1) % 2]
                    k.cp(r(An.t[:]), pr["pb"].t[:, 0:128], [pr["pb"].b], [An.b], eng=k.act)
                    k.cp(r(BPn.t[:, 0:128]), pr["pa"].t[:, 0:128], [pr["pa"].b], [BPn.b], eng=k.dve)
                    k.tt(k.dve, r(BPn.t[:, 128:256]), pr["pa"].t[:, 128:256], BPo.t[:, 128:256], ALU.add, [pr["pa"].b, BPo.b], [BPn.b])
                    k.psF(pr["pa"], pr["pb"])
                    pr["Ao"], pr["BPo"] = An, BPn
            yield

    def neumann_pairs(pairs, nlev):
        for pr in pairs:
            pr["Ao"] = pr["A1_2"]; pr["BPo"] = pr["BP2"][0]
        for lv in range(nlev):
            last = lv == nlev - 1
            yield from k.need((1 if last else 2) * len(pairs))
            for pr in pairs:
                Ao, BPo = pr["Ao"], pr["BPo"]
                pr["pa"] = k.psA()
                if not last:
                    pr["pb"] = k.psA()
                for i in range(2):
                    a_i = r(Ao.t[:, i * 128:(i + 1) * 128])
                    if last:
                        k.mm(pr["pa"].t[:, i * 128:(i + 1) * 128], a_i, r(BPo.t[:, i * 256 + 128:(i + 1) * 256]), True, True,
                             [Ao.b, BPo.b], [pr["pa"].b])
                    else:
                        k.mm(pr["pa"].t[:, i * 256:(i + 1) * 256], a_i, r(BPo.t[:, i * 256:(i + 1) * 256]), True, True,
                             [Ao.b, BPo.b], [pr["pa"].b])
                        k.mm(pr["pb"].t[:, i * 128:(i + 1) * 128], r(BPo.t[:, i * 256:i * 256 + 128]), a_i, True, True,
                             [Ao.b, BPo.b], [pr["pb"].b])
            yield
            for pr in pairs:
                BPo = pr["BPo"]
                bpo3 = BPo.t[:].rearrange("p (i c) -> p i c", c=256)
                if last:
                    k.tt(k.dve, r(pr["Tout2"].t[:].rearrange("p (i c) -> p i c", c=128)),
                         pr["pa"].t[:, 0:256].rearrange("p (i c) -> p i c", c=128), bpo3[:, :, 128:256], ALU.add,
                         [pr["pa"].b, BPo.b], [pr["Tout2"].b])
                    k.psF(pr["pa"])
                else:
                    An, BPn = pr["Ap2"][lv % 2], pr["BP2"][(lv + 1) % 2]
                    bpn3 = BPn.t[:].rearrange("p (i c) -> p i c", c=256)
                    pa3 = pr["pa"].t[:, :].rearrange("p (i c) -> p i c", c=256)
                    k.cp(r(An.t[:]), pr["pb"].t[:, 0:256], [pr["pb"].b], [An.b], eng=k.act)
                    k.cp(r(bpn3[:, :, 0:128]), pa3[:, :, 0:128], [pr["pa"].b], [BPn.b], eng=k.act)
                    k.tt(k.dve, r(bpn3[:, :, 128:256]), pa3[:, :, 128:256], bpo3[:, :, 128:256], ALU.add, [pr["pa"].b, BPo.b], [BPn.b])
                    k.psF(pr["pa"], pr["pb"])
                    pr["Ao"], pr["BPo"] = An, BPn
            yield

    def load_rows(dst, r0, nrows=128):
        k.dma(k.sp, dst.t[0:nrows, :], pT.t[r0:r0 + nrows, :], pT.bl[r0 // 128:(r0 + nrows - 1) // 128 + 1], [dst.b], dst.b)

    ones_bf = sm("ones_bf", [128, 128], BF16); blk_bf = sm("blk_bf", [128, 128], BF16)
    k.cp(ones_bf.t[:], ones.t[:], [ones.b], [ones_bf.b], eng=k.dve)
    k.cp(blk_bf.t[:], blk.t[:], [blk.b], [blk_bf.b], eng=k.dve)

    def bfv(t_):
        return t_.t[:].bitcast(BF16)[:, 0:S]

    def psum_bcast_sum(src, lhsT, lhsTb, fn):
        sv = bfv(src)
        for tb in range(4):
            p = k.ps()
            k.mm(p.t[:, :], lhsT, sv[:, tb * 512:(tb + 1) * 512], True, True, [lhsTb, src.b], [p.b])
            fn(tb, p)

    obf = [sm("obf0", [128, S], BF16)] * 2
    octr = [0]

    gc16 = balloc()
    st_dn = contextlib.ExitStack()

    def smd(name, shape, dt=F32):
        return k.sb("m_" + name, shape, dt, st_dn)
    gcT = smd("gcT", [128, 256]); betaT = smd("betaT", [128, 256]); kdT = smd("kdT", [128, 256]); egT = smd("egT", [128, 256])
    bgT = smd("bgT", [128, 16, 8]); negA = smd("negA", [16, 1])
    if True:
        ab = balloc(); t1 = balloc(); t2 = balloc(); beta16 = balloc(); kd16 = balloc()
        R16 = slice(0, 16)
        load_rows(ab, 4096, 16)
        dtb = dnsc.t[:, 1:2]
        k.actf(t1.t[R16, :], ab.t[R16, :], AF.Abs, [ab.b, dnsc.b], [t1.b], bias=dtb)
        k.actf(t1.t[R16, :], t1.t[R16, :], AF.Exp, [t1.b], [t1.b], scale=-1.0)
        k.actf(t1.t[R16, :], t1.t[R16, :], AF.Ln, [t1.b, ones.b], [t1.b], bias=ones.t[0:16, 0:1])
        k.ts(k.dve, t2.t[R16, :], ab.t[R16, :], dtb, 0.0, ALU.add, ALU.max, [ab.b, dnsc.b], [t2.b])
        k.tt(k.dve, t1.t[R16, :], t1.t[R16, :], t2.t[R16, :], ALU.add, [t1.b, t2.b], [t1.b])
        k.actf(negA.t[:], dnsc.t[:, 0:1], AF.Exp, [dnsc.b], [negA.b])
        k.ts(k.dve, negA.t[:], negA.t[:], -1.0, None, ALU.mult, None, [negA.b], [negA.b])
        k.ts(k.dve, t1.t[R16, :], t1.t[R16, :], negA.t[:, 0:1], None, ALU.mult, None, [t1.b, negA.b], [t1.b])
        k.actf(beta16.t[R16, :], ab.t[R16, :], AF.Sigmoid, [ab.b], [beta16.b])
        k.op(k.dve, lambda e: e.tensor_tensor_scan(gc16.t[R16, :], rmask.t[R16, :], t1.t[R16, :], 0.0, ALU.mult, ALU.add),
             [rmask.b, t1.b], [gc16.b])
        for n in range(NT):
            cs = slice(n * 128, (n + 1) * 128)
            k.ts(k.dve, kd16.t[R16, cs], gc16.t[R16, cs], gc16.t[R16, n * 128 + 127:n * 128 + 128], None, ALU.subtract, None,
                 [gc16.b], [kd16.b])
        k.actf(kd16.t[R16, :], kd16.t[R16, :], AF.Exp, [kd16.b], [kd16.b], scale=-1.0)
        for src, dst in ((gc16, gcT), (beta16, betaT), (kd16, kdT)):
            p = k.ps()
            for n in range(NT):
                k.mm(p.t[:, n * 16:(n + 1) * 16], src.t[R16, n * 128:(n + 1) * 128], ident.t[0:16, 0:16], True, True, [src.b, ident.b], [p.b])
            k.cp(dst.t[:], p.t[:, 0:256], [p.b], [dst.b])
        k.actf(egT.t[:], gcT.t[:], AF.Exp, [gcT.b], [egT.b])
        k.tt(k.dve, bgT.t[:], betaT.t[:].rearrange("p (n r) -> p n r", r=16)[:, :, 8:16],
             egT.t[:].rearrange("p (n r) -> p n r", r=16)[:, :, 0:8], ALU.mult, [betaT.b, egT.b], [bgT.b])
        bfree(ab, t1, t2, beta16, kd16)
    ngcT = smd("ngcT", [128, 256])
    k.ts(k.dve, ngcT.t[:], gcT.t[:], -1.0, None, ALU.mult, None, [gcT.b], [ngcT.b])
    ngcT3 = ngcT.t[:].rearrange("p (n r) -> p n r", r=16)
    gcT3 = gcT.t[:].rearrange("p (n r) -> p n r", r=16)
    betaT3 = betaT.t[:].rearrange("p (n r) -> p n r", r=16)
    kdT3 = kdT.t[:].rearrange("p (n r) -> p n r", r=16)

    WDN = 6

    def sqd(name, w=128):
        return k.sb("m_" + name, [128, w], F32, st_dn)
    St = [sqd("St0"), sqd("St1")]
    qTr = sqd("qTr", S); kTr = sqd("kTr", S); qgr = sqd("qgr", S)
    dnw_t = []
    WDP = 4
    for w_ in range(WDP):
        d_ = {nm: sqd(f"{nm}{w_}", 256) for nm in ("t1_2", "El_2", "MA_2", "MD_2", "at_2", "attnT_2", "nwT_2", "A1_2", "Tout2")}
        d_["Ap2"] = [sqd(f"Ap20_{w_}", 256), sqd(f"Ap21_{w_}", 256)]; d_["BP2"] = [sqd(f"BP20_{w_}", 512), sqd(f"BP21_{w_}", 512)]
        d_["c"] = [{nm: sqd(f"{nm}{w_}_{i_}") for nm in ("kbg", "kd", "vb", "vnew")} for i_ in range(2)]
        dnw_t.append(d_)

    def conv_silu(xr, gi):
        c = balloc()
        w = lambda j: convw.t[:, gi * 4 + j:gi * 4 + j + 1]
        k.ts(k.dve, c.t[:], xr.t[:], w(3), None, ALU.mult, None, [xr.b, convw.b], [c.b])
        for sh in (1, 2, 3):
            k.stt(c.t[:, sh:S], xr.t[:, 0:S - sh], w(3 - sh), c.t[:, sh:S], ALU.mult, ALU.add, [xr.b, convw.b, c.b], [c.b])
        k.actf(c.t[:], c.t[:], AF.Silu, [c.b], [c.b])
        bfree(xr)
        return c

    def l2n(xc, scale, dst):
        sq_ = balloc(); rn = balloc()
        k.actf(bfv(sq_), xc.t[:], AF.Square, [xc.b], [sq_.b])

        def fn(tb, p):
            ts_ = slice(tb * 512, (tb + 1) * 512)
            k.actf(rn.t[:, ts_], p.t[:, :], AF.Ln, [p.b, epsG.b], [rn.b], bias=epsG.t[:, 1:2])
        psum_bcast_sum(sq_, ones_bf.t[:], ones_bf.b, fn)
        k.actf(rn.t[:], rn.t[:], AF.Exp, [rn.b], [rn.b], scale=-0.5)
        k.stt(r(dst.t[:]), xc.t[:], scale, rn.t[:], ALU.mult, ALU.mult, [xc.b, rn.b], [dst.b])
        bfree(sq_, rn, xc)

    pre_ld = None
    for h in range(DBG_DN):
        if pre_ld is None:
            qr = balloc(); load_rows(qr, h * 128)
            kr = balloc(); load_rows(kr, 1024 + h * 128)
            vr = balloc(); load_rows(vr, 2048 + h * 128)
        else:
            qr, kr, vr = pre_ld
            pre_ld = None
        if DBG_STEP == 0:
            st.close(); return
        qT = conv_silu(qr, h); kT = conv_silu(kr, 8 + h); vT = conv_silu(vr, 16 + h)
        if DBG_STEP == 1:
            st.close(); return
        l2n(qT, 128 ** -0.5, qTr); l2n(kT, 1.0, kTr)
        qT, kT = qTr, kTr
        if DBG_STEP == 2:
            st.close(); return
        gcb = balloc(); egcb = balloc()

        def fn(tb, p):
            ts_ = slice(tb * 512, (tb + 1) * 512)
            k.cp(gcb.t[:, ts_], p.t[:, :], [p.b], [gcb.b], eng=k.dve)
            k.actf(egcb.t[:, ts_], p.t[:, :], AF.Exp, [p.b], [egcb.b])
        k.ts(k.dve, selh.t[:], ones.t[0:16, :], ident.t[0:16, h:h + 1], None, ALU.mult, None, [ones.b, ident.b], [selh.b])
        for tb in range(4):
            p = k.ps()
            k.mm(p.t[:, :], selh.t[:], gc16.t[0:16, tb * 512:(tb + 1) * 512], True, True, [selh.b, gc16.b], [p.b])
            fn(tb, p)
        qg = qgr
        k.tt(k.dve, r(qg.t[:]), qT.t[:], egcb.t[:], ALU.mult, [qT.b, egcb.b], [qg.b])
        oT = balloc()
        if DBG_STEP == 3:
            st.close(); return
        k.ts(k.dve, r(St[0].t[:]), ident.t[:], 0.0, None, ALU.mult, None, [ident.b], [St[0].b])
        seq_done = [0]

        def dn_worker(w_, h=h, qT=qT, kT=kT, vT=vT, gcb=gcb, egcb=egcb, qg=qg, oT=oT, seq_done=seq_done):
            d_ = dnw_t[w_]
            C = d_["c"]
            H2 = [slice(0, 128), slice(128, 256)]
            for n0 in range(2 * w_, DBG_CH, 2 * WDP):
                ns = [n0, n0 + 1]
                css = [slice(n * 128, (n + 1) * 128) for n in ns]
                yield from k.need(2)
                pkt = k.psA(); pvt = k.psA()
                for i, n in enumerate(ns):
                    k.tr(pkt.t[:, H2[i]], kT.t[:, css[i]], ident.t[:], [kT.b, ident.b], [pkt.b])
                    k.tr(pvt.t[:, H2[i]], vT.t[:, css[i]], ident.t[:], [vT.b, ident.b], [pvt.b])
                    k.actf(d_["t1_2"].t[:, H2[i]], gcb.t[:, css[i]], AF.Relu, [gcb.b, ngcT.b], [d_["t1_2"].b], bias=ngcT3[:, n, h:h + 1])
                yield
                for i, n in enumerate(ns):
                    k.actf(r(C[i]["kbg"].t[:]), pkt.t[:, H2[i]], AF.Copy, [pkt.b, bgT.b], [C[i]["kbg"].b], scale=bgT.t[:, n, h:h + 1])
                    k.actf(r(C[i]["kd"].t[:]), pkt.t[:, H2[i]], AF.Copy, [pkt.b, kdT.b], [C[i]["kd"].b], scale=kdT3[:, n, h:h + 1])
                    k.ts(k.dve, r(C[i]["vb"].t[:]), pvt.t[:, H2[i]], betaT3[:, n, 8 + h:9 + h], None, ALU.mult, None, [pvt.b, betaT.b], [C[i]["vb"].b])
                k.psF(pkt, pvt)
                k.actf(d_["El_2"].t[:], d_["t1_2"].t[:], AF.Exp, [d_["t1_2"].b], [d_["El_2"].b], scale=-1.0)
                yield
                yield from k.need(2)
                pk = k.psA(); pq = k.psA()
                for i, n in enumerate(ns):
                    k.mm(pk.t[:, H2[i]], r(kT.t[:, css[i]]), r(kT.t[:, css[i]]), True, True, [kT.b], [pk.b])
                    k.mm(pq.t[:, H2[i]], r(qT.t[:, css[i]]), r(kT.t[:, css[i]]), True, True, [qT.b, kT.b], [pq.b])
                    k.stt(d_["MA_2"].t[:, H2[i]], d_["El_2"].t[:, H2[i]], betaT3[:, n, 8 + h:9 + h], Ls.t[:], ALU.mult, ALU.mult,
                          [d_["El_2"].b, betaT.b, Ls.b], [d_["MA_2"].b])
                k.tt(k.pool, d_["MD_2"].t[:], d_["El_2"].t[:], Li2.t[:], ALU.mult, [d_["El_2"].b, Li2.b], [d_["MD_2"].b])
                yield
                k.tt(k.dve, r(d_["A1_2"].t[:]), pk.t[:, 0:256], d_["MA_2"].t[:], ALU.mult, [pk.b, d_["MA_2"].b], [d_["A1_2"].b])
                k.tt(k.dve, d_["at_2"].t[:], pq.t[:, 0:256], d_["MD_2"].t[:], ALU.mult, [pq.b, d_["MD_2"].b], [d_["at_2"].b])
                k.psF(pk, pq)
                yield
                yield from k.need(2)
                pa = k.psA(); pb = k.psA()
                for i in range(2):
                    k.tr(pb.t[:, H2[i]], d_["A1_2"].t[:, H2[i]], ident.t[:], [d_["A1_2"].b, ident.b], [pb.b])
                    k.tr(pa.t[:, H2[i]], d_["at_2"].t[:, H2[i]], ident.t[:], [d_["at_2"].b, ident.b], [pa.b])
                yield
                bp3 = d_["BP2"][0].t[:].rearrange("p (i c) -> p i c", c=256)
                k.cp(r(bp3[:, :, 0:128]), pb.t[:, 0:256].rearrange("p (i c) -> p i c", c=128), [pb.b], [d_["BP2"][0].b], eng=k.act)
                k.cp(r(bp3[:, :, 128:256]), II2.t[:].rearrange("p (i c) -> p i c", c=128), [II2.b], [d_["BP2"][0].b], eng=k.pool)
                k.cp(r(d_["attnT_2"].t[:]), pa.t[:, 0:256], [pa.b], [d_["attnT_2"].b], eng=k.act)
                k.psF(pa, pb)
                yield
                yield from neumann_pairs([d_], 7)
                yield from k.need(1)
                pw = k.psA()
                for i in range(2):
                    k.mm(pw.t[:, H2[i]], r(C[i]["kbg"].t[:]), r(d_["Tout2"].t[:, H2[i]]), True, True, [C[i]["kbg"].b, d_["Tout2"].b], [pw.b])
                yield
                k.actf(r(d_["nwT_2"].t[:]), pw.t[:, 0:256], AF.Copy, [pw.b], [d_["nwT_2"].b], scale=-1.0)
                k.psF(pw)
                yield
                for i, n in enumerate(ns):
                    while seq_done[0] < n:
                        yield
                    cs = css[i]
                    So, Sn = St[n % 2], St[(n + 1) % 2]
                    yield from k.need(1)
                    pv = k.psA()
                    k.mm(pv.t[:, 0:128], r(d_["Tout2"].t[:, H2[i]]), r(C[i]["vb"].t[:]), True, False, [d_["Tout2"].b, C[i]["vb"].b], [pv.b])
                    k.mm(pv.t[:, 0:128], r(d_["nwT_2"].t[:, H2[i]]), r(So.t[:]), False, True, [d_["nwT_2"].b, So.b], [pv.b])
                    vn = C[i]["vnew"]
                    k.cp(r(vn.t[:]), pv.t[:, 0:128], [pv.b], [vn.b], eng=k.act)
                    k.psF(pv)
                    yield from k.need(2)
                    po = k.psA(); pS = k.psA()
                    k.mm(pS.t[:, 0:128], r(C[i]["kd"].t[:]), r(vn.t[:]), True, True, [C[i]["kd"].b, vn.b], [pS.b])
                    k.mm(po.t[:, 0:128], r(So.t[:]), r(qg.t[:, cs]), True, False, [So.b, qg.b], [po.b])
                    k.mm(po.t[:, 0:128], r(vn.t[:]), r(d_["attnT_2"].t[:, H2[i]]), False, True, [vn.b, d_["attnT_2"].b], [po.b])
                    k.stt(r(Sn.t[:]), So.t[:], egcb.t[:, n * 128 + 127:n * 128 + 128], pS.t[:, 0:128], ALU.mult, ALU.add,
                          [So.b, egcb.b, pS.b], [Sn.b])
                    k.cp(oT.t[:, cs], po.t[:, 0:128], [po.b], [oT.b], eng=k.act)
                    k.psF(po, pS)
                    seq_done[0] = n + 1
                    yield
        zr = balloc(); load_rows(zr, 3072 + h * 128)
        bg_next(3)
        run_rr([dn_worker(w_) for w_ in range(WDP)])
        if DBG_STEP == 14:
            st.close(); return
        bfree(vT, gcb, egcb)
        if h + 1 < DBG_DN:
            nq = balloc(); load_rows(nq, (h + 1) * 128)
            nk = balloc(); load_rows(nk, 1024 + (h + 1) * 128)
            nv = balloc(); load_rows(nv, 2048 + (h + 1) * 128)
            pre_ld = (nq, nk, nv)
        sq_ = balloc(); rn = balloc()
        k.actf(bfv(sq_), oT.t[:], AF.Square, [oT.b], [sq_.b])

        def fn2(tb, p):
            ts_ = slice(tb * 512, (tb + 1) * 512)
            k.actf(rn.t[:, ts_], p.t[:, :], AF.Ln, [p.b, epsG.b], [rn.b], bias=epsG.t[:, 1:2], scale=1.0 / 128)
        psum_bcast_sum(sq_, ones_bf.t[:], ones_bf.b, fn2)
        k.actf(rn.t[:], rn.t[:], AF.Exp, [rn.b], [rn.b], scale=-0.5)
        k.actf(zr.t[:], zr.t[:], AF.Silu, [zr.b], [zr.b])
        k.stt(oT.t[:], oT.t[:], dnw.t[:, 0:1], rn.t[:], ALU.mult, ALU.mult, [oT.b, dnw.b, rn.b], [oT.b])
        ob = obf[octr[0] % 2]; octr[0] += 1
        k.tt(k.dve, ob.t[:], oT.t[:], zr.t[:], ALU.mult, [oT.b, zr.b], [ob.b])
        k.dma(k.pool, oTd.t[h * 128:(h + 1) * 128, :], ob.t[:], [ob.b], [oTd.bl[h]], ob.b)
        bfree(zr, sq_, rn, oT)
    bfree(gc16)
    st_dn.close()
    k.barrier()

    wa = balloc(); sg = balloc()
    RW0 = DNC

    def lerp(xr, gi):
        t_ = balloc()
        k.op(k.pool, lambda e: e.memset(t_.t[:, 0:1], 0.0), [], [t_.b])
        k.ts(k.dve, t_.t[:, 1:S], xr.t[:, 0:S - 1], mu.t[:, gi:gi + 1], None, ALU.mult, None, [xr.b, mu.b], [t_.b])
        k.stt(xr.t[:], xr.t[:], omm.t[:, gi:gi + 1], t_.t[:], ALU.mult, ALU.add, [xr.b, omm.b, t_.b], [xr.b])
        bfree(t_)
    load_rows(wa, RW0 + 3072); lerp(wa, 24)
    load_rows(sg, RW0 + 3200); lerp(sg, 25)
    k.actf(wa.t[0:64, :], wa.t[0:64, :], AF.Tanh, [wa.b], [wa.b])
    k.actf(sg.t[:], sg.t[:], AF.Sigmoid, [sg.b], [sg.b])

    WRW = 4
    st_rw = contextlib.ExitStack()
    lora = k.sb("m_lora", [128, 1024], F32, st_rw); g2 = k.sb("m_g2", [128, 1024], F32, st_rw)
    for t_, d_ in ((lora, lora_d), (g2, g2_d)):
        k.dma(k.sp, t_.t[:], d_, [], [t_.b], t_.b)

    def sqr(name, w=128):
        return k.sb("m_" + name, [128, w], F32, st_rw)
    Ht = [sqr("Ht0", 128), sqr("Ht1", 128)]
    rww_t = []
    for w_ in range(WRW):
        d_ = {nm: sqr(f"r{nm}{w_}") for nm in ("e1", "e2", "e3", "e4", "Bt", "Kt", "bh", "kh", "rhs1", "AVc", "KVc", "YVc", "Gt")}
        for nm in ("BhP", "KhP", "VP", "UP"):
            t_ = sqr(f"r{nm}2_{w_}", 256)
            d_[nm + "2"] = t_
            k.ts(k.dve, r(t_.t[:]), II2.t[:], 0.0, None, ALU.mult, None, [II2.b], [t_.b])
            d_[nm] = [TV(t_.t[:, 0:128], t_.b), TV(t_.t[:, 64:192], t_.b)]
        d_["ar"] = sqr(f"rar{w_}", 256)
        d_["hd"] = []
        for hh in range(2):
            e_ = {}
            e_["mb"] = sqr(f"rmb{w_}_{hh}", 256); e_["mk"] = sqr(f"rmk{w_}_{hh}", 256)
            d_["hd"].append(e_)
        d_["A1_2"] = sqr(f"rA12_{w_}", 256); d_["Tout2"] = sqr(f"rTr2_{w_}", 256)
        d_["Ap2"] = [sqr(f"rAp20_{w_}", 256), sqr(f"rAp21_{w_}", 256)]
        d_["BP2"] = [sqr(f"rBP20_{w_}", 512), sqr(f"rBP21_{w_}", 512)]
        rww_t.append(d_)
    V = lambda j: rwv.t[:, j * 8:(j + 1) * 8]

    for g in range(DBG_RW):
        rT = balloc(); load_rows(rT, RW0 + g * 128); lerp(rT, g)
        kl = balloc(); load_rows(kl, RW0 + 1024 + g * 128); lerp(kl, 8 + g)
        vT = balloc(); load_rows(vT, RW0 + 2048 + g * 128); lerp(vT, 16 + g)
        sig = balloc(); a_ = balloc()
        gsl = slice(g * 128, (g + 1) * 128)
        for tb in range(4):
            ts_ = slice(tb * 512, (tb + 1) * 512)
            p = k.ps()
            k.mm(p.t[:, :], lora.t[0:64, gsl], wa.t[0:64, ts_], True, True, [lora.b, wa.b], [p.b])
            k.actf(sig.t[:, ts_], p.t[:, :], AF.Sigmoid, [p.b, rwv.b], [sig.b], bias=V(0)[:, g:g + 1])
            p = k.ps()
            k.mm(p.t[:, :], lora.t[64:128, gsl], wa.t[64:128, ts_], True, True, [lora.b, wa.b], [p.b])
            k.actf(a_.t[:, ts_], p.t[:, :], AF.Sigmoid, [p.b, rwv.b], [a_.b], bias=V(1)[:, g:g + 1])
        kk = balloc(); sq_ = balloc(); rn = balloc()
        k.ts(k.dve, kk.t[:], kl.t[:], V(2)[:, g:g + 1], None, ALU.mult, None, [kl.b, rwv.b], [kk.b])
        k.actf(bfv(sq_), kk.t[:], AF.Square, [kk.b], [sq_.b])

        def fnk(tb, p):
            ts_ = slice(tb * 512, (tb + 1) * 512)
            k.ts(k.dve, rn.t[:, ts_], p.t[:, :], 1e-24, None, ALU.max, None, [p.b], [rn.b])
        psum_bcast_sum(sq_, blk_bf.t[:], blk_bf.b, fnk)
        k.actf(rn.t[:], rn.t[:], AF.Ln, [rn.b], [rn.b])
        k.actf(rn.t[:], rn.t[:], AF.Exp, [rn.b], [rn.b], scale=-0.5)
        k.tt(k.dve, kk.t[:], kk.t[:], rn.t[:], ALU.mult, [kk.b, rn.b], [kk.b])
        bfree(sq_, rn)
        kf = balloc()
        k.ts(k.dve, kf.t[:], a_.t[:], -1.0, V(3)[:, g:g + 1], ALU.add, ALU.mult, [a_.b, rwv.b], [kf.b])
        k.stt(kf.t[:], kf.t[:], 1.0, kl.t[:], ALU.add, ALU.mult, [kf.b, kl.b], [kf.b])
        bT = balloc()
        k.tt(k.dve, bT.t[:], a_.t[:], kk.t[:], ALU.mult, [a_.b, kk.b], [bT.b])
        bfree(kl, a_)
        cum = balloc()
        k.op(k.dve, lambda e: e.tensor_tensor_scan(cum.t[:], rmask.t[:], sig.t[:], 0.0, ALU.mult, ALU.add), [rmask.b, sig.b], [cum.b])
        yT = balloc()
        k.ts(k.dve, r(Ht[0].t[:]), ident.t[:], 0.0, None, ALU.mult, None, [ident.b], [Ht[0].b])
        seq_done = [0]

        def rw_worker(w_, rT=rT, vT=vT, kk=kk, kf=kf, bT=bT, sig=sig, cum=cum, yT=yT, seq_done=seq_done):
            d_ = rww_t[w_]
            ar = d_["ar"]; e1 = d_["e1"]; e2 = d_["e2"]; e3 = d_["e3"]; e4 = d_["e4"]
            Bt_, Kt_, bh, kh = d_["Bt"], d_["Kt"], d_["bh"], d_["kh"]
            BhP, KhP, VP, UP, rhs1 = d_["BhP"], d_["KhP"], d_["VP"], d_["UP"], d_["rhs1"]
            HD = d_["hd"]
            RS = [slice(0, 64), slice(64, 128)]
            for n in range(w_, DBG_CH, WRW):
                cs = slice(n * 128, (n + 1) * 128)
                k.actf(e1.t[:], cum.t[:, cs], AF.Exp, [cum.b], [e1.b], scale=CDEC)
                k.actf(e2.t[:], cum.t[:, cs], AF.Exp, [cum.b], [e2.b], scale=-CDEC)
                k.tt(k.pool, e3.t[:], cum.t[:, cs], sig.t[:, cs], ALU.subtract, [cum.b, sig.b], [e3.b])
                k.ts(k.dve, e4.t[:], cum.t[:, cs], cum.t[:, n * 128 + 127:n * 128 + 128], None, ALU.subtract, None, [cum.b], [e4.b])
                yield
                k.actf(e3.t[:], e3.t[:], AF.Exp, [e3.b], [e3.b], scale=CDEC)
                k.actf(e4.t[:], e4.t[:], AF.Exp, [e4.b], [e4.b], scale=-CDEC)
                k.tt(k.pool, r(ar.t[:, 128:256]), rT.t[:, cs], e1.t[:], ALU.mult, [rT.b, e1.b], [ar.b])
                k.tt(k.pool, r(Bt_.t[:]), bT.t[:, cs], e2.t[:], ALU.mult, [bT.b, e2.b], [Bt_.b])
                k.tt(k.pool, r(Kt_.t[:]), kf.t[:, cs], e2.t[:], ALU.mult, [kf.b, e2.b], [Kt_.b])
                yield
                k.tt(k.dve, r(ar.t[:, 0:128]), kk.t[:, cs], e3.t[:], ALU.mult, [kk.b, e3.b], [ar.b])
                k.tt(k.pool, bh.t[:], bT.t[:, cs], e4.t[:], ALU.mult, [bT.b, e4.b], [bh.b])
                k.tt(k.pool, kh.t[:], kf.t[:, cs], e4.t[:], ALU.mult, [kf.b, e4.b], [kh.b])
                yield
                yield from k.need(3)
                trs = []
                for src, srcB, dst in ((bh.t[:], bh.b, d_["BhP2"]), (kh.t[:], kh.b, d_["KhP2"]), (vT.t[:, cs], vT.b, d_["VP2"])):
                    p = k.psA()
                    k.tr(p.t[:, 0:128], src, ident.t[:], [srcB, ident.b], [p.b])
                    trs.append((p, dst))
                yield
                for ti_, (p, dst2) in enumerate(trs):
                    k.cp(r(dst2.t[:].rearrange("p (i c) -> p i c", c=128)[:, :, 0:64]), p.t[:, 0:128].rearrange("p (i c) -> p i c", c=64),
                         [p.b], [dst2.b], eng=k.act)
                    k.psF(p)
                yield from k.need(4)
                pms = []
                for hh in range(2):
                    R = RS[hh]
                    pm = k.psA(); pm2 = k.psA()
                    k.mm(pm.t[:, 0:256], r(Bt_.t[R, :]), r(ar.t[R, :]), True, True, [Bt_.b, ar.b], [pm.b])
                    k.mm(pm2.t[:, 0:256], r(Kt_.t[R, :]), r(ar.t[R, :]), True, True, [Kt_.b, ar.b], [pm2.b])
                    pms.append((pm, pm2))
                yield
                for hh in range(2):
                    pm, pm2 = pms[hh]
                    k.tt(k.dve, r(HD[hh]["mb"].t[:]), pm.t[:, 0:256], UU.t[:], ALU.mult, [pm.b, UU.b], [HD[hh]["mb"].b])
                    k.tt(k.dve, r(HD[hh]["mk"].t[:]), pm2.t[:, 0:256], UU.t[:], ALU.mult, [pm2.b, UU.b], [HD[hh]["mk"].b])
                    k.psF(pm, pm2)
                yield from k.need(2)
                pas = []
                for hh in range(2):
                    R = RS[hh]
                    pa = k.psA()
                    k.mm(pa.t[:, 0:128], r(ar.t[R, 0:128]), r(Bt_.t[R, :]), True, True, [ar.b, Bt_.b], [pa.b])
                    pas.append(pa)
                yield
                for hh in range(2):
                    e_ = HD[hh]
                    k.tt(k.dve, r(d_["A1_2"].t[:, hh * 128:(hh + 1) * 128]), pas[hh].t[:, 0:128], Ls.t[:], ALU.mult,
                         [pas[hh].b, Ls.b], [d_["A1_2"].b])
                    k.actf(r(d_["BP2"][0].t[:, hh * 256:hh * 256 + 128]), e_["mb"].t[:, 0:128], AF.Copy, [e_["mb"].b], [d_["BP2"][0].b], scale=-1.0)
                    k.psF(pas[hh])
                yield
                k.cp(r(d_["BP2"][0].t[:].rearrange("p (i c) -> p i c", c=256)[:, :, 128:256]), II2.t[:].rearrange("p (i c) -> p i c", c=128),
                     [II2.b], [d_["BP2"][0].b], eng=k.pool)
                yield from k.need(3)
                pAV = k.psA(); pKV = k.psA(); pYV = k.psA()
                for hh in range(2):
                    e_ = HD[hh]
                    k.mm(pAV.t[:, 0:128], r(e_["mk"].t[:, 0:128]), r(VP[hh].t[:]), hh == 0, hh == 1, [e_["mk"].b, VP[hh].b], [pAV.b])
                    k.mm(pKV.t[:, RS[hh]], r(KhP[hh].t[:]), r(VP[hh].t[:, RS[hh]]), True, True, [KhP[hh].b, VP[hh].b], [pKV.b])
                    k.mm(pYV.t[:, 0:128], r(VP[hh].t[:]), r(e_["mk"].t[:, 128:256]), hh == 0, hh == 1, [VP[hh].b, e_["mk"].b], [pYV.b])
                yield
                k.cp(d_["AVc"].t[:], pAV.t[:, 0:128], [pAV.b], [d_["AVc"].b], eng=k.act)
                k.cp(d_["KVc"].t[:], pKV.t[:, 0:128], [pKV.b], [d_["KVc"].b], eng=k.act)
                k.cp(d_["YVc"].t[:], pYV.t[:, 0:128], [pYV.b], [d_["YVc"].b], eng=k.act)
                k.psF(pAV, pKV, pYV)
                yield
                yield from neumann_pairs([d_], 7)
                while seq_done[0] < n:
                    yield
                Ho, Hn = Ht[n % 2], Ht[(n + 1) % 2]
                yield from k.need(1)
                pr_ = k.psA()
                k.mm(pr_.t[:, 0:128], r(ar.t[:, 0:128]), r(Ho.t[:]), True, True, [ar.b, Ho.b], [pr_.b])
                k.stt(d_["Gt"].t[:], Ho.t[:], e1.t[:, 127:128], d_["KVc"].t[:], ALU.mult, ALU.add, [Ho.b, e1.b, d_["KVc"].b], [d_["Gt"].b])
                k.stt(r(rhs1.t[:]), pr_.t[:, 0:128], -1.0, d_["AVc"].t[:], ALU.mult, ALU.subtract, [pr_.b, d_["AVc"].b], [rhs1.b])
                k.psF(pr_)
                yield from k.need(1)
                pu = k.psA()
                for hh in range(2):
                    k.mm(pu.t[:, RS[hh]], r(d_["Tout2"].t[:, hh * 128:(hh + 1) * 128]), r(rhs1.t[:, RS[hh]]), True, True,
                         [d_["Tout2"].b, rhs1.b], [pu.b])
                k.cp(r(d_["UP2"].t[:].rearrange("p (i c) -> p i c", c=128)[:, :, 0:64]), pu.t[:, 0:128].rearrange("p (i c) -> p i c", c=64),
                     [pu.b], [d_["UP2"].b], eng=k.act)
                k.psF(pu)
                yield from k.need(2)
                pY = k.psA(); pS = k.psA()
                for hh in range(2):
                    k.mm(pS.t[:, RS[hh]], r(BhP[hh].t[:]), r(UP[hh].t[:, RS[hh]]), True, True, [BhP[hh].b, UP[hh].b], [pS.b])
                k.mm(pY.t[:, 0:128], r(Ho.t[:]), r(ar.t[:, 128:256]), True, False, [Ho.b, ar.b], [pY.b])
                for hh in range(2):
                    e_ = HD[hh]
                    k.mm(pY.t[:, 0:128], r(UP[hh].t[:]), r(e_["mb"].t[:, 128:256]), False, hh == 1, [UP[hh].b, e_["mb"].b], [pY.b])
                k.tt(k.dve, r(Hn.t[:]), pS.t[:, 0:128], d_["Gt"].t[:], ALU.add, [pS.b, d_["Gt"].b], [Hn.b])
                k.tt(k.dve, yT.t[:, cs], pY.t[:, 0:128], d_["YVc"].t[:], ALU.add, [pY.b, d_["YVc"].b], [yT.b])
                k.psF(pY, pS)
                seq_done[0] = n + 1
                yield
        bg_next(3)
        run_rr([rw_worker(w_) for w_ in range(WRW)])
        bfree(kk, bT, sig, cum)
        rk = balloc(); yc = balloc(); sq_ = balloc(); rs_ = balloc()
        k.stt(bfv(rk), rT.t[:], V(4)[:, g:g + 1], kf.t[:], ALU.mult, ALU.mult, [rT.b, rwv.b, kf.b], [rk.b])
        k.actf(bfv(sq_), yT.t[:], AF.Copy, [yT.b], [sq_.b])

        def fnm(tb, p):
            ts_ = slice(tb * 512, (tb + 1) * 512)
            k.stt(yc.t[:, ts_], p.t[:, :], -1.0 / 64, yT.t[:, ts_], ALU.mult, ALU.add, [p.b, yT.b], [yc.b])
        psum_bcast_sum(sq_, blk_bf.t[:], blk_bf.b, fnm)
        k.actf(bfv(sq_), yc.t[:], AF.Square, [yc.b], [sq_.b])

        def fnv(tb, p):
            ts_ = slice(tb * 512, (tb + 1) * 512)
            k.actf(rs_.t[:, ts_], p.t[:, :], AF.Ln, [p.b, epsG.b], [rs_.b], bias=epsG.t[:, 0:1], scale=1.0 / 64)
        psum_bcast_sum(sq_, blk_bf.t[:], blk_bf.b, fnv)
        k.actf(rs_.t[:], rs_.t[:], AF.Exp, [rs_.b], [rs_.b], scale=-0.5)
        k.tt(k.dve, yc.t[:], yc.t[:], rs_.t[:], ALU.mult, [yc.b, rs_.b], [yc.b])
        k.ts(k.dve, yc.t[:], yc.t[:], V(5)[:, g:g + 1], V(6)[:, g:g + 1], ALU.mult, ALU.add, [yc.b, rwv.b], [yc.b])

        def fnb(tb, p):
            ts_ = slice(tb * 512, (tb + 1) * 512)
            k.tt(k.dve, rs_.t[:, ts_], p.t[:, :], vT.t[:, ts_], ALU.mult, [p.b, vT.b], [rs_.b])
        psum_bcast_sum(rk, blk_bf.t[:], blk_bf.b, fnb)
        k.tt(k.dve, yc.t[:], yc.t[:], rs_.t[:], ALU.add, [yc.b, rs_.b], [yc.b])
        ob = obf[octr[0] % 2]; octr[0] += 1
        for tb in range(4):
            ts_ = slice(tb * 512, (tb + 1) * 512)
            p = k.ps()
            k.mm(p.t[:, :], g2.t[:, gsl], sg.t[:, ts_], True, True, [g2.b, sg.b], [p.b])
            k.tt(k.dve, ob.t[:, ts_], p.t[:, :], yc.t[:, ts_], ALU.mult, [p.b, yc.b], [ob.b])
        k.dma(k.pool, oTd.t[1024 + g * 128:1024 + (g + 1) * 128, :], ob.t[:], [ob.b], [oTd.bl[8 + g]], ob.b)
        bfree(rk, yc, sq_, rs_, rT, kf, vT, yT)
    bfree(wa, sg)
    st_rw.close()
    st.close()


def prep_shared(inp):
    f = lambda a: np.ascontiguousarray(np.asarray(a, dtype=np.float32))
    sh = {}
    for kk_ in ("w_in", "w_out", "xa_wq", "xa_wk", "xa_wv", "xa_wo", "ffn_w1", "ffn_w2"):
        sh[kk_] = f(inp[kk_][0])
    sh["norms"] = f(np.stack([inp["mix_norm_w"][0], inp["xa_norm_w"][0], inp["mem_norm_w"][0], inp["ffn_norm_w"][0],
                              inp["final_norm_w"]], axis=0))
    cw = np.asarray(inp["dn_conv_w"][0])
    sh["convw"] = f(cw.reshape(4, 24, 128).transpose(2, 1, 0).reshape(128, 96))
    dn = np.zeros((16, 2), np.float32)
    dn[0:8, 0] = np.asarray(inp["dn_a_log"][0]); dn[0:8, 1] = np.asarray(inp["dn_dt_bias"][0])
    sh["dnsc"] = dn
    sh["dnw"] = f(np.asarray(inp["dn_norm_w"][0]).reshape(128, 1))
    sh["mu"] = f(np.asarray(inp["rw_mu"][0]).reshape(26, 128).T)
    vs = [np.asarray(inp[n][0]).reshape(8, 128).T for n in ("rw_w0", "rw_a0", "rw_k_k", "rw_k_a", "rw_r_k", "rw_ln_w", "rw_ln_b")]
    sh["rwv"] = f(np.concatenate(vs, axis=1))
    sh["lora"] = f(np.concatenate([np.asarray(inp["rw_w2"][0]), np.asarray(inp["rw_a2"][0])], axis=0))
    sh["g2"] = f(inp["rw_g2"][0])
    return sh


def kernel(**inp):
    sh = prep_shared(inp)
    xs = np.asarray(inp["x"], dtype=np.float32)
    ms = np.asarray(inp["mem"], dtype=np.float32)
    nc = build()
    in_maps = []
    for b in range(8):
        m = dict(sh)
        m["x"] = np.ascontiguousarray(xs[b])
        m["mem"] = np.ascontiguousarray(ms[b])
        in_maps.append(m)
    res = run_bass_kernel_spmd(nc, in_maps, core_ids=list(range(8)))
    return np.stack([np.asarray(r["out"], dtype=np.float32) for r in res.results], axis=0)
```
